# Optimizing a Trainium2 kernel written in Bass

```python
import math
import jax
import jax.numpy as jnp
from jax import lax
import numpy as np

D_MODEL = 4096
BATCH = 4
SEQ = 4096
DEPTH = 2

CTX_LEN = 256
GRID_W = 64
MIX_WIDTH = D_MODEL
BRANCH_W = MIX_WIDTH // 4
CHUNK = 64
EPS = 1e-6
F32 = jnp.float32

A_HEAD_DIM = 128
A_HEADS = BRANCH_W // A_HEAD_DIM
A_MIN_FORGET = 1e-6
B_HEADS = 4
B_KEY_W = BRANCH_W // 2
B_DK = B_KEY_W // B_HEADS
B_DV = BRANCH_W // B_HEADS
B_GATE_RANK = 16
B_GATE_NORM = 16.0
C_GROUP = 16
C_GROUPS = BRANCH_W // C_GROUP
C_STATE = 64
C_MAX_RE = -1e-4
DT_MIN = 1e-3
DT_MAX = 1e-1
D_HEADS = 4
D_KEY_W = BRANCH_W // 2
D_DK = D_KEY_W // D_HEADS
D_DV = BRANCH_W // D_HEADS
ROPE_BASE = 10000.0

IN_SPLITS = (
    BRANCH_W, BRANCH_W, BRANCH_W, BRANCH_W, BRANCH_W,
    B_KEY_W, B_KEY_W, BRANCH_W, B_GATE_RANK, B_GATE_RANK, BRANCH_W,
    BRANCH_W, BRANCH_W,
    D_KEY_W, D_KEY_W, BRANCH_W, BRANCH_W,
)
IN_WIDTH = sum(IN_SPLITS)

kernel_name = 'hybrid_hgrn2_gla_s5_retention_prefix_dit'


def rms_norm(x, g):
    xf = x.astype(F32)
    y = xf * lax.rsqrt(jnp.mean(xf * xf, axis=-1, keepdims=True) + EPS)
    return (y * g.astype(F32)).astype(x.dtype)


def head_layer_norm(x, g):
    xf = x.astype(F32)
    mu = jnp.mean(xf, axis=-1, keepdims=True)
    var = jnp.mean(jnp.square(xf - mu), axis=-1, keepdims=True)
    return ((xf - mu) * lax.rsqrt(var + EPS) * g.astype(F32)).astype(x.dtype)


def seg_flip(z, n_ctx):
    return jnp.concatenate([jnp.flip(z[:, :n_ctx], axis=1), jnp.flip(z[:, n_ctx:], axis=1)], axis=1)


def to_chunks(a):
    bsz, l = a.shape[:2]
    return jnp.moveaxis(a.reshape(bsz, l // CHUNK, CHUNK, *a.shape[2:]), 1, 0)


def from_chunks(a):
    a = jnp.moveaxis(a, 0, 1)
    return a.reshape(a.shape[0], -1, *a.shape[3:])


def split_columns(proj):
    offsets = np.cumsum(IN_SPLITS)[:-1].tolist()
    return jnp.split(proj, offsets, axis=-1)


def axial_rope(rows, n_ctx):
    quarter = D_DK // 4
    freqs = ROPE_BASE ** (-jnp.arange(quarter, dtype=F32) / quarter)
    t = jnp.arange(rows * GRID_W)
    r = (t // GRID_W).astype(F32)
    col = (t % GRID_W).astype(F32)
    ang = jnp.concatenate([r[:, None] * freqs, col[:, None] * freqs], axis=-1)
    ang = jnp.concatenate([jnp.zeros((n_ctx, D_DK // 2), F32), ang], axis=0)
    return jnp.cos(ang), jnp.sin(ang)


def apply_rope(x, cos, sin):
    half = x.shape[-1] // 2
    x1 = x[..., :half].astype(F32)
    x2 = x[..., half:].astype(F32)
    cs = cos[None, :, None, :]
    sn = sin[None, :, None, :]
    return jnp.concatenate([x1 * cs - x2 * sn, x1 * sn + x2 * cs], axis=-1).astype(x.dtype)


def gla_chunked(q, k, v, log_decay):
    bsz, _, h, dk = q.shape
    dv = v.shape[-1]
    lower = jnp.tril(jnp.ones((CHUNK, CHUNK), dtype=bool))[None, :, :, None, None]

    def step(state, inp):
        qi, ki, vi, gi = inp
        qi = qi.astype(F32)
        ki = ki.astype(F32)
        vi = vi.astype(F32)
        b = jnp.cumsum(gi, axis=1)
        diff = jnp.where(lower, b[:, :, None] - b[:, None, :], 0.0)
        rel = jnp.where(lower, jnp.exp(diff), 0.0)
        scores = jnp.einsum('bihd,bjhd,bijhd->bhij', qi, ki, rel)
        b_last = b[:, -1]
        o = (jnp.einsum('bhij,bjhe->bihe', scores, vi)
             + jnp.einsum('bihd,bhde->bihe', qi * jnp.exp(b), state))
        state = (state * jnp.exp(b_last)[..., None]
                 + jnp.einsum('bjhd,bjhe->bhde', ki * jnp.exp(b_last[:, None] - b), vi))
        return state, o

    s0 = jnp.zeros((bsz, h, dk, dv), F32)
    _, o = lax.scan(step, s0, (to_chunks(q), to_chunks(k), to_chunks(v), to_chunks(log_decay.astype(F32))))
    return from_chunks(o).astype(v.dtype)


def retention_chunked(q, k, v, log_gamma):
    bsz, _, h, dk = q.shape
    dv = v.shape[-1]
    pos = jnp.arange(CHUNK, dtype=F32)
    lg = log_gamma.astype(F32)
    rel = pos[:, None] - pos[None, :]
    dmat = jnp.where(rel[None] >= 0, jnp.exp(lg[:, None, None] * jnp.maximum(rel, 0.0)[None]), 0.0)
    xi = jnp.exp(lg[None, :] * (pos[:, None] + 1.0))[None, :, :, None]
    zeta = jnp.exp(lg[None, :] * (CHUNK - 1.0 - pos[:, None]))[None, :, :, None]
    chunk_decay = jnp.exp(lg * CHUNK)[None, :, None, None]

    def step(state, inp):
        qi, ki, vi = inp
        qi = qi.astype(F32)
        ki = ki.astype(F32)
        vi = vi.astype(F32)
        scores = jnp.einsum('bihd,bjhd->bhij', qi, ki) * dmat[None]
        o = (jnp.einsum('bhij,bjhe->bihe', scores, vi)
             + jnp.einsum('bihd,bhde->bihe', qi, state) * xi)
        state = state * chunk_decay + jnp.einsum('bjhd,bjhe->bhde', ki * zeta, vi)
        return state, o

    s0 = jnp.zeros((bsz, h, dk, dv), F32)
    _, o = lax.scan(step, s0, (to_chunks(q), to_chunks(k), to_chunks(v)))
    return from_chunks(o).astype(v.dtype)


def s5_scan(u, lam_re, lam_im, log_dt, b_re, b_im, c_re, c_im):
    lam = lax.complex(jnp.minimum(lam_re.astype(F32), C_MAX_RE), lam_im.astype(F32))
    dt = jnp.exp(log_dt.astype(F32))[:, None]
    lam_bar = jnp.exp(lam * dt)
    b_bar = ((lam_bar - 1.0) / lam)[..., None] * lax.complex(b_re.astype(F32), b_im.astype(F32))
    bu = jnp.einsum('gph,blgh->blgp', b_bar, u.astype(F32).astype(jnp.complex64))
    a = jnp.broadcast_to(lam_bar, bu.shape)

    def combine(e1, e2):
        a1, b1 = e1
        a2, b2 = e2
        return a1 * a2, a2 * b1 + b2

    _, states = lax.associative_scan(combine, (a, bu), axis=1)
    c_mat = lax.complex(c_re.astype(F32), c_im.astype(F32))
    return jnp.einsum('ghp,blgp->blgh', c_mat, states).real


def hgrn2_branch(q, f_fwd, f_bwd, i, gate, lower_bound, norm_g, n_ctx):
    bsz, l, _ = q.shape

    def heads(a):
        return a.reshape(bsz, l, A_HEADS, A_HEAD_DIM)

    qh, ih = heads(q), heads(i)

    def direction(qd, zd, idd, lb):
        lb = lb.reshape(A_HEADS, A_HEAD_DIM).astype(F32)
        z = heads(zd).astype(F32)
        f = lb + (1.0 - lb) * jax.nn.sigmoid(z)
        log_f = jnp.log(jnp.maximum(f, A_MIN_FORGET))
        key = (1.0 - lb) * jax.nn.sigmoid(-z)
        return gla_chunked(qd, key.astype(qd.dtype), idd, log_f)

    o_f = direction(qh, f_fwd, ih, lower_bound[0])
    o_b = seg_flip(direction(seg_flip(qh, n_ctx), seg_flip(f_bwd, n_ctx), seg_flip(ih, n_ctx), lower_bound[1]), n_ctx)
    o = rms_norm(o_f + o_b, norm_g).reshape(bsz, l, BRANCH_W)
    return o * jax.nn.silu(gate)


def gla_branch(q, k, v, lr_fwd, lr_bwd, gate, w_gk, b_gk, norm_g, n_ctx):
    bsz, l, _ = q.shape
    qh = q.reshape(bsz, l, B_HEADS, B_DK) * (B_DK ** -0.5)
    kh = k.reshape(bsz, l, B_HEADS, B_DK)
    vh = v.reshape(bsz, l, B_HEADS, B_DV)

    def log_decay(lr, w, b):
        g = jax.nn.log_sigmoid((lr @ w + b).astype(F32)) / B_GATE_NORM
        return g.reshape(bsz, l, B_HEADS, B_DK)

    o_f = gla_chunked(qh, kh, vh, log_decay(lr_fwd, w_gk[0], b_gk[0]))
    o_b = seg_flip(gla_chunked(seg_flip(qh, n_ctx), seg_flip(kh, n_ctx), seg_flip(vh, n_ctx),
                               log_decay(seg_flip(lr_bwd, n_ctx), w_gk[1], b_gk[1])), n_ctx)
    o = rms_norm(o_f + o_b, norm_g).reshape(bsz, l, BRANCH_W)
    return o * jax.nn.silu(gate)


def s5_branch(u, gate, lam_re, lam_im, log_dt, b_re, b_im, c_re, c_im, d, w_glu, b_glu, n_ctx):
    bsz, l, _ = u.shape
    ug = u.reshape(bsz, l, C_GROUPS, C_GROUP)
    y_f = s5_scan(ug, lam_re[0], lam_im[0], log_dt[0], b_re[0], b_im[0], c_re[0], c_im[0])
    y_b = seg_flip(s5_scan(seg_flip(ug, n_ctx), lam_re[1], lam_im[1], log_dt[1],
                           b_re[1], b_im[1], c_re[1], c_im[1]), n_ctx)
    y = (y_f + y_b + d.astype(F32) * ug.astype(F32)).reshape(bsz, l, BRANCH_W).astype(u.dtype)
    z = jax.nn.gelu(y)
    out = z * jax.nn.sigmoid(z @ w_glu + b_glu)
    return out * jax.nn.silu(gate)


def retention_branch(q, k, v, gate, log_gamma, norm_g, cos, sin, n_ctx):
    bsz, l, _ = q.shape
    qh = apply_rope(q.reshape(bsz, l, D_HEADS, D_DK), cos, sin) * (D_DK ** -0.5)
    kh = apply_rope(k.reshape(bsz, l, D_HEADS, D_DK), cos, sin)
    vh = v.reshape(bsz, l, D_HEADS, D_DV)
    o_f = retention_chunked(qh, kh, vh, log_gamma[0])
    o_b = seg_flip(retention_chunked(seg_flip(qh, n_ctx), seg_flip(kh, n_ctx), seg_flip(vh, n_ctx), log_gamma[1]), n_ctx)
    o = head_layer_norm(o_f + o_b, norm_g).reshape(bsz, l, BRANCH_W)
    return o * jax.nn.silu(gate)


def setup_inputs(seed: int = 0) -> dict:
    key = jax.random.key(seed)
    ks = jax.random.split(key, 27)

    def nrm(k, shape, scale):
        return jax.random.normal(k, shape, F32) * scale

    n_idx = jnp.arange(C_STATE, dtype=F32)
    gammas = 1.0 - 2.0 ** (-5.0 - jnp.arange(D_HEADS, dtype=F32))
    gamma_logit = jnp.log(gammas) - jnp.log1p(-gammas)
    return {
        'x': nrm(ks[0], (BATCH, SEQ, D_MODEL), 1.0),
        'c': nrm(ks[1], (BATCH, D_MODEL), 1.0),
        'ctx': nrm(ks[2], (BATCH, CTX_LEN, D_MODEL), 1.0),
        'c_ctx': nrm(ks[3], (D_MODEL,), 1.0),
        'norm_g': 1.0 + nrm(ks[4], (DEPTH, D_MODEL), 0.02),
        'w_ada': nrm(ks[5], (DEPTH, D_MODEL, 3 * D_MODEL), 0.5 * D_MODEL ** -0.5),
        'b_ada': nrm(ks[6], (DEPTH, 3 * D_MODEL), 0.02),
        'w_in': nrm(ks[7], (DEPTH, D_MODEL, IN_WIDTH), D_MODEL ** -0.5),
        'hgrn_lb_logits': nrm(ks[8], (DEPTH, 2, BRANCH_W), 0.1),
        'hgrn_norm_g': 1.0 + nrm(ks[9], (DEPTH, A_HEAD_DIM), 0.02),
        'gla_w_gk': nrm(ks[10], (DEPTH, 2, B_GATE_RANK, B_KEY_W), B_GATE_RANK ** -0.5),
        'gla_b_gk': nrm(ks[11], (DEPTH, 2, B_KEY_W), 0.1),
        'gla_norm_g': 1.0 + nrm(ks[12], (DEPTH, B_DV), 0.02),
        's5_lam_re': -0.5 + nrm(ks[13], (DEPTH, 2, C_GROUPS, C_STATE), 0.01),
        's5_lam_im': math.pi * n_idx + nrm(ks[14], (DEPTH, 2, C_GROUPS, C_STATE), 0.01),
        's5_log_dt': jax.random.uniform(ks[15], (DEPTH, 2, C_GROUPS), F32, math.log(DT_MIN), math.log(DT_MAX)),
        's5_b_re': nrm(ks[16], (DEPTH, 2, C_GROUPS, C_STATE, C_GROUP), (2 * C_GROUP) ** -0.5),
        's5_b_im': nrm(ks[17], (DEPTH, 2, C_GROUPS, C_STATE, C_GROUP), (2 * C_GROUP) ** -0.5),
        's5_c_re': nrm(ks[18], (DEPTH, 2, C_GROUPS, C_GROUP, C_STATE), (2 * C_STATE) ** -0.5),
        's5_c_im': nrm(ks[19], (DEPTH, 2, C_GROUPS, C_GROUP, C_STATE), (2 * C_STATE) ** -0.5),
        's5_d': nrm(ks[20], (DEPTH, C_GROUPS, C_GROUP), 0.5),
        's5_w_glu': nrm(ks[21], (DEPTH, BRANCH_W, BRANCH_W), BRANCH_W ** -0.5),
        's5_b_glu': nrm(ks[22], (DEPTH, BRANCH_W), 0.02),
        'ret_decay_logit': gamma_logit + nrm(ks[23], (DEPTH, 2, D_HEADS), 0.01),
        'ret_norm_g': 1.0 + nrm(ks[24], (DEPTH, D_DV), 0.02),
        'w_out': nrm(ks[25], (DEPTH, MIX_WIDTH, D_MODEL), MIX_WIDTH ** -0.5),
        'final_norm_g': 1.0 + nrm(ks[26], (D_MODEL,), 0.02),
    }


def reference(x, c, ctx, c_ctx, norm_g, w_ada, b_ada, w_in, hgrn_lb_logits, hgrn_norm_g,
              gla_w_gk, gla_b_gk, gla_norm_g, s5_lam_re, s5_lam_im, s5_log_dt, s5_b_re, s5_b_im,
              s5_c_re, s5_c_im, s5_d, s5_w_glu, s5_b_glu, ret_decay_logit, ret_norm_g, w_out,
              final_norm_g):
    n_ctx = ctx.shape[1]
    rows = x.shape[1] // GRID_W
    cos, sin = axial_rope(rows, n_ctx)
    lb_p = jax.nn.softmax(hgrn_lb_logits.astype(F32), axis=0)
    lower_bounds = jnp.cumsum(lb_p, axis=0) - lb_p[0:1]

    h_ctx, h_lat = ctx, x
    for layer in range(DEPTH):
        last = layer == DEPTH - 1
        mod_lat = jax.nn.silu(c) @ w_ada[layer] + b_ada[layer]
        mod_ctx = jax.nn.silu(c_ctx) @ w_ada[layer] + b_ada[layer]
        sh_l, sc_l, gt_l = jnp.split(mod_lat, 3, axis=-1)
        sh_c, sc_c, gt_c = jnp.split(mod_ctx, 3, axis=-1)
        hn = jnp.concatenate([
            rms_norm(h_ctx, norm_g[layer]) * (1.0 + sc_c) + sh_c,
            rms_norm(h_lat, norm_g[layer]) * (1.0 + sc_l[:, None]) + sh_l[:, None],
        ], axis=1)
        (a_q, a_ff, a_fb, a_i, a_g,
         b_q, b_k, b_v, b_lf, b_lb, b_g,
         c_u, c_g,
         d_q, d_k, d_v, d_g) = split_columns(hn @ w_in[layer])
        o_a = hgrn2_branch(a_q, a_ff, a_fb, a_i, a_g, lower_bounds[layer], hgrn_norm_g[layer], n_ctx)
        o_b = gla_branch(b_q, b_k, b_v, b_lf, b_lb, b_g, gla_w_gk[layer], gla_b_gk[layer], gla_norm_g[layer], n_ctx)
        o_c = s5_branch(c_u, c_g, s5_lam_re[layer], s5_lam_im[layer], s5_log_dt[layer], s5_b_re[layer],
                        s5_b_im[layer], s5_c_re[layer], s5_c_im[layer], s5_d[layer], s5_w_glu[layer],
                        s5_b_glu[layer], n_ctx)
        log_gamma = jax.nn.log_sigmoid(ret_decay_logit[layer].astype(F32))
        o_d = retention_branch(d_q, d_k, d_v, d_g, log_gamma, ret_norm_g[layer], cos, sin, n_ctx)
        o = jnp.concatenate([o_a, o_b, o_c, o_d], axis=-1)
        if last:
            h_lat = h_lat + gt_l[:, None] * (o[:, n_ctx:] @ w_out[layer])
        else:
            y = o @ w_out[layer]
            h_ctx = h_ctx + gt_c * y[:, :n_ctx]
            h_lat = h_lat + gt_l[:, None] * y[:, n_ctx:]
    return rms_norm(h_lat, final_norm_g)
```

```python
import numpy as np
import concourse.bass as bass
import concourse.mybir as mybir
from concourse.bass_utils import run_bass_kernel_spmd

F32 = mybir.dt.float32; BF16 = mybir.dt.bfloat16; I32 = mybir.dt.int32
AF = mybir.ActivationFunctionType; ALU = mybir.AluOpType
D = 4096; KC = 32; EPS = 1e-6
TWO_PI = 6.283185307179586; PI = 3.141592653589793


class Prog:
    NDS = 8

    def __init__(self, nc):
        self.nc = nc
        self.eng = {'pe': nc.tensor, 'act': nc.scalar, 'dve': nc.vector, 'pool': nc.gpsimd, 'sp': nc.sync}
        self.sem = {}; self.cnt = {}; self.dsem = {}; self.dcnt = {}
        self.waited = {k: {} for k in self.eng}
        self.res = {}
        self._stack = []
        self.nins = 0

    def start(self):
        nc = self.nc
        for k in self.eng:
            cm = nc.semaphore("s_" + k); self._stack.append(cm); self.sem[k] = cm.__enter__(); self.cnt[k] = 0
        for k in ['sp', 'pool', 'act']:
            self.dsem[k] = []
            for i in range(self.NDS):
                cm = nc.semaphore("d_%s%d" % (k, i)); self._stack.append(cm); self.dsem[k].append(cm.__enter__())
            self.dcnt[k] = 0
        self.last_dma_tok = {k: [None] * self.NDS for k in self.dsem}

    def finish(self):
        for cm in reversed(self._stack):
            cm.__exit__(None, None, None)

    def _wait(self, e, tok):
        if tok is None:
            return
        sem, val, owner = tok
        w = self.waited[e]
        key = id(sem)
        if w.get(key, 0) >= val:
            return
        if owner == e and e == 'pe':
            return
        self.eng[e].wait_ge(sem, val)
        w[key] = val

    def _deps(self, e, reads, writes):
        toks = []
        for k in reads:
            st = self.res.get(k)
            if st and st['w']:
                toks.append(st['w'])
        for k in writes:
            st = self.res.get(k)
            if st:
                if st['w']:
                    toks.append(st['w'])
                toks.extend(st['r'])
        for t in toks:
            self._wait(e, t)

    def _update(self, tok, reads, writes):
        for k in reads:
            st = self.res.setdefault(k, {'w': None, 'r': []})
            st['r'].append(tok)
            if len(st['r']) > 48:
                best = {}
                for t in st['r']:
                    kk = id(t[0])
                    if kk not in best or best[kk][1] < t[1]:
                        best[kk] = t
                st['r'] = list(best.values())
        for k in writes:
            self.res[k] = {'w': tok, 'r': []}

    def op(self, e, fn, reads=(), writes=()):
        self._deps(e, reads, writes)
        ins = fn(self.eng[e])
        self.cnt[e] += 1
        self.nins += 1
        ins.then_inc(self.sem[e], 1)
        tok = (self.sem[e], self.cnt[e], e)
        self._update(tok, reads, writes)
        return tok

    def dma(self, q, out, in_, reads=(), writes=(), **kw):
        self._deps(q, reads, writes)
        i = self.dcnt[q]; k = i % self.NDS
        prev = self.last_dma_tok[q][k]
        if prev is not None:
            self._wait(q, prev)
        ins = self.eng[q].dma_start(out=out, in_=in_, **kw)
        val = 16 * (i // self.NDS + 1)
        ins.then_inc(self.dsem[q][k], 16)
        tok = (self.dsem[q][k], val, 'dma_' + q)
        self.last_dma_tok[q][k] = tok
        self.dcnt[q] += 1
        self.nins += 1
        self._update(tok, reads, writes)
        return tok

    def barrier(self):
        toks = []
        for e in self.eng:
            if self.cnt[e] > 0:
                toks.append((self.sem[e], self.cnt[e], e))
        for q in self.dsem:
            for t in self.last_dma_tok[q]:
                if t is not None:
                    toks.append(t)
        for e in self.eng:
            for t in toks:
                self._wait(e, t)
        self.res = {}

    def mm(self, out, lhsT, rhs, start, stop, reads, writes):
        return self.op('pe', lambda e: e.matmul(out, lhsT=lhsT, rhs=rhs, start=start, stop=stop), reads, writes)

    def tr(self, out, in_, ident, reads, writes):
        return self.op('pe', lambda e: e.transpose(out=out, in_=in_, identity=ident), reads, writes)

    def act(self, out, in_, func, reads, writes, bias=None, scale=None, eng='act'):
        kw = {}
        if bias is not None:
            kw['bias'] = bias
        if scale is not None:
            kw['scale'] = scale
        return self.op(eng, lambda e: e.activation(out=out, in_=in_, func=func, **kw), reads, writes)

    def tt(self, out, in0, in1, op, reads, writes, eng='dve'):
        return self.op(eng, lambda e: e.tensor_tensor(out=out, in0=in0, in1=in1, op=op), reads, writes)

    def ts(self, out, in0, s1, op0, reads, writes, s2=None, op1=None, eng='dve'):
        if op1 is None:
            return self.op(eng, lambda e: e.tensor_scalar(out=out, in0=in0, scalar1=s1, scalar2=None, op0=op0), reads, writes)
        return self.op(eng, lambda e: e.tensor_scalar(out=out, in0=in0, scalar1=s1, scalar2=s2, op0=op0, op1=op1), reads, writes)

    def stt(self, out, in0, scalar, in1, op0, op1, reads, writes):
        return self.op('dve', lambda e: e.scalar_tensor_tensor(out=out, in0=in0, scalar=scalar, in1=in1, op0=op0, op1=op1), reads, writes)

    def cp(self, out, in_, reads, writes, eng='dve'):
        if eng == 'act':
            return self.op('act', lambda e: e.copy(out=out, in_=in_), reads, writes)
        return self.op(eng, lambda e: e.tensor_copy(out=out, in_=in_), reads, writes)

    def memset(self, ap, val, writes, eng='pool'):
        return self.op(eng, lambda e: e.memset(ap, val), (), writes)


class Cfg:
    def __init__(self, CTX=256, LAT=4096, DEPTH=2, debug=False, stop=None):
        self.CTX = CTX; self.LAT = LAT; self.DEPTH = DEPTH; self.L = CTX + LAT
        self.debug = debug; self.stop = stop


def tiles(lo, hi, w):
    out = []
    t = lo
    while t < hi:
        ww = min(w, hi - t)
        out.append((t, ww))
        t += ww
    return out


class Builder:
    def __init__(self, cfg):
        self.cfg = cfg
        self.nc = bass.Bass("TRN2", target_bir_lowering=False)
        self.P = Prog(self.nc)
        self.T = {}
        self.dbg_names = []

    def din(self, name, shape, dt=F32):
        self.T[name] = self.nc.dram_tensor(name, list(shape), dt, kind="ExternalInput").ap()

    def dscr(self, name, shape, dt, out=False):
        if name in getattr(self.cfg, 'inject', ()):
            self.T[name] = self.nc.dram_tensor(name, list(shape), dt, kind="ExternalInput").ap()
            return
        if out or (self.cfg.debug and name in self.cfg.debug):
            self.T[name] = self.nc.dram_tensor(name, list(shape), dt, kind="ExternalOutput").ap()
            self.dbg_names.append(name)
        else:
            self.T[name] = self.nc.dram_tensor(name, list(shape), dt).ap()

    def sb(self, name, shape, dt):
        self._uid = getattr(self, '_uid', 0) + 1
        return self.nc.sbuf_tensor("%s_%d" % (name, self._uid), list(shape), dt)

    def ps(self, name, shape, dt=F32):
        self._uid = getattr(self, '_uid', 0) + 1
        return self.nc.psum_tensor("%s_%d" % (name, self._uid), list(shape), dt)

    def declare(self):
        c = self.cfg; L = c.L; DP = c.DEPTH
        self.din("hT0", [D, L]); self.din("c2T", [128, 32, 2])
        self.din("w_ada", [DP, D, 3 * D]); self.din("b_adaT", [DP, 128, 96])
        self.din("norm_gT", [DP, 128, 32]); self.din("final_gT", [128, 32])
        self.din("w_tm", [DP, D, 8192]); self.din("w_fm", [DP, D, 5152])
        self.din("w_out", [DP, D, D]); self.din("w_glu", [DP, 1024, 1024]); self.din("b_gluT", [DP, 128, 8])
        self.din("hg_lbrep", [128, 2, 2, 1024]); self.din("hg_ngT", [DP, 128, 1])
        self.din("gla_wg", [DP, 2, 17, 512]); self.din("gla_ngT", [DP, 128, 2])
        self.din("ret_ngT", [DP, 128, 2]); self.din("ret_lrep", [DP, 2, 128, 4])
        self.din("pos", [L, 2]); self.din("gl_consts", [2, 64, 577]); self.din("gl_consts2", [2, 128, 513])
        self.din("s5_lamP", [DP, 2, 2, 128, 32])
        self.din("s5_dtP", [DP, 2, 128, 32])
        self.din("s5_lam64", [DP, 2, 2, 64, 64])
        self.din("s5_dt64", [DP, 2, 64, 64])
        self.din("s5_b64", [DP, 2, 2, 64, 64, 16])
        self.din("s5_c64", [DP, 2, 2, 64, 64, 16])
        self.din("s5_lamF", [DP, 2, 2, 128, 8, 64])
        self.din("s5_dtF", [DP, 2, 128, 8, 64])
        self.din("s5_bF", [DP, 2, 2, 128, 8, 64])
        self.din("s5_cP", [DP, 2, 2, 128, 32, 16])
        self.din("s5_dT", [DP, 128, 8]); self.din("s5_maskQ", [128, 4, 128]); self.din("s5_maskC", [128, 2, 16])
        self.dscr("hT1", [D, L], F32); self.dscr("hT2", [D, L], F32)
        self.dscr("hnT", [D, L], BF16)
        self.dscr("tmb", [L, 8192], BF16); self.dscr("tmf", [L, 2048], F32)
        self.dscr("fmb", [5120, L], BF16); self.dscr("lrT", [32, L], F32)
        self.dscr("ofT", [3072, L], F32); self.dscr("oT", [D, L], BF16)
        self.dscr("rope", [L, 512], F32)
        self.dscr("s5pf", [8, 2, 128, 17, 64], F32); self.dscr("s5cf", [8, 2, 128, 64], F32)
        self.dscr("yfT", [1024, L], F32); self.dscr("zT", [1024, L], BF16)
        self.dscr("outT", [D, c.LAT], F32, out=True)

    def consts(self, stack):
        P = self.P
        def mk(name, shape, dt):
            cm = self.sb(name, shape, dt); stack.append(cm); return cm.__enter__()
        self.onesf = mk("onesf", [128, 128], F32)
        self.onesb = mk("onesb", [128, 128], BF16)
        self.identb = mk("identb", [128, 128], BF16)
        self.identf = mk("identf", [128, 128], F32)
        self.sT = mk("sT", [128, 32, 2], BF16)
        self.GS = mk("GS", [128, 6, 32], F32)
        P.memset(self.onesf[:], 1.0, ['onesf'])
        P.memset(self.onesb[:], 1.0, ['onesb'])
        P.op('pool', lambda e: e.affine_select(out=self.identf[:], in_=self.onesf[:], pattern=[[1, 128]], compare_op=ALU.is_equal,
                                               fill=0.0, base=0, channel_multiplier=-1), ['onesf'], ['identf'])
        P.cp(self.identb[:], self.identf[:], ['identf'], ['identb'])
        with self.sb("c2f", [128, 32, 2], F32) as c2f:
            P.dma('sp', c2f[:], self.T['c2T'], writes=['c2f'])
            P.act(self.sT[:], c2f[:], AF.Silu, ['c2f'], ['sT'])
            P.barrier()

    def phase_ada(self, layer):
        P = self.P; T = self.T; GS = self.GS
        wv = T['w_ada'][layer].rearrange("(k p) c -> p k c", p=128)
        with self.sb("wa0", [128, 32, 256], BF16) as wa0, self.sb("wa1", [128, 32, 256], BF16) as wa1, \
                self.ps("pa", [128, 512], F32) as pa, self.sb("modT", [128, 96, 2], F32) as modT, \
                self.sb("bT", [128, 96], F32) as bT, self.sb("ng", [128, 32], F32) as ng:
            was = [wa0, wa1]
            P.dma('sp', bT[:], T['b_adaT'][layer], writes=['bT'])
            P.dma('sp', ng[:], T['norm_gT'][layer], writes=['ng'])
            for cb in range(48):
                wa = was[cb % 2]; key = ('wa', cb % 2)
                P.dma('pool', wa[:], wv[:, :, cb * 256:(cb + 1) * 256], writes=[key])
                for s in range(2):
                    j = cb * 2 + s
                    for k in range(32):
                        P.mm(pa[:, 2 * j:2 * j + 2], wa[:, k, s * 128:(s + 1) * 128], self.sT[:, k, :], k == 0, k == 31,
                             [key, 'sT'], ['pa'])
            P.tt(modT[:], pa[:, 0:192].rearrange("p (j r) -> p j r", r=2), bT[:].unsqueeze(2).to_broadcast([128, 96, 2]), ALU.add,
                 ['pa', 'bT'], ['modT'])
            for r in range(2):
                P.stt(GS[:, 3 * r + 0, :], modT[:, 32:64, r], 1.0, ng[:], ALU.add, ALU.mult, ['modT', 'ng'], ['GS'])
                P.cp(GS[:, 3 * r + 1, :], modT[:, 0:32, r], ['modT'], ['GS'])
                P.cp(GS[:, 3 * r + 2, :], modT[:, 64:96, r], ['modT'], ['GS'])
            P.barrier()

    def phase_norm(self, hsrc, mode, dst):
        c = self.cfg; P = self.P; T = self.T; GS = self.GS; L = c.L; CTX = c.CTX
        TW = 256
        hv = hsrc.rearrange("(k p) t -> p k t", p=128)
        dv = dst.rearrange("(k p) t -> p k t", p=128)
        toks = tiles(0, L, TW) if mode == 'hn' else tiles(CTX, L, TW)
        with self.sb("hTa", [128, 32, TW], F32) as hTa, self.sb("hTb", [128, 32, TW], F32) as hTb, \
                self.sb("sq", [128, 32, TW], BF16) as sq, self.sb("hna", [128, 32, TW], BF16) as hna, \
                self.sb("hnb", [128, 32, TW], BF16) as hnb, self.sb("rstd", [128, TW], F32) as rstd, \
                self.sb("fg", [128, 32], F32) as fg, \
                self.ps("pssa", [128, 512], F32) as pssa, self.ps("pssb", [128, 512], F32) as pssb:
            hTs = [hTa, hTb]; hns = [hna, hnb]; psss = [pssa, pssb]
            if mode == 'final':
                P.dma('sp', fg[:], T['final_gT'], writes=['fg'])
            for it, (t0, w) in enumerate(toks):
                b = it % 2
                hT = hTs[b]; hn = hns[b]; pss = psss[b]
                P.dma('sp', hT[:, :, :w], hv[:, :, t0:t0 + w], writes=[('hT', b)])
                P.act(sq[:, :, :w], hT[:, :, :w], AF.Square, [('hT', b)], ['sq'])
                for k in range(32):
                    P.mm(pss[:, :w], self.onesb[:], sq[:, k, :w], k == 0, k == 31, ['sq', 'onesb'], [('pss', b)])
                P.act(rstd[:, :w], pss[:, :w], AF.Sqrt, [('pss', b)], ['rstd'], bias=EPS, scale=1.0 / D)
                P.op('dve', lambda e: e.reciprocal(out=rstd[:, :w], in_=rstd[:, :w]), ['rstd'], ['rstd'])
                P.tt(hT[:, :, :w], hT[:, :, :w], rstd[:, :w].unsqueeze(1).to_broadcast([128, 32, w]), ALU.mult,
                     [('hT', b), 'rstd'], [('hT', b)])
                if mode == 'hn':
                    gi = 3 if t0 < CTX else 0
                    for k in range(32):
                        if k % 2 == 0:
                            P.act(hn[:, k, :w], hT[:, k, :w], AF.Identity, [('hT', b), 'GS'], [('hn', b, k)],
                                  bias=GS[:, gi + 1, k:k + 1], scale=GS[:, gi, k:k + 1])
                        else:
                            P.ts(hn[:, k, :w], hT[:, k, :w], GS[:, gi, k:k + 1], ALU.mult, [('hT', b), 'GS'], [('hn', b, k)],
                                 s2=GS[:, gi + 1, k:k + 1], op1=ALU.add)
                    P.dma('act', dv[:, :, t0:t0 + w], hn[:, :, :w], reads=[('hn', b, k) for k in range(32)], writes=[('dst', it)])
                else:
                    P.tt(hT[:, :, :w], hT[:, :, :w], fg[:].unsqueeze(2).to_broadcast([128, 32, w]), ALU.mult,
                         [('hT', b), 'fg'], [('hT', b)])
                    P.dma('act', dv[:, :, t0 - CTX:t0 - CTX + w], hT[:, :, :w], reads=[('hT', b)], writes=[('dst', it)])
            P.barrier()

    def phase_inproj(self, layer):
        c = self.cfg; P = self.P; T = self.T; L = c.L
        blocks = []
        for bi in range(16):
            blocks.append(('tm', 'w_tm', bi * 512, 512))
        for bi in range(10):
            blocks.append(('fm', 'w_fm', bi * 512, 512))
        blocks.append(('lr', 'w_fm', 5120, 32))
        hv = T['hnT'].rearrange("(k p) t -> p k t", p=128)
        ttiles = tiles(0, L, 512)
        with self.sb("Wb0", [128, 32, 512], BF16) as Wb0, self.sb("Wb1", [128, 32, 512], BF16) as Wb1, \
                self.sb("hb0", [128, 32, 512], BF16) as hb0, self.sb("hb1", [128, 32, 512], BF16) as hb1, \
                self.sb("stgf", [128, 4, 512], F32) as stgf, self.sb("stgb", [128, 4, 512], BF16) as stgb, \
                self.ps("pd", [128, 4, 512], F32) as pd:
            Wbs = [Wb0, Wb1]; hbs = [hb0, hb1]
            gt = 0; gp = 0
            for bi, (kind, wn, c0, ncol) in enumerate(blocks):
                Wb = Wbs[bi % 2]; wkey = ('Wb', bi % 2)
                wv = T[wn][layer].rearrange("(k p) c -> p k c", p=128)
                P.dma('pool', Wb[:, :, :ncol], wv[:, :, c0:c0 + ncol], writes=[wkey])
                for (t0, w) in ttiles:
                    hb = hbs[gt % 2]; hkey = ('hb', gt % 2); gt += 1
                    P.dma('sp', hb[:, :, :w], hv[:, :, t0:t0 + w], writes=[hkey])
                    if kind in ('fm', 'lr'):
                        nsub = max(1, ncol // 128); m = min(128, ncol)
                        for cs in range(nsub):
                            bk = gp % 4; gp += 1
                            for k in range(32):
                                P.mm(pd[:m, bk, :w], Wb[:, k, cs * 128:cs * 128 + m], hb[:, k, :w], k == 0, k == 31,
                                     [wkey, hkey], [('pd', bk)])
                            if kind == 'fm':
                                stg = stgb; skey = ('stgb', bk); dst = T['fmb'][c0 + cs * 128:c0 + cs * 128 + m, t0:t0 + w]
                            else:
                                stg = stgf; skey = ('stgf', bk); dst = T['lrT'][0:32, t0:t0 + w]
                            P.cp(stg[:m, bk, :w], pd[:m, bk, :w], [('pd', bk)], [skey], eng=('act' if bk % 2 else 'dve'))
                            P.dma('act', dst, stg[:m, bk, :w], reads=[skey], writes=[('o', gp)])
                    else:
                        isf = (1024 <= c0 < 3072)
                        for ts in range(w // 128):
                            bk = gp % 4; gp += 1
                            for k in range(32):
                                P.mm(pd[:, bk, :], hb[:, k, ts * 128:(ts + 1) * 128], Wb[:, k, :], k == 0, k == 31,
                                     [wkey, hkey], [('pd', bk)])
                            r0 = t0 + ts * 128
                            if isf:
                                stg = stgf; skey = ('stgf', bk); dst = T['tmf'][r0:r0 + 128, c0 - 1024:c0 - 1024 + 512]
                            else:
                                stg = stgb; skey = ('stgb', bk); dst = T['tmb'][r0:r0 + 128, c0:c0 + 512]
                            P.cp(stg[:, bk, :], pd[:, bk, :], [('pd', bk)], [skey], eng=('act' if bk % 2 else 'dve'))
                            P.dma('act', dst, stg[:, bk, :], reads=[skey], writes=[('o', gp)])
            P.barrier()

    def phase_outproj(self, layer, hsrc, hdst, last):
        c = self.cfg; P = self.P; T = self.T; L = c.L; CTX = c.CTX; GS = self.GS
        wv = T['w_out'][layer].rearrange("(k p) c -> p k c", p=128)
        ov = T['oT'].rearrange("(k p) t -> p k t", p=128)
        ttiles = ([] if last else tiles(0, CTX, 512)) + tiles(CTX, L, 512)
        with self.sb("Wo0", [128, 32, 512], BF16) as Wb0, self.sb("Wo1", [128, 32, 512], BF16) as Wb1, \
                self.sb("ob0", [128, 32, 512], BF16) as hb0, self.sb("ob1", [128, 32, 512], BF16) as hb1, \
                self.sb("hold0", [128, 4, 512], F32) as hold0, self.sb("hold1", [128, 4, 512], F32) as hold1, \
                self.ps("po", [128, 4, 512], F32) as pd:
            Wbs = [Wb0, Wb1]; hbs = [hb0, hb1]; holds = [hold0, hold1]
            gt = 0; gp = 0
            for fb in range(8):
                Wb = Wbs[fb % 2]; wkey = ('Wb', fb % 2)
                P.dma('pool', Wb[:], wv[:, :, fb * 512:(fb + 1) * 512], writes=[wkey])
                hs = hsrc[fb * 512:(fb + 1) * 512, :].rearrange("(f p) t -> p f t", p=128)
                hd = hdst[fb * 512:(fb + 1) * 512, :].rearrange("(f p) t -> p f t", p=128)
                for (t0, w) in ttiles:
                    hb = hbs[gt % 2]; hkey = ('hb', gt % 2); hold = holds[gt % 2]; okey = ('hold', gt % 2); gt += 1
                    P.dma('sp', hb[:, :, :w], ov[:, :, t0:t0 + w], writes=[hkey])
                    P.dma('sp', hold[:, :, :w], hs[:, :, t0:t0 + w], writes=[okey])
                    gi = 3 if t0 < CTX else 0
                    for fs in range(4):
                        bk = gp % 4; gp += 1
                        fch = fb * 4 + fs
                        for k in range(32):
                            P.mm(pd[:, bk, :w], Wb[:, k, fs * 128:(fs + 1) * 128], hb[:, k, :w], k == 0, k == 31,
                                 [wkey, hkey], [('pd', bk)])
                        P.stt(hold[:, fs, :w], pd[:, bk, :w], GS[:, gi + 2, fch:fch + 1], hold[:, fs, :w], ALU.mult, ALU.add,
                              [('pd', bk), okey, 'GS'], [okey])
                    P.dma('act', hd[:, :, t0:t0 + w], hold[:, :, :w], reads=[okey], writes=[('o', gt)])
            P.barrier()

    BR = {
        'hg': dict(H=8, dv=128, q=0, k=None, v=3072, gate=0, orow=0, ofrow=0, gs=1.0, qs=1.0, norm='rms'),
        'gl': dict(H=4, dv=256, q=4096, k=4608, v=5120, gate=1024, orow=1024, ofrow=1024, gs=-1.0 / 16.0, qs=128.0 ** -0.5, norm='rms'),
        'rt': dict(H=4, dv=256, q=6144, k=6656, v=7168, gate=4096, orow=3072, ofrow=2048, gs=1.0, qs=128.0 ** -0.5, norm='ln'),
    }

    def phase_gla(self, layer, name):
        c = self.cfg; P = self.P; T = self.T; L = c.L; CTX = c.CTX
        br = self.BR[name]; H = br['H']; dv = br['dv']; dvc = dv // 128; HW = H * 128; C = 64
        NCH = L // C; ctxch = CTX // C; r = C // 2
        orders = [list(range(NCH)), list(range(ctxch - 1, -1, -1)) + list(range(NCH - 1, ctxch - 1, -1))]
        ofv = T['ofT'][br['ofrow']:br['ofrow'] + 1024, :].rearrange("(c p) t -> p c t", p=128)
        ov = T['oT'][br['orow']:br['orow'] + 1024, :].rearrange("(c p) t -> p c t", p=128)
        gv = T['fmb'][br['gate']:br['gate'] + 1024, :].rearrange("(c p) t -> p c t", p=128)
        sb = self.sb; ps = self.ps
        from contextlib import ExitStack
        with ExitStack() as es:
            Mcat = es.enter_context(sb("Mcat", [C, 577], F32))
            kdz = es.enter_context(sb("kdz", [128, 2, 256], BF16))
            koz = es.enter_context(sb("koz", [128, 2, 3, 64], BF16))
            qtd = es.enter_context(sb("qtd", [128, 2, C], BF16))
            TriS = es.enter_context(sb("TriS", [C, C], F32))
            mask = es.enter_context(sb("mask", [C, C], F32))
            ctmp = es.enter_context(sb("ctmp", [C, C], F32))
            q_t = es.enter_context(sb("q_t", [C, 2, HW], BF16))
            v_t = es.enter_context(sb("v_t", [C, 2, 1024], BF16))
            z_t = es.enter_context(sb("z_t", [C, 2, 1024], F32))
            k_t = es.enter_context(sb("k_t", [C, 2, HW], BF16))
            lr_t = es.enter_context(sb("lr_t", [17, 2, C], F32))
            rp_t = es.enter_context(sb("rp_t", [C, 2, 512], F32))
            of_t = es.enter_context(sb("of_t", [128, 2, 8, C], F32))
            gate_t = es.enter_context(sb("gate_t", [128, 2, 8, C], BF16))
            sig = es.enter_context(sb("sig", [C, 1024], F32))
            ff = es.enter_context(sb("ff", [C, 1024], F32))
            graw = es.enter_context(sb("graw", [C, 2, 1024], F32))
            kt = es.enter_context(sb("kt", [C, 2, 1024], BF16))
            qr = es.enter_context(sb("qr", [C, 2, HW], BF16))
            khat = es.enter_context(sb("khat", [C, 2, 1024], BF16))
            Ek = es.enter_context(sb("Ek", [C, 512], F32))
            rt1 = es.enter_context(sb("rt1", [C, 256], F32))
            rt2 = es.enter_context(sb("rt2", [C, 256], F32))
            rt3 = es.enter_context(sb("rt3", [C, 256], F32))
            rt4 = es.enter_context(sb("rt4", [C, 256], F32))
            E13 = es.enter_context(sb("E13", [128, 2, 449], F32))
            E2 = es.enter_context(sb("E2", [128, 2, C], F32))
            qtl = es.enter_context(sb("qtl", [128, 2, C], BF16))
            qtp = es.enter_context(sb("qtp", [128, 2, C], BF16))
            ktl = es.enter_context(sb("ktl", [128, 2, C], BF16))
            scb = es.enter_context(sb("scb", [C, 2, C], BF16))
            ofs = es.enter_context(sb("ofs", [128, 2, 8, C], F32))
            osum = es.enter_context(sb("osum", [128, 2, 8, C], F32))
            sg = es.enter_context(sb("sg", [128, 8, C], F32))
            sqt = es.enter_context(sb("sqt", [128, 8, C], F32))
            rs = es.enter_context(sb("rs", [128, 8, C], F32))
            mean = es.enter_context(sb("mean", [128, 8, C], F32))
            ob = es.enter_context(sb("ob", [128, 2, 8, C], BF16))
            otmp = es.enter_context(sb("otmp", [128, 8, C], F32))
            S = es.enter_context(sb("S", [128, 1024], F32))
            Sb = es.enter_context(sb("Sb", [128, 1024], BF16))
            lbr = es.enter_context(sb("lbr", [128, 1024], F32))
            oml = es.enter_context(sb("oml", [128, 1024], F32))
            lg = es.enter_context(sb("lg", [128, 2, 1024], F32))
            wg = es.enter_context(sb("wg", [17, 512], F32))
            gn = es.enter_context(sb("gn", [128, 2], F32))
            rl = es.enter_context(sb("rl", [128, 4], F32))
            gconst = es.enter_context(sb("gconst", [C, 512], F32))
            pc = es.enter_context(ps("pc", [128, 512], F32))
            pP0 = es.enter_context(ps("pP0", [128, 512], F32)); pP1 = es.enter_context(ps("pP1", [128, 512], F32)); pPs = [pP0, pP1]
            pT0 = es.enter_context(ps("pT0", [128, 8, 128], BF16)); pT1 = es.enter_context(ps("pT1", [128, 8, 128], BF16)); pTs = [pT0, pT1]
            pS = es.enter_context(ps("pS", [128, 512], F32))
            pO = es.enter_context(ps("pO", [128, 8, C], F32))
            pU = es.enter_context(ps("pU", [128, 2, 256], F32))
            if name == 'hg':
                P.dma('sp', gn[:, 0:1], T['hg_ngT'][layer], writes=['gn'])
            elif name == 'gl':
                P.dma('sp', gn[:], T['gla_ngT'][layer], writes=['gn'])
            else:
                P.dma('sp', gn[:], T['ret_ngT'][layer], writes=['gn'])
            P.memset(lr_t[:], 1.0, ['lr0', 'lr1'])
            step = 0
            for d in (0, 1):
                gs = br['gs']
                P.dma('sp', Mcat[:, :], T['gl_consts'][d], writes=['Mcat'])
                P.ts(Mcat[:, 0:513], Mcat[:, 0:513], gs, ALU.mult, ['Mcat'], ['Mcat'])
                P.memset(kdz[:], 0.0, ['kdz0', 'kdz1'])
                P.memset(koz[:], 0.0, ['koz0', 'koz1'])
                if name == 'hg':
                    P.dma('sp', lg[:], T['hg_lbrep'][:, :, d, :], writes=['lg'])
                    P.tt(lbr[:], lg[:, 1, :], lg[:, 0, :], ALU.subtract, ['lg'], ['lbr'])
                    P.act(lbr[:], lbr[:], AF.Sigmoid, ['lbr'], ['lbr'])
                    P.ts(lbr[:], lbr[:], float(layer), ALU.mult, ['lbr'], ['lbr'])
                    P.ts(oml[:], lbr[:], -1.0, ALU.mult, ['lbr'], ['oml'], s2=1.0, op1=ALU.add)
                elif name == 'gl':
                    P.dma('sp', wg[:], T['gla_wg'][layer, d], writes=['wg'])
                else:
                    P.dma('sp', rl[:], T['ret_lrep'][layer, d], writes=['rl'])
                    P.act(rl[:], rl[:], AF.Exp, ['rl'], ['rl'], scale=-1.0)
                    P.act(rl[:], rl[:], AF.Ln, ['rl'], ['rl'], bias=1.0)
                    P.ts(rl[:], rl[:], -1.0, ALU.mult, ['rl'], ['rl'])
                    for h in range(4):
                        P.ts(gconst[:, h * 128:(h + 1) * 128], self.onesf[:C, :], rl[:C, h:h + 1], ALU.mult, ['onesf', 'rl'], ['gconst'])
                P.memset(S[:], 0.0, [('S', h) for h in range(H)])
                P.memset(Sb[:], 0.0, [('Sb', h) for h in range(H)])
                order = orders[d]

                def emit_prep(si):
                    n = order[si]; t0 = n * C; pb = si % 2
                    K = lambda nm: (nm, pb)
                    P.dma('sp', q_t[:, pb, :], T['tmb'][t0:t0 + C, br['q']:br['q'] + HW], writes=[K('q_t')])
                    P.dma('sp', v_t[:, pb, :], T['tmb'][t0:t0 + C, br['v']:br['v'] + 1024], writes=[K('v_t')])
                    if name == 'hg':
                        P.dma('sp', z_t[:, pb, :], T['tmf'][t0:t0 + C, d * 1024:(d + 1) * 1024], writes=[K('z_t')])
                    else:
                        P.dma('sp', k_t[:, pb, :], T['tmb'][t0:t0 + C, br['k']:br['k'] + HW], writes=[K('k_t')])
                    if name == 'gl':
                        P.dma('sp', lr_t[0:16, pb, :], T['lrT'][d * 16:(d + 1) * 16, t0:t0 + C], writes=[('lr%d' % pb)])
                    if name == 'rt':
                        P.dma('sp', rp_t[:, pb, :], T['rope'][t0:t0 + C, :], writes=[K('rp_t')])
                    if d == 1:
                        P.dma('sp', of_t[:, pb], ofv[:, :, t0:t0 + C], writes=[K('of_t')])
                        P.dma('sp', gate_t[:, pb], gv[:, :, t0:t0 + C], writes=[K('gate_t')])
                    if name == 'hg':
                        P.act(sig[:], z_t[:, pb, :], AF.Sigmoid, [K('z_t')], ['sig'])
                        P.tt(ff[:], sig[:], oml[:C, :], ALU.mult, ['sig', 'oml'], ['ff'])
                        P.tt(ff[:], ff[:], lbr[:C, :], ALU.add, ['ff', 'lbr'], ['ff'])
                        P.ts(ff[:], ff[:], 1e-6, ALU.max, ['ff'], ['ff'])
                        P.act(graw[:, pb, :], ff[:], AF.Ln, ['ff'], [K('graw')])
                        P.ts(ff[:], sig[:], -1.0, ALU.mult, ['sig', 'ff'], ['ff'], s2=1.0, op1=ALU.add)
                        P.tt(kt[:, pb, :], ff[:], oml[:C, :], ALU.mult, ['ff', 'oml'], [K('kt')], eng='pool')
                    elif name == 'gl':
                        P.mm(pc[:C, :], lr_t[0:17, pb, :], wg[0:17, :], True, True, ['lr%d' % pb, 'wg'], ['pc'])
                        P.act(sig[:, 0:512], pc[:C, :], AF.Exp, ['pc'], ['sig'], scale=-1.0)
                        P.act(graw[:, pb, 0:512], sig[:, 0:512], AF.Ln, ['sig'], [K('graw')], bias=1.0)
                    else:
                        cos4 = rp_t[:, pb, 0:256].rearrange("p (h x) -> p h x", x=64)
                        sin4 = rp_t[:, pb, 256:512].rearrange("p (h x) -> p h x", x=64)
                        for (src, skey, dst, dkey, eng, ta, tb) in ((q_t, K('q_t'), qr, K('qr'), 'dve', rt1, rt2), (k_t, K('k_t'), kt, K('kt'), 'pool', rt3, rt4)):
                            xv = src[:, pb, :].rearrange("p (h two x) -> p h two x", two=2, x=64)
                            ov_ = dst[:, pb, 0:512].rearrange("p (h two x) -> p h two x", two=2, x=64)
                            tav = ta[:].rearrange("p (h x) -> p h x", x=64); tbv = tb[:].rearrange("p (h x) -> p h x", x=64)
                            P.tt(tav, xv[:, :, 0, :], cos4, ALU.mult, [skey, K('rp_t')], [ta.name], eng=eng)
                            P.tt(tbv, xv[:, :, 1, :], sin4, ALU.mult, [skey, K('rp_t')], [tb.name], eng=eng)
                            P.tt(ov_[:, :, 0, :], tav, tbv, ALU.subtract, [ta.name, tb.name], [dkey], eng=eng)
                            P.tt(tav, xv[:, :, 0, :], sin4, ALU.mult, [skey, K('rp_t')], [ta.name], eng=eng)
                            P.tt(tbv, xv[:, :, 1, :], cos4, ALU.mult, [skey, K('rp_t')], [tb.name], eng=eng)
                            P.tt(ov_[:, :, 1, :], tav, tbv, ALU.add, [ta.name, tb.name], [dkey], eng=eng)
                    gsrc_, gkey_, ksl_, kkey_ = srcs(pb)
                    for hf in range(HW // 512):
                        cs = slice(hf * 512, (hf + 1) * 512)
                        P.mm(pc[:C, :], Mcat[:, 449:513], gsrc_[:, cs], True, True, ['Mcat', gkey_], ['pc'])
                        P.act(Ek[:], pc[:C, :], AF.Exp, ['pc'], ['Ek'])
                        P.tt(khat[:, pb, cs], ksl_(hf * 512, (hf + 1) * 512), Ek[:], ALU.mult, [kkey_, 'Ek'], [('khat', pb, hf)])

                def srcs(pb):
                    K = lambda nm: (nm, pb)
                    if name == 'rt':
                        gsrc_ = gconst; gkey_ = 'gconst'
                    else:
                        gsrc_ = graw[:, pb, :]; gkey_ = K('graw')
                    if name == 'gl':
                        ksl_ = lambda a, b: k_t[:, pb, a:b]; kkey_ = K('k_t')
                    else:
                        ksl_ = lambda a, b: kt[:, pb, a:b]; kkey_ = K('kt')
                    return gsrc_, gkey_, ksl_, kkey_

                def emit_front(si, h):
                    pb = si % 2; sl = h % 2; hs = slice(h * 128, (h + 1) * 128)
                    K = lambda nm: (nm, pb)
                    gsrc_, gkey_, ksl_, kkey_ = srcs(pb)
                    if name == 'rt':
                        qsrc = qr[:, pb, 0:512]; qkey = K('qr')
                    else:
                        qsrc = q_t[:, pb, :]; qkey = K('q_t')
                    pP = pPs[sl]; pT = pTs[sl]; kP = 'pP%d' % sl; kT = 'pT%d' % sl
                    P.mm(pP[:, 0:449], gsrc_[:, hs], Mcat[:, 0:449], True, True, [gkey_, 'Mcat'], [kP])
                    P.act(E13[:, sl, :], pP[:, 0:449], AF.Exp, [kP], [('E13', sl)])
                    P.tr(pT[:, 0, 0:C], qsrc[:, hs], self.identb[:C, :C], [qkey, 'identb'], [kT])
                    P.tr(pT[:, 1, 0:C], ksl_(h * 128, (h + 1) * 128), self.identb[:C, :C], [kkey_, 'identb'], [kT])
                    P.stt(qtl[:, sl, :], pT[:, 0, 0:C], br['qs'], E13[:, sl, 0:64], ALU.mult, ALU.mult, [kT, ('E13', sl)], [('qtl', sl)])
                    P.stt(qtd[:, sl, :], pT[:, 0, 0:C], br['qs'], E13[:, sl, 64:128], ALU.mult, ALU.mult, [kT, ('E13', sl)], [('qtd', sl)])
                    P.stt(qtp[:, sl, :], pT[:, 0, 0:C], br['qs'], E13[:, sl, 384:448], ALU.mult, ALU.mult, [kT, ('E13', sl)], [('qtp', sl)])
                    kbase = kdz[:, sl, :]
                    kd_out = bass.AP(kbase.tensor, kbase.offset, [list(kbase.ap[0]), [80, 4], [1, 16]])
                    P.tt(kd_out, pT[:, 1, 0:C].rearrange("p (a b) -> p a b", b=16), E13[:, sl, 128:192].rearrange("p (a b) -> p a b", b=16),
                         ALU.mult, [kT, ('E13', sl)], ['kdz%d' % sl])
                    offI = [1, 2, 3] if d == 0 else [0, 1, 2]
                    for x, I in enumerate(offI):
                        lo, hi = (0, 16 * I) if d == 0 else (16 * (I + 1), 64)
                        P.tt(koz[:, sl, x, lo:hi], pT[:, 1, lo:hi], E13[:, sl, 192 + 64 * x + lo:192 + 64 * x + hi], ALU.mult,
                             [kT, ('E13', sl)], ['koz%d' % sl])

                def emit_back(si, h):
                    pb = si % 2; sl = h % 2; hs = slice(h * 128, (h + 1) * 128)
                    K = lambda nm: (nm, pb)
                    offI = [1, 2, 3] if d == 0 else [0, 1, 2]
                    for I in range(4):
                        has_off = I in offI
                        P.mm(pS[:C, 16 * I:16 * I + 16], kdz[:, sl, 64 * I:64 * I + 64], qtd[:, sl, 16 * I:16 * I + 16], True, not has_off,
                             ['kdz%d' % sl, ('qtd', sl)], ['pS'])
                        if has_off:
                            x = offI.index(I)
                            P.mm(pS[:C, 16 * I:16 * I + 16], koz[:, sl, x, :], qtl[:, sl, 16 * I:16 * I + 16], False, True,
                                 ['koz%d' % sl, ('qtl', sl)], ['pS'])
                    P.tt(scb[:, sl, :], pS[:C, 0:C], Mcat[:, 513:577], ALU.mult, ['pS', 'Mcat'], [('scb', sl)])
                    for ec in range(dvc):
                        col = h * dv + ec * 128; ci = col // 128; so = ci % 4
                        P.mm(pO[:, so, :], v_t[:, pb, col:col + 128], scb[:, sl, :], True, False, [K('v_t'), ('scb', sl)], ['pO'])
                        P.mm(pO[:, so, :], Sb[:, col:col + 128], qtp[:, sl, :], False, True, [('Sb', h), ('qtp', sl)], ['pO'])
                        if d == 0:
                            P.cp(ofs[:, pb, ci, :], pO[:, so, :], ['pO'], [('ofs', pb)], eng='act')
                        else:
                            P.tt(osum[:, pb, ci, :], pO[:, so, :], of_t[:, pb, ci, :], ALU.add, ['pO', K('of_t')], [K('osum')])
                    P.mm(pU[:, 0, 0:dv], khat[:, pb, hs], v_t[:, pb, h * dv:(h + 1) * dv], True, True,
                         [('khat', pb, (h * 128) // 512), K('v_t')], ['pU'])
                    P.stt(S[:, h * dv:(h + 1) * dv], S[:, h * dv:(h + 1) * dv], E13[:, sl, 448:449], pU[:, 0, 0:dv],
                          ALU.mult, ALU.add, [('S', h), ('E13', sl), 'pU'], [('S', h)])
                    P.cp(Sb[:, h * dv:(h + 1) * dv], S[:, h * dv:(h + 1) * dv], [('S', h)], [('Sb', h)], eng='act')

                def emit_out(si):
                    n = order[si]; t0 = n * C; pb = si % 2
                    K = lambda nm: (nm, pb)
                    if d == 0:
                        P.dma('act', ofv[:, :, t0:t0 + C], ofs[:, pb], reads=[('ofs', pb)], writes=[('ofT', n)])
                        return
                    osm = osum[:, pb]
                    P.act(sg[:], gate_t[:, pb], AF.Silu, [K('gate_t')], ['sg'])
                    if br['norm'] == 'ln':
                        for h in range(H):
                            for ec in range(dvc):
                                P.mm(pc[:, h * C:(h + 1) * C], self.onesf[:], osm[:, h * dvc + ec, :], ec == 0, ec == dvc - 1, ['onesf', K('osum')], ['pc'])
                        P.ts(mean[:, 0:H, :], pc[:, 0:H * C].rearrange("p (h c) -> p h c", c=C), 1.0 / dv, ALU.mult, ['pc'], ['mean'])
                        for ci in range(8):
                            P.tt(osm[:, ci, :], osm[:, ci, :], mean[:, ci // dvc, :], ALU.subtract, [K('osum'), 'mean'], [K('osum')])
                    P.act(sqt[:], osm, AF.Square, [K('osum')], ['sqt'])
                    for h in range(H):
                        for ec in range(dvc):
                            P.mm(pc[:, h * C:(h + 1) * C], self.onesf[:], sqt[:, h * dvc + ec, :], ec == 0, ec == dvc - 1, ['onesf', 'sqt'], ['pc'])
                    P.act(rs[:, 0:H, :], pc[:, 0:H * C].rearrange("p (h c) -> p h c", c=C), AF.Sqrt, ['pc'], ['rs'], bias=EPS, scale=1.0 / dv)
                    P.op('dve', lambda e: e.reciprocal(out=rs[:, 0:H, :], in_=rs[:, 0:H, :]), ['rs'], ['rs'])
                    for ci in range(8):
                        P.stt(otmp[:, ci, :], osm[:, ci, :], gn[:, (ci % dvc):(ci % dvc) + 1], rs[:, ci // dvc, :], ALU.mult, ALU.mult,
                              [K('osum'), 'gn', 'rs'], ['otmp'])
                    P.tt(ob[:, pb], otmp[:], sg[:], ALU.mult, ['otmp', 'sg'], [('ob', pb)])
                    P.dma('act', ov[:, :, t0:t0 + C], ob[:, pb], reads=[('ob', pb)], writes=[('oT', n)])

                nch = len(order)
                PIPE = getattr(c, 'pipe', True)
                if PIPE:
                    emit_prep(0)
                for si in range(nch):
                    if not PIPE:
                        emit_prep(si)
                        for h in range(H):
                            emit_front(si, h)
                            emit_back(si, h)
                        emit_out(si)
                        continue
                    emit_front(si, 0)
                    for h in range(H):
                        if h + 1 < H:
                            emit_front(si, h + 1)
                        emit_back(si, h)
                        if h == H // 2 - 1 and si + 1 < nch:
                            emit_prep(si + 1)
                    emit_out(si)
                P.barrier()

    def phase_gla2(self, layer, name):
        c = self.cfg; P = self.P; T = self.T; L = c.L; CTX = c.CTX
        from contextlib import ExitStack
        br = self.BR[name]; H = br['H']; dv = br['dv']; dvc = dv // 128; HW = H * 128; C = 128
        assert H == 4 and dv == 256
        NCH = L // C; ctxch = CTX // C
        orders = [list(range(NCH)), list(range(ctxch - 1, -1, -1)) + list(range(NCH - 1, ctxch - 1, -1))]
        ofv = T['ofT'][br['ofrow']:br['ofrow'] + 1024, :].rearrange("(c p) t -> p c t", p=128)
        ov = T['oT'][br['orow']:br['orow'] + 1024, :].rearrange("(c p) t -> p c t", p=128)
        gv = T['fmb'][br['gate']:br['gate'] + 1024, :].rearrange("(c p) t -> p c t", p=128)
        sb = self.sb; ps = self.ps
        with ExitStack() as es:
            def mk(nm, shape, dt=F32):
                return es.enter_context(sb(nm, shape, dt))
            M = mk("M2", [128, 513]); q_t = mk("q_t2", [128, 2, 512], BF16); v_t = mk("v_t2", [128, 2, 1024], BF16)
            k_t = mk("k_t2", [128, 2, 512], BF16); lr_t = mk("lr_t2", [17, 2, C]); rp_t = mk("rp_t2", [128, 2, 512])
            of_t = mk("of_t2", [128, 2, 8, C]); gate_t = mk("gate_t2", [128, 2, 8, C], BF16)
            sig = mk("sig2", [128, 512]); graw = mk("graw2", [128, 2, 512]); kt = mk("kt2", [128, 2, 512], BF16)
            qr = mk("qr2", [128, 2, 512], BF16); khat = mk("khat2", [128, 2, 512], BF16); Ek = mk("Ek2", [128, 512])
            rt1 = mk("rt1b", [128, 256]); rt2 = mk("rt2b", [128, 256]); rt3 = mk("rt3b", [128, 256]); rt4 = mk("rt4b", [128, 256])
            E = mk("E_2", [128, 2, 257]); E2 = mk("E2_2", [128, 2, 128])
            qtl = mk("qtl2", [128, 2, C], BF16); qtp = mk("qtp2", [128, 2, C], BF16); ktl = mk("ktl2", [128, 2, C], BF16)
            scb = mk("scb2", [128, 2, C], BF16); ofs = mk("ofs2", [128, 2, 8, C]); osum = mk("osum2", [128, 2, 8, C])
            sg = mk("sg2_", [128, 8, C]); sqt = mk("sqt2", [128, 8, C]); rs = mk("rs2", [128, 8, C]); mean = mk("mean2", [128, 8, C])
            otmp = mk("otmp2", [128, 8, C]); ob = mk("ob2", [128, 2, 8, C], BF16)
            S = mk("S2", [128, 1024]); Sb = mk("Sb2", [128, 1024], BF16)
            wg = mk("wg2", [17, 512]); gn = mk("gn2", [128, 2]); rl = mk("rl2", [128, 4]); gconst = mk("gconst2", [128, 512])
            pc = es.enter_context(ps("pc2", [128, 512], F32))
            pPs = [es.enter_context(ps("pPa", [128, 512], F32)), es.enter_context(ps("pPb", [128, 512], F32))]
            pTs = [es.enter_context(ps("pTa", [128, 8, 128], BF16)), es.enter_context(ps("pTb", [128, 8, 128], BF16))]
            pS = es.enter_context(ps("pS2", [128, 512], F32)); pO = es.enter_context(ps("pO2", [128, 4, C], F32))
            pU = es.enter_context(ps("pU2", [128, 512], F32))
            P.dma('sp', gn[:], T['gla_ngT' if name == 'gl' else 'ret_ngT'][layer], writes=['gn'])
            P.memset(lr_t[:], 1.0, ['lr0', 'lr1'])
            for d in (0, 1):
                gs = br['gs']
                P.dma('sp', M[:, :], T['gl_consts2'][d], writes=['M'])
                P.ts(M[:, 0:385], M[:, 0:385], gs, ALU.mult, ['M'], ['M'])
                if name == 'gl':
                    P.dma('sp', wg[:], T['gla_wg'][layer, d], writes=['wg'])
                else:
                    P.dma('sp', rl[:], T['ret_lrep'][layer, d], writes=['rl'])
                    P.act(rl[:], rl[:], AF.Exp, ['rl'], ['rl'], scale=-1.0)
                    P.act(rl[:], rl[:], AF.Ln, ['rl'], ['rl'], bias=1.0)
                    P.ts(rl[:], rl[:], -1.0, ALU.mult, ['rl'], ['rl'])
                    for h in range(4):
                        P.ts(gconst[:, h * 128:(h + 1) * 128], self.onesf[:, :], rl[:, h:h + 1], ALU.mult, ['onesf', 'rl'], ['gconst'])
                P.memset(S[:], 0.0, [('S', h) for h in range(H)])
                P.memset(Sb[:], 0.0, [('Sb', h) for h in range(H)])
                order = orders[d]

                def srcs(pb):
                    K = lambda nm: (nm, pb)
                    if name == 'rt':
                        return gconst[:, :], 'gconst', (lambda a, b: kt[:, pb, a:b]), K('kt'), qr[:, pb, :], K('qr')
                    return graw[:, pb, :], K('graw'), (lambda a, b: k_t[:, pb, a:b]), K('k_t'), q_t[:, pb, :], K('q_t')

                def emit_prep(si):
                    n = order[si]; t0 = n * C; pb = si % 2
                    K = lambda nm: (nm, pb)
                    P.dma('sp', q_t[:, pb, :], T['tmb'][t0:t0 + C, br['q']:br['q'] + HW], writes=[K('q_t')])
                    P.dma('sp', v_t[:, pb, :], T['tmb'][t0:t0 + C, br['v']:br['v'] + 1024], writes=[K('v_t')])
                    P.dma('sp', k_t[:, pb, :], T['tmb'][t0:t0 + C, br['k']:br['k'] + HW], writes=[K('k_t')])
                    if name == 'gl':
                        P.dma('sp', lr_t[0:16, pb, :], T['lrT'][d * 16:(d + 1) * 16, t0:t0 + C], writes=[('lr%d' % pb)])
                    else:
                        P.dma('sp', rp_t[:, pb, :], T['rope'][t0:t0 + C, :], writes=[K('rp_t')])
                    if d == 1:
                        P.dma('sp', of_t[:, pb], ofv[:, :, t0:t0 + C], writes=[K('of_t')])
                        P.dma('sp', gate_t[:, pb], gv[:, :, t0:t0 + C], writes=[K('gate_t')])
                    if name == 'gl':
                        P.mm(pc[:, :], lr_t[0:17, pb, :], wg[0:17, :], True, True, ['lr%d' % pb, 'wg'], ['pc'])
                        P.act(sig[:], pc[:, :], AF.Exp, ['pc'], ['sig'], scale=-1.0)
                        P.act(graw[:, pb, :], sig[:], AF.Ln, ['sig'], [K('graw')], bias=1.0)
                    else:
                        cos4 = rp_t[:, pb, 0:256].rearrange("p (h x) -> p h x", x=64)
                        sin4 = rp_t[:, pb, 256:512].rearrange("p (h x) -> p h x", x=64)
                        for (src, skey, dst, dkey, eng, ta, tb) in ((q_t, K('q_t'), qr, K('qr'), 'dve', rt1, rt2), (k_t, K('k_t'), kt, K('kt'), 'pool', rt3, rt4)):
                            xv = src[:, pb, :].rearrange("p (h two x) -> p h two x", two=2, x=64)
                            ov_ = dst[:, pb, :].rearrange("p (h two x) -> p h two x", two=2, x=64)
                            tav = ta[:].rearrange("p (h x) -> p h x", x=64); tbv = tb[:].rearrange("p (h x) -> p h x", x=64)
                            P.tt(tav, xv[:, :, 0, :], cos4, ALU.mult, [skey, K('rp_t')], [ta.name], eng=eng)
                            P.tt(tbv, xv[:, :, 1, :], sin4, ALU.mult, [skey, K('rp_t')], [tb.name], eng=eng)
                            P.tt(ov_[:, :, 0, :], tav, tbv, ALU.subtract, [ta.name, tb.name], [dkey], eng=eng)
                            P.tt(tav, xv[:, :, 0, :], sin4, ALU.mult, [skey, K('rp_t')], [ta.name], eng=eng)
                            P.tt(tbv, xv[:, :, 1, :], cos4, ALU.mult, [skey, K('rp_t')], [tb.name], eng=eng)
                            P.tt(ov_[:, :, 1, :], tav, tbv, ALU.add, [ta.name, tb.name], [dkey], eng=eng)
                    gsrc_, gkey_, ksl_, kkey_, qsrc_, qkey_ = srcs(pb)
                    P.mm(pc[:, :], M[:, 257:385], gsrc_, True, True, ['M', gkey_], ['pc'])
                    P.act(Ek[:], pc[:, :], AF.Exp, ['pc'], ['Ek'])
                    P.tt(khat[:, pb, :], ksl_(0, 512), Ek[:], ALU.mult, [kkey_, 'Ek'], [K('khat')])

                def emit_front(si, h):
                    pb = si % 2; sl = h % 2; hs = slice(h * 128, (h + 1) * 128)
                    gsrc_, gkey_, ksl_, kkey_, qsrc_, qkey_ = srcs(pb)
                    pP = pPs[sl]; pT = pTs[sl]; kP = 'pP%d' % sl; kT = 'pT%d' % sl
                    P.mm(pP[:, 0:257], gsrc_[:, hs], M[:, 0:257], True, True, [gkey_, 'M'], [kP])
                    P.act(E[:, sl, :], pP[:, 0:257], AF.Exp, [kP], [('E', sl)])
                    P.act(E2[:, sl, :], pP[:, 0:128], AF.Exp, [kP], [('E2', sl)], scale=-1.0)
                    P.tr(pT[:, 0, 0:C], qsrc_[:, hs], self.identb[:, :], [qkey_, 'identb'], [kT])
                    P.tr(pT[:, 1, 0:C], ksl_(h * 128, (h + 1) * 128), self.identb[:, :], [kkey_, 'identb'], [kT])
                    P.stt(qtl[:, sl, :], pT[:, 0, 0:C], br['qs'], E[:, sl, 0:128], ALU.mult, ALU.mult, [kT, ('E', sl)], [('qtl', sl)])
                    P.stt(qtp[:, sl, :], pT[:, 0, 0:C], br['qs'], E[:, sl, 128:256], ALU.mult, ALU.mult, [kT, ('E', sl)], [('qtp', sl)])
                    P.tt(ktl[:, sl, :], pT[:, 1, 0:C], E2[:, sl, :], ALU.mult, [kT, ('E2', sl)], [('ktl', sl)])

                def emit_back(si, h):
                    pb = si % 2; sl = h % 2; hs = slice(h * 128, (h + 1) * 128)
                    K = lambda nm: (nm, pb)
                    P.mm(pS[:, 0:C], ktl[:, sl, :], qtl[:, sl, :], True, True, [('ktl', sl), ('qtl', sl)], ['pS'])
                    P.tt(scb[:, sl, :], pS[:, 0:C], M[:, 385:513], ALU.mult, ['pS', 'M'], [('scb', sl)])
                    for ec in range(dvc):
                        col = h * dv + ec * 128; ci = col // 128; so = ci % 4
                        P.mm(pO[:, so, :], v_t[:, pb, col:col + 128], scb[:, sl, :], True, False, [K('v_t'), ('scb', sl)], ['pO'])
                        P.mm(pO[:, so, :], Sb[:, col:col + 128], qtp[:, sl, :], False, True, [('Sb', h), ('qtp', sl)], ['pO'])
                        if d == 0:
                            P.cp(ofs[:, pb, ci, :], pO[:, so, :], ['pO'], [('ofs', pb)], eng='act')
                        else:
                            P.tt(osum[:, pb, ci, :], pO[:, so, :], of_t[:, pb, ci, :], ALU.add, ['pO', K('of_t')], [K('osum')])
                    P.mm(pU[:, 0:dv], khat[:, pb, hs], v_t[:, pb, h * dv:(h + 1) * dv], True, True, [K('khat'), K('v_t')], ['pU'])
                    P.stt(S[:, h * dv:(h + 1) * dv], S[:, h * dv:(h + 1) * dv], E[:, sl, 256:257], pU[:, 0:dv],
                          ALU.mult, ALU.add, [('S', h), ('E', sl), 'pU'], [('S', h)])
                    P.cp(Sb[:, h * dv:(h + 1) * dv], S[:, h * dv:(h + 1) * dv], [('S', h)], [('Sb', h)], eng='act')

                def emit_out(si):
                    n = order[si]; t0 = n * C; pb = si % 2
                    K = lambda nm: (nm, pb)
                    if d == 0:
                        P.dma('act', ofv[:, :, t0:t0 + C], ofs[:, pb], reads=[('ofs', pb)], writes=[('ofT', n)])
                        return
                    osm = osum[:, pb]
                    P.act(sg[:], gate_t[:, pb], AF.Silu, [K('gate_t')], ['sg'])
                    if br['norm'] == 'ln':
                        for h in range(H):
                            for ec in range(dvc):
                                P.mm(pc[:, h * C:(h + 1) * C], self.onesf[:], osm[:, h * dvc + ec, :], ec == 0, ec == dvc - 1, ['onesf', K('osum')], ['pc'])
                        P.ts(mean[:, 0:H, :], pc[:, 0:H * C].rearrange("p (h c) -> p h c", c=C), 1.0 / dv, ALU.mult, ['pc'], ['mean'])
                        for ci in range(8):
                            P.tt(osm[:, ci, :], osm[:, ci, :], mean[:, ci // dvc, :], ALU.subtract, [K('osum'), 'mean'], [K('osum')])
                    P.act(sqt[:], osm, AF.Square, [K('osum')], ['sqt'])
                    for h in range(H):
                        for ec in range(dvc):
                            P.mm(pc[:, h * C:(h + 1) * C], self.onesf[:], sqt[:, h * dvc + ec, :], ec == 0, ec == dvc - 1, ['onesf', 'sqt'], ['pc'])
                    P.act(rs[:, 0:H, :], pc[:, 0:H * C].rearrange("p (h c) -> p h c", c=C), AF.Sqrt, ['pc'], ['rs'], bias=EPS, scale=1.0 / dv)
                    P.op('dve', lambda e: e.reciprocal(out=rs[:, 0:H, :], in_=rs[:, 0:H, :]), ['rs'], ['rs'])
                    for ci in range(8):
                        P.stt(otmp[:, ci, :], osm[:, ci, :], gn[:, (ci % dvc):(ci % dvc) + 1], rs[:, ci // dvc, :], ALU.mult, ALU.mult,
                              [K('osum'), 'gn', 'rs'], ['otmp'])
                    P.tt(ob[:, pb], otmp[:], sg[:], ALU.mult, ['otmp', 'sg'], [('ob', pb)])
                    P.dma('act', ov[:, :, t0:t0 + C], ob[:, pb], reads=[('ob', pb)], writes=[('oT', n)])

                nch = len(order)
                emit_prep(0)
                for si in range(nch):
                    emit_front(si, 0)
                    for h in range(H):
                        if h + 1 < H:
                            emit_front(si, h + 1)
                        emit_back(si, h)
                        if h == H // 2 - 1 and si + 1 < nch:
                            emit_prep(si + 1)
                    emit_out(si)
                P.barrier()

    def s5_powers(self, src_re, src_im, src_dt, Pn, Fn, pw_re, pw_im, coef_re, coef_im, tag):
        P = self.P
        from contextlib import ExitStack
        with ExitStack() as es:
            def mk(nm, shape, dt=F32):
                return es.enter_context(self.sb(tag + nm, shape, dt))
            lre = mk("lre", [Pn, Fn]); lim = mk("lim", [Pn, Fn]); dtv = mk("dtv", [Pn, Fn])
            a = mk("a", [Pn, Fn]); th = mk("th", [Pn, Fn]); kki = mk("kki", [Pn, 17, Fn], I32); kk = mk("kk", [Pn, 17, Fn])
            A = mk("A", [Pn, 17, Fn]); TH = mk("TH", [Pn, 17 * Fn]); SN = mk("SN", [Pn, 17 * Fn]); CS = mk("CS", [Pn, 17 * Fn])
            t1 = mk("t1", [Pn, Fn]); t2 = mk("t2", [Pn, Fn]); den = mk("den", [Pn, Fn])
            P.dma('sp', lre[:], src_re, writes=['lre']); P.dma('sp', lim[:], src_im, writes=['lim']); P.dma('sp', dtv[:], src_dt, writes=['dtv'])
            P.act(dtv[:], dtv[:], AF.Exp, ['dtv'], ['dtv'])
            P.ts(lre[:], lre[:], -1e-4, ALU.min, ['lre'], ['lre'])
            P.tt(a[:], lre[:], dtv[:], ALU.mult, ['lre', 'dtv'], ['a'])
            P.tt(th[:], lim[:], dtv[:], ALU.mult, ['lim', 'dtv'], ['th'])
            P.op('pool', lambda e: e.iota(kki[:], pattern=[[1, 17], [0, Fn]], base=0, channel_multiplier=0), [], ['kki'])
            P.cp(kk[:], kki[:], ['kki'], ['kk'])
            P.tt(A[:], kk[:], a[:].unsqueeze(1).to_broadcast([Pn, 17, Fn]), ALU.mult, ['kk', 'a'], ['A'])
            P.tt(TH[:].rearrange("p (k f) -> p k f", f=Fn), kk[:], th[:].unsqueeze(1).to_broadcast([Pn, 17, Fn]), ALU.mult, ['kk', 'th'], [tag + 'scx'])
            P.act(A[:], A[:], AF.Exp, ['A'], ['A'])
            self.sincos(TH[:], [Pn, 17 * Fn], SN[:], CS[:], tag + 'sc')
            P.tt(pw_re, A[:], CS[:].rearrange("p (k f) -> p k f", f=Fn), ALU.mult, ['A'], ['pwre'])
            P.tt(pw_im, A[:], SN[:].rearrange("p (k f) -> p k f", f=Fn), ALU.mult, ['A'], ['pwim'])
            if coef_re is not None:
                P.ts(t1[:], pw_re[:, 1, :], -1.0, ALU.add, ['pwre'], ['t1'])
                P.tt(den[:], lre[:], lre[:], ALU.mult, ['lre'], ['den'])
                P.tt(t2[:], lim[:], lim[:], ALU.mult, ['lim'], ['t2'])
                P.tt(den[:], den[:], t2[:], ALU.add, ['den', 't2'], ['den'])
                P.op('dve', lambda e: e.reciprocal(out=den[:], in_=den[:]), ['den'], ['den'])
                P.tt(coef_re, t1[:], lre[:], ALU.mult, ['t1', 'lre'], ['cre'])
                P.tt(t2[:], pw_im[:, 1, :], lim[:], ALU.mult, ['pwim', 'lim'], ['t2'])
                P.tt(coef_re, coef_re, t2[:], ALU.add, ['cre', 't2'], ['cre'])
                P.tt(coef_re, coef_re, den[:], ALU.mult, ['cre', 'den'], ['cre'])
                P.tt(coef_im, pw_im[:, 1, :], lre[:], ALU.mult, ['pwim', 'lre'], ['cim'])
                P.tt(t2[:], t1[:], lim[:], ALU.mult, ['t1', 'lim'], ['t2'])
                P.tt(coef_im, coef_im, t2[:], ALU.subtract, ['cim', 't2'], ['cim'])
                P.tt(coef_im, coef_im, den[:], ALU.mult, ['cim', 'den'], ['cim'])
            P.barrier()

    def cmul(self, out_re, out_im, a_re, a_im, b_re, b_im, t, neg_im=False, eng='dve'):
        P = self.P
        P.tt(out_re, a_re, b_re, ALU.mult, ['cm_in'], ['cm_re'], eng=eng)
        P.tt(t, a_im, b_im, ALU.mult, ['cm_in'], ['cm_t'], eng=eng)
        P.tt(out_re, out_re, t, ALU.subtract, ['cm_re', 'cm_t'], ['cm_re'], eng=eng)
        P.tt(out_im, a_re, b_im, ALU.mult, ['cm_in'], ['cm_im'], eng=eng)
        P.tt(t, a_im, b_re, ALU.mult, ['cm_in', 'cm_re'], ['cm_t'], eng=eng)
        if neg_im:
            P.stt(out_im, out_im, -1.0, t, ALU.mult, ALU.subtract, ['cm_im', 'cm_t'], ['cm_im'])
        else:
            P.tt(out_im, out_im, t, ALU.add, ['cm_im', 'cm_t'], ['cm_im'], eng=eng)

    def phase_s5(self, layer):
        c = self.cfg; P = self.P; T = self.T; L = c.L; CTX = c.CTX
        from contextlib import ExitStack
        SB = 16; NB = L // SB; NBc = CTX // SB; NBP = NB + 2
        sb = self.sb; ps = self.ps
        urows = T['fmb'][2048:3072, :]
        for d in (0, 1):
            with ExitStack() as esd:
                KtL = esd.enter_context(sb("KtL", [128, 8, 16, 128], BF16))
                PWPr = esd.enter_context(sb("PWPr", [128, 17, 32], F32)); PWPi = esd.enter_context(sb("PWPi", [128, 17, 32], F32))
                self.s5_powers(T['s5_lamP'][layer, d, 0], T['s5_lamP'][layer, d, 1], T['s5_dtP'][layer, d], 128, 32,
                               PWPr[:], PWPi[:], None, None, 'pp')
                with ExitStack() as es:
                    def mk(nm, shape, dt=F32):
                        return es.enter_context(sb(nm, shape, dt))
                    pwr = mk("pwr", [64, 17, 64]); pwi = mk("pwi", [64, 17, 64]); cre = mk("cre", [64, 64]); cim = mk("cim", [64, 64])
                    self.s5_powers(T['s5_lam64'][layer, d, 0], T['s5_lam64'][layer, d, 1], T['s5_dt64'][layer, d], 64, 64,
                                   pwr[:], pwi[:], cre[:], cim[:], 'p64')
                    B64 = mk("B64", [64, 2, 64, 16]); C64 = mk("C64", [64, 2, 64, 16])
                    bbr = mk("bbr", [64, 64, 16]); bbi = mk("bbi", [64, 64, 16]); tt_ = mk("tt_", [64, 64, 16])
                    Wr = mk("Wr", [64, 64, 16]); Wi = mk("Wi", [64, 64, 16])
                    Wpr = mk("Wpr", [64, 64 * 128], BF16); Wpi = mk("Wpi", [64, 64 * 128], BF16)
                    Cpr = mk("Cpr", [64, 64 * 128], BF16); Cpi = mk("Cpi", [64, 64 * 128], BF16)
                    pk0 = es.enter_context(ps("pk0", [128, 512], F32)); pk1 = es.enter_context(ps("pk1", [128, 512], F32))
                    pks = [pk0, pk1]
                    P.dma('sp', B64[:], T['s5_b64'][layer, d].rearrange("x p g h -> p x g h"), writes=['B64'])
                    P.dma('sp', C64[:], T['s5_c64'][layer, d].rearrange("x p g h -> p x g h"), writes=['C64'])
                    for t_ in (Wpr, Wpi, Cpr, Cpi):
                        P.memset(t_[:], 0.0, ['pad' + t_.name])
                    P.barrier()
                    bc = lambda ap: ap.unsqueeze(2).to_broadcast([64, 64, 16])
                    self.cmul(bbr[:], bbi[:], bc(cre[:]), bc(cim[:]), B64[:, 0], B64[:, 1], tt_[:])
                    P.barrier()

                    def diag(t_):
                        b_ = t_[:]
                        return bass.AP(b_.tensor, b_.offset, [list(b_.ap[0]), [1024, 8], [144, 8], [1, 16]])
                    P.cp(diag(Cpr), C64[:, 0].rearrange("p (a b) h -> p a b h", b=8), [], ['cpr'])
                    P.cp(diag(Cpi), C64[:, 1].rearrange("p (a b) h -> p a b h", b=8), [], ['cpi'])
                    P.barrier()
                    for tau in range(16):
                        self.cmul(Wr[:], Wi[:], bc(pwr[:, tau, :]), bc(pwi[:, tau, :]), bbr[:], bbi[:], tt_[:], neg_im=True)
                        P.cp(diag(Wpr), Wr[:].rearrange("p (a b) h -> p a b h", b=8), ['cm_re'], ['Wpr'])
                        P.cp(diag(Wpi), Wi[:].rearrange("p (a b) h -> p a b h", b=8), ['cm_im'], ['Wpi'])
                        for gb in range(8):
                            pk = pks[gb % 2]; pkey = 'pk%d' % (gb % 2)
                            for g8 in range(8):
                                g = gb * 8 + g8
                                P.mm(pk[:, 0:128], Wpr[:, g * 128:(g + 1) * 128], Cpr[:, g * 128:(g + 1) * 128], g8 == 0, False, ['Wpr'], [pkey])
                                P.mm(pk[:, 0:128], Wpi[:, g * 128:(g + 1) * 128], Cpi[:, g * 128:(g + 1) * 128], False, g8 == 7, ['Wpi'], [pkey])
                            P.cp(KtL[:, gb, tau, :], pk[:, 0:128], [pkey], ['KtL'], eng=('act' if gb % 2 else 'dve'))
                    P.barrier()
                with ExitStack() as es:
                    pfr = es.enter_context(sb("pfrA", [128, 17, 64], F32)); pfi = es.enter_context(sb("pfiA", [128, 17, 64], F32))
                    cfr = es.enter_context(sb("cfrA", [128, 64], F32)); cfi = es.enter_context(sb("cfiA", [128, 64], F32))
                    for gb in range(8):
                        self.s5_powers(T['s5_lamF'][layer, d, 0][:, gb, :], T['s5_lamF'][layer, d, 1][:, gb, :], T['s5_dtF'][layer, d][:, gb, :],
                                       128, 64, pfr[:], pfi[:], cfr[:], cfi[:], 'pf')
                        P.dma('sp', T['s5pf'][gb, 0], pfr[:], reads=[], writes=['d1'])
                        P.dma('sp', T['s5pf'][gb, 1], pfi[:], reads=[], writes=['d2'])
                        P.dma('sp', T['s5cf'][gb, 0], cfr[:], reads=[], writes=['d3'])
                        P.dma('sp', T['s5cf'][gb, 1], cfi[:], reads=[], writes=['d4'])
                        P.barrier()
                with ExitStack() as es:
                    def mk(nm, shape, dt=F32):
                        return es.enter_context(sb(nm, shape, dt))
                    Ar = mk("Ar", [128, 32, NBP]); Ai = mk("Ai", [128, 32, NBP])
                    Pad = mk("Pad", [128, 4, 16, 2, 128], BF16)
                    Abr = mk("Abr", [128, 4, NBP], BF16); Abi = mk("Abi", [128, 4, NBP], BF16)
                    uT0 = mk("uT0", [128, L], BF16); uTs = [uT0, uT0]
                    maskQ = mk("maskQ", [128, 4, 128]); maskC = mk("maskC", [128, 2, 16])
                    pfr = mk("pfr", [128, 17, 64]); pfi = mk("pfi", [128, 17, 64]); cfr = mk("cfr", [128, 64]); cfi = mk("cfi", [128, 64])
                    BF_ = mk("BF_", [128, 2, 64]); bfr = mk("bfr", [128, 64]); bfi = mk("bfi", [128, 64]); tf = mk("tf", [128, 64])
                    Vr = mk("Vr", [128, 64]); Vi = mk("Vi", [128, 64])
                    cP = mk("cP", [128, 2, 32, 16]); CLr = mk("CLr", [128, 4, 16]); CLi = mk("CLi", [128, 4, 16]); tc_ = mk("tc_", [128, 4, 16])
                    s1 = mk("s1", [128, 32]); s2 = mk("s2", [128, 32]); s3 = mk("s3", [128, 32]); s4 = mk("s4", [128, 32])
                    ysb0 = mk("ysb0", [128, 512]); ysb1 = mk("ysb1", [128, 512]); yf_t = mk("yf_t", [128, 512]); y2 = mk("y2", [128, 512])
                    zb = mk("zb", [128, 512], BF16); dT = mk("dT", [128, 8])
                    pw0 = es.enter_context(ps("pw0", [128, 512], F32)); pw1 = es.enter_context(ps("pw1", [128, 512], F32))
                    py0 = es.enter_context(ps("py0", [128, 512], F32)); py1 = es.enter_context(ps("py1", [128, 512], F32))
                    P.dma('sp', maskQ[:], T['s5_maskQ'], writes=['maskQ'])
                    P.dma('sp', maskC[:], T['s5_maskC'], writes=['maskC'])
                    P.dma('sp', cP[:], T['s5_cP'][layer, d].rearrange("x p q h -> p x q h"), writes=['cP'])
                    P.dma('sp', dT[:], T['s5_dT'][layer], writes=['dT'])
                    P.memset(Pad[:], 0.0, ['Pad'])
                    P.memset(Ar[:], 0.0, ['A']); P.memset(Ai[:], 0.0, ['A'])
                    P.barrier()
                    nwp = 0
                    for gb in range(8):
                        uT = uTs[gb % 2]
                        P.dma('sp', uT[:], urows[gb * 128:(gb + 1) * 128, :], writes=[('uT', 0)])
                        P.dma('sp', pfr[:], T['s5pf'][gb, 0], writes=['cm_in'])
                        P.dma('sp', pfi[:], T['s5pf'][gb, 1], writes=['cm_in'])
                        P.dma('sp', cfr[:], T['s5cf'][gb, 0], writes=['cm_in'])
                        P.dma('sp', cfi[:], T['s5cf'][gb, 1], writes=['cm_in'])
                        P.dma('sp', BF_[:], T['s5_bF'][layer, d][:, :, gb, :].rearrange("x p f -> p x f"), writes=['cm_in'])
                        P.barrier()
                        self.cmul(bfr[:], bfi[:], cfr[:], cfi[:], BF_[:, 0], BF_[:, 1], tf[:])
                        P.barrier()
                        for j in range(16):
                            pw_ = (15 - j) if d == 0 else j
                            self.cmul(Vr[:], Vi[:], pfr[:, pw_, :], pfi[:, pw_, :], bfr[:], bfi[:], tf[:])
                            for q in range(4):
                                for x, V in ((0, Vr), (1, Vi)):
                                    P.tt(Pad[:, q, j, x, :].rearrange("p (a b) -> p a b", b=64), V[:].unsqueeze(1).to_broadcast([128, 2, 64]),
                                         maskQ[:, q, :].rearrange("p (a b) -> p a b", b=64), ALU.mult,
                                         ['cm_re', 'cm_im', 'maskQ'], ['Pad'])
                        uv = uT[:].rearrange("p (n j) -> p n j", j=16)
                        for q in range(4):
                            pair = gb * 4 + q
                            for x, Ax in ((0, Ar), (1, Ai)):
                                pw = (pw0, pw1)[nwp % 2]; pwk = 'pw%d' % (nwp % 2); nwp += 1
                                for j in range(16):
                                    P.mm(pw[:, 0:NB], Pad[:, q, j, x, :], uv[:, :, j], j == 0, j == 15, ['Pad', ('uT', 0)], [pwk])
                                if d == 0:
                                    P.cp(Ax[:, pair, 1:NB + 1], pw[:, 0:NB], [pwk], ['A'], eng=('act' if x else 'dve'))
                                else:
                                    P.cp(Ax[:, pair, 0:NBc], pw[:, 0:NBc], [pwk], ['A'], eng=('act' if x else 'dve'))
                                    P.cp(Ax[:, pair, NBc + 1:NB + 1], pw[:, NBc:NB], [pwk], ['A'], eng=('act' if x else 'dve'))
                        P.barrier()
                    s5stop = getattr(c, 's5stop', 9)
                    if s5stop <= 2:
                        continue
                    ar = PWPr[:, 16, :]; ai = PWPi[:, 16, :]
                    if d == 0:
                        steps = [(n + 1, n) for n in range(NB)]
                    else:
                        steps = [(n, n + 1) for n in range(NBc - 1, -1, -1)] + ['copy'] + [(n + 1, n + 2) for n in range(NB - 1, NBc - 1, -1)]
                    for st in steps:
                        if st == 'copy':
                            P.cp(Ar[:, :, NB + 1], Ar[:, :, 0], ['A'], ['A'])
                            P.cp(Ai[:, :, NB + 1], Ai[:, :, 0], ['A'], ['A'])
                            continue
                        pos, prev = st
                        P.tt(s1[:], ar, Ar[:, :, prev], ALU.mult, ['A'], ['s1'])
                        P.tt(s2[:], ai, Ai[:, :, prev], ALU.mult, ['A'], ['s2'])
                        P.tt(s3[:], ar, Ai[:, :, prev], ALU.mult, ['A'], ['s3'])
                        P.tt(s4[:], ai, Ar[:, :, prev], ALU.mult, ['A'], ['s4'])
                        P.tt(s1[:], s1[:], s2[:], ALU.subtract, ['s1', 's2'], ['s1'])
                        P.tt(s3[:], s3[:], s4[:], ALU.add, ['s3', 's4'], ['s3'])
                        P.tt(Ar[:, :, pos], Ar[:, :, pos], s1[:], ALU.add, ['A', 's1'], ['A'])
                        P.tt(Ai[:, :, pos], Ai[:, :, pos], s3[:], ALU.add, ['A', 's3'], ['A'])
                    P.barrier()
                    if s5stop <= 3:
                        continue
                    P.memset(Pad[:], 0.0, ['Pad'])
                    P.barrier()
                    npy = 0
                    for gb in range(8):
                        uT = uTs[gb % 2]
                        P.dma('sp', uT[:], urows[gb * 128:(gb + 1) * 128, :], writes=[('uT', 0)])
                        for i in range(16):
                            pw_ = (i + 1) if d == 0 else (16 - i)
                            bq = lambda ap: ap[:, gb * 4:gb * 4 + 4].unsqueeze(2).to_broadcast([128, 4, 16])
                            self.cmul(CLr[:], CLi[:], cP[:, 0, gb * 4:gb * 4 + 4, :], cP[:, 1, gb * 4:gb * 4 + 4, :],
                                      bq(PWPr[:, pw_, :]), bq(PWPi[:, pw_, :]), tc_[:], neg_im=True)
                            for q in range(4):
                                for x, CL in ((0, CLr), (1, CLi)):
                                    P.tt(Pad[:, q, i, x, 32 * q:32 * q + 32].rearrange("p (a b) -> p a b", b=16),
                                         CL[:, q, :].unsqueeze(1).to_broadcast([128, 2, 16]), maskC[:], ALU.mult,
                                         ['cm_re', 'cm_im', 'maskC'], ['Pad'])
                        uv = uT[:].rearrange("p (n j) -> p n j", j=16)
                        P.cp(Abr[:], Ar[:, gb * 4:gb * 4 + 4, :], ['A'], ['Ab'])
                        P.cp(Abi[:], Ai[:, gb * 4:gb * 4 + 4, :], ['A'], ['Ab'], eng='act')
                        for (t0, w) in tiles(0, L, 512):
                            py = (py0, py1)[npy % 2]; pyk = 'py%d' % (npy % 2); ysb = (ysb0, ysb1)[npy % 2]; ysk = 'ysb%d' % (npy % 2); npy += 1
                            n0 = t0 // SB; nbt = w // SB
                            pv = py[:, 0:w].rearrange("p (n j) -> p n j", j=16)
                            runs = []
                            if d == 0:
                                runs.append((n0, nbt, n0))
                            else:
                                a0 = n0; a1 = min(n0 + nbt, NBc)
                                if a1 > a0:
                                    runs.append((a0, a1 - a0, a0 + 1))
                                b0 = max(n0, NBc); b1 = n0 + nbt
                                if b1 > b0:
                                    runs.append((b0, b1 - b0, b0 + 2))
                            mms = []
                            for tau in range(16):
                                if d == 0:
                                    mms.append((pv[:, :, tau:16], KtL[:, gb, tau, :], uv[:, n0:n0 + nbt, 0:16 - tau]))
                                else:
                                    mms.append((pv[:, :, 0:16 - tau], KtL[:, gb, tau, :], uv[:, n0:n0 + nbt, tau:16]))
                            for q in range(4):
                                pair = gb * 4 + q
                                for i in range(16):
                                    for x, Ax in ((0, Abr), (1, Abi)):
                                        for (r0, rn, p0) in runs:
                                            mms.append((pv[:, r0 - n0:r0 - n0 + rn, i], Pad[:, q, i, x, :], Ax[:, q, p0:p0 + rn]))
                            for mi, (o_, l_, r_) in enumerate(mms):
                                P.mm(o_, l_, r_, mi == 0, mi == len(mms) - 1, ['KtL', 'Pad', 'Ab', ('uT', 0)], [pyk])
                            yrow = T['yfT'][gb * 128:(gb + 1) * 128, t0:t0 + w]
                            if d == 0:
                                P.cp(ysb[:, :w], py[:, :w], [pyk], [ysk], eng='act')
                                P.dma('act', yrow, ysb[:, :w], reads=[ysk], writes=[('yfT', npy)])
                            else:
                                P.dma('sp', yf_t[:, :w], yrow, writes=['yf_t'])
                                P.tt(ysb[:, :w], py[:, :w], yf_t[:, :w], ALU.add, [pyk, 'yf_t'], [ysk])
                                P.stt(ysb[:, :w], uT[:, t0:t0 + w], dT[:, gb:gb + 1], ysb[:, :w], ALU.mult, ALU.add, [('uT', 0), ysk, 'dT'], [ysk])
                                P.tt(y2[:, :w], ysb[:, :w], ysb[:, :w], ALU.mult, [ysk], ['y2'])
                                P.ts(y2[:, :w], y2[:, :w], 0.044715, ALU.mult, ['y2'], ['y2'], s2=1.0, op1=ALU.add)
                                P.tt(y2[:, :w], y2[:, :w], ysb[:, :w], ALU.mult, ['y2', ysk], ['y2'])
                                P.act(y2[:, :w], y2[:, :w], AF.Sigmoid, ['y2'], ['y2'], scale=1.5957691216057308)
                                P.tt(zb[:, :w], y2[:, :w], ysb[:, :w], ALU.mult, ['y2', ysk], ['zb'])
                                P.dma('act', T['zT'][gb * 128:(gb + 1) * 128, t0:t0 + w], zb[:, :w], reads=['zb'], writes=[('zT', npy)])
                        P.barrier()
                    P.barrier()
        if getattr(c, 's5stop', 9) <= 4:
            return
        with ExitStack() as es:
            def mk(nm, shape, dt=F32):
                return es.enter_context(sb(nm, shape, dt))
            Wg = mk("Wg", [128, 8, 1024], BF16); bg = mk("bg", [128, 8])
            z0 = mk("z0", [128, 8, 512], BF16); z1 = mk("z1", [128, 8, 512], BF16); g0 = mk("g0", [128, 8, 512], BF16); g1 = mk("g1", [128, 8, 512], BF16)
            sgt = mk("sgt", [128, 512]); sg2 = mk("sg2", [128, 512]); ob0 = mk("obx0", [128, 8, 512], BF16); ob1 = mk("obx1", [128, 8, 512], BF16)
            pg0 = es.enter_context(ps("pg0", [128, 512], F32)); pg1 = es.enter_context(ps("pg1", [128, 512], F32))
            P.dma('pool', Wg[:], T['w_glu'][layer].rearrange("(k p) c -> p k c", p=128), writes=['Wg'])
            P.dma('sp', bg[:], T['b_gluT'][layer], writes=['bg'])
            zv = T['zT'].rearrange("(k p) t -> p k t", p=128)
            gv_ = T['fmb'][3072:4096, :].rearrange("(k p) t -> p k t", p=128)
            ovs = T['oT'][2048:3072, :].rearrange("(k p) t -> p k t", p=128)
            npg = 0
            for ti, (t0, w) in enumerate(tiles(0, L, 512)):
                zt = (z0, z1)[ti % 2]; gt_ = (g0, g1)[ti % 2]; obx = (ob0, ob1)[ti % 2]; kz = ('z', ti % 2); kg = ('g', ti % 2); ko = ('obx', ti % 2)
                P.dma('sp', zt[:, :, :w], zv[:, :, t0:t0 + w], writes=[kz])
                P.dma('sp', gt_[:, :, :w], gv_[:, :, t0:t0 + w], writes=[kg])
                for oc in range(8):
                    pg = (pg0, pg1)[npg % 2]; pgk = 'pg%d' % (npg % 2); npg += 1
                    for k in range(8):
                        P.mm(pg[:, :w], Wg[:, k, oc * 128:(oc + 1) * 128], zt[:, k, :w], k == 0, k == 7, ['Wg', kz], [pgk])
                    P.act(sgt[:, :w], pg[:, :w], AF.Sigmoid, [pgk, 'bg'], ['sgt'], bias=bg[:, oc:oc + 1])
                    P.tt(sgt[:, :w], sgt[:, :w], zt[:, oc, :w], ALU.mult, ['sgt', kz], ['sgt'])
                    P.act(sg2[:, :w], gt_[:, oc, :w], AF.Silu, [kg], ['sg2'])
                    P.tt(obx[:, oc, :w], sgt[:, :w], sg2[:, :w], ALU.mult, ['sgt', 'sg2'], [ko])
                P.dma('act', ovs[:, :, t0:t0 + w], obx[:, :, :w], reads=[ko], writes=[('oTs', ti)])
            P.barrier()

    def sincos(self, x, shape, sin_out, cos_out, tag):
        P = self.P
        with self.sb(tag + "_ni", shape, I32) as ni, self.sb(tag + "_nf", shape, F32) as nf, \
                self.sb(tag + "_r", shape, F32) as r, self.sb(tag + "_m", shape, F32) as m:
            for which, out in ((0, sin_out), (1, cos_out)):
                kx = tag + 'x'
                if which == 1:
                    P.ts(r[:], x, PI / 2, ALU.add, [kx], [tag + 'r0'])
                    src = r[:]
                else:
                    P.cp(r[:], x, [kx], [tag + 'r0'])
                    src = r[:]
                P.ts(ni[:], src, 1.0 / TWO_PI, ALU.mult, [tag + 'r0'], [tag + 'ni'])
                P.cp(nf[:], ni[:], [tag + 'ni'], [tag + 'nf'])
                P.stt(r[:], nf[:], -TWO_PI, src, ALU.mult, ALU.add, [tag + 'nf', tag + 'r0'], [tag + 'r0'])
                P.ts(m[:], r[:], PI, ALU.is_gt, [tag + 'r0'], [tag + 'm'], s2=TWO_PI, op1=ALU.mult)
                P.tt(r[:], r[:], m[:], ALU.subtract, [tag + 'r0', tag + 'm'], [tag + 'r0'])
                P.ts(m[:], r[:], -PI, ALU.is_lt, [tag + 'r0'], [tag + 'm'], s2=TWO_PI, op1=ALU.mult)
                P.tt(r[:], r[:], m[:], ALU.add, [tag + 'r0', tag + 'm'], [tag + 'r0'])
                P.act(out, r[:], AF.Sin, [tag + 'r0'], [tag + 'out%d' % which])
            P.barrier()

    def phase_rope(self):
        c = self.cfg; P = self.P; T = self.T; L = c.L
        with self.sb("fi", [128, 32], I32) as fi, self.sb("fr", [128, 32], F32) as fr, self.sb("pp", [128, 2], F32) as pp, \
                self.sb("ang", [128, 64], F32) as ang, self.sb("sn", [128, 64], F32) as sn, self.sb("cs", [128, 64], F32) as cs, \
                self.sb("rp", [128, 512], F32) as rp:
            P.op('pool', lambda e: e.iota(fi[:], pattern=[[1, 32]], base=0, channel_multiplier=0), [], ['fi'])
            P.cp(fr[:], fi[:], ['fi'], ['fr'])
            P.act(fr[:], fr[:], AF.Exp, ['fr'], ['fr'], scale=-float(np.log(10000.0)) / 32.0)
            P.barrier()
            for (t0, w) in tiles(0, L, 128):
                P.dma('sp', pp[:w, :], T['pos'][t0:t0 + w, :], writes=['pp'])
                P.ts(ang[:w, 0:32], fr[:w, :], pp[:w, 0:1], ALU.mult, ['pp'], ['rpx'])
                P.ts(ang[:w, 32:64], fr[:w, :], pp[:w, 1:2], ALU.mult, ['pp'], ['rpx'])
                self.sincos(ang[:, :], [128, 64], sn[:, :], cs[:, :], 'rp')
                for h in range(4):
                    P.cp(rp[:, h * 64:(h + 1) * 64], cs[:, :], [], ['rp'])
                    P.cp(rp[:, 256 + h * 64:256 + (h + 1) * 64], sn[:, :], [], ['rp'], eng='pool')
                P.dma('sp', T['rope'][t0:t0 + w, :], rp[:w, :], reads=['rp'], writes=['ropeD'])
                P.barrier()

    def build(self):
        c = self.cfg; nc = self.nc; P = self.P; T = None
        self.declare(); T = self.T
        stack = []
        with nc.Block() as block:
            P.start()
            self.consts(stack)
            skip_pre = bool(getattr(c, 'inject', ()))
            if not skip_pre:
                self.phase_rope()
            hs = [T['hT0'], T['hT1'], T['hT2']]
            for layer in range(c.DEPTH):
                last = (layer == c.DEPTH - 1)
                if not getattr(c, 'noada', False):
                    self.phase_ada(layer)
                if not skip_pre:
                    self.phase_norm(hs[layer], 'hn', T['hnT'])
                    self.phase_inproj(layer)
                if c.stop == 'inproj':
                    break
                for name in ('hg', 'gl', 'rt'):
                    if c.stop is None or name in c.stop:
                        if name == 'hg' or getattr(c, 'oldgla', False):
                            self.phase_gla(layer, name)
                        else:
                            self.phase_gla2(layer, name)
                if c.stop is None or 's5' in c.stop:
                    self.phase_s5(layer)
                if c.stop is not None and 'out' not in c.stop:
                    break
                self.phase_outproj(layer, hs[layer], hs[layer + 1], last)
            if c.stop is None or 'out' in c.stop:
                self.phase_norm(hs[c.DEPTH], 'final', T['outT'])
            P.barrier()
            for cm in reversed(stack):
                cm.__exit__(None, None, None)
            P.finish()
        return nc


TM_COLS = np.concatenate([np.arange(0, 4096), np.arange(5120, 7168), np.arange(10272, 12320)])
FM_COLS = np.concatenate([np.arange(4096, 5120), np.arange(7200, 8224), np.arange(8224, 9248), np.arange(9248, 10272),
                          np.arange(12320, 13344), np.arange(7168, 7200)])


def fmT(v, nchunk):
    return np.ascontiguousarray(np.asarray(v, np.float32).reshape(nchunk, 128).T)


def gla_consts():
    C = 64; s = 16; f = np.float32
    out = np.zeros((2, 64, 577), f)
    j = np.arange(64)[:, None]; i = np.arange(64)[None, :]
    for d in (0, 1):
        T = (j <= i) if d == 0 else (j >= i)
        blk = i // s
        m = blk * s + s // 2
        if d == 0:
            QO = (j >= blk * s) & (j <= i)
            Tm = (j <= m)
        else:
            QO = (j <= blk * s + s - 1) & (j >= i)
            Tm = (j >= m)
        QD = T.astype(f) - Tm.astype(f)
        M = out[d]
        M[:, 0:64] = QO; M[:, 64:128] = QD; M[:, 128:192] = -QD
        for x, I in enumerate([1, 2, 3] if d == 0 else [0, 1, 2]):
            if d == 0:
                KO = (i < I * s) & (j > i) & (j <= I * s - 1)
            else:
                KO = (i >= (I + 1) * s) & (j >= (I + 1) * s) & (j < i)
            M[:, 192 + 64 * x:192 + 64 * (x + 1)] = KO
        M[:, 384:448] = T; M[:, 448] = 1.0
        M[:, 449:513] = (j > i) if d == 0 else (j < i)
        M[:, 513:577] = T
    return out


def gla_consts2():
    C = 128; r = 64; f = np.float32
    out = np.zeros((2, C, 513), f)
    j = np.arange(C)[:, None]; i = np.arange(C)[None, :]
    for d in (0, 1):
        T = (j <= i) if d == 0 else (j >= i)
        R = (j <= r) if d == 0 else (j >= r)
        M = out[d]
        M[:, 0:128] = T.astype(f) - (R & (i >= 0)).astype(f)
        M[:, 128:256] = T; M[:, 256] = 1.0
        M[:, 257:385] = (j > i) if d == 0 else (j < i)
        M[:, 385:513] = T
    return out


def prep_shared(cfg, inp):
    DP = cfg.DEPTH; f = np.float32
    sh = {}
    sh['w_ada'] = np.ascontiguousarray(inp['w_ada'][:DP], f)
    sh['b_adaT'] = np.stack([fmT(inp['b_ada'][l], 96) for l in range(DP)])
    sh['norm_gT'] = np.stack([fmT(inp['norm_g'][l], 32) for l in range(DP)])
    sh['final_gT'] = fmT(inp['final_norm_g'], 32)
    w_in = np.asarray(inp['w_in'][:DP], f)
    sh['w_tm'] = np.ascontiguousarray(w_in[:, :, TM_COLS])
    sh['w_fm'] = np.ascontiguousarray(w_in[:, :, FM_COLS])
    sh['w_out'] = np.ascontiguousarray(inp['w_out'][:DP], f)
    sh['w_glu'] = np.ascontiguousarray(inp['s5_w_glu'][:DP], f)
    sh['b_gluT'] = np.stack([fmT(inp['s5_b_glu'][l], 8) for l in range(DP)])
    lb = np.asarray(inp['hgrn_lb_logits'], f)
    sh['hg_lbrep'] = np.ascontiguousarray(np.broadcast_to(lb[None], (128, 2, 2, 1024)))
    sh['hg_ngT'] = np.asarray(inp['hgrn_norm_g'][:DP], f).reshape(DP, 128, 1).copy()
    wg = np.concatenate([np.asarray(inp['gla_w_gk'][:DP], f), np.asarray(inp['gla_b_gk'][:DP], f)[:, :, None, :]], axis=2)
    sh['gla_wg'] = np.ascontiguousarray(wg)
    sh['gla_ngT'] = np.stack([fmT(inp['gla_norm_g'][l], 2) for l in range(DP)])
    sh['ret_ngT'] = np.stack([fmT(inp['ret_norm_g'][l], 2) for l in range(DP)])
    rl = np.asarray(inp['ret_decay_logit'][:DP], f)
    sh['ret_lrep'] = np.ascontiguousarray(np.broadcast_to(rl[:, :, None, :], (DP, 2, 128, 4)))
    L = cfg.L
    pos = np.zeros((L, 2), f)
    t = np.arange(cfg.LAT)
    pos[cfg.CTX:, 0] = t // 64; pos[cfg.CTX:, 1] = t % 64
    sh['pos'] = pos
    sh['gl_consts'] = gla_consts()
    sh['gl_consts2'] = gla_consts2()
    lam = np.stack([np.asarray(inp['s5_lam_re'][:DP], f), np.asarray(inp['s5_lam_im'][:DP], f)], axis=2)
    dt = np.asarray(inp['s5_log_dt'][:DP], f)
    B = np.stack([np.asarray(inp['s5_b_re'][:DP], f), np.asarray(inp['s5_b_im'][:DP], f)], axis=2)
    Cm = np.stack([np.asarray(inp['s5_c_re'][:DP], f), np.asarray(inp['s5_c_im'][:DP], f)], axis=2)
    sh['s5_lamP'] = np.ascontiguousarray(lam.reshape(DP, 2, 2, 32, 2, 64).transpose(0, 1, 2, 4, 5, 3).reshape(DP, 2, 2, 128, 32))
    dtb = np.broadcast_to(dt[:, :, :, None], (DP, 2, 64, 64))
    sh['s5_dtP'] = np.ascontiguousarray(dtb.reshape(DP, 2, 32, 2, 64).transpose(0, 1, 3, 4, 2).reshape(DP, 2, 128, 32))
    sh['s5_lam64'] = np.ascontiguousarray(lam.transpose(0, 1, 2, 4, 3))
    sh['s5_dt64'] = np.ascontiguousarray(dtb.transpose(0, 1, 3, 2))
    sh['s5_b64'] = np.ascontiguousarray(B.transpose(0, 1, 2, 4, 3, 5))
    sh['s5_c64'] = np.ascontiguousarray(Cm.transpose(0, 1, 2, 5, 3, 4))
    lamF = np.broadcast_to(lam.reshape(DP, 2, 2, 8, 8, 1, 64), (DP, 2, 2, 8, 8, 16, 64))
    sh['s5_lamF'] = np.ascontiguousarray(lamF.transpose(0, 1, 2, 4, 5, 3, 6).reshape(DP, 2, 2, 128, 8, 64))
    dtF = np.broadcast_to(dt.reshape(DP, 2, 8, 8, 1, 1), (DP, 2, 8, 8, 16, 64))
    sh['s5_dtF'] = np.ascontiguousarray(dtF.transpose(0, 1, 3, 4, 2, 5).reshape(DP, 2, 128, 8, 64))
    BF = B.reshape(DP, 2, 2, 8, 8, 64, 16)
    sh['s5_bF'] = np.ascontiguousarray(BF.transpose(0, 1, 2, 4, 6, 3, 5).reshape(DP, 2, 2, 128, 8, 64))
    CP = Cm.reshape(DP, 2, 2, 32, 2, 16, 64)
    sh['s5_cP'] = np.ascontiguousarray(CP.transpose(0, 1, 2, 4, 6, 3, 5).reshape(DP, 2, 2, 128, 32, 16))
    dd = np.asarray(inp['s5_d'][:DP], f).reshape(DP, 8, 128)
    sh['s5_dT'] = np.ascontiguousarray(dd.transpose(0, 2, 1))
    row = np.arange(128)[:, None, None]; qq = np.arange(4)[None, :, None]; col = np.arange(128)[None, None, :]
    sh['s5_maskQ'] = ((row // 32 == qq) & ((row % 32) // 16 == col // 64)).astype(f)
    sh['s5_maskC'] = np.ascontiguousarray(np.broadcast_to((np.arange(128)[:, None, None] // 64 == np.arange(2)[None, :, None]), (128, 2, 16))).astype(f)
    return sh


def prep_core(cfg, inp, b):
    f = np.float32
    m = {}
    h0 = np.concatenate([np.asarray(inp['ctx'][b], f), np.asarray(inp['x'][b], f)], axis=0)
    m['hT0'] = np.ascontiguousarray(h0.T)
    c2 = np.stack([np.asarray(inp['c'][b], f), np.asarray(inp['c_ctx'], f)], axis=0)
    m['c2T'] = np.ascontiguousarray(c2.reshape(2, 32, 128).transpose(2, 1, 0))
    return m


_CACHE = {}


def kernel(**inputs):
    cfg = Cfg()
    if 'nc' not in _CACHE:
        _CACHE['nc'] = Builder(cfg).build()
    nc = _CACHE['nc']
    sh = prep_shared(cfg, inputs)
    in_maps = []
    for core in range(8):
        m = dict(sh)
        m.update(prep_core(cfg, inputs, core % 4))
        in_maps.append(m)
    res = run_bass_kernel_spmd(nc, in_maps, core_ids=list(range(8)))
    out = np.stack([np.ascontiguousarray(res.results[b]['outT'].T) for b in range(4)], axis=0)
    return out.astype(np.float32)
```

```python
import numpy as np
import concourse.bass as bass
import concourse.mybir as mybir
from concourse.bass_utils import run_bass_kernel_spmd

F32 = mybir.dt.float32; BF16 = mybir.dt.bfloat16; I32 = mybir.dt.int32
AF = mybir.ActivationFunctionType; ALU = mybir.AluOpType
D = 4096; KC = 32; EPS = 1e-6
TWO_PI = 6.283185307179586; PI = 3.141592653589793


class Prog:
    NDS = 8

    def __init__(self, nc):
        self.nc = nc
        self.eng = {'pe': nc.tensor, 'act': nc.scalar, 'dve': nc.vector, 'pool': nc.gpsimd, 'sp': nc.sync}
        self.sem = {}; self.cnt = {}; self.dsem = {}; self.dcnt = {}
        self.waited = {k: {} for k in self.eng}
        self.res = {}
        self._stack = []
        self.nins = 0

    def start(self):
        nc = self.nc
        for k in self.eng:
            cm = nc.semaphore("s_" + k); self._stack.append(cm); self.sem[k] = cm.__enter__(); self.cnt[k] = 0
        for k in ['sp', 'pool', 'act']:
            self.dsem[k] = []
            for i in range(self.NDS):
                cm = nc.semaphore("d_%s%d" % (k, i)); self._stack.append(cm); self.dsem[k].append(cm.__enter__())
            self.dcnt[k] = 0
        self.last_dma_tok = {k: [None] * self.NDS for k in self.dsem}

    def finish(self):
        for cm in reversed(self._stack):
            cm.__exit__(None, None, None)

    def _wait(self, e, tok):
        if tok is None:
            return
        sem, val, owner = tok
        w = self.waited[e]
        key = id(sem)
        if w.get(key, 0) >= val:
            return
        if owner == e and e == 'pe':
            return
        self.eng[e].wait_ge(sem, val)
        w[key] = val

    def _deps(self, e, reads, writes):
        toks = []
        for k in reads:
            st = self.res.get(k)
            if st and st['w']:
                toks.append(st['w'])
        for k in writes:
            st = self.res.get(k)
            if st:
                if st['w']:
                    toks.append(st['w'])
                toks.extend(st['r'])
        for t in toks:
            self._wait(e, t)

    def _update(self, tok, reads, writes):
        for k in reads:
            st = self.res.setdefault(k, {'w': None, 'r': []})
            st['r'].append(tok)
            if len(st['r']) > 48:
                best = {}
                for t in st['r']:
                    kk = id(t[0])
                    if kk not in best or best[kk][1] < t[1]:
                        best[kk] = t
                st['r'] = list(best.values())
        for k in writes:
            self.res[k] = {'w': tok, 'r': []}

    def op(self, e, fn, reads=(), writes=()):
        self._deps(e, reads, writes)
        ins = fn(self.eng[e])
        self.cnt[e] += 1
        self.nins += 1
        ins.then_inc(self.sem[e], 1)
        tok = (self.sem[e], self.cnt[e], e)
        self._update(tok, reads, writes)
        return tok

    def dma(self, q, out, in_, reads=(), writes=(), **kw):
        self._deps(q, reads, writes)
        i = self.dcnt[q]; k = i % self.NDS
        prev = self.last_dma_tok[q][k]
        if prev is not None:
            self._wait(q, prev)
        ins = self.eng[q].dma_start(out=out, in_=in_, **kw)
        val = 16 * (i // self.NDS + 1)
        ins.then_inc(self.dsem[q][k], 16)
        tok = (self.dsem[q][k], val, 'dma_' + q)
        self.last_dma_tok[q][k] = tok
        self.dcnt[q] += 1
        self.nins += 1
        self._update(tok, reads, writes)
        return tok

    def barrier(self):
        toks = []
        for e in self.eng:
            if self.cnt[e] > 0:
                toks.append((self.sem[e], self.cnt[e], e))
        for q in self.dsem:
            for t in self.last_dma_tok[q]:
                if t is not None:
                    toks.append(t)
        for e in self.eng:
            for t in toks:
                self._wait(e, t)
        self.res = {}

    def mm(self, out, lhsT, rhs, start, stop, reads, writes):
        return self.op('pe', lambda e: e.matmul(out, lhsT=lhsT, rhs=rhs, start=start, stop=stop), reads, writes)

    def tr(self, out, in_, ident, reads, writes):
        return self.op('pe', lambda e: e.transpose(out=out, in_=in_, identity=ident), reads, writes)

    def act(self, out, in_, func, reads, writes, bias=None, scale=None, eng='act'):
        kw = {}
        if bias is not None:
            kw['bias'] = bias
        if scale is not None:
            kw['scale'] = scale
        return self.op(eng, lambda e: e.activation(out=out, in_=in_, func=func, **kw), reads, writes)

    def tt(self, out, in0, in1, op, reads, writes, eng='dve'):
        return self.op(eng, lambda e: e.tensor_tensor(out=out, in0=in0, in1=in1, op=op), reads, writes)

    def ts(self, out, in0, s1, op0, reads, writes, s2=None, op1=None, eng='dve'):
        if op1 is None:
            return self.op(eng, lambda e: e.tensor_scalar(out=out, in0=in0, scalar1=s1, scalar2=None, op0=op0), reads, writes)
        return self.op(eng, lambda e: e.tensor_scalar(out=out, in0=in0, scalar1=s1, scalar2=s2, op0=op0, op1=op1), reads, writes)

    def stt(self, out, in0, scalar, in1, op0, op1, reads, writes):
        return self.op('dve', lambda e: e.scalar_tensor_tensor(out=out, in0=in0, scalar=scalar, in1=in1, op0=op0, op1=op1), reads, writes)

    def cp(self, out, in_, reads, writes, eng='dve'):
        if eng == 'act':
            return self.op('act', lambda e: e.copy(out=out, in_=in_), reads, writes)
        return self.op(eng, lambda e: e.tensor_copy(out=out, in_=in_), reads, writes)

    def memset(self, ap, val, writes, eng='pool'):
        return self.op(eng, lambda e: e.memset(ap, val), (), writes)


class Cfg:
    def __init__(self, CTX=256, LAT=4096, DEPTH=2, debug=False, stop=None):
        self.CTX = CTX; self.LAT = LAT; self.DEPTH = DEPTH; self.L = CTX + LAT
        self.debug = debug; self.stop = stop


def tiles(lo, hi, w):
    out = []
    t = lo
    while t < hi:
        ww = min(w, hi - t)
        out.append((t, ww))
        t += ww
    return out


class Builder:
    def __init__(self, cfg):
        self.cfg = cfg
        self.nc = bass.Bass("TRN2", target_bir_lowering=False)
        self.P = Prog(self.nc)
        self.T = {}
        self.dbg_names = []

    def din(self, name, shape, dt=F32):
        self.T[name] = self.nc.dram_tensor(name, list(shape), dt, kind="ExternalInput").ap()

    def dscr(self, name, shape, dt, out=False):
        if name in getattr(self.cfg, 'inject', ()):
            self.T[name] = self.nc.dram_tensor(name, list(shape), dt, kind="ExternalInput").ap()
            return
        if out or (self.cfg.debug and name in self.cfg.debug):
            self.T[name] = self.nc.dram_tensor(name, list(shape), dt, kind="ExternalOutput").ap()
            self.dbg_names.append(name)
        else:
            self.T[name] = self.nc.dram_tensor(name, list(shape), dt).ap()

    def sb(self, name, shape, dt):
        self._uid = getattr(self, '_uid', 0) + 1
        return self.nc.sbuf_tensor("%s_%d" % (name, self._uid), list(shape), dt)

    def ps(self, name, shape, dt=F32):
        self._uid = getattr(self, '_uid', 0) + 1
        return self.nc.psum_tensor("%s_%d" % (name, self._uid), list(shape), dt)

    def declare(self):
        c = self.cfg; L = c.L; DP = c.DEPTH
        self.din("hT0", [D, L]); self.din("c2T", [128, 32, 2])
        self.din("w_ada", [DP, D, 3 * D]); self.din("b_adaT", [DP, 128, 96])
        self.din("norm_gT", [DP, 128, 32]); self.din("final_gT", [128, 32])
        self.din("w_tm", [DP, D, 8192]); self.din("w_fm", [DP, D, 5152])
        self.din("w_out", [DP, D, D]); self.din("w_glu", [DP, 1024, 1024]); self.din("b_gluT", [DP, 128, 8])
        self.din("hg_lbrep", [128, 2, 2, 1024]); self.din("hg_ngT", [DP, 128, 1])
        self.din("gla_wg", [DP, 2, 17, 512]); self.din("gla_ngT", [DP, 128, 2])
        self.din("ret_ngT", [DP, 128, 2]); self.din("ret_lrep", [DP, 2, 128, 4])
        self.din("pos", [L, 2]); self.din("gl_consts", [2, 64, 449]); self.din("gl_consts2", [2, 128, 513])
        self.din("s5_lamP", [DP, 2, 2, 128, 32])
        self.din("s5_dtP", [DP, 2, 128, 32])
        self.din("s5_lam64", [DP, 2, 2, 64, 64])
        self.din("s5_dt64", [DP, 2, 64, 64])
        self.din("s5_b64", [DP, 2, 2, 64, 64, 16])
        self.din("s5_c64", [DP, 2, 2, 64, 64, 16])
        self.din("s5_lamF", [DP, 2, 2, 128, 8, 64])
        self.din("s5_dtF", [DP, 2, 128, 8, 64])
        self.din("s5_bF", [DP, 2, 2, 128, 8, 64])
        self.din("s5_cP", [DP, 2, 2, 128, 32, 16])
        self.din("s5_dT", [DP, 128, 8]); self.din("s5_maskQ", [128, 4, 128]); self.din("s5_maskC", [128, 2, 16])
        self.dscr("hT1", [D, L], F32); self.dscr("hT2", [D, L], F32)
        self.dscr("hnT", [D, L], BF16)
        self.dscr("tmb", [L, 8192], BF16); self.dscr("tmf", [L, 2048], F32)
        self.dscr("fmb", [5120, L], BF16); self.dscr("lrT", [32, L], F32)
        self.dscr("ofT", [3072, L], F32); self.dscr("oT", [D, L], BF16)
        self.dscr("rope", [L, 512], F32)
        self.dscr("s5pf", [8, 2, 128, 17, 64], F32); self.dscr("s5cf", [8, 2, 128, 64], F32)
        self.dscr("yfT", [1024, L], F32); self.dscr("zT", [1024, L], BF16)
        self.dscr("outT", [D, c.LAT], F32, out=True)

    def consts(self, stack):
        P = self.P
        def mk(name, shape, dt):
            cm = self.sb(name, shape, dt); stack.append(cm); return cm.__enter__()
        self.onesf = mk("onesf", [128, 128], F32)
        self.onesb = mk("onesb", [128, 128], BF16)
        self.identb = mk("identb", [128, 128], BF16)
        self.identf = mk("identf", [128, 128], F32)
        self.sT = mk("sT", [128, 32, 2], BF16)
        self.GS = mk("GS", [128, 6, 32], F32)
        P.memset(self.onesf[:], 1.0, ['onesf'])
        P.memset(self.onesb[:], 1.0, ['onesb'])
        P.op('pool', lambda e: e.affine_select(out=self.identf[:], in_=self.onesf[:], pattern=[[1, 128]], compare_op=ALU.is_equal,
                                               fill=0.0, base=0, channel_multiplier=-1), ['onesf'], ['identf'])
        P.cp(self.identb[:], self.identf[:], ['identf'], ['identb'])
        with self.sb("c2f", [128, 32, 2], F32) as c2f:
            P.dma('sp', c2f[:], self.T['c2T'], writes=['c2f'])
            P.act(self.sT[:], c2f[:], AF.Silu, ['c2f'], ['sT'])
            P.barrier()

    def phase_ada(self, layer):
        P = self.P; T = self.T; GS = self.GS
        wv = T['w_ada'][layer].rearrange("(k p) c -> p k c", p=128)
        with self.sb("wa0", [128, 32, 256], BF16) as wa0, self.sb("wa1", [128, 32, 256], BF16) as wa1, \
                self.ps("pa", [128, 512], F32) as pa, self.sb("modT", [128, 96, 2], F32) as modT, \
                self.sb("bT", [128, 96], F32) as bT, self.sb("ng", [128, 32], F32) as ng:
            was = [wa0, wa1]
            P.dma('sp', bT[:], T['b_adaT'][layer], writes=['bT'])
            P.dma('sp', ng[:], T['norm_gT'][layer], writes=['ng'])
            for cb in range(48):
                wa = was[cb % 2]; key = ('wa', cb % 2)
                P.dma('pool', wa[:], wv[:, :, cb * 256:(cb + 1) * 256], writes=[key])
                for s in range(2):
                    j = cb * 2 + s
                    for k in range(32):
                        P.mm(pa[:, 2 * j:2 * j + 2], wa[:, k, s * 128:(s + 1) * 128], self.sT[:, k, :], k == 0, k == 31,
                             [key, 'sT'], ['pa'])
            P.tt(modT[:], pa[:, 0:192].rearrange("p (j r) -> p j r", r=2), bT[:].unsqueeze(2).to_broadcast([128, 96, 2]), ALU.add,
                 ['pa', 'bT'], ['modT'])
            for r in range(2):
                P.stt(GS[:, 3 * r + 0, :], modT[:, 32:64, r], 1.0, ng[:], ALU.add, ALU.mult, ['modT', 'ng'], ['GS'])
                P.cp(GS[:, 3 * r + 1, :], modT[:, 0:32, r], ['modT'], ['GS'])
                P.cp(GS[:, 3 * r + 2, :], modT[:, 64:96, r], ['modT'], ['GS'])
            P.barrier()

    def phase_norm(self, hsrc, mode, dst):
        c = self.cfg; P = self.P; T = self.T; GS = self.GS; L = c.L; CTX = c.CTX
        TW = 256
        hv = hsrc.rearrange("(k p) t -> p k t", p=128)
        dv = dst.rearrange("(k p) t -> p k t", p=128)
        toks = tiles(0, L, TW) if mode == 'hn' else tiles(CTX, L, TW)
        with self.sb("hTa", [128, 32, TW], F32) as hTa, self.sb("hTb", [128, 32, TW], F32) as hTb, \
                self.sb("sq", [128, 32, TW], BF16) as sq, self.sb("hna", [128, 32, TW], BF16) as hna, \
                self.sb("hnb", [128, 32, TW], BF16) as hnb, self.sb("rstd", [128, TW], F32) as rstd, \
                self.sb("fg", [128, 32], F32) as fg, \
                self.ps("pssa", [128, 512], F32) as pssa, self.ps("pssb", [128, 512], F32) as pssb:
            hTs = [hTa, hTb]; hns = [hna, hnb]; psss = [pssa, pssb]
            if mode == 'final':
                P.dma('sp', fg[:], T['final_gT'], writes=['fg'])
            for it, (t0, w) in enumerate(toks):
                b = it % 2
                hT = hTs[b]; hn = hns[b]; pss = psss[b]
                P.dma('sp', hT[:, :, :w], hv[:, :, t0:t0 + w], writes=[('hT', b)])
                P.act(sq[:, :, :w], hT[:, :, :w], AF.Square, [('hT', b)], ['sq'])
                for k in range(32):
                    P.mm(pss[:, :w], self.onesb[:], sq[:, k, :w], k == 0, k == 31, ['sq', 'onesb'], [('pss', b)])
                P.act(rstd[:, :w], pss[:, :w], AF.Sqrt, [('pss', b)], ['rstd'], bias=EPS, scale=1.0 / D)
                P.op('dve', lambda e: e.reciprocal(out=rstd[:, :w], in_=rstd[:, :w]), ['rstd'], ['rstd'])
                P.tt(hT[:, :, :w], hT[:, :, :w], rstd[:, :w].unsqueeze(1).to_broadcast([128, 32, w]), ALU.mult,
                     [('hT', b), 'rstd'], [('hT', b)])
                if mode == 'hn':
                    gi = 3 if t0 < CTX else 0
                    for k in range(32):
                        if k % 2 == 0:
                            P.act(hn[:, k, :w], hT[:, k, :w], AF.Identity, [('hT', b), 'GS'], [('hn', b, k)],
                                  bias=GS[:, gi + 1, k:k + 1], scale=GS[:, gi, k:k + 1])
                        else:
                            P.ts(hn[:, k, :w], hT[:, k, :w], GS[:, gi, k:k + 1], ALU.mult, [('hT', b), 'GS'], [('hn', b, k)],
                                 s2=GS[:, gi + 1, k:k + 1], op1=ALU.add)
                    P.dma('act', dv[:, :, t0:t0 + w], hn[:, :, :w], reads=[('hn', b, k) for k in range(32)], writes=[('dst', it)])
                else:
                    P.tt(hT[:, :, :w], hT[:, :, :w], fg[:].unsqueeze(2).to_broadcast([128, 32, w]), ALU.mult,
                         [('hT', b), 'fg'], [('hT', b)])
                    P.dma('act', dv[:, :, t0 - CTX:t0 - CTX + w], hT[:, :, :w], reads=[('hT', b)], writes=[('dst', it)])
            P.barrier()

    def phase_inproj(self, layer):
        c = self.cfg; P = self.P; T = self.T; L = c.L
        blocks = []
        for bi in range(16):
            blocks.append(('tm', 'w_tm', bi * 512, 512))
        for bi in range(10):
            blocks.append(('fm', 'w_fm', bi * 512, 512))
        blocks.append(('lr', 'w_fm', 5120, 32))
        hv = T['hnT'].rearrange("(k p) t -> p k t", p=128)
        ttiles = tiles(0, L, 512)
        with self.sb("Wb0", [128, 32, 512], BF16) as Wb0, self.sb("Wb1", [128, 32, 512], BF16) as Wb1, \
                self.sb("hb0", [128, 32, 512], BF16) as hb0, self.sb("hb1", [128, 32, 512], BF16) as hb1, \
                self.sb("stgf", [128, 4, 512], F32) as stgf, self.sb("stgb", [128, 4, 512], BF16) as stgb, \
                self.ps("pd", [128, 4, 512], F32) as pd:
            Wbs = [Wb0, Wb1]; hbs = [hb0, hb1]
            gt = 0; gp = 0
            for bi, (kind, wn, c0, ncol) in enumerate(blocks):
                Wb = Wbs[bi % 2]; wkey = ('Wb', bi % 2)
                wv = T[wn][layer].rearrange("(k p) c -> p k c", p=128)
                P.dma('pool', Wb[:, :, :ncol], wv[:, :, c0:c0 + ncol], writes=[wkey])
                for (t0, w) in ttiles:
                    hb = hbs[gt % 2]; hkey = ('hb', gt % 2); gt += 1
                    P.dma('sp', hb[:, :, :w], hv[:, :, t0:t0 + w], writes=[hkey])
                    if kind in ('fm', 'lr'):
                        nsub = max(1, ncol // 128); m = min(128, ncol)
                        for cs in range(nsub):
                            bk = gp % 4; gp += 1
                            for k in range(32):
                                P.mm(pd[:m, bk, :w], Wb[:, k, cs * 128:cs * 128 + m], hb[:, k, :w], k == 0, k == 31,
                                     [wkey, hkey], [('pd', bk)])
                            if kind == 'fm':
                                stg = stgb; skey = ('stgb', bk); dst = T['fmb'][c0 + cs * 128:c0 + cs * 128 + m, t0:t0 + w]
                            else:
                                stg = stgf; skey = ('stgf', bk); dst = T['lrT'][0:32, t0:t0 + w]
                            P.cp(stg[:m, bk, :w], pd[:m, bk, :w], [('pd', bk)], [skey], eng=('act' if bk % 2 else 'dve'))
                            P.dma('act', dst, stg[:m, bk, :w], reads=[skey], writes=[('o', gp)])
                    else:
                        isf = (1024 <= c0 < 3072)
                        for ts in range(w // 128):
                            bk = gp % 4; gp += 1
                            for k in range(32):
                                P.mm(pd[:, bk, :], hb[:, k, ts * 128:(ts + 1) * 128], Wb[:, k, :], k == 0, k == 31,
                                     [wkey, hkey], [('pd', bk)])
                            r0 = t0 + ts * 128
                            if isf:
                                stg = stgf; skey = ('stgf', bk); dst = T['tmf'][r0:r0 + 128, c0 - 1024:c0 - 1024 + 512]
                            else:
                                stg = stgb; skey = ('stgb', bk); dst = T['tmb'][r0:r0 + 128, c0:c0 + 512]
                            P.cp(stg[:, bk, :], pd[:, bk, :], [('pd', bk)], [skey], eng=('act' if bk % 2 else 'dve'))
                            P.dma('act', dst, stg[:, bk, :], reads=[skey], writes=[('o', gp)])
            P.barrier()

    def phase_outproj(self, layer, hsrc, hdst, last):
        c = self.cfg; P = self.P; T = self.T; L = c.L; CTX = c.CTX; GS = self.GS
        wv = T['w_out'][layer].rearrange("(k p) c -> p k c", p=128)
        ov = T['oT'].rearrange("(k p) t -> p k t", p=128)
        ttiles = ([] if last else tiles(0, CTX, 512)) + tiles(CTX, L, 512)
        with self.sb("Wo0", [128, 32, 512], BF16) as Wb0, self.sb("Wo1", [128, 32, 512], BF16) as Wb1, \
                self.sb("ob0", [128, 32, 512], BF16) as hb0, self.sb("ob1", [128, 32, 512], BF16) as hb1, \
                self.sb("hold0", [128, 4, 512], F32) as hold0, self.sb("hold1", [128, 4, 512], F32) as hold1, \
                self.ps("po", [128, 4, 512], F32) as pd:
            Wbs = [Wb0, Wb1]; hbs = [hb0, hb1]; holds = [hold0, hold1]
            gt = 0; gp = 0
            for fb in range(8):
                Wb = Wbs[fb % 2]; wkey = ('Wb', fb % 2)
                P.dma('pool', Wb[:], wv[:, :, fb * 512:(fb + 1) * 512], writes=[wkey])
                hs = hsrc[fb * 512:(fb + 1) * 512, :].rearrange("(f p) t -> p f t", p=128)
                hd = hdst[fb * 512:(fb + 1) * 512, :].rearrange("(f p) t -> p f t", p=128)
                for (t0, w) in ttiles:
                    hb = hbs[gt % 2]; hkey = ('hb', gt % 2); hold = holds[gt % 2]; okey = ('hold', gt % 2); gt += 1
                    P.dma('sp', hb[:, :, :w], ov[:, :, t0:t0 + w], writes=[hkey])
                    P.dma('sp', hold[:, :, :w], hs[:, :, t0:t0 + w], writes=[okey])
                    gi = 3 if t0 < CTX else 0
                    for fs in range(4):
                        bk = gp % 4; gp += 1
                        fch = fb * 4 + fs
                        for k in range(32):
                            P.mm(pd[:, bk, :w], Wb[:, k, fs * 128:(fs + 1) * 128], hb[:, k, :w], k == 0, k == 31,
                                 [wkey, hkey], [('pd', bk)])
                        P.stt(hold[:, fs, :w], pd[:, bk, :w], GS[:, gi + 2, fch:fch + 1], hold[:, fs, :w], ALU.mult, ALU.add,
                              [('pd', bk), okey, 'GS'], [okey])
                    P.dma('act', hd[:, :, t0:t0 + w], hold[:, :, :w], reads=[okey], writes=[('o', gt)])
            P.barrier()

    BR = {
        'hg': dict(H=8, dv=128, q=0, k=None, v=3072, gate=0, orow=0, ofrow=0, gs=1.0, qs=1.0, norm='rms'),
        'gl': dict(H=4, dv=256, q=4096, k=4608, v=5120, gate=1024, orow=1024, ofrow=1024, gs=-1.0 / 16.0, qs=128.0 ** -0.5, norm='rms'),
        'rt': dict(H=4, dv=256, q=6144, k=6656, v=7168, gate=4096, orow=3072, ofrow=2048, gs=1.0, qs=128.0 ** -0.5, norm='ln'),
    }

    def phase_gla(self, layer, name):
        c = self.cfg; P = self.P; T = self.T; L = c.L; CTX = c.CTX
        br = self.BR[name]; H = br['H']; dv = br['dv']; dvc = dv // 128; HW = H * 128; C = 64
        NCH = L // C; ctxch = CTX // C; r = C // 2
        orders = [list(range(NCH)), list(range(ctxch - 1, -1, -1)) + list(range(NCH - 1, ctxch - 1, -1))]
        ofv = T['ofT'][br['ofrow']:br['ofrow'] + 1024, :].rearrange("(c p) t -> p c t", p=128)
        ov = T['oT'][br['orow']:br['orow'] + 1024, :].rearrange("(c p) t -> p c t", p=128)
        gv = T['fmb'][br['gate']:br['gate'] + 1024, :].rearrange("(c p) t -> p c t", p=128)
        sb = self.sb; ps = self.ps
        from contextlib import ExitStack
        with ExitStack() as es:
            Mcat = es.enter_context(sb("Mcat", [C, 449], F32))
            kdz = es.enter_context(sb("kdz", [128, 2, 128], BF16))
            koz = es.enter_context(sb("koz", [128, 2, 1, 64], BF16))
            qtd = es.enter_context(sb("qtd", [128, 2, C], BF16))
            TriS = es.enter_context(sb("TriS", [C, C], F32))
            mask = es.enter_context(sb("mask", [C, C], F32))
            ctmp = es.enter_context(sb("ctmp", [C, C], F32))
            q_t = es.enter_context(sb("q_t", [C, 2, HW], BF16))
            v_t = es.enter_context(sb("v_t", [C, 2, 1024], BF16))
            z_t = es.enter_context(sb("z_t", [C, 2, 1024], F32))
            k_t = es.enter_context(sb("k_t", [C, 2, HW], BF16))
            lr_t = es.enter_context(sb("lr_t", [17, 2, C], F32))
            rp_t = es.enter_context(sb("rp_t", [C, 2, 512], F32))
            of_t = es.enter_context(sb("of_t", [128, 2, 8, C], F32))
            gate_t = es.enter_context(sb("gate_t", [128, 2, 8, C], BF16))
            sig = es.enter_context(sb("sig", [C, 1024], F32))
            ff = es.enter_context(sb("ff", [C, 1024], F32))
            graw = es.enter_context(sb("graw", [C, 2, 1024], F32))
            kt = es.enter_context(sb("kt", [C, 2, 1024], BF16))
            qr = es.enter_context(sb("qr", [C, 2, HW], BF16))
            khat = es.enter_context(sb("khat", [C, 2, 1024], BF16))
            Ek = es.enter_context(sb("Ek", [C, 512], F32))
            rt1 = es.enter_context(sb("rt1", [C, 256], F32))
            rt2 = es.enter_context(sb("rt2", [C, 256], F32))
            rt3 = es.enter_context(sb("rt3", [C, 256], F32))
            rt4 = es.enter_context(sb("rt4", [C, 256], F32))
            E13 = es.enter_context(sb("E13", [128, 2, 321], F32))
            E2 = es.enter_context(sb("E2", [128, 2, C], F32))
            qtl = es.enter_context(sb("qtl", [128, 2, C], BF16))
            qtp = es.enter_context(sb("qtp", [128, 2, C], BF16))
            ktl = es.enter_context(sb("ktl", [128, 2, C], BF16))
            scb = es.enter_context(sb("scb", [C, 2, C], BF16))
            ofs = es.enter_context(sb("ofs", [128, 2, 8, C], F32))
            osum = es.enter_context(sb("osum", [128, 2, 8, C], F32))
            sg = es.enter_context(sb("sg", [128, 8, C], F32))
            sqt = es.enter_context(sb("sqt", [128, 8, C], F32))
            rs = es.enter_context(sb("rs", [128, 8, C], F32))
            mean = es.enter_context(sb("mean", [128, 8, C], F32))
            ob = es.enter_context(sb("ob", [128, 2, 8, C], BF16))
            otmp = es.enter_context(sb("otmp", [128, 8, C], F32))
            S = es.enter_context(sb("S", [128, 1024], F32))
            Sb = es.enter_context(sb("Sb", [128, 1024], BF16))
            lbr = es.enter_context(sb("lbr", [128, 1024], F32))
            oml = es.enter_context(sb("oml", [128, 1024], F32))
            lg = es.enter_context(sb("lg", [128, 2, 1024], F32))
            wg = es.enter_context(sb("wg", [17, 512], F32))
            gn = es.enter_context(sb("gn", [128, 2], F32))
            rl = es.enter_context(sb("rl", [128, 4], F32))
            gconst = es.enter_context(sb("gconst", [C, 512], F32))
            pc = es.enter_context(ps("pc", [128, 512], F32))
            pP0 = es.enter_context(ps("pP0", [128, 512], F32)); pP1 = es.enter_context(ps("pP1", [128, 512], F32)); pPs = [pP0, pP1]
            pT0 = es.enter_context(ps("pT0", [128, 8, 128], BF16)); pT1 = es.enter_context(ps("pT1", [128, 8, 128], BF16)); pTs = [pT0, pT1]
            pS = es.enter_context(ps("pS", [128, 512], F32))
            pO = es.enter_context(ps("pO", [128, 8, C], F32))
            pU = es.enter_context(ps("pU", [128, 2, 256], F32))
            if name == 'hg':
                P.dma('sp', gn[:, 0:1], T['hg_ngT'][layer], writes=['gn'])
            elif name == 'gl':
                P.dma('sp', gn[:], T['gla_ngT'][layer], writes=['gn'])
            else:
                P.dma('sp', gn[:], T['ret_ngT'][layer], writes=['gn'])
            P.memset(lr_t[:], 1.0, ['lr0', 'lr1'])
            step = 0
            for d in (0, 1):
                gs = br['gs']
                P.dma('sp', Mcat[:, :], T['gl_consts'][d], writes=['Mcat'])
                P.ts(Mcat[:, 0:385], Mcat[:, 0:385], gs, ALU.mult, ['Mcat'], ['Mcat'])
                P.memset(kdz[:], 0.0, ['kdz0', 'kdz1'])
                P.memset(koz[:], 0.0, ['koz0', 'koz1'])
                if name == 'hg':
                    P.dma('sp', lg[:], T['hg_lbrep'][:, :, d, :], writes=['lg'])
                    P.tt(lbr[:], lg[:, 1, :], lg[:, 0, :], ALU.subtract, ['lg'], ['lbr'])
                    P.act(lbr[:], lbr[:], AF.Sigmoid, ['lbr'], ['lbr'])
                    P.ts(lbr[:], lbr[:], float(layer), ALU.mult, ['lbr'], ['lbr'])
                    P.ts(oml[:], lbr[:], -1.0, ALU.mult, ['lbr'], ['oml'], s2=1.0, op1=ALU.add)
                elif name == 'gl':
                    P.dma('sp', wg[:], T['gla_wg'][layer, d], writes=['wg'])
                else:
                    P.dma('sp', rl[:], T['ret_lrep'][layer, d], writes=['rl'])
                    P.act(rl[:], rl[:], AF.Exp, ['rl'], ['rl'], scale=-1.0)
                    P.act(rl[:], rl[:], AF.Ln, ['rl'], ['rl'], bias=1.0)
                    P.ts(rl[:], rl[:], -1.0, ALU.mult, ['rl'], ['rl'])
                    for h in range(4):
                        P.ts(gconst[:, h * 128:(h + 1) * 128], self.onesf[:C, :], rl[:C, h:h + 1], ALU.mult, ['onesf', 'rl'], ['gconst'])
                P.memset(S[:], 0.0, [('S', h) for h in range(H)])
                P.memset(Sb[:], 0.0, [('Sb', h) for h in range(H)])
                order = orders[d]

                def emit_prep(si):
                    n = order[si]; t0 = n * C; pb = si % 2
                    K = lambda nm: (nm, pb)
                    P.dma('sp', q_t[:, pb, :], T['tmb'][t0:t0 + C, br['q']:br['q'] + HW], writes=[K('q_t')])
                    P.dma('sp', v_t[:, pb, :], T['tmb'][t0:t0 + C, br['v']:br['v'] + 1024], writes=[K('v_t')])
                    if name == 'hg':
                        P.dma('sp', z_t[:, pb, :], T['tmf'][t0:t0 + C, d * 1024:(d + 1) * 1024], writes=[K('z_t')])
                    else:
                        P.dma('sp', k_t[:, pb, :], T['tmb'][t0:t0 + C, br['k']:br['k'] + HW], writes=[K('k_t')])
                    if name == 'gl':
                        P.dma('sp', lr_t[0:16, pb, :], T['lrT'][d * 16:(d + 1) * 16, t0:t0 + C], writes=[('lr%d' % pb)])
                    if name == 'rt':
                        P.dma('sp', rp_t[:, pb, :], T['rope'][t0:t0 + C, :], writes=[K('rp_t')])
                    if d == 1:
                        P.dma('sp', of_t[:, pb], ofv[:, :, t0:t0 + C], writes=[K('of_t')])
                        P.dma('sp', gate_t[:, pb], gv[:, :, t0:t0 + C], writes=[K('gate_t')])
                    if name == 'hg':
                        P.act(sig[:], z_t[:, pb, :], AF.Sigmoid, [K('z_t')], ['sig'])
                        P.tt(ff[:], sig[:], oml[:C, :], ALU.mult, ['sig', 'oml'], ['ff'])
                        P.tt(ff[:], ff[:], lbr[:C, :], ALU.add, ['ff', 'lbr'], ['ff'])
                        P.ts(ff[:], ff[:], 1e-6, ALU.max, ['ff'], ['ff'])
                        P.act(graw[:, pb, :], ff[:], AF.Ln, ['ff'], [K('graw')])
                        P.ts(ff[:], sig[:], -1.0, ALU.mult, ['sig', 'ff'], ['ff'], s2=1.0, op1=ALU.add)
                        P.tt(kt[:, pb, :], ff[:], oml[:C, :], ALU.mult, ['ff', 'oml'], [K('kt')], eng='pool')
                    elif name == 'gl':
                        P.mm(pc[:C, :], lr_t[0:17, pb, :], wg[0:17, :], True, True, ['lr%d' % pb, 'wg'], ['pc'])
                        P.act(sig[:, 0:512], pc[:C, :], AF.Exp, ['pc'], ['sig'], scale=-1.0)
                        P.act(graw[:, pb, 0:512], sig[:, 0:512], AF.Ln, ['sig'], [K('graw')], bias=1.0)
                    else:
                        cos4 = rp_t[:, pb, 0:256].rearrange("p (h x) -> p h x", x=64)
                        sin4 = rp_t[:, pb, 256:512].rearrange("p (h x) -> p h x", x=64)
                        for (src, skey, dst, dkey, eng, ta, tb) in ((q_t, K('q_t'), qr, K('qr'), 'dve', rt1, rt2), (k_t, K('k_t'), kt, K('kt'), 'pool', rt3, rt4)):
                            xv = src[:, pb, :].rearrange("p (h two x) -> p h two x", two=2, x=64)
                            ov_ = dst[:, pb, 0:512].rearrange("p (h two x) -> p h two x", two=2, x=64)
                            tav = ta[:].rearrange("p (h x) -> p h x", x=64); tbv = tb[:].rearrange("p (h x) -> p h x", x=64)
                            P.tt(tav, xv[:, :, 0, :], cos4, ALU.mult, [skey, K('rp_t')], [ta.name], eng=eng)
                            P.tt(tbv, xv[:, :, 1, :], sin4, ALU.mult, [skey, K('rp_t')], [tb.name], eng=eng)
                            P.tt(ov_[:, :, 0, :], tav, tbv, ALU.subtract, [ta.name, tb.name], [dkey], eng=eng)
                            P.tt(tav, xv[:, :, 0, :], sin4, ALU.mult, [skey, K('rp_t')], [ta.name], eng=eng)
                            P.tt(tbv, xv[:, :, 1, :], cos4, ALU.mult, [skey, K('rp_t')], [tb.name], eng=eng)
                            P.tt(ov_[:, :, 1, :], tav, tbv, ALU.add, [ta.name, tb.name], [dkey], eng=eng)
                    gsrc_, gkey_, ksl_, kkey_ = srcs(pb)
                    for hf in range(HW // 512):
                        cs = slice(hf * 512, (hf + 1) * 512)
                        P.mm(pc[:C, :], Mcat[:, 321:385], gsrc_[:, cs], True, True, ['Mcat', gkey_], ['pc'])
                        P.act(Ek[:], pc[:C, :], AF.Exp, ['pc'], ['Ek'])
                        P.tt(khat[:, pb, cs], ksl_(hf * 512, (hf + 1) * 512), Ek[:], ALU.mult, [kkey_, 'Ek'], [('khat', pb, hf)])

                def srcs(pb):
                    K = lambda nm: (nm, pb)
                    if name == 'rt':
                        gsrc_ = gconst; gkey_ = 'gconst'
                    else:
                        gsrc_ = graw[:, pb, :]; gkey_ = K('graw')
                    if name == 'gl':
                        ksl_ = lambda a, b: k_t[:, pb, a:b]; kkey_ = K('k_t')
                    else:
                        ksl_ = lambda a, b: kt[:, pb, a:b]; kkey_ = K('kt')
                    return gsrc_, gkey_, ksl_, kkey_

                def emit_front(si, h):
                    pb = si % 2; sl = h % 2; hs = slice(h * 128, (h + 1) * 128)
                    K = lambda nm: (nm, pb)
                    gsrc_, gkey_, ksl_, kkey_ = srcs(pb)
                    if name == 'rt':
                        qsrc = qr[:, pb, 0:512]; qkey = K('qr')
                    else:
                        qsrc = q_t[:, pb, :]; qkey = K('q_t')
                    pP = pPs[sl]; pT = pTs[sl]; kP = 'pP%d' % sl; kT = 'pT%d' % sl
                    P.mm(pP[:, 0:321], gsrc_[:, hs], Mcat[:, 0:321], True, True, [gkey_, 'Mcat'], [kP])
                    P.act(E13[:, sl, :], pP[:, 0:321], AF.Exp, [kP], [('E13', sl)])
                    P.ts(E13[:, sl, 64:192], E13[:, sl, 64:192], 2.0e17, ALU.min, [('E13', sl)], [('E13', sl)])
                    P.tr(pT[:, 0, 0:C], qsrc[:, hs], self.identb[:C, :C], [qkey, 'identb'], [kT])
                    P.tr(pT[:, 1, 0:C], ksl_(h * 128, (h + 1) * 128), self.identb[:C, :C], [kkey_, 'identb'], [kT])
                    P.stt(qtl[:, sl, :], pT[:, 0, 0:C], br['qs'], E13[:, sl, 0:64], ALU.mult, ALU.mult, [kT, ('E13', sl)], [('qtl', sl)])
                    P.stt(qtd[:, sl, :], pT[:, 0, 0:C], br['qs'], E13[:, sl, 64:128], ALU.mult, ALU.mult, [kT, ('E13', sl)], [('qtd', sl)])
                    P.stt(qtp[:, sl, :], pT[:, 0, 0:C], br['qs'], E13[:, sl, 256:320], ALU.mult, ALU.mult, [kT, ('E13', sl)], [('qtp', sl)])
                    kbase = kdz[:, sl, :]
                    kd_out = bass.AP(kbase.tensor, kbase.offset, [list(kbase.ap[0]), [96, 2], [1, 32]])
                    P.tt(kd_out, pT[:, 1, 0:C].rearrange("p (a b) -> p a b", b=32), E13[:, sl, 128:192].rearrange("p (a b) -> p a b", b=32),
                         ALU.mult, [kT, ('E13', sl)], ['kdz%d' % sl])
                    lo, hi = (0, 32) if d == 0 else (32, 64)
                    P.tt(koz[:, sl, 0, lo:hi], pT[:, 1, lo:hi], E13[:, sl, 192 + lo:192 + hi], ALU.mult,
                         [kT, ('E13', sl)], ['koz%d' % sl])

                def emit_back(si, h):
                    pb = si % 2; sl = h % 2; hs = slice(h * 128, (h + 1) * 128)
                    K = lambda nm: (nm, pb)
                    offI = 1 if d == 0 else 0
                    for I in range(2):
                        has_off = (I == offI)
                        P.mm(pS[:C, 32 * I:32 * I + 32], kdz[:, sl, 64 * I:64 * I + 64], qtd[:, sl, 32 * I:32 * I + 32], True, not has_off,
                             ['kdz%d' % sl, ('qtd', sl)], ['pS'])
                        if has_off:
                            P.mm(pS[:C, 32 * I:32 * I + 32], koz[:, sl, 0, :], qtl[:, sl, 32 * I:32 * I + 32], False, True,
                                 ['koz%d' % sl, ('qtl', sl)], ['pS'])
                    P.tt(scb[:, sl, :], pS[:C, 0:C], Mcat[:, 385:449], ALU.mult, ['pS', 'Mcat'], [('scb', sl)])
                    for ec in range(dvc):
                        col = h * dv + ec * 128; ci = col // 128; so = ci % 4
                        P.mm(pO[:, so, :], v_t[:, pb, col:col + 128], scb[:, sl, :], True, False, [K('v_t'), ('scb', sl)], ['pO'])
                        P.mm(pO[:, so, :], Sb[:, col:col + 128], qtp[:, sl, :], False, True, [('Sb', h), ('qtp', sl)], ['pO'])
                        if d == 0:
                            P.cp(ofs[:, pb, ci, :], pO[:, so, :], ['pO'], [('ofs', pb)], eng='act')
                        else:
                            P.tt(osum[:, pb, ci, :], pO[:, so, :], of_t[:, pb, ci, :], ALU.add, ['pO', K('of_t')], [K('osum')])
                    P.mm(pU[:, 0, 0:dv], khat[:, pb, hs], v_t[:, pb, h * dv:(h + 1) * dv], True, True,
                         [('khat', pb, (h * 128) // 512), K('v_t')], ['pU'])
                    P.stt(S[:, h * dv:(h + 1) * dv], S[:, h * dv:(h + 1) * dv], E13[:, sl, 320:321], pU[:, 0, 0:dv],
                          ALU.mult, ALU.add, [('S', h), ('E13', sl), 'pU'], [('S', h)])
                    P.cp(Sb[:, h * dv:(h + 1) * dv], S[:, h * dv:(h + 1) * dv], [('S', h)], [('Sb', h)], eng='act')

                def emit_out(si):
                    n = order[si]; t0 = n * C; pb = si % 2
                    K = lambda nm: (nm, pb)
                    if d == 0:
                        P.dma('act', ofv[:, :, t0:t0 + C], ofs[:, pb], reads=[('ofs', pb)], writes=[('ofT', n)])
                        return
                    osm = osum[:, pb]
                    P.act(sg[:], gate_t[:, pb], AF.Silu, [K('gate_t')], ['sg'])
                    if br['norm'] == 'ln':
                        for h in range(H):
                            for ec in range(dvc):
                                P.mm(pc[:, h * C:(h + 1) * C], self.onesf[:], osm[:, h * dvc + ec, :], ec == 0, ec == dvc - 1, ['onesf', K('osum')], ['pc'])
                        P.ts(mean[:, 0:H, :], pc[:, 0:H * C].rearrange("p (h c) -> p h c", c=C), 1.0 / dv, ALU.mult, ['pc'], ['mean'])
                        for ci in range(8):
                            P.tt(osm[:, ci, :], osm[:, ci, :], mean[:, ci // dvc, :], ALU.subtract, [K('osum'), 'mean'], [K('osum')])
                    P.act(sqt[:], osm, AF.Square, [K('osum')], ['sqt'])
                    for h in range(H):
                        for ec in range(dvc):
                            P.mm(pc[:, h * C:(h + 1) * C], self.onesf[:], sqt[:, h * dvc + ec, :], ec == 0, ec == dvc - 1, ['onesf', 'sqt'], ['pc'])
                    P.act(rs[:, 0:H, :], pc[:, 0:H * C].rearrange("p (h c) -> p h c", c=C), AF.Sqrt, ['pc'], ['rs'], bias=EPS, scale=1.0 / dv)
                    P.op('dve', lambda e: e.reciprocal(out=rs[:, 0:H, :], in_=rs[:, 0:H, :]), ['rs'], ['rs'])
                    for ci in range(8):
                        P.stt(otmp[:, ci, :], osm[:, ci, :], gn[:, (ci % dvc):(ci % dvc) + 1], rs[:, ci // dvc, :], ALU.mult, ALU.mult,
                              [K('osum'), 'gn', 'rs'], ['otmp'])
                    P.tt(ob[:, pb], otmp[:], sg[:], ALU.mult, ['otmp', 'sg'], [('ob', pb)])
                    P.dma('act', ov[:, :, t0:t0 + C], ob[:, pb], reads=[('ob', pb)], writes=[('oT', n)])

                nch = len(order)
                PIPE = getattr(c, 'pipe', True)
                if PIPE:
                    emit_prep(0)
                for si in range(nch):
                    if not PIPE:
                        emit_prep(si)
                        for h in range(H):
                            emit_front(si, h)
                            emit_back(si, h)
                        emit_out(si)
                        continue
                    emit_front(si, 0)
                    for h in range(H):
                        if h + 1 < H:
                            emit_front(si, h + 1)
                        emit_back(si, h)
                        if h == H // 2 - 1 and si + 1 < nch:
                            emit_prep(si + 1)
                    emit_out(si)
                P.barrier()

    def phase_gla2(self, layer, name):
        c = self.cfg; P = self.P; T = self.T; L = c.L; CTX = c.CTX
        from contextlib import ExitStack
        br = self.BR[name]; H = br['H']; dv = br['dv']; dvc = dv // 128; HW = H * 128; C = 128
        assert H == 4 and dv == 256
        NCH = L // C; ctxch = CTX // C
        orders = [list(range(NCH)), list(range(ctxch - 1, -1, -1)) + list(range(NCH - 1, ctxch - 1, -1))]
        ofv = T['ofT'][br['ofrow']:br['ofrow'] + 1024, :].rearrange("(c p) t -> p c t", p=128)
        ov = T['oT'][br['orow']:br['orow'] + 1024, :].rearrange("(c p) t -> p c t", p=128)
        gv = T['fmb'][br['gate']:br['gate'] + 1024, :].rearrange("(c p) t -> p c t", p=128)
        sb = self.sb; ps = self.ps
        with ExitStack() as es:
            def mk(nm, shape, dt=F32):
                return es.enter_context(sb(nm, shape, dt))
            M = mk("M2", [128, 513]); q_t = mk("q_t2", [128, 2, 512], BF16); v_t = mk("v_t2", [128, 2, 1024], BF16)
            k_t = mk("k_t2", [128, 2, 512], BF16); lr_t = mk("lr_t2", [17, 2, C]); rp_t = mk("rp_t2", [128, 2, 512])
            of_t = mk("of_t2", [128, 2, 8, C]); gate_t = mk("gate_t2", [128, 2, 8, C], BF16)
            sig = mk("sig2", [128, 512]); graw = mk("graw2", [128, 2, 512]); kt = mk("kt2", [128, 2, 512], BF16)
            qr = mk("qr2", [128, 2, 512], BF16); khat = mk("khat2", [128, 2, 512], BF16); Ek = mk("Ek2", [128, 512])
            rt1 = mk("rt1b", [128, 256]); rt2 = mk("rt2b", [128, 256]); rt3 = mk("rt3b", [128, 256]); rt4 = mk("rt4b", [128, 256])
            E = mk("E_2", [128, 2, 257]); E2 = mk("E2_2", [128, 2, 128])
            qtl = mk("qtl2", [128, 2, C], BF16); qtp = mk("qtp2", [128, 2, C], BF16); ktl = mk("ktl2", [128, 2, C], BF16)
            scb = mk("scb2", [128, 2, C], BF16); ofs = mk("ofs2", [128, 2, 8, C]); osum = mk("osum2", [128, 2, 8, C])
            sg = mk("sg2_", [128, 8, C]); sqt = mk("sqt2", [128, 8, C]); rs = mk("rs2", [128, 8, C]); mean = mk("mean2", [128, 8, C])
            otmp = mk("otmp2", [128, 8, C]); ob = mk("ob2", [128, 2, 8, C], BF16)
            S = mk("S2", [128, 1024]); Sb = mk("Sb2", [128, 1024], BF16)
            wg = mk("wg2", [17, 512]); gn = mk("gn2", [128, 2]); rl = mk("rl2", [128, 4]); gconst = mk("gconst2", [128, 512])
            pc = es.enter_context(ps("pc2", [128, 512], F32))
            pPs = [es.enter_context(ps("pPa", [128, 512], F32)), es.enter_context(ps("pPb", [128, 512], F32))]
            pTs = [es.enter_context(ps("pTa", [128, 8, 128], BF16)), es.enter_context(ps("pTb", [128, 8, 128], BF16))]
            pS = es.enter_context(ps("pS2", [128, 512], F32)); pO = es.enter_context(ps("pO2", [128, 4, C], F32))
            pU = es.enter_context(ps("pU2", [128, 512], F32))
            P.dma('sp', gn[:], T['gla_ngT' if name == 'gl' else 'ret_ngT'][layer], writes=['gn'])
            P.memset(lr_t[:], 1.0, ['lr0', 'lr1'])
            for d in (0, 1):
                gs = br['gs']
                P.dma('sp', M[:, :], T['gl_consts2'][d], writes=['M'])
                P.ts(M[:, 0:385], M[:, 0:385], gs, ALU.mult, ['M'], ['M'])
                if name == 'gl':
                    P.dma('sp', wg[:], T['gla_wg'][layer, d], writes=['wg'])
                else:
                    P.dma('sp', rl[:], T['ret_lrep'][layer, d], writes=['rl'])
                    P.act(rl[:], rl[:], AF.Exp, ['rl'], ['rl'], scale=-1.0)
                    P.act(rl[:], rl[:], AF.Ln, ['rl'], ['rl'], bias=1.0)
                    P.ts(rl[:], rl[:], -1.0, ALU.mult, ['rl'], ['rl'])
                    for h in range(4):
                        P.ts(gconst[:, h * 128:(h + 1) * 128], self.onesf[:, :], rl[:, h:h + 1], ALU.mult, ['onesf', 'rl'], ['gconst'])
                P.memset(S[:], 0.0, [('S', h) for h in range(H)])
                P.memset(Sb[:], 0.0, [('Sb', h) for h in range(H)])
                order = orders[d]

                def srcs(pb):
                    K = lambda nm: (nm, pb)
                    if name == 'rt':
                        return gconst[:, :], 'gconst', (lambda a, b: kt[:, pb, a:b]), K('kt'), qr[:, pb, :], K('qr')
                    return graw[:, pb, :], K('graw'), (lambda a, b: k_t[:, pb, a:b]), K('k_t'), q_t[:, pb, :], K('q_t')

                def emit_prep(si):
                    n = order[si]; t0 = n * C; pb = si % 2
                    K = lambda nm: (nm, pb)
                    P.dma('sp', q_t[:, pb, :], T['tmb'][t0:t0 + C, br['q']:br['q'] + HW], writes=[K('q_t')])
                    P.dma('sp', v_t[:, pb, :], T['tmb'][t0:t0 + C, br['v']:br['v'] + 1024], writes=[K('v_t')])
                    P.dma('sp', k_t[:, pb, :], T['tmb'][t0:t0 + C, br['k']:br['k'] + HW], writes=[K('k_t')])
                    if name == 'gl':
                        P.dma('sp', lr_t[0:16, pb, :], T['lrT'][d * 16:(d + 1) * 16, t0:t0 + C], writes=[('lr%d' % pb)])
                    else:
                        P.dma('sp', rp_t[:, pb, :], T['rope'][t0:t0 + C, :], writes=[K('rp_t')])
                    if d == 1:
                        P.dma('sp', of_t[:, pb], ofv[:, :, t0:t0 + C], writes=[K('of_t')])
                        P.dma('sp', gate_t[:, pb], gv[:, :, t0:t0 + C], writes=[K('gate_t')])
                    if name == 'gl':
                        P.mm(pc[:, :], lr_t[0:17, pb, :], wg[0:17, :], True, True, ['lr%d' % pb, 'wg'], ['pc'])
                        P.act(sig[:], pc[:, :], AF.Exp, ['pc'], ['sig'], scale=-1.0)
                        P.act(graw[:, pb, :], sig[:], AF.Ln, ['sig'], [K('graw')], bias=1.0)
                    else:
                        cos4 = rp_t[:, pb, 0:256].rearrange("p (h x) -> p h x", x=64)
                        sin4 = rp_t[:, pb, 256:512].rearrange("p (h x) -> p h x", x=64)
                        for (src, skey, dst, dkey, eng, ta, tb) in ((q_t, K('q_t'), qr, K('qr'), 'dve', rt1, rt2), (k_t, K('k_t'), kt, K('kt'), 'pool', rt3, rt4)):
                            xv = src[:, pb, :].rearrange("p (h two x) -> p h two x", two=2, x=64)
                            ov_ = dst[:, pb, :].rearrange("p (h two x) -> p h two x", two=2, x=64)
                            tav = ta[:].rearrange("p (h x) -> p h x", x=64); tbv = tb[:].rearrange("p (h x) -> p h x", x=64)
                            P.tt(tav, xv[:, :, 0, :], cos4, ALU.mult, [skey, K('rp_t')], [ta.name], eng=eng)
                            P.tt(tbv, xv[:, :, 1, :], sin4, ALU.mult, [skey, K('rp_t')], [tb.name], eng=eng)
                            P.tt(ov_[:, :, 0, :], tav, tbv, ALU.subtract, [ta.name, tb.name], [dkey], eng=eng)
                            P.tt(tav, xv[:, :, 0, :], sin4, ALU.mult, [skey, K('rp_t')], [ta.name], eng=eng)
                            P.tt(tbv, xv[:, :, 1, :], cos4, ALU.mult, [skey, K('rp_t')], [tb.name], eng=eng)
                            P.tt(ov_[:, :, 1, :], tav, tbv, ALU.add, [ta.name, tb.name], [dkey], eng=eng)
                    gsrc_, gkey_, ksl_, kkey_, qsrc_, qkey_ = srcs(pb)
                    P.mm(pc[:, :], M[:, 257:385], gsrc_, True, True, ['M', gkey_], ['pc'])
                    P.act(Ek[:], pc[:, :], AF.Exp, ['pc'], ['Ek'])
                    P.tt(khat[:, pb, :], ksl_(0, 512), Ek[:], ALU.mult, [kkey_, 'Ek'], [K('khat')])

                def emit_front(si, h):
                    pb = si % 2; sl = h % 2; hs = slice(h * 128, (h + 1) * 128)
                    gsrc_, gkey_, ksl_, kkey_, qsrc_, qkey_ = srcs(pb)
                    pP = pPs[sl]; pT = pTs[sl]; kP = 'pP%d' % sl; kT = 'pT%d' % sl
                    P.mm(pP[:, 0:257], gsrc_[:, hs], M[:, 0:257], True, True, [gkey_, 'M'], [kP])
                    P.act(E[:, sl, :], pP[:, 0:257], AF.Exp, [kP], [('E', sl)])
                    P.act(E2[:, sl, :], pP[:, 0:128], AF.Exp, [kP], [('E2', sl)], scale=-1.0)
                    P.tr(pT[:, 0, 0:C], qsrc_[:, hs], self.identb[:, :], [qkey_, 'identb'], [kT])
                    P.tr(pT[:, 1, 0:C], ksl_(h * 128, (h + 1) * 128), self.identb[:, :], [kkey_, 'identb'], [kT])
                    P.stt(qtl[:, sl, :], pT[:, 0, 0:C], br['qs'], E[:, sl, 0:128], ALU.mult, ALU.mult, [kT, ('E', sl)], [('qtl', sl)])
                    P.stt(qtp[:, sl, :], pT[:, 0, 0:C], br['qs'], E[:, sl, 128:256], ALU.mult, ALU.mult, [kT, ('E', sl)], [('qtp', sl)])
                    P.tt(ktl[:, sl, :], pT[:, 1, 0:C], E2[:, sl, :], ALU.mult, [kT, ('E2', sl)], [('ktl', sl)])

                def emit_back(si, h):
                    pb = si % 2; sl = h % 2; hs = slice(h * 128, (h + 1) * 128)
                    K = lambda nm: (nm, pb)
                    P.mm(pS[:, 0:C], ktl[:, sl, :], qtl[:, sl, :], True, True, [('ktl', sl), ('qtl', sl)], ['pS'])
                    P.tt(scb[:, sl, :], pS[:, 0:C], M[:, 385:513], ALU.mult, ['pS', 'M'], [('scb', sl)])
                    for ec in range(dvc):
                        col = h * dv + ec * 128; ci = col // 128; so = ci % 4
                        P.mm(pO[:, so, :], v_t[:, pb, col:col + 128], scb[:, sl, :], True, False, [K('v_t'), ('scb', sl)], ['pO'])
                        P.mm(pO[:, so, :], Sb[:, col:col + 128], qtp[:, sl, :], False, True, [('Sb', h), ('qtp', sl)], ['pO'])
                        if d == 0:
                            P.cp(ofs[:, pb, ci, :], pO[:, so, :], ['pO'], [('ofs', pb)], eng='act')
                        else:
                            P.tt(osum[:, pb, ci, :], pO[:, so, :], of_t[:, pb, ci, :], ALU.add, ['pO', K('of_t')], [K('osum')])
                    P.mm(pU[:, 0:dv], khat[:, pb, hs], v_t[:, pb, h * dv:(h + 1) * dv], True, True, [K('khat'), K('v_t')], ['pU'])
                    P.stt(S[:, h * dv:(h + 1) * dv], S[:, h * dv:(h + 1) * dv], E[:, sl, 256:257], pU[:, 0:dv],
                          ALU.mult, ALU.add, [('S', h), ('E', sl), 'pU'], [('S', h)])
                    P.cp(Sb[:, h * dv:(h + 1) * dv], S[:, h * dv:(h + 1) * dv], [('S', h)], [('Sb', h)], eng='act')

                def emit_out(si):
                    n = order[si]; t0 = n * C; pb = si % 2
                    K = lambda nm: (nm, pb)
                    if d == 0:
                        P.dma('act', ofv[:, :, t0:t0 + C], ofs[:, pb], reads=[('ofs', pb)], writes=[('ofT', n)])
                        return
                    osm = osum[:, pb]
                    P.act(sg[:], gate_t[:, pb], AF.Silu, [K('gate_t')], ['sg'])
                    if br['norm'] == 'ln':
                        for h in range(H):
                            for ec in range(dvc):
                                P.mm(pc[:, h * C:(h + 1) * C], self.onesf[:], osm[:, h * dvc + ec, :], ec == 0, ec == dvc - 1, ['onesf', K('osum')], ['pc'])
                        P.ts(mean[:, 0:H, :], pc[:, 0:H * C].rearrange("p (h c) -> p h c", c=C), 1.0 / dv, ALU.mult, ['pc'], ['mean'])
                        for ci in range(8):
                            P.tt(osm[:, ci, :], osm[:, ci, :], mean[:, ci // dvc, :], ALU.subtract, [K('osum'), 'mean'], [K('osum')])
                    P.act(sqt[:], osm, AF.Square, [K('osum')], ['sqt'])
                    for h in range(H):
                        for ec in range(dvc):
                            P.mm(pc[:, h * C:(h + 1) * C], self.onesf[:], sqt[:, h * dvc + ec, :], ec == 0, ec == dvc - 1, ['onesf', 'sqt'], ['pc'])
                    P.act(rs[:, 0:H, :], pc[:, 0:H * C].rearrange("p (h c) -> p h c", c=C), AF.Sqrt, ['pc'], ['rs'], bias=EPS, scale=1.0 / dv)
                    P.op('dve', lambda e: e.reciprocal(out=rs[:, 0:H, :], in_=rs[:, 0:H, :]), ['rs'], ['rs'])
                    for ci in range(8):
                        P.stt(otmp[:, ci, :], osm[:, ci, :], gn[:, (ci % dvc):(ci % dvc) + 1], rs[:, ci // dvc, :], ALU.mult, ALU.mult,
                              [K('osum'), 'gn', 'rs'], ['otmp'])
                    P.tt(ob[:, pb], otmp[:], sg[:], ALU.mult, ['otmp', 'sg'], [('ob', pb)])
                    P.dma('act', ov[:, :, t0:t0 + C], ob[:, pb], reads=[('ob', pb)], writes=[('oT', n)])

                nch = len(order)
                emit_prep(0)
                for si in range(nch):
                    emit_front(si, 0)
                    for h in range(H):
                        if h + 1 < H:
                            emit_front(si, h + 1)
                        emit_back(si, h)
                        if h == H // 2 - 1 and si + 1 < nch:
                            emit_prep(si + 1)
                    emit_out(si)
                P.barrier()

    def s5_powers(self, src_re, src_im, src_dt, Pn, Fn, pw_re, pw_im, coef_re, coef_im, tag):
        P = self.P
        from contextlib import ExitStack
        with ExitStack() as es:
            def mk(nm, shape, dt=F32):
                return es.enter_context(self.sb(tag + nm, shape, dt))
            lre = mk("lre", [Pn, Fn]); lim = mk("lim", [Pn, Fn]); dtv = mk("dtv", [Pn, Fn])
            a = mk("a", [Pn, Fn]); th = mk("th", [Pn, Fn]); kki = mk("kki", [Pn, 17, Fn], I32); kk = mk("kk", [Pn, 17, Fn])
            A = mk("A", [Pn, 17, Fn]); TH = mk("TH", [Pn, 17 * Fn]); SN = mk("SN", [Pn, 17 * Fn]); CS = mk("CS", [Pn, 17 * Fn])
            t1 = mk("t1", [Pn, Fn]); t2 = mk("t2", [Pn, Fn]); den = mk("den", [Pn, Fn])
            P.dma('sp', lre[:], src_re, writes=['lre']); P.dma('sp', lim[:], src_im, writes=['lim']); P.dma('sp', dtv[:], src_dt, writes=['dtv'])
            P.act(dtv[:], dtv[:], AF.Exp, ['dtv'], ['dtv'])
            P.ts(lre[:], lre[:], -1e-4, ALU.min, ['lre'], ['lre'])
            P.tt(a[:], lre[:], dtv[:], ALU.mult, ['lre', 'dtv'], ['a'])
            P.tt(th[:], lim[:], dtv[:], ALU.mult, ['lim', 'dtv'], ['th'])
            P.op('pool', lambda e: e.iota(kki[:], pattern=[[1, 17], [0, Fn]], base=0, channel_multiplier=0), [], ['kki'])
            P.cp(kk[:], kki[:], ['kki'], ['kk'])
            P.tt(A[:], kk[:], a[:].unsqueeze(1).to_broadcast([Pn, 17, Fn]), ALU.mult, ['kk', 'a'], ['A'])
            P.tt(TH[:].rearrange("p (k f) -> p k f", f=Fn), kk[:], th[:].unsqueeze(1).to_broadcast([Pn, 17, Fn]), ALU.mult, ['kk', 'th'], [tag + 'scx'])
            P.act(A[:], A[:], AF.Exp, ['A'], ['A'])
            self.sincos(TH[:], [Pn, 17 * Fn], SN[:], CS[:], tag + 'sc')
            P.tt(pw_re, A[:], CS[:].rearrange("p (k f) -> p k f", f=Fn), ALU.mult, ['A'], ['pwre'])
            P.tt(pw_im, A[:], SN[:].rearrange("p (k f) -> p k f", f=Fn), ALU.mult, ['A'], ['pwim'])
            if coef_re is not None:
                P.ts(t1[:], pw_re[:, 1, :], -1.0, ALU.add, ['pwre'], ['t1'])
                P.tt(den[:], lre[:], lre[:], ALU.mult, ['lre'], ['den'])
                P.tt(t2[:], lim[:], lim[:], ALU.mult, ['lim'], ['t2'])
                P.tt(den[:], den[:], t2[:], ALU.add, ['den', 't2'], ['den'])
                P.op('dve', lambda e: e.reciprocal(out=den[:], in_=den[:]), ['den'], ['den'])
                P.tt(coef_re, t1[:], lre[:], ALU.mult, ['t1', 'lre'], ['cre'])
                P.tt(t2[:], pw_im[:, 1, :], lim[:], ALU.mult, ['pwim', 'lim'], ['t2'])
                P.tt(coef_re, coef_re, t2[:], ALU.add, ['cre', 't2'], ['cre'])
                P.tt(coef_re, coef_re, den[:], ALU.mult, ['cre', 'den'], ['cre'])
                P.tt(coef_im, pw_im[:, 1, :], lre[:], ALU.mult, ['pwim', 'lre'], ['cim'])
                P.tt(t2[:], t1[:], lim[:], ALU.mult, ['t1', 'lim'], ['t2'])
                P.tt(coef_im, coef_im, t2[:], ALU.subtract, ['cim', 't2'], ['cim'])
                P.tt(coef_im, coef_im, den[:], ALU.mult, ['cim', 'den'], ['cim'])
            P.barrier()

    def cmul(self, out_re, out_im, a_re, a_im, b_re, b_im, t, neg_im=False, eng='dve'):
        P = self.P
        P.tt(out_re, a_re, b_re, ALU.mult, ['cm_in'], ['cm_re'], eng=eng)
        P.tt(t, a_im, b_im, ALU.mult, ['cm_in'], ['cm_t'], eng=eng)
        P.tt(out_re, out_re, t, ALU.subtract, ['cm_re', 'cm_t'], ['cm_re'], eng=eng)
        P.tt(out_im, a_re, b_im, ALU.mult, ['cm_in'], ['cm_im'], eng=eng)
        P.tt(t, a_im, b_re, ALU.mult, ['cm_in', 'cm_re'], ['cm_t'], eng=eng)
        if neg_im:
            P.stt(out_im, out_im, -1.0, t, ALU.mult, ALU.subtract, ['cm_im', 'cm_t'], ['cm_im'])
        else:
            P.tt(out_im, out_im, t, ALU.add, ['cm_im', 'cm_t'], ['cm_im'], eng=eng)

    def phase_s5(self, layer):
        c = self.cfg; P = self.P; T = self.T; L = c.L; CTX = c.CTX
        from contextlib import ExitStack
        SB = 16; NB = L // SB; NBc = CTX // SB; NBP = NB + 2
        sb = self.sb; ps = self.ps
        urows = T['fmb'][2048:3072, :]
        for d in (0, 1):
            with ExitStack() as esd:
                KtL = esd.enter_context(sb("KtL", [128, 8, 16, 128], BF16))
                PWPr = esd.enter_context(sb("PWPr", [128, 17, 32], F32)); PWPi = esd.enter_context(sb("PWPi", [128, 17, 32], F32))
                self.s5_powers(T['s5_lamP'][layer, d, 0], T['s5_lamP'][layer, d, 1], T['s5_dtP'][layer, d], 128, 32,
                               PWPr[:], PWPi[:], None, None, 'pp')
                with ExitStack() as es:
                    def mk(nm, shape, dt=F32):
                        return es.enter_context(sb(nm, shape, dt))
                    pwr = mk("pwr", [64, 17, 64]); pwi = mk("pwi", [64, 17, 64]); cre = mk("cre", [64, 64]); cim = mk("cim", [64, 64])
                    self.s5_powers(T['s5_lam64'][layer, d, 0], T['s5_lam64'][layer, d, 1], T['s5_dt64'][layer, d], 64, 64,
                                   pwr[:], pwi[:], cre[:], cim[:], 'p64')
                    B64 = mk("B64", [64, 2, 64, 16]); C64 = mk("C64", [64, 2, 64, 16])
                    bbr = mk("bbr", [64, 64, 16]); bbi = mk("bbi", [64, 64, 16]); tt_ = mk("tt_", [64, 64, 16])
                    Wr = mk("Wr", [64, 64, 16]); Wi = mk("Wi", [64, 64, 16])
                    Wpr = mk("Wpr", [64, 64 * 128], BF16); Wpi = mk("Wpi", [64, 64 * 128], BF16)
                    Cpr = mk("Cpr", [64, 64 * 128], BF16); Cpi = mk("Cpi", [64, 64 * 128], BF16)
                    pk0 = es.enter_context(ps("pk0", [128, 512], F32)); pk1 = es.enter_context(ps("pk1", [128, 512], F32))
                    pks = [pk0, pk1]
                    P.dma('sp', B64[:], T['s5_b64'][layer, d].rearrange("x p g h -> p x g h"), writes=['B64'])
                    P.dma('sp', C64[:], T['s5_c64'][layer, d].rearrange("x p g h -> p x g h"), writes=['C64'])
                    for t_ in (Wpr, Wpi, Cpr, Cpi):
                        P.memset(t_[:], 0.0, ['pad' + t_.name])
                    P.barrier()
                    bc = lambda ap: ap.unsqueeze(2).to_broadcast([64, 64, 16])
                    self.cmul(bbr[:], bbi[:], bc(cre[:]), bc(cim[:]), B64[:, 0], B64[:, 1], tt_[:])
                    P.barrier()

                    def diag(t_):
                        b_ = t_[:]
                        return bass.AP(b_.tensor, b_.offset, [list(b_.ap[0]), [1024, 8], [144, 8], [1, 16]])
                    P.cp(diag(Cpr), C64[:, 0].rearrange("p (a b) h -> p a b h", b=8), [], ['cpr'])
                    P.cp(diag(Cpi), C64[:, 1].rearrange("p (a b) h -> p a b h", b=8), [], ['cpi'])
                    P.barrier()
                    for tau in range(16):
                        self.cmul(Wr[:], Wi[:], bc(pwr[:, tau, :]), bc(pwi[:, tau, :]), bbr[:], bbi[:], tt_[:], neg_im=True)
                        P.cp(diag(Wpr), Wr[:].rearrange("p (a b) h -> p a b h", b=8), ['cm_re'], ['Wpr'])
                        P.cp(diag(Wpi), Wi[:].rearrange("p (a b) h -> p a b h", b=8), ['cm_im'], ['Wpi'])
                        for gb in range(8):
                            pk = pks[gb % 2]; pkey = 'pk%d' % (gb % 2)
                            for g8 in range(8):
                                g = gb * 8 + g8
                                P.mm(pk[:, 0:128], Wpr[:, g * 128:(g + 1) * 128], Cpr[:, g * 128:(g + 1) * 128], g8 == 0, False, ['Wpr'], [pkey])
                                P.mm(pk[:, 0:128], Wpi[:, g * 128:(g + 1) * 128], Cpi[:, g * 128:(g + 1) * 128], False, g8 == 7, ['Wpi'], [pkey])
                            P.cp(KtL[:, gb, tau, :], pk[:, 0:128], [pkey], ['KtL'], eng=('act' if gb % 2 else 'dve'))
                    P.barrier()
                with ExitStack() as es:
                    pfr = es.enter_context(sb("pfrA", [128, 17, 64], F32)); pfi = es.enter_context(sb("pfiA", [128, 17, 64], F32))
                    cfr = es.enter_context(sb("cfrA", [128, 64], F32)); cfi = es.enter_context(sb("cfiA", [128, 64], F32))
                    for gb in range(8):
                        self.s5_powers(T['s5_lamF'][layer, d, 0][:, gb, :], T['s5_lamF'][layer, d, 1][:, gb, :], T['s5_dtF'][layer, d][:, gb, :],
                                       128, 64, pfr[:], pfi[:], cfr[:], cfi[:], 'pf')
                        P.dma('sp', T['s5pf'][gb, 0], pfr[:], reads=[], writes=['d1'])
                        P.dma('sp', T['s5pf'][gb, 1], pfi[:], reads=[], writes=['d2'])
                        P.dma('sp', T['s5cf'][gb, 0], cfr[:], reads=[], writes=['d3'])
                        P.dma('sp', T['s5cf'][gb, 1], cfi[:], reads=[], writes=['d4'])
                        P.barrier()
                with ExitStack() as es:
                    def mk(nm, shape, dt=F32):
                        return es.enter_context(sb(nm, shape, dt))
                    Ar = mk("Ar", [128, 32, NBP]); Ai = mk("Ai", [128, 32, NBP])
                    Pad = mk("Pad", [128, 4, 16, 2, 128], BF16)
                    Abr = mk("Abr", [128, 4, NBP], BF16); Abi = mk("Abi", [128, 4, NBP], BF16)
                    uT0 = mk("uT0", [128, L], BF16); uTs = [uT0, uT0]
                    maskQ = mk("maskQ", [128, 4, 128]); maskC = mk("maskC", [128, 2, 16])
                    pfr = mk("pfr", [128, 17, 64]); pfi = mk("pfi", [128, 17, 64]); cfr = mk("cfr", [128, 64]); cfi = mk("cfi", [128, 64])
                    BF_ = mk("BF_", [128, 2, 64]); bfr = mk("bfr", [128, 64]); bfi = mk("bfi", [128, 64]); tf = mk("tf", [128, 64])
                    Vr = mk("Vr", [128, 64]); Vi = mk("Vi", [128, 64])
                    cP = mk("cP", [128, 2, 32, 16]); CLr = mk("CLr", [128, 4, 16]); CLi = mk("CLi", [128, 4, 16]); tc_ = mk("tc_", [128, 4, 16])
                    s1 = mk("s1", [128, 32]); s2 = mk("s2", [128, 32]); s3 = mk("s3", [128, 32]); s4 = mk("s4", [128, 32])
                    ysb0 = mk("ysb0", [128, 512]); ysb1 = mk("ysb1", [128, 512]); yf_t = mk("yf_t", [128, 512]); y2 = mk("y2", [128, 512])
                    zb = mk("zb", [128, 512], BF16); dT = mk("dT", [128, 8])
                    pw0 = es.enter_context(ps("pw0", [128, 512], F32)); pw1 = es.enter_context(ps("pw1", [128, 512], F32))
                    py0 = es.enter_context(ps("py0", [128, 512], F32)); py1 = es.enter_context(ps("py1", [128, 512], F32))
                    P.dma('sp', maskQ[:], T['s5_maskQ'], writes=['maskQ'])
                    P.dma('sp', maskC[:], T['s5_maskC'], writes=['maskC'])
                    P.dma('sp', cP[:], T['s5_cP'][layer, d].rearrange("x p q h -> p x q h"), writes=['cP'])
                    P.dma('sp', dT[:], T['s5_dT'][layer], writes=['dT'])
                    P.memset(Pad[:], 0.0, ['Pad'])
                    P.memset(Ar[:], 0.0, ['A']); P.memset(Ai[:], 0.0, ['A'])
                    P.barrier()
                    nwp = 0
                    for gb in range(8):
                        uT = uTs[gb % 2]
                        P.dma('sp', uT[:], urows[gb * 128:(gb + 1) * 128, :], writes=[('uT', 0)])
                        P.dma('sp', pfr[:], T['s5pf'][gb, 0], writes=['cm_in'])
                        P.dma('sp', pfi[:], T['s5pf'][gb, 1], writes=['cm_in'])
                        P.dma('sp', cfr[:], T['s5cf'][gb, 0], writes=['cm_in'])
                        P.dma('sp', cfi[:], T['s5cf'][gb, 1], writes=['cm_in'])
                        P.dma('sp', BF_[:], T['s5_bF'][layer, d][:, :, gb, :].rearrange("x p f -> p x f"), writes=['cm_in'])
                        P.barrier()
                        self.cmul(bfr[:], bfi[:], cfr[:], cfi[:], BF_[:, 0], BF_[:, 1], tf[:])
                        P.barrier()
                        for j in range(16):
                            pw_ = (15 - j) if d == 0 else j
                            self.cmul(Vr[:], Vi[:], pfr[:, pw_, :], pfi[:, pw_, :], bfr[:], bfi[:], tf[:])
                            for x, V in ((0, Vr), (1, Vi)):
                                P.tt(Pad[:, :, j, x, :].rearrange("p q (a b) -> p q a b", b=64),
                                     V[:].unsqueeze(1).unsqueeze(1).to_broadcast([128, 4, 2, 64]),
                                     maskQ[:].rearrange("p q (a b) -> p q a b", b=64), ALU.mult,
                                     ['cm_re', 'cm_im', 'maskQ'], ['Pad'])
                        uv = uT[:].rearrange("p (n j) -> p n j", j=16)
                        for q in range(4):
                            pair = gb * 4 + q
                            for x, Ax in ((0, Ar), (1, Ai)):
                                pw = (pw0, pw1)[nwp % 2]; pwk = 'pw%d' % (nwp % 2); nwp += 1
                                for j in range(16):
                                    P.mm(pw[:, 0:NB], Pad[:, q, j, x, :], uv[:, :, j], j == 0, j == 15, ['Pad', ('uT', 0)], [pwk])
                                if d == 0:
                                    P.cp(Ax[:, pair, 1:NB + 1], pw[:, 0:NB], [pwk], ['A'], eng=('act' if x else 'dve'))
                                else:
                                    P.cp(Ax[:, pair, 0:NBc], pw[:, 0:NBc], [pwk], ['A'], eng=('act' if x else 'dve'))
                                    P.cp(Ax[:, pair, NBc + 1:NB + 1], pw[:, NBc:NB], [pwk], ['A'], eng=('act' if x else 'dve'))
                        P.barrier()
                    s5stop = getattr(c, 's5stop', 9)
                    if s5stop <= 2:
                        continue
                    ar = PWPr[:, 16, :]; ai = PWPi[:, 16, :]
                    if d == 0:
                        steps = [(n + 1, n) for n in range(NB)]
                    else:
                        steps = [(n, n + 1) for n in range(NBc - 1, -1, -1)] + ['copy'] + [(n + 1, n + 2) for n in range(NB - 1, NBc - 1, -1)]
                    for st in steps:
                        if st == 'copy':
                            P.cp(Ar[:, :, NB + 1], Ar[:, :, 0], ['A'], ['A'])
                            P.cp(Ai[:, :, NB + 1], Ai[:, :, 0], ['A'], ['A'])
                            continue
                        pos, prev = st
                        P.tt(s1[:], ar, Ar[:, :, prev], ALU.mult, ['A'], ['s1'])
                        P.tt(s2[:], ai, Ai[:, :, prev], ALU.mult, ['A'], ['s2'])
                        P.tt(s3[:], ar, Ai[:, :, prev], ALU.mult, ['A'], ['s3'])
                        P.tt(s4[:], ai, Ar[:, :, prev], ALU.mult, ['A'], ['s4'])
                        P.tt(s1[:], s1[:], s2[:], ALU.subtract, ['s1', 's2'], ['s1'])
                        P.tt(s3[:], s3[:], s4[:], ALU.add, ['s3', 's4'], ['s3'])
                        P.tt(Ar[:, :, pos], Ar[:, :, pos], s1[:], ALU.add, ['A', 's1'], ['A'])
                        P.tt(Ai[:, :, pos], Ai[:, :, pos], s3[:], ALU.add, ['A', 's3'], ['A'])
                    P.barrier()
                    if s5stop <= 3:
                        continue
                    P.memset(Pad[:], 0.0, ['Pad'])
                    P.barrier()
                    npy = 0
                    for gb in range(8):
                        uT = uTs[gb % 2]
                        P.dma('sp', uT[:], urows[gb * 128:(gb + 1) * 128, :], writes=[('uT', 0)])
                        for i in range(16):
                            pw_ = (i + 1) if d == 0 else (16 - i)
                            bq = lambda ap: ap[:, gb * 4:gb * 4 + 4].unsqueeze(2).to_broadcast([128, 4, 16])
                            self.cmul(CLr[:], CLi[:], cP[:, 0, gb * 4:gb * 4 + 4, :], cP[:, 1, gb * 4:gb * 4 + 4, :],
                                      bq(PWPr[:, pw_, :]), bq(PWPi[:, pw_, :]), tc_[:], neg_im=True)
                            for x, CL in ((0, CLr), (1, CLi)):
                                pb_ = Pad[:, 0, i, x, :]
                                dst_ = bass.AP(pb_.tensor, pb_.offset, [list(pb_.ap[0]), [16 * 2 * 128 + 32, 4], [16, 2], [1, 16]])
                                P.tt(dst_, CL[:].unsqueeze(2).to_broadcast([128, 4, 2, 16]),
                                     maskC[:].unsqueeze(1).to_broadcast([128, 4, 2, 16]), ALU.mult,
                                     ['cm_re', 'cm_im', 'maskC'], ['Pad'])
                        uv = uT[:].rearrange("p (n j) -> p n j", j=16)
                        P.cp(Abr[:], Ar[:, gb * 4:gb * 4 + 4, :], ['A'], ['Ab'])
                        P.cp(Abi[:], Ai[:, gb * 4:gb * 4 + 4, :], ['A'], ['Ab'], eng='act')
                        for (t0, w) in tiles(0, L, 512):
                            py = (py0, py1)[npy % 2]; pyk = 'py%d' % (npy % 2); ysb = (ysb0, ysb1)[npy % 2]; ysk = 'ysb%d' % (npy % 2); npy += 1
                            n0 = t0 // SB; nbt = w // SB
                            pv = py[:, 0:w].rearrange("p (n j) -> p n j", j=16)
                            runs = []
                            if d == 0:
                                runs.append((n0, nbt, n0))
                            else:
                                a0 = n0; a1 = min(n0 + nbt, NBc)
                                if a1 > a0:
                                    runs.append((a0, a1 - a0, a0 + 1))
                                b0 = max(n0, NBc); b1 = n0 + nbt
                                if b1 > b0:
                                    runs.append((b0, b1 - b0, b0 + 2))
                            mms = []
                            for tau in range(16):
                                if d == 0:
                                    mms.append((pv[:, :, tau:16], KtL[:, gb, tau, :], uv[:, n0:n0 + nbt, 0:16 - tau]))
                                else:
                                    mms.append((pv[:, :, 0:16 - tau], KtL[:, gb, tau, :], uv[:, n0:n0 + nbt, tau:16]))
                            for q in range(4):
                                pair = gb * 4 + q
                                for i in range(16):
                                    for x, Ax in ((0, Abr), (1, Abi)):
                                        for (r0, rn, p0) in runs:
                                            mms.append((pv[:, r0 - n0:r0 - n0 + rn, i], Pad[:, q, i, x, :], Ax[:, q, p0:p0 + rn]))
                            for mi, (o_, l_, r_) in enumerate(mms):
                                P.mm(o_, l_, r_, mi == 0, mi == len(mms) - 1, ['KtL', 'Pad', 'Ab', ('uT', 0)], [pyk])
                            yrow = T['yfT'][gb * 128:(gb + 1) * 128, t0:t0 + w]
                            if d == 0:
                                P.cp(ysb[:, :w], py[:, :w], [pyk], [ysk], eng='act')
                                P.dma('act', yrow, ysb[:, :w], reads=[ysk], writes=[('yfT', npy)])
                            else:
                                P.dma('sp', yf_t[:, :w], yrow, writes=['yf_t'])
                                P.tt(ysb[:, :w], py[:, :w], yf_t[:, :w], ALU.add, [pyk, 'yf_t'], [ysk])
                                P.stt(ysb[:, :w], uT[:, t0:t0 + w], dT[:, gb:gb + 1], ysb[:, :w], ALU.mult, ALU.add, [('uT', 0), ysk, 'dT'], [ysk])
                                P.tt(y2[:, :w], ysb[:, :w], ysb[:, :w], ALU.mult, [ysk], ['y2'])
                                P.ts(y2[:, :w], y2[:, :w], 0.044715, ALU.mult, ['y2'], ['y2'], s2=1.0, op1=ALU.add)
                                P.tt(y2[:, :w], y2[:, :w], ysb[:, :w], ALU.mult, ['y2', ysk], ['y2'])
                                P.act(y2[:, :w], y2[:, :w], AF.Sigmoid, ['y2'], ['y2'], scale=1.5957691216057308)
                                P.tt(zb[:, :w], y2[:, :w], ysb[:, :w], ALU.mult, ['y2', ysk], ['zb'])
                                P.dma('act', T['zT'][gb * 128:(gb + 1) * 128, t0:t0 + w], zb[:, :w], reads=['zb'], writes=[('zT', npy)])
                        P.barrier()
                    P.barrier()
        if getattr(c, 's5stop', 9) <= 4:
            return
        with ExitStack() as es:
            def mk(nm, shape, dt=F32):
                return es.enter_context(sb(nm, shape, dt))
            Wg = mk("Wg", [128, 8, 1024], BF16); bg = mk("bg", [128, 8])
            z0 = mk("z0", [128, 8, 512], BF16); z1 = mk("z1", [128, 8, 512], BF16); g0 = mk("g0", [128, 8, 512], BF16); g1 = mk("g1", [128, 8, 512], BF16)
            sgt = mk("sgt", [128, 512]); sg2 = mk("sg2", [128, 512]); ob0 = mk("obx0", [128, 8, 512], BF16); ob1 = mk("obx1", [128, 8, 512], BF16)
            pg0 = es.enter_context(ps("pg0", [128, 512], F32)); pg1 = es.enter_context(ps("pg1", [128, 512], F32))
            P.dma('pool', Wg[:], T['w_glu'][layer].rearrange("(k p) c -> p k c", p=128), writes=['Wg'])
            P.dma('sp', bg[:], T['b_gluT'][layer], writes=['bg'])
            zv = T['zT'].rearrange("(k p) t -> p k t", p=128)
            gv_ = T['fmb'][3072:4096, :].rearrange("(k p) t -> p k t", p=128)
            ovs = T['oT'][2048:3072, :].rearrange("(k p) t -> p k t", p=128)
            npg = 0
            for ti, (t0, w) in enumerate(tiles(0, L, 512)):
                zt = (z0, z1)[ti % 2]; gt_ = (g0, g1)[ti % 2]; obx = (ob0, ob1)[ti % 2]; kz = ('z', ti % 2); kg = ('g', ti % 2); ko = ('obx', ti % 2)
                P.dma('sp', zt[:, :, :w], zv[:, :, t0:t0 + w], writes=[kz])
                P.dma('sp', gt_[:, :, :w], gv_[:, :, t0:t0 + w], writes=[kg])
                for oc in range(8):
                    pg = (pg0, pg1)[npg % 2]; pgk = 'pg%d' % (npg % 2); npg += 1
                    for k in range(8):
                        P.mm(pg[:, :w], Wg[:, k, oc * 128:(oc + 1) * 128], zt[:, k, :w], k == 0, k == 7, ['Wg', kz], [pgk])
                    P.act(sgt[:, :w], pg[:, :w], AF.Sigmoid, [pgk, 'bg'], ['sgt'], bias=bg[:, oc:oc + 1])
                    P.tt(sgt[:, :w], sgt[:, :w], zt[:, oc, :w], ALU.mult, ['sgt', kz], ['sgt'])
                    P.act(sg2[:, :w], gt_[:, oc, :w], AF.Silu, [kg], ['sg2'])
                    P.tt(obx[:, oc, :w], sgt[:, :w], sg2[:, :w], ALU.mult, ['sgt', 'sg2'], [ko])
                P.dma('act', ovs[:, :, t0:t0 + w], obx[:, :, :w], reads=[ko], writes=[('oTs', ti)])
            P.barrier()

    def sincos(self, x, shape, sin_out, cos_out, tag):
        P = self.P
        with self.sb(tag + "_ni", shape, I32) as ni, self.sb(tag + "_nf", shape, F32) as nf, \
                self.sb(tag + "_r", shape, F32) as r, self.sb(tag + "_m", shape, F32) as m:
            for which, out in ((0, sin_out), (1, cos_out)):
                kx = tag + 'x'
                if which == 1:
                    P.ts(r[:], x, PI / 2, ALU.add, [kx], [tag + 'r0'])
                    src = r[:]
                else:
                    P.cp(r[:], x, [kx], [tag + 'r0'])
                    src = r[:]
                P.ts(ni[:], src, 1.0 / TWO_PI, ALU.mult, [tag + 'r0'], [tag + 'ni'])
                P.cp(nf[:], ni[:], [tag + 'ni'], [tag + 'nf'])
                P.stt(r[:], nf[:], -TWO_PI, src, ALU.mult, ALU.add, [tag + 'nf', tag + 'r0'], [tag + 'r0'])
                P.ts(m[:], r[:], PI, ALU.is_gt, [tag + 'r0'], [tag + 'm'], s2=TWO_PI, op1=ALU.mult)
                P.tt(r[:], r[:], m[:], ALU.subtract, [tag + 'r0', tag + 'm'], [tag + 'r0'])
                P.ts(m[:], r[:], -PI, ALU.is_lt, [tag + 'r0'], [tag + 'm'], s2=TWO_PI, op1=ALU.mult)
                P.tt(r[:], r[:], m[:], ALU.add, [tag + 'r0', tag + 'm'], [tag + 'r0'])
                P.act(out, r[:], AF.Sin, [tag + 'r0'], [tag + 'out%d' % which])
            P.barrier()

    def phase_rope(self):
        c = self.cfg; P = self.P; T = self.T; L = c.L
        with self.sb("fi", [128, 32], I32) as fi, self.sb("fr", [128, 32], F32) as fr, self.sb("pp", [128, 2], F32) as pp, \
                self.sb("ang", [128, 64], F32) as ang, self.sb("sn", [128, 64], F32) as sn, self.sb("cs", [128, 64], F32) as cs, \
                self.sb("rp", [128, 512], F32) as rp:
            P.op('pool', lambda e: e.iota(fi[:], pattern=[[1, 32]], base=0, channel_multiplier=0), [], ['fi'])
            P.cp(fr[:], fi[:], ['fi'], ['fr'])
            P.act(fr[:], fr[:], AF.Exp, ['fr'], ['fr'], scale=-float(np.log(10000.0)) / 32.0)
            P.barrier()
            for (t0, w) in tiles(0, L, 128):
                P.dma('sp', pp[:w, :], T['pos'][t0:t0 + w, :], writes=['pp'])
                P.ts(ang[:w, 0:32], fr[:w, :], pp[:w, 0:1], ALU.mult, ['pp'], ['rpx'])
                P.ts(ang[:w, 32:64], fr[:w, :], pp[:w, 1:2], ALU.mult, ['pp'], ['rpx'])
                self.sincos(ang[:, :], [128, 64], sn[:, :], cs[:, :], 'rp')
                for h in range(4):
                    P.cp(rp[:, h * 64:(h + 1) * 64], cs[:, :], [], ['rp'])
                    P.cp(rp[:, 256 + h * 64:256 + (h + 1) * 64], sn[:, :], [], ['rp'], eng='pool')
                P.dma('sp', T['rope'][t0:t0 + w, :], rp[:w, :], reads=['rp'], writes=['ropeD'])
                P.barrier()

    def build(self):
        c = self.cfg; nc = self.nc; P = self.P; T = None
        self.declare(); T = self.T
        stack = []
        with nc.Block() as block:
            P.start()
            self.consts(stack)
            skip_pre = bool(getattr(c, 'inject', ()))
            if not skip_pre:
                self.phase_rope()
            hs = [T['hT0'], T['hT1'], T['hT2']]
            for layer in range(c.DEPTH):
                last = (layer == c.DEPTH - 1)
                if not getattr(c, 'noada', False):
                    self.phase_ada(layer)
                if c.stop == 'ada':
                    break
                if not skip_pre:
                    self.phase_norm(hs[layer], 'hn', T['hnT'])
                    if c.stop == 'norm':
                        break
                    self.phase_inproj(layer)
                if c.stop == 'inproj':
                    break
                for name in ('hg', 'gl', 'rt'):
                    if c.stop is None or name in c.stop:
                        if name == 'hg' or getattr(c, 'oldgla', False):
                            self.phase_gla(layer, name)
                        else:
                            self.phase_gla2(layer, name)
                if c.stop is None or 's5' in c.stop:
                    self.phase_s5(layer)
                if c.stop is not None and 'out' not in c.stop:
                    break
                self.phase_outproj(layer, hs[layer], hs[layer + 1], last)
            if c.stop is None or 'out' in c.stop:
                self.phase_norm(hs[c.DEPTH], 'final', T['outT'])
            P.barrier()
            for cm in reversed(stack):
                cm.__exit__(None, None, None)
            P.finish()
        return nc


TM_COLS = np.concatenate([np.arange(0, 4096), np.arange(5120, 7168), np.arange(10272, 12320)])
FM_COLS = np.concatenate([np.arange(4096, 5120), np.arange(7200, 8224), np.arange(8224, 9248), np.arange(9248, 10272),
                          np.arange(12320, 13344), np.arange(7168, 7200)])


def fmT(v, nchunk):
    return np.ascontiguousarray(np.asarray(v, np.float32).reshape(nchunk, 128).T)


def gla_consts():
    C = 64; s = 32; f = np.float32
    out = np.zeros((2, 64, 449), f)
    j = np.arange(64)[:, None]; i = np.arange(64)[None, :]
    for d in (0, 1):
        T = (j <= i) if d == 0 else (j >= i)
        blk = i // s
        m = blk * s + s // 2
        if d == 0:
            QO = (j >= blk * s) & (j <= i)
            Tm = (j <= m)
            I = 1
            KO = (i < I * s) & (j > i) & (j <= I * s - 1)
        else:
            QO = (j <= blk * s + s - 1) & (j >= i)
            Tm = (j >= m)
            I = 0
            KO = (i >= (I + 1) * s) & (j >= (I + 1) * s) & (j < i)
        QD = T.astype(f) - Tm.astype(f)
        M = out[d]
        M[:, 0:64] = QO; M[:, 64:128] = QD; M[:, 128:192] = -QD; M[:, 192:256] = KO
        M[:, 256:320] = T; M[:, 320] = 1.0
        M[:, 321:385] = (j > i) if d == 0 else (j < i)
        M[:, 385:449] = T
    return out


def gla_consts2():
    C = 128; r = 64; f = np.float32
    out = np.zeros((2, C, 513), f)
    j = np.arange(C)[:, None]; i = np.arange(C)[None, :]
    for d in (0, 1):
        T = (j <= i) if d == 0 else (j >= i)
        R = (j <= r) if d == 0 else (j >= r)
        M = out[d]
        M[:, 0:128] = T.astype(f) - (R & (i >= 0)).astype(f)
        M[:, 128:256] = T; M[:, 256] = 1.0
        M[:, 257:385] = (j > i) if d == 0 else (j < i)
        M[:, 385:513] = T
    return out


def prep_shared(cfg, inp):
    DP = cfg.DEPTH; f = np.float32
    sh = {}
    sh['w_ada'] = np.ascontiguousarray(inp['w_ada'][:DP], f)
    sh['b_adaT'] = np.stack([fmT(inp['b_ada'][l], 96) for l in range(DP)])
    sh['norm_gT'] = np.stack([fmT(inp['norm_g'][l], 32) for l in range(DP)])
    sh['final_gT'] = fmT(inp['final_norm_g'], 32)
    w_in = np.asarray(inp['w_in'][:DP], f)
    sh['w_tm'] = np.ascontiguousarray(w_in[:, :, TM_COLS])
    sh['w_fm'] = np.ascontiguousarray(w_in[:, :, FM_COLS])
    sh['w_out'] = np.ascontiguousarray(inp['w_out'][:DP], f)
    sh['w_glu'] = np.ascontiguousarray(inp['s5_w_glu'][:DP], f)
    sh['b_gluT'] = np.stack([fmT(inp['s5_b_glu'][l], 8) for l in range(DP)])
    lb = np.asarray(inp['hgrn_lb_logits'], f)
    sh['hg_lbrep'] = np.ascontiguousarray(np.broadcast_to(lb[None], (128, 2, 2, 1024)))
    sh['hg_ngT'] = np.asarray(inp['hgrn_norm_g'][:DP], f).reshape(DP, 128, 1).copy()
    wg = np.concatenate([np.asarray(inp['gla_w_gk'][:DP], f), np.asarray(inp['gla_b_gk'][:DP], f)[:, :, None, :]], axis=2)
    sh['gla_wg'] = np.ascontiguousarray(wg)
    sh['gla_ngT'] = np.stack([fmT(inp['gla_norm_g'][l], 2) for l in range(DP)])
    sh['ret_ngT'] = np.stack([fmT(inp['ret_norm_g'][l], 2) for l in range(DP)])
    rl = np.asarray(inp['ret_decay_logit'][:DP], f)
    sh['ret_lrep'] = np.ascontiguousarray(np.broadcast_to(rl[:, :, None, :], (DP, 2, 128, 4)))
    L = cfg.L
    pos = np.zeros((L, 2), f)
    t = np.arange(cfg.LAT)
    pos[cfg.CTX:, 0] = t // 64; pos[cfg.CTX:, 1] = t % 64
    sh['pos'] = pos
    sh['gl_consts'] = gla_consts()
    sh['gl_consts2'] = gla_consts2()
    lam = np.stack([np.asarray(inp['s5_lam_re'][:DP], f), np.asarray(inp['s5_lam_im'][:DP], f)], axis=2)
    dt = np.asarray(inp['s5_log_dt'][:DP], f)
    B = np.stack([np.asarray(inp['s5_b_re'][:DP], f), np.asarray(inp['s5_b_im'][:DP], f)], axis=2)
    Cm = np.stack([np.asarray(inp['s5_c_re'][:DP], f), np.asarray(inp['s5_c_im'][:DP], f)], axis=2)
    sh['s5_lamP'] = np.ascontiguousarray(lam.reshape(DP, 2, 2, 32, 2, 64).transpose(0, 1, 2, 4, 5, 3).reshape(DP, 2, 2, 128, 32))
    dtb = np.broadcast_to(dt[:, :, :, None], (DP, 2, 64, 64))
    sh['s5_dtP'] = np.ascontiguousarray(dtb.reshape(DP, 2, 32, 2, 64).transpose(0, 1, 3, 4, 2).reshape(DP, 2, 128, 32))
    sh['s5_lam64'] = np.ascontiguousarray(lam.transpose(0, 1, 2, 4, 3))
    sh['s5_dt64'] = np.ascontiguousarray(dtb.transpose(0, 1, 3, 2))
    sh['s5_b64'] = np.ascontiguousarray(B.transpose(0, 1, 2, 4, 3, 5))
    sh['s5_c64'] = np.ascontiguousarray(Cm.transpose(0, 1, 2, 5, 3, 4))
    lamF = np.broadcast_to(lam.reshape(DP, 2, 2, 8, 8, 1, 64), (DP, 2, 2, 8, 8, 16, 64))
    sh['s5_lamF'] = np.ascontiguousarray(lamF.transpose(0, 1, 2, 4, 5, 3, 6).reshape(DP, 2, 2, 128, 8, 64))
    dtF = np.broadcast_to(dt.reshape(DP, 2, 8, 8, 1, 1), (DP, 2, 8, 8, 16, 64))
    sh['s5_dtF'] = np.ascontiguousarray(dtF.transpose(0, 1, 3, 4, 2, 5).reshape(DP, 2, 128, 8, 64))
    BF = B.reshape(DP, 2, 2, 8, 8, 64, 16)
    sh['s5_bF'] = np.ascontiguousarray(BF.transpose(0, 1, 2, 4, 6, 3, 5).reshape(DP, 2, 2, 128, 8, 64))
    CP = Cm.reshape(DP, 2, 2, 32, 2, 16, 64)
    sh['s5_cP'] = np.ascontiguousarray(CP.transpose(0, 1, 2, 4, 6, 3, 5).reshape(DP, 2, 2, 128, 32, 16))
    dd = np.asarray(inp['s5_d'][:DP], f).reshape(DP, 8, 128)
    sh['s5_dT'] = np.ascontiguousarray(dd.transpose(0, 2, 1))
    row = np.arange(128)[:, None, None]; qq = np.arange(4)[None, :, None]; col = np.arange(128)[None, None, :]
    sh['s5_maskQ'] = ((row // 32 == qq) & ((row % 32) // 16 == col // 64)).astype(f)
    sh['s5_maskC'] = np.ascontiguousarray(np.broadcast_to((np.arange(128)[:, None, None] // 64 == np.arange(2)[None, :, None]), (128, 2, 16))).astype(f)
    return sh


def prep_core(cfg, inp, b):
    f = np.float32
    m = {}
    h0 = np.concatenate([np.asarray(inp['ctx'][b], f), np.asarray(inp['x'][b], f)], axis=0)
    m['hT0'] = np.ascontiguousarray(h0.T)
    c2 = np.stack([np.asarray(inp['c'][b], f), np.asarray(inp['c_ctx'], f)], axis=0)
    m['c2T'] = np.ascontiguousarray(c2.reshape(2, 32, 128).transpose(2, 1, 0))
    return m


_CACHE = {}


def kernel(**inputs):
    cfg = Cfg()
    if 'nc' not in _CACHE:
        _CACHE['nc'] = Builder(cfg).build()
    nc = _CACHE['nc']
    sh = prep_shared(cfg, inputs)
    in_maps = []
    for core in range(8):
        m = dict(sh)
        m.update(prep_core(cfg, inputs, core % 4))
        in_maps.append(m)
    res = run_bass_kernel_spmd(nc, in_maps, core_ids=list(range(8)))
    out = np.stack([np.ascontiguousarray(res.results[b]['outT'].T) for b in range(4)], axis=0)
    return out.astype(np.float32)
```

```python
import numpy as np
import concourse.bass as bass
import concourse.mybir as mybir
from concourse.bass_utils import run_bass_kernel_spmd

F32 = mybir.dt.float32; BF16 = mybir.dt.bfloat16; I32 = mybir.dt.int32
AF = mybir.ActivationFunctionType; ALU = mybir.AluOpType
D = 4096; KC = 32; EPS = 1e-6
TWO_PI = 6.283185307179586; PI = 3.141592653589793


class Prog:
    NDS = 8

    def __init__(self, nc):
        self.nc = nc
        self.eng = {'pe': nc.tensor, 'act': nc.scalar, 'dve': nc.vector, 'pool': nc.gpsimd, 'sp': nc.sync}
        self.sem = {}; self.cnt = {}; self.dsem = {}; self.dcnt = {}
        self.waited = {k: {} for k in self.eng}
        self.res = {}
        self._stack = []
        self.nins = 0

    def start(self):
        nc = self.nc
        for k in self.eng:
            cm = nc.semaphore("s_" + k); self._stack.append(cm); self.sem[k] = cm.__enter__(); self.cnt[k] = 0
        for k in ['sp', 'pool', 'act']:
            self.dsem[k] = []
            for i in range(self.NDS):
                cm = nc.semaphore("d_%s%d" % (k, i)); self._stack.append(cm); self.dsem[k].append(cm.__enter__())
            self.dcnt[k] = 0
        self.last_dma_tok = {k: [None] * self.NDS for k in self.dsem}

    def finish(self):
        for cm in reversed(self._stack):
            cm.__exit__(None, None, None)

    def _wait(self, e, tok):
        if tok is None:
            return
        sem, val, owner = tok
        w = self.waited[e]
        key = id(sem)
        if w.get(key, 0) >= val:
            return
        if owner == e and e == 'pe':
            return
        self.eng[e].wait_ge(sem, val)
        w[key] = val

    def _deps(self, e, reads, writes):
        toks = []
        for k in reads:
            st = self.res.get(k)
            if st and st['w']:
                toks.append(st['w'])
        for k in writes:
            st = self.res.get(k)
            if st:
                if st['w']:
                    toks.append(st['w'])
                toks.extend(st['r'])
        for t in toks:
            self._wait(e, t)

    def _update(self, tok, reads, writes):
        for k in reads:
            st = self.res.setdefault(k, {'w': None, 'r': []})
            st['r'].append(tok)
            if len(st['r']) > 48:
                best = {}
                for t in st['r']:
                    kk = id(t[0])
                    if kk not in best or best[kk][1] < t[1]:
                        best[kk] = t
                st['r'] = list(best.values())
        for k in writes:
            self.res[k] = {'w': tok, 'r': []}

    def op(self, e, fn, reads=(), writes=()):
        self._deps(e, reads, writes)
        ins = fn(self.eng[e])
        self.cnt[e] += 1
        self.nins += 1
        ins.then_inc(self.sem[e], 1)
        tok = (self.sem[e], self.cnt[e], e)
        self._update(tok, reads, writes)
        return tok

    def dma(self, q, out, in_, reads=(), writes=(), **kw):
        self._deps(q, reads, writes)
        i = self.dcnt[q]; k = i % self.NDS
        prev = self.last_dma_tok[q][k]
        if prev is not None:
            self._wait(q, prev)
        ins = self.eng[q].dma_start(out=out, in_=in_, **kw)
        val = 16 * (i // self.NDS + 1)
        ins.then_inc(self.dsem[q][k], 16)
        tok = (self.dsem[q][k], val, 'dma_' + q)
        self.last_dma_tok[q][k] = tok
        self.dcnt[q] += 1
        self.nins += 1
        self._update(tok, reads, writes)
        return tok

    def barrier(self):
        toks = []
        for e in self.eng:
            if self.cnt[e] > 0:
                toks.append((self.sem[e], self.cnt[e], e))
        for q in self.dsem:
            for t in self.last_dma_tok[q]:
                if t is not None:
                    toks.append(t)
        for e in self.eng:
            for t in toks:
                self._wait(e, t)
        self.res = {}

    def mm(self, out, lhsT, rhs, start, stop, reads, writes):
        return self.op('pe', lambda e: e.matmul(out, lhsT=lhsT, rhs=rhs, start=start, stop=stop), reads, writes)

    def tr(self, out, in_, ident, reads, writes):
        return self.op('pe', lambda e: e.transpose(out=out, in_=in_, identity=ident), reads, writes)

    def act(self, out, in_, func, reads, writes, bias=None, scale=None, eng='act'):
        kw = {}
        if bias is not None:
            kw['bias'] = bias
        if scale is not None:
            kw['scale'] = scale
        return self.op(eng, lambda e: e.activation(out=out, in_=in_, func=func, **kw), reads, writes)

    def tt(self, out, in0, in1, op, reads, writes, eng='dve'):
        return self.op(eng, lambda e: e.tensor_tensor(out=out, in0=in0, in1=in1, op=op), reads, writes)

    def ts(self, out, in0, s1, op0, reads, writes, s2=None, op1=None, eng='dve'):
        if op1 is None:
            return self.op(eng, lambda e: e.tensor_scalar(out=out, in0=in0, scalar1=s1, scalar2=None, op0=op0), reads, writes)
        return self.op(eng, lambda e: e.tensor_scalar(out=out, in0=in0, scalar1=s1, scalar2=s2, op0=op0, op1=op1), reads, writes)

    def stt(self, out, in0, scalar, in1, op0, op1, reads, writes):
        return self.op('dve', lambda e: e.scalar_tensor_tensor(out=out, in0=in0, scalar=scalar, in1=in1, op0=op0, op1=op1), reads, writes)

    def cp(self, out, in_, reads, writes, eng='dve'):
        if eng == 'act':
            return self.op('act', lambda e: e.copy(out=out, in_=in_), reads, writes)
        return self.op(eng, lambda e: e.tensor_copy(out=out, in_=in_), reads, writes)

    def memset(self, ap, val, writes, eng='pool'):
        return self.op(eng, lambda e: e.memset(ap, val), (), writes)


class Cfg:
    def __init__(self, CTX=256, LAT=4096, DEPTH=2, debug=False, stop=None):
        self.CTX = CTX; self.LAT = LAT; self.DEPTH = DEPTH; self.L = CTX + LAT
        self.debug = debug; self.stop = stop
        self.half = False

    def need_hi(self, last):
        return (self.CTX + self.LAT // 2) if (last and self.half) else self.L

    def need_lo(self, last):
        return self.CTX if last else 0


def tiles(lo, hi, w):
    out = []
    t = lo
    while t < hi:
        ww = min(w, hi - t)
        out.append((t, ww))
        t += ww
    return out


class Builder:
    def __init__(self, cfg):
        self.cfg = cfg
        self.nc = bass.Bass("TRN2", target_bir_lowering=False)
        self.P = Prog(self.nc)
        self.T = {}
        self.dbg_names = []

    def din(self, name, shape, dt=F32):
        self.T[name] = self.nc.dram_tensor(name, list(shape), dt, kind="ExternalInput").ap()

    def dscr(self, name, shape, dt, out=False):
        if name in getattr(self.cfg, 'inject', ()):
            self.T[name] = self.nc.dram_tensor(name, list(shape), dt, kind="ExternalInput").ap()
            return
        if out or (self.cfg.debug and name in self.cfg.debug):
            self.T[name] = self.nc.dram_tensor(name, list(shape), dt, kind="ExternalOutput").ap()
            self.dbg_names.append(name)
        else:
            self.T[name] = self.nc.dram_tensor(name, list(shape), dt).ap()

    def sb(self, name, shape, dt):
        self._uid = getattr(self, '_uid', 0) + 1
        return self.nc.sbuf_tensor("%s_%d" % (name, self._uid), list(shape), dt)

    def ps(self, name, shape, dt=F32):
        self._uid = getattr(self, '_uid', 0) + 1
        return self.nc.psum_tensor("%s_%d" % (name, self._uid), list(shape), dt)

    def declare(self):
        c = self.cfg; L = c.L; DP = c.DEPTH
        self.din("hT0", [D, L]); self.din("c2T", [128, 32, 2])
        self.din("w_ada", [DP, D, 3 * D]); self.din("b_adaT", [DP, 128, 96])
        self.din("norm_gT", [DP, 128, 32]); self.din("final_gT", [128, 32])
        self.din("w_tm", [DP, D, 8192]); self.din("w_fm", [DP, D, 5152])
        self.din("w_out", [DP, D, D]); self.din("w_glu", [DP, 1024, 1024]); self.din("b_gluT", [DP, 128, 8])
        self.din("hg_lbrep", [128, 2, 2, 1024]); self.din("hg_ngT", [DP, 128, 1])
        self.din("gla_wg", [DP, 2, 17, 512]); self.din("gla_ngT", [DP, 128, 2])
        self.din("ret_ngT", [DP, 128, 2]); self.din("ret_lrep", [DP, 2, 128, 4])
        self.din("pos", [L, 2]); self.din("gl_consts", [2, 64, 449]); self.din("gl_consts2", [2, 128, 513])
        self.din("s5_lamP", [DP, 2, 2, 128, 32])
        self.din("s5_dtP", [DP, 2, 128, 32])
        self.din("s5_lam64", [DP, 2, 2, 64, 64])
        self.din("s5_dt64", [DP, 2, 64, 64])
        self.din("s5_b64", [DP, 2, 2, 64, 64, 16])
        self.din("s5_c64", [DP, 2, 2, 64, 64, 16])
        self.din("s5_lamF", [DP, 2, 2, 128, 8, 64])
        self.din("s5_dtF", [DP, 2, 128, 8, 64])
        self.din("s5_bF", [DP, 2, 2, 128, 8, 64])
        self.din("s5_cP", [DP, 2, 2, 128, 32, 16])
        self.din("s5_dT", [DP, 128, 8]); self.din("s5_maskQ", [128, 4, 128]); self.din("s5_maskC", [128, 2, 16])
        self.dscr("hT1", [D, L], F32); self.dscr("hT2", [D, L], F32)
        self.dscr("hnT", [D, L], BF16)
        self.dscr("tmb", [L, 8192], BF16); self.dscr("tmf", [L, 2048], F32)
        self.dscr("fmb", [5120, L], BF16); self.dscr("lrT", [32, L], F32)
        self.dscr("ofT", [3072, L], F32); self.dscr("oT", [D, L], BF16)
        self.dscr("rope", [L, 512], F32)
        self.dscr("s5pf", [8, 2, 128, 17, 64], F32); self.dscr("s5cf", [8, 2, 128, 64], F32)
        self.dscr("yfT", [1024, L], F32); self.dscr("zT", [1024, L], BF16)
        self.dscr("outT", [D, (c.LAT // 2) if c.half else c.LAT], F32, out=True)

    def consts(self, stack):
        P = self.P
        def mk(name, shape, dt):
            cm = self.sb(name, shape, dt); stack.append(cm); return cm.__enter__()
        self.onesf = mk("onesf", [128, 128], F32)
        self.onesb = mk("onesb", [128, 128], BF16)
        self.identb = mk("identb", [128, 128], BF16)
        self.identf = mk("identf", [128, 128], F32)
        self.sT = mk("sT", [128, 32, 2], BF16)
        self.GS = mk("GS", [128, 6, 32], F32)
        P.memset(self.onesf[:], 1.0, ['onesf'])
        P.memset(self.onesb[:], 1.0, ['onesb'])
        P.op('pool', lambda e: e.affine_select(out=self.identf[:], in_=self.onesf[:], pattern=[[1, 128]], compare_op=ALU.is_equal,
                                               fill=0.0, base=0, channel_multiplier=-1), ['onesf'], ['identf'])
        P.cp(self.identb[:], self.identf[:], ['identf'], ['identb'])
        with self.sb("c2f", [128, 32, 2], F32) as c2f:
            P.dma('sp', c2f[:], self.T['c2T'], writes=['c2f'])
            P.act(self.sT[:], c2f[:], AF.Silu, ['c2f'], ['sT'])
            P.barrier()

    def phase_ada(self, layer):
        P = self.P; T = self.T; GS = self.GS
        wv = T['w_ada'][layer].rearrange("(k p) c -> p k c", p=128)
        with self.sb("wa0", [128, 32, 256], BF16) as wa0, self.sb("wa1", [128, 32, 256], BF16) as wa1, \
                self.ps("pa", [128, 512], F32) as pa, self.sb("modT", [128, 96, 2], F32) as modT, \
                self.sb("bT", [128, 96], F32) as bT, self.sb("ng", [128, 32], F32) as ng:
            was = [wa0, wa1]
            P.dma('sp', bT[:], T['b_adaT'][layer], writes=['bT'])
            P.dma('sp', ng[:], T['norm_gT'][layer], writes=['ng'])
            for cb in range(48):
                wa = was[cb % 2]; key = ('wa', cb % 2)
                P.dma('pool', wa[:], wv[:, :, cb * 256:(cb + 1) * 256], writes=[key])
                for s in range(2):
                    j = cb * 2 + s
                    for k in range(32):
                        P.mm(pa[:, 2 * j:2 * j + 2], wa[:, k, s * 128:(s + 1) * 128], self.sT[:, k, :], k == 0, k == 31,
                             [key, 'sT'], ['pa'])
            P.tt(modT[:], pa[:, 0:192].rearrange("p (j r) -> p j r", r=2), bT[:].unsqueeze(2).to_broadcast([128, 96, 2]), ALU.add,
                 ['pa', 'bT'], ['modT'])
            for r in range(2):
                P.stt(GS[:, 3 * r + 0, :], modT[:, 32:64, r], 1.0, ng[:], ALU.add, ALU.mult, ['modT', 'ng'], ['GS'])
                P.cp(GS[:, 3 * r + 1, :], modT[:, 0:32, r], ['modT'], ['GS'])
                P.cp(GS[:, 3 * r + 2, :], modT[:, 64:96, r], ['modT'], ['GS'])
            P.barrier()

    def phase_norm(self, hsrc, mode, dst):
        c = self.cfg; P = self.P; T = self.T; GS = self.GS; L = c.L; CTX = c.CTX
        TW = 256
        hv = hsrc.rearrange("(k p) t -> p k t", p=128)
        dv = dst.rearrange("(k p) t -> p k t", p=128)
        toks = tiles(0, L, TW) if mode == 'hn' else tiles(CTX, c.need_hi(True), TW)
        with self.sb("hTa", [128, 32, TW], F32) as hTa, self.sb("hTb", [128, 32, TW], F32) as hTb, \
                self.sb("sq", [128, 32, TW], BF16) as sq, self.sb("hna", [128, 32, TW], BF16) as hna, \
                self.sb("hnb", [128, 32, TW], BF16) as hnb, self.sb("rstd", [128, TW], F32) as rstd, \
                self.sb("fg", [128, 32], F32) as fg, \
                self.ps("pssa", [128, 512], F32) as pssa, self.ps("pssb", [128, 512], F32) as pssb:
            hTs = [hTa, hTb]; hns = [hna, hnb]; psss = [pssa, pssb]
            if mode == 'final':
                P.dma('sp', fg[:], T['final_gT'], writes=['fg'])
            for it, (t0, w) in enumerate(toks):
                b = it % 2
                hT = hTs[b]; hn = hns[b]; pss = psss[b]
                P.dma('sp', hT[:, :, :w], hv[:, :, t0:t0 + w], writes=[('hT', b)])
                P.act(sq[:, :, :w], hT[:, :, :w], AF.Square, [('hT', b)], ['sq'])
                for k in range(32):
                    P.mm(pss[:, :w], self.onesb[:], sq[:, k, :w], k == 0, k == 31, ['sq', 'onesb'], [('pss', b)])
                P.act(rstd[:, :w], pss[:, :w], AF.Sqrt, [('pss', b)], ['rstd'], bias=EPS, scale=1.0 / D)
                P.op('dve', lambda e: e.reciprocal(out=rstd[:, :w], in_=rstd[:, :w]), ['rstd'], ['rstd'])
                P.tt(hT[:, :, :w], hT[:, :, :w], rstd[:, :w].unsqueeze(1).to_broadcast([128, 32, w]), ALU.mult,
                     [('hT', b), 'rstd'], [('hT', b)])
                if mode == 'hn':
                    gi = 3 if t0 < CTX else 0
                    for k in range(32):
                        if k % 2 == 0:
                            P.act(hn[:, k, :w], hT[:, k, :w], AF.Identity, [('hT', b), 'GS'], [('hn', b, k)],
                                  bias=GS[:, gi + 1, k:k + 1], scale=GS[:, gi, k:k + 1])
                        else:
                            P.ts(hn[:, k, :w], hT[:, k, :w], GS[:, gi, k:k + 1], ALU.mult, [('hT', b), 'GS'], [('hn', b, k)],
                                 s2=GS[:, gi + 1, k:k + 1], op1=ALU.add)
                    P.dma('act', dv[:, :, t0:t0 + w], hn[:, :, :w], reads=[('hn', b, k) for k in range(32)], writes=[('dst', it)])
                else:
                    P.tt(hT[:, :, :w], hT[:, :, :w], fg[:].unsqueeze(2).to_broadcast([128, 32, w]), ALU.mult,
                         [('hT', b), 'fg'], [('hT', b)])
                    P.dma('act', dv[:, :, t0 - CTX:t0 - CTX + w], hT[:, :, :w], reads=[('hT', b)], writes=[('dst', it)])
            P.barrier()

    def phase_inproj(self, layer):
        c = self.cfg; P = self.P; T = self.T; L = c.L
        blocks = []
        for bi in range(16):
            blocks.append(('tm', 'w_tm', bi * 512, 512))
        for bi in range(10):
            blocks.append(('fm', 'w_fm', bi * 512, 512))
        blocks.append(('lr', 'w_fm', 5120, 32))
        hv = T['hnT'].rearrange("(k p) t -> p k t", p=128)
        ttiles = tiles(0, L, 512)
        with self.sb("Wb0", [128, 32, 512], BF16) as Wb0, self.sb("Wb1", [128, 32, 512], BF16) as Wb1, \
                self.sb("hb0", [128, 32, 512], BF16) as hb0, self.sb("hb1", [128, 32, 512], BF16) as hb1, \
                self.sb("stgf", [128, 4, 512], F32) as stgf, self.sb("stgb", [128, 4, 512], BF16) as stgb, \
                self.ps("pd", [128, 4, 512], F32) as pd:
            Wbs = [Wb0, Wb1]; hbs = [hb0, hb1]
            gt = 0; gp = 0
            for bi, (kind, wn, c0, ncol) in enumerate(blocks):
                Wb = Wbs[bi % 2]; wkey = ('Wb', bi % 2)
                wv = T[wn][layer].rearrange("(k p) c -> p k c", p=128)
                P.dma('pool', Wb[:, :, :ncol], wv[:, :, c0:c0 + ncol], writes=[wkey])
                for (t0, w) in ttiles:
                    hb = hbs[gt % 2]; hkey = ('hb', gt % 2); gt += 1
                    P.dma('sp', hb[:, :, :w], hv[:, :, t0:t0 + w], writes=[hkey])
                    if kind in ('fm', 'lr'):
                        nsub = max(1, ncol // 128); m = min(128, ncol)
                        for cs in range(nsub):
                            bk = gp % 4; gp += 1
                            for k in range(32):
                                P.mm(pd[:m, bk, :w], Wb[:, k, cs * 128:cs * 128 + m], hb[:, k, :w], k == 0, k == 31,
                                     [wkey, hkey], [('pd', bk)])
                            if kind == 'fm':
                                stg = stgb; skey = ('stgb', bk); dst = T['fmb'][c0 + cs * 128:c0 + cs * 128 + m, t0:t0 + w]
                            else:
                                stg = stgf; skey = ('stgf', bk); dst = T['lrT'][0:32, t0:t0 + w]
                            P.cp(stg[:m, bk, :w], pd[:m, bk, :w], [('pd', bk)], [skey], eng=('act' if bk % 2 else 'dve'))
                            P.dma('act', dst, stg[:m, bk, :w], reads=[skey], writes=[('o', gp)])
                    else:
                        isf = (1024 <= c0 < 3072)
                        for ts in range(w // 128):
                            bk = gp % 4; gp += 1
                            for k in range(32):
                                P.mm(pd[:, bk, :], hb[:, k, ts * 128:(ts + 1) * 128], Wb[:, k, :], k == 0, k == 31,
                                     [wkey, hkey], [('pd', bk)])
                            r0 = t0 + ts * 128
                            if isf:
                                stg = stgf; skey = ('stgf', bk); dst = T['tmf'][r0:r0 + 128, c0 - 1024:c0 - 1024 + 512]
                            else:
                                stg = stgb; skey = ('stgb', bk); dst = T['tmb'][r0:r0 + 128, c0:c0 + 512]
                            P.cp(stg[:, bk, :], pd[:, bk, :], [('pd', bk)], [skey], eng=('act' if bk % 2 else 'dve'))
                            P.dma('act', dst, stg[:, bk, :], reads=[skey], writes=[('o', gp)])
            P.barrier()

    def phase_outproj(self, layer, hsrc, hdst, last):
        c = self.cfg; P = self.P; T = self.T; L = c.L; CTX = c.CTX; GS = self.GS
        wv = T['w_out'][layer].rearrange("(k p) c -> p k c", p=128)
        ov = T['oT'].rearrange("(k p) t -> p k t", p=128)
        ttiles = ([] if last else tiles(0, CTX, 512)) + tiles(CTX, c.need_hi(last), 512)
        with self.sb("Wo0", [128, 32, 512], BF16) as Wb0, self.sb("Wo1", [128, 32, 512], BF16) as Wb1, \
                self.sb("ob0", [128, 32, 512], BF16) as hb0, self.sb("ob1", [128, 32, 512], BF16) as hb1, \
                self.sb("hold0", [128, 4, 512], F32) as hold0, self.sb("hold1", [128, 4, 512], F32) as hold1, \
                self.ps("po", [128, 4, 512], F32) as pd:
            Wbs = [Wb0, Wb1]; hbs = [hb0, hb1]; holds = [hold0, hold1]
            gt = 0; gp = 0
            for fb in range(8):
                Wb = Wbs[fb % 2]; wkey = ('Wb', fb % 2)
                P.dma('pool', Wb[:], wv[:, :, fb * 512:(fb + 1) * 512], writes=[wkey])
                hs = hsrc[fb * 512:(fb + 1) * 512, :].rearrange("(f p) t -> p f t", p=128)
                hd = hdst[fb * 512:(fb + 1) * 512, :].rearrange("(f p) t -> p f t", p=128)
                for (t0, w) in ttiles:
                    hb = hbs[gt % 2]; hkey = ('hb', gt % 2); hold = holds[gt % 2]; okey = ('hold', gt % 2); gt += 1
                    P.dma('sp', hb[:, :, :w], ov[:, :, t0:t0 + w], writes=[hkey])
                    P.dma('sp', hold[:, :, :w], hs[:, :, t0:t0 + w], writes=[okey])
                    gi = 3 if t0 < CTX else 0
                    for fs in range(4):
                        bk = gp % 4; gp += 1
                        fch = fb * 4 + fs
                        for k in range(32):
                            P.mm(pd[:, bk, :w], Wb[:, k, fs * 128:(fs + 1) * 128], hb[:, k, :w], k == 0, k == 31,
                                 [wkey, hkey], [('pd', bk)])
                        P.stt(hold[:, fs, :w], pd[:, bk, :w], GS[:, gi + 2, fch:fch + 1], hold[:, fs, :w], ALU.mult, ALU.add,
                              [('pd', bk), okey, 'GS'], [okey])
                    P.dma('act', hd[:, :, t0:t0 + w], hold[:, :, :w], reads=[okey], writes=[('o', gt)])
            P.barrier()

    BR = {
        'hg': dict(H=8, dv=128, q=0, k=None, v=3072, gate=0, orow=0, ofrow=0, gs=1.0, qs=1.0, norm='rms'),
        'gl': dict(H=4, dv=256, q=4096, k=4608, v=5120, gate=1024, orow=1024, ofrow=1024, gs=-1.0 / 16.0, qs=128.0 ** -0.5, norm='rms'),
        'rt': dict(H=4, dv=256, q=6144, k=6656, v=7168, gate=4096, orow=3072, ofrow=2048, gs=1.0, qs=128.0 ** -0.5, norm='ln'),
    }

    def phase_gla(self, layer, name, last=False):
        c = self.cfg; P = self.P; T = self.T; L = c.L; CTX = c.CTX
        nlo = c.need_lo(last); nhi = c.need_hi(last)
        br = self.BR[name]; H = br['H']; dv = br['dv']; dvc = dv // 128; HW = H * 128; C = 64
        NCH = L // C; ctxch = CTX // C; r = C // 2
        orders = [[n for n in range(NCH) if n * C < nhi], list(range(ctxch - 1, -1, -1)) + list(range(NCH - 1, ctxch - 1, -1))]
        need_o = lambda n: (nlo <= n * C < nhi)
        ofv = T['ofT'][br['ofrow']:br['ofrow'] + 1024, :].rearrange("(c p) t -> p c t", p=128)
        ov = T['oT'][br['orow']:br['orow'] + 1024, :].rearrange("(c p) t -> p c t", p=128)
        gv = T['fmb'][br['gate']:br['gate'] + 1024, :].rearrange("(c p) t -> p c t", p=128)
        sb = self.sb; ps = self.ps
        from contextlib import ExitStack
        with ExitStack() as es:
            Mcat = es.enter_context(sb("Mcat", [C, 449], F32))
            kdz = es.enter_context(sb("kdz", [128, 2, 128], BF16))
            koz = es.enter_context(sb("koz", [128, 2, 1, 64], BF16))
            qtd = es.enter_context(sb("qtd", [128, 2, C], BF16))
            TriS = es.enter_context(sb("TriS", [C, C], F32))
            mask = es.enter_context(sb("mask", [C, C], F32))
            ctmp = es.enter_context(sb("ctmp", [C, C], F32))
            q_t = es.enter_context(sb("q_t", [C, 2, HW], BF16))
            v_t = es.enter_context(sb("v_t", [C, 2, 1024], BF16))
            z_t = es.enter_context(sb("z_t", [C, 2, 1024], F32))
            k_t = es.enter_context(sb("k_t", [C, 2, HW], BF16))
            lr_t = es.enter_context(sb("lr_t", [17, 2, C], F32))
            rp_t = es.enter_context(sb("rp_t", [C, 2, 512], F32))
            of_t = es.enter_context(sb("of_t", [128, 2, 8, C], F32))
            gate_t = es.enter_context(sb("gate_t", [128, 2, 8, C], BF16))
            sig = es.enter_context(sb("sig", [C, 1024], F32))
            ff = es.enter_context(sb("ff", [C, 1024], F32))
            graw = es.enter_context(sb("graw", [C, 2, 1024], F32))
            kt = es.enter_context(sb("kt", [C, 2, 1024], BF16))
            qr = es.enter_context(sb("qr", [C, 2, HW], BF16))
            khat = es.enter_context(sb("khat", [C, 2, 1024], BF16))
            Ek = es.enter_context(sb("Ek", [C, 512], F32))
            rt1 = es.enter_context(sb("rt1", [C, 256], F32))
            rt2 = es.enter_context(sb("rt2", [C, 256], F32))
            rt3 = es.enter_context(sb("rt3", [C, 256], F32))
            rt4 = es.enter_context(sb("rt4", [C, 256], F32))
            E13 = es.enter_context(sb("E13", [128, 2, 321], F32))
            E2 = es.enter_context(sb("E2", [128, 2, C], F32))
            qtl = es.enter_context(sb("qtl", [128, 2, C], BF16))
            qtp = es.enter_context(sb("qtp", [128, 2, C], BF16))
            ktl = es.enter_context(sb("ktl", [128, 2, C], BF16))
            scb = es.enter_context(sb("scb", [C, 2, C], BF16))
            ofs = es.enter_context(sb("ofs", [128, 2, 8, C], F32))
            osum = es.enter_context(sb("osum", [128, 2, 8, C], F32))
            sg = es.enter_context(sb("sg", [128, 8, C], F32))
            sqt = es.enter_context(sb("sqt", [128, 8, C], F32))
            rs = es.enter_context(sb("rs", [128, 8, C], F32))
            mean = es.enter_context(sb("mean", [128, 8, C], F32))
            ob = es.enter_context(sb("ob", [128, 2, 8, C], BF16))
            otmp = es.enter_context(sb("otmp", [128, 8, C], F32))
            S = es.enter_context(sb("S", [128, 1024], F32))
            Sb = es.enter_context(sb("Sb", [128, 1024], BF16))
            lbr = es.enter_context(sb("lbr", [128, 1024], F32))
            oml = es.enter_context(sb("oml", [128, 1024], F32))
            lg = es.enter_context(sb("lg", [128, 2, 1024], F32))
            wg = es.enter_context(sb("wg", [17, 512], F32))
            gn = es.enter_context(sb("gn", [128, 2], F32))
            rl = es.enter_context(sb("rl", [128, 4], F32))
            gconst = es.enter_context(sb("gconst", [C, 512], F32))
            pc = es.enter_context(ps("pc", [128, 512], F32))
            pP0 = es.enter_context(ps("pP0", [128, 512], F32)); pP1 = es.enter_context(ps("pP1", [128, 512], F32)); pPs = [pP0, pP1]
            pT0 = es.enter_context(ps("pT0", [128, 8, 128], BF16)); pT1 = es.enter_context(ps("pT1", [128, 8, 128], BF16)); pTs = [pT0, pT1]
            pS = es.enter_context(ps("pS", [128, 512], F32))
            pO = es.enter_context(ps("pO", [128, 8, C], F32))
            pU = es.enter_context(ps("pU", [128, 2, 256], F32))
            if name == 'hg':
                P.dma('sp', gn[:, 0:1], T['hg_ngT'][layer], writes=['gn'])
            elif name == 'gl':
                P.dma('sp', gn[:], T['gla_ngT'][layer], writes=['gn'])
            else:
                P.dma('sp', gn[:], T['ret_ngT'][layer], writes=['gn'])
            P.memset(lr_t[:], 1.0, ['lr0', 'lr1'])
            step = 0
            for d in (0, 1):
                gs = br['gs']
                P.dma('sp', Mcat[:, :], T['gl_consts'][d], writes=['Mcat'])
                P.ts(Mcat[:, 0:385], Mcat[:, 0:385], gs, ALU.mult, ['Mcat'], ['Mcat'])
                P.memset(kdz[:], 0.0, ['kdz0', 'kdz1'])
                P.memset(koz[:], 0.0, ['koz0', 'koz1'])
                if name == 'hg':
                    P.dma('sp', lg[:], T['hg_lbrep'][:, :, d, :], writes=['lg'])
                    P.tt(lbr[:], lg[:, 1, :], lg[:, 0, :], ALU.subtract, ['lg'], ['lbr'])
                    P.act(lbr[:], lbr[:], AF.Sigmoid, ['lbr'], ['lbr'])
                    P.ts(lbr[:], lbr[:], float(layer), ALU.mult, ['lbr'], ['lbr'])
                    P.ts(oml[:], lbr[:], -1.0, ALU.mult, ['lbr'], ['oml'], s2=1.0, op1=ALU.add)
                elif name == 'gl':
                    P.dma('sp', wg[:], T['gla_wg'][layer, d], writes=['wg'])
                else:
                    P.dma('sp', rl[:], T['ret_lrep'][layer, d], writes=['rl'])
                    P.act(rl[:], rl[:], AF.Exp, ['rl'], ['rl'], scale=-1.0)
                    P.act(rl[:], rl[:], AF.Ln, ['rl'], ['rl'], bias=1.0)
                    P.ts(rl[:], rl[:], -1.0, ALU.mult, ['rl'], ['rl'])
                    for h in range(4):
                        P.ts(gconst[:, h * 128:(h + 1) * 128], self.onesf[:C, :], rl[:C, h:h + 1], ALU.mult, ['onesf', 'rl'], ['gconst'])
                P.memset(S[:], 0.0, [('S', h) for h in range(H)])
                P.memset(Sb[:], 0.0, [('Sb', h) for h in range(H)])
                order = orders[d]

                def emit_prep(si):
                    n = order[si]; t0 = n * C; pb = si % 2
                    K = lambda nm: (nm, pb)
                    P.dma('sp', q_t[:, pb, :], T['tmb'][t0:t0 + C, br['q']:br['q'] + HW], writes=[K('q_t')])
                    P.dma('sp', v_t[:, pb, :], T['tmb'][t0:t0 + C, br['v']:br['v'] + 1024], writes=[K('v_t')])
                    if name == 'hg':
                        P.dma('sp', z_t[:, pb, :], T['tmf'][t0:t0 + C, d * 1024:(d + 1) * 1024], writes=[K('z_t')])
                    else:
                        P.dma('sp', k_t[:, pb, :], T['tmb'][t0:t0 + C, br['k']:br['k'] + HW], writes=[K('k_t')])
                    if name == 'gl':
                        P.dma('sp', lr_t[0:16, pb, :], T['lrT'][d * 16:(d + 1) * 16, t0:t0 + C], writes=[('lr%d' % pb)])
                    if name == 'rt':
                        P.dma('sp', rp_t[:, pb, :], T['rope'][t0:t0 + C, :], writes=[K('rp_t')])
                    if d == 1 and need_o(n):
                        P.dma('sp', of_t[:, pb], ofv[:, :, t0:t0 + C], writes=[K('of_t')])
                        P.dma('sp', gate_t[:, pb], gv[:, :, t0:t0 + C], writes=[K('gate_t')])
                    if name == 'hg':
                        P.act(sig[:], z_t[:, pb, :], AF.Sigmoid, [K('z_t')], ['sig'])
                        P.tt(ff[:], sig[:], oml[:C, :], ALU.mult, ['sig', 'oml'], ['ff'])
                        P.tt(ff[:], ff[:], lbr[:C, :], ALU.add, ['ff', 'lbr'], ['ff'])
                        P.ts(ff[:], ff[:], 1e-6, ALU.max, ['ff'], ['ff'])
                        P.act(graw[:, pb, :], ff[:], AF.Ln, ['ff'], [K('graw')])
                        P.ts(ff[:], sig[:], -1.0, ALU.mult, ['sig', 'ff'], ['ff'], s2=1.0, op1=ALU.add)
                        P.tt(kt[:, pb, :], ff[:], oml[:C, :], ALU.mult, ['ff', 'oml'], [K('kt')], eng='pool')
                    elif name == 'gl':
                        P.mm(pc[:C, :], lr_t[0:17, pb, :], wg[0:17, :], True, True, ['lr%d' % pb, 'wg'], ['pc'])
                        P.act(sig[:, 0:512], pc[:C, :], AF.Exp, ['pc'], ['sig'], scale=-1.0)
                        P.act(graw[:, pb, 0:512], sig[:, 0:512], AF.Ln, ['sig'], [K('graw')], bias=1.0)
                    else:
                        cos4 = rp_t[:, pb, 0:256].rearrange("p (h x) -> p h x", x=64)
                        sin4 = rp_t[:, pb, 256:512].rearrange("p (h x) -> p h x", x=64)
                        for (src, skey, dst, dkey, eng, ta, tb) in ((q_t, K('q_t'), qr, K('qr'), 'dve', rt1, rt2), (k_t, K('k_t'), kt, K('kt'), 'pool', rt3, rt4)):
                            xv = src[:, pb, :].rearrange("p (h two x) -> p h two x", two=2, x=64)
                            ov_ = dst[:, pb, 0:512].rearrange("p (h two x) -> p h two x", two=2, x=64)
                            tav = ta[:].rearrange("p (h x) -> p h x", x=64); tbv = tb[:].rearrange("p (h x) -> p h x", x=64)
                            P.tt(tav, xv[:, :, 0, :], cos4, ALU.mult, [skey, K('rp_t')], [ta.name], eng=eng)
                            P.tt(tbv, xv[:, :, 1, :], sin4, ALU.mult, [skey, K('rp_t')], [tb.name], eng=eng)
                            P.tt(ov_[:, :, 0, :], tav, tbv, ALU.subtract, [ta.name, tb.name], [dkey], eng=eng)
                            P.tt(tav, xv[:, :, 0, :], sin4, ALU.mult, [skey, K('rp_t')], [ta.name], eng=eng)
                            P.tt(tbv, xv[:, :, 1, :], cos4, ALU.mult, [skey, K('rp_t')], [tb.name], eng=eng)
                            P.tt(ov_[:, :, 1, :], tav, tbv, ALU.add, [ta.name, tb.name], [dkey], eng=eng)
                    gsrc_, gkey_, ksl_, kkey_ = srcs(pb)
                    for hf in range(HW // 512):
                        cs = slice(hf * 512, (hf + 1) * 512)
                        P.mm(pc[:C, :], Mcat[:, 321:385], gsrc_[:, cs], True, True, ['Mcat', gkey_], ['pc'])
                        P.act(Ek[:], pc[:C, :], AF.Exp, ['pc'], ['Ek'])
                        P.tt(khat[:, pb, cs], ksl_(hf * 512, (hf + 1) * 512), Ek[:], ALU.mult, [kkey_, 'Ek'], [('khat', pb, hf)])

                def srcs(pb):
                    K = lambda nm: (nm, pb)
                    if name == 'rt':
                        gsrc_ = gconst; gkey_ = 'gconst'
                    else:
                        gsrc_ = graw[:, pb, :]; gkey_ = K('graw')
                    if name == 'gl':
                        ksl_ = lambda a, b: k_t[:, pb, a:b]; kkey_ = K('k_t')
                    else:
                        ksl_ = lambda a, b: kt[:, pb, a:b]; kkey_ = K('kt')
                    return gsrc_, gkey_, ksl_, kkey_

                def emit_front(si, h):
                    pb = si % 2; sl = h % 2; hs = slice(h * 128, (h + 1) * 128)
                    K = lambda nm: (nm, pb)
                    gsrc_, gkey_, ksl_, kkey_ = srcs(pb)
                    if name == 'rt':
                        qsrc = qr[:, pb, 0:512]; qkey = K('qr')
                    else:
                        qsrc = q_t[:, pb, :]; qkey = K('q_t')
                    pP = pPs[sl]; pT = pTs[sl]; kP = 'pP%d' % sl; kT = 'pT%d' % sl
                    P.mm(pP[:, 0:321], gsrc_[:, hs], Mcat[:, 0:321], True, True, [gkey_, 'Mcat'], [kP])
                    P.act(E13[:, sl, :], pP[:, 0:321], AF.Exp, [kP], [('E13', sl)])
                    P.ts(E13[:, sl, 64:192], E13[:, sl, 64:192], 2.0e17, ALU.min, [('E13', sl)], [('E13', sl)])
                    P.tr(pT[:, 0, 0:C], qsrc[:, hs], self.identb[:C, :C], [qkey, 'identb'], [kT])
                    P.tr(pT[:, 1, 0:C], ksl_(h * 128, (h + 1) * 128), self.identb[:C, :C], [kkey_, 'identb'], [kT])
                    P.stt(qtl[:, sl, :], pT[:, 0, 0:C], br['qs'], E13[:, sl, 0:64], ALU.mult, ALU.mult, [kT, ('E13', sl)], [('qtl', sl)])
                    P.stt(qtd[:, sl, :], pT[:, 0, 0:C], br['qs'], E13[:, sl, 64:128], ALU.mult, ALU.mult, [kT, ('E13', sl)], [('qtd', sl)])
                    P.stt(qtp[:, sl, :], pT[:, 0, 0:C], br['qs'], E13[:, sl, 256:320], ALU.mult, ALU.mult, [kT, ('E13', sl)], [('qtp', sl)])
                    kbase = kdz[:, sl, :]
                    kd_out = bass.AP(kbase.tensor, kbase.offset, [list(kbase.ap[0]), [96, 2], [1, 32]])
                    P.tt(kd_out, pT[:, 1, 0:C].rearrange("p (a b) -> p a b", b=32), E13[:, sl, 128:192].rearrange("p (a b) -> p a b", b=32),
                         ALU.mult, [kT, ('E13', sl)], ['kdz%d' % sl])
                    lo, hi = (0, 32) if d == 0 else (32, 64)
                    P.tt(koz[:, sl, 0, lo:hi], pT[:, 1, lo:hi], E13[:, sl, 192 + lo:192 + hi], ALU.mult,
                         [kT, ('E13', sl)], ['koz%d' % sl])

                def emit_back(si, h):
                    pb = si % 2; sl = h % 2; hs = slice(h * 128, (h + 1) * 128)
                    K = lambda nm: (nm, pb)
                    offI = 1 if d == 0 else 0
                    for I in (range(2) if need_o(order[si]) else ()):
                        has_off = (I == offI)
                        P.mm(pS[:C, 32 * I:32 * I + 32], kdz[:, sl, 64 * I:64 * I + 64], qtd[:, sl, 32 * I:32 * I + 32], True, not has_off,
                             ['kdz%d' % sl, ('qtd', sl)], ['pS'])
                        if has_off:
                            P.mm(pS[:C, 32 * I:32 * I + 32], koz[:, sl, 0, :], qtl[:, sl, 32 * I:32 * I + 32], False, True,
                                 ['koz%d' % sl, ('qtl', sl)], ['pS'])
                    if need_o(order[si]):
                        P.tt(scb[:, sl, :], pS[:C, 0:C], Mcat[:, 385:449], ALU.mult, ['pS', 'Mcat'], [('scb', sl)])
                    for ec in (range(dvc) if need_o(order[si]) else ()):
                        col = h * dv + ec * 128; ci = col // 128; so = ci % 4
                        P.mm(pO[:, so, :], v_t[:, pb, col:col + 128], scb[:, sl, :], True, False, [K('v_t'), ('scb', sl)], ['pO'])
                        P.mm(pO[:, so, :], Sb[:, col:col + 128], qtp[:, sl, :], False, True, [('Sb', h), ('qtp', sl)], ['pO'])
                        if d == 0:
                            P.cp(ofs[:, pb, ci, :], pO[:, so, :], ['pO'], [('ofs', pb)], eng='act')
                        else:
                            P.tt(osum[:, pb, ci, :], pO[:, so, :], of_t[:, pb, ci, :], ALU.add, ['pO', K('of_t')], [K('osum')])
                    P.mm(pU[:, 0, 0:dv], khat[:, pb, hs], v_t[:, pb, h * dv:(h + 1) * dv], True, True,
                         [('khat', pb, (h * 128) // 512), K('v_t')], ['pU'])
                    P.stt(S[:, h * dv:(h + 1) * dv], S[:, h * dv:(h + 1) * dv], E13[:, sl, 320:321], pU[:, 0, 0:dv],
                          ALU.mult, ALU.add, [('S', h), ('E13', sl), 'pU'], [('S', h)])
                    P.cp(Sb[:, h * dv:(h + 1) * dv], S[:, h * dv:(h + 1) * dv], [('S', h)], [('Sb', h)], eng='act')

                def emit_out(si):
                    n = order[si]; t0 = n * C; pb = si % 2
                    K = lambda nm: (nm, pb)
                    if not need_o(n):
                        return
                    if d == 0:
                        P.dma('act', ofv[:, :, t0:t0 + C], ofs[:, pb], reads=[('ofs', pb)], writes=[('ofT', n)])
                        return
                    osm = osum[:, pb]
                    P.act(sg[:], gate_t[:, pb], AF.Silu, [K('gate_t')], ['sg'])
                    if br['norm'] == 'ln':
                        for h in range(H):
                            for ec in range(dvc):
                                P.mm(pc[:, h * C:(h + 1) * C], self.onesf[:], osm[:, h * dvc + ec, :], ec == 0, ec == dvc - 1, ['onesf', K('osum')], ['pc'])
                        P.ts(mean[:, 0:H, :], pc[:, 0:H * C].rearrange("p (h c) -> p h c", c=C), 1.0 / dv, ALU.mult, ['pc'], ['mean'])
                        for ci in range(8):
                            P.tt(osm[:, ci, :], osm[:, ci, :], mean[:, ci // dvc, :], ALU.subtract, [K('osum'), 'mean'], [K('osum')])
                    P.act(sqt[:], osm, AF.Square, [K('osum')], ['sqt'])
                    for h in range(H):
                        for ec in range(dvc):
                            P.mm(pc[:, h * C:(h + 1) * C], self.onesf[:], sqt[:, h * dvc + ec, :], ec == 0, ec == dvc - 1, ['onesf', 'sqt'], ['pc'])
                    P.act(rs[:, 0:H, :], pc[:, 0:H * C].rearrange("p (h c) -> p h c", c=C), AF.Sqrt, ['pc'], ['rs'], bias=EPS, scale=1.0 / dv)
                    P.op('dve', lambda e: e.reciprocal(out=rs[:, 0:H, :], in_=rs[:, 0:H, :]), ['rs'], ['rs'])
                    for ci in range(8):
                        P.stt(otmp[:, ci, :], osm[:, ci, :], gn[:, (ci % dvc):(ci % dvc) + 1], rs[:, ci // dvc, :], ALU.mult, ALU.mult,
                              [K('osum'), 'gn', 'rs'], ['otmp'])
                    P.tt(ob[:, pb], otmp[:], sg[:], ALU.mult, ['otmp', 'sg'], [('ob', pb)])
                    P.dma('act', ov[:, :, t0:t0 + C], ob[:, pb], reads=[('ob', pb)], writes=[('oT', n)])

                nch = len(order)
                PIPE = getattr(c, 'pipe', True)
                if PIPE:
                    emit_prep(0)
                for si in range(nch):
                    if not PIPE:
                        emit_prep(si)
                        for h in range(H):
                            emit_front(si, h)
                            emit_back(si, h)
                        emit_out(si)
                        continue
                    emit_front(si, 0)
                    for h in range(H):
                        if h + 1 < H:
                            emit_front(si, h + 1)
                        emit_back(si, h)
                        if h == H // 2 - 1 and si + 1 < nch:
                            emit_prep(si + 1)
                    emit_out(si)
                P.barrier()

    def phase_gla2(self, layer, name, last=False):
        c = self.cfg; P = self.P; T = self.T; L = c.L; CTX = c.CTX
        nlo = c.need_lo(last); nhi = c.need_hi(last)
        from contextlib import ExitStack
        br = self.BR[name]; H = br['H']; dv = br['dv']; dvc = dv // 128; HW = H * 128; C = 128
        assert H == 4 and dv == 256
        NCH = L // C; ctxch = CTX // C
        orders = [[n for n in range(NCH) if n * C < nhi], list(range(ctxch - 1, -1, -1)) + list(range(NCH - 1, ctxch - 1, -1))]
        need_o = lambda n: (nlo <= n * C < nhi)
        ofv = T['ofT'][br['ofrow']:br['ofrow'] + 1024, :].rearrange("(c p) t -> p c t", p=128)
        ov = T['oT'][br['orow']:br['orow'] + 1024, :].rearrange("(c p) t -> p c t", p=128)
        gv = T['fmb'][br['gate']:br['gate'] + 1024, :].rearrange("(c p) t -> p c t", p=128)
        sb = self.sb; ps = self.ps
        with ExitStack() as es:
            def mk(nm, shape, dt=F32):
                return es.enter_context(sb(nm, shape, dt))
            M = mk("M2", [128, 513]); q_t = mk("q_t2", [128, 2, 512], BF16); v_t = mk("v_t2", [128, 2, 1024], BF16)
            k_t = mk("k_t2", [128, 2, 512], BF16); lr_t = mk("lr_t2", [17, 2, C]); rp_t = mk("rp_t2", [128, 2, 512])
            of_t = mk("of_t2", [128, 2, 8, C]); gate_t = mk("gate_t2", [128, 2, 8, C], BF16)
            sig = mk("sig2", [128, 512]); graw = mk("graw2", [128, 2, 512]); kt = mk("kt2", [128, 2, 512], BF16)
            qr = mk("qr2", [128, 2, 512], BF16); khat = mk("khat2", [128, 2, 512], BF16); Ek = mk("Ek2", [128, 512])
            rt1 = mk("rt1b", [128, 256]); rt2 = mk("rt2b", [128, 256]); rt3 = mk("rt3b", [128, 256]); rt4 = mk("rt4b", [128, 256])
            E = mk("E_2", [128, 2, 257]); E2 = mk("E2_2", [128, 2, 128])
            qtl = mk("qtl2", [128, 2, C], BF16); qtp = mk("qtp2", [128, 2, C], BF16); ktl = mk("ktl2", [128, 2, C], BF16)
            scb = mk("scb2", [128, 2, C], BF16); ofs = mk("ofs2", [128, 2, 8, C]); osum = mk("osum2", [128, 2, 8, C])
            sg = mk("sg2_", [128, 8, C]); sqt = mk("sqt2", [128, 8, C]); rs = mk("rs2", [128, 8, C]); mean = mk("mean2", [128, 8, C])
            otmp = mk("otmp2", [128, 8, C]); ob = mk("ob2", [128, 2, 8, C], BF16)
            S = mk("S2", [128, 1024]); Sb = mk("Sb2", [128, 1024], BF16)
            wg = mk("wg2", [17, 512]); gn = mk("gn2", [128, 2]); rl = mk("rl2", [128, 4]); gconst = mk("gconst2", [128, 512])
            pc = es.enter_context(ps("pc2", [128, 512], F32))
            pPs = [es.enter_context(ps("pPa", [128, 512], F32)), es.enter_context(ps("pPb", [128, 512], F32))]
            pTs = [es.enter_context(ps("pTa", [128, 8, 128], BF16)), es.enter_context(ps("pTb", [128, 8, 128], BF16))]
            pS = es.enter_context(ps("pS2", [128, 512], F32)); pO = es.enter_context(ps("pO2", [128, 4, C], F32))
            pU = es.enter_context(ps("pU2", [128, 512], F32))
            P.dma('sp', gn[:], T['gla_ngT' if name == 'gl' else 'ret_ngT'][layer], writes=['gn'])
            P.memset(lr_t[:], 1.0, ['lr0', 'lr1'])
            for d in (0, 1):
                gs = br['gs']
                P.dma('sp', M[:, :], T['gl_consts2'][d], writes=['M'])
                P.ts(M[:, 0:385], M[:, 0:385], gs, ALU.mult, ['M'], ['M'])
                if name == 'gl':
                    P.dma('sp', wg[:], T['gla_wg'][layer, d], writes=['wg'])
                else:
                    P.dma('sp', rl[:], T['ret_lrep'][layer, d], writes=['rl'])
                    P.act(rl[:], rl[:], AF.Exp, ['rl'], ['rl'], scale=-1.0)
                    P.act(rl[:], rl[:], AF.Ln, ['rl'], ['rl'], bias=1.0)
                    P.ts(rl[:], rl[:], -1.0, ALU.mult, ['rl'], ['rl'])
                    for h in range(4):
                        P.ts(gconst[:, h * 128:(h + 1) * 128], self.onesf[:, :], rl[:, h:h + 1], ALU.mult, ['onesf', 'rl'], ['gconst'])
                P.memset(S[:], 0.0, [('S', h) for h in range(H)])
                P.memset(Sb[:], 0.0, [('Sb', h) for h in range(H)])
                order = orders[d]

                def srcs(pb):
                    K = lambda nm: (nm, pb)
                    if name == 'rt':
                        return gconst[:, :], 'gconst', (lambda a, b: kt[:, pb, a:b]), K('kt'), qr[:, pb, :], K('qr')
                    return graw[:, pb, :], K('graw'), (lambda a, b: k_t[:, pb, a:b]), K('k_t'), q_t[:, pb, :], K('q_t')

                def emit_prep(si):
                    n = order[si]; t0 = n * C; pb = si % 2
                    K = lambda nm: (nm, pb)
                    P.dma('sp', q_t[:, pb, :], T['tmb'][t0:t0 + C, br['q']:br['q'] + HW], writes=[K('q_t')])
                    P.dma('sp', v_t[:, pb, :], T['tmb'][t0:t0 + C, br['v']:br['v'] + 1024], writes=[K('v_t')])
                    P.dma('sp', k_t[:, pb, :], T['tmb'][t0:t0 + C, br['k']:br['k'] + HW], writes=[K('k_t')])
                    if name == 'gl':
                        P.dma('sp', lr_t[0:16, pb, :], T['lrT'][d * 16:(d + 1) * 16, t0:t0 + C], writes=[('lr%d' % pb)])
                    else:
                        P.dma('sp', rp_t[:, pb, :], T['rope'][t0:t0 + C, :], writes=[K('rp_t')])
                    if d == 1 and need_o(n):
                        P.dma('sp', of_t[:, pb], ofv[:, :, t0:t0 + C], writes=[K('of_t')])
                        P.dma('sp', gate_t[:, pb], gv[:, :, t0:t0 + C], writes=[K('gate_t')])
                    if name == 'gl':
                        P.mm(pc[:, :], lr_t[0:17, pb, :], wg[0:17, :], True, True, ['lr%d' % pb, 'wg'], ['pc'])
                        P.act(sig[:], pc[:, :], AF.Exp, ['pc'], ['sig'], scale=-1.0)
                        P.act(graw[:, pb, :], sig[:], AF.Ln, ['sig'], [K('graw')], bias=1.0)
                    else:
                        cos4 = rp_t[:, pb, 0:256].rearrange("p (h x) -> p h x", x=64)
                        sin4 = rp_t[:, pb, 256:512].rearrange("p (h x) -> p h x", x=64)
                        for (src, skey, dst, dkey, eng, ta, tb) in ((q_t, K('q_t'), qr, K('qr'), 'dve', rt1, rt2), (k_t, K('k_t'), kt, K('kt'), 'pool', rt3, rt4)):
                            xv = src[:, pb, :].rearrange("p (h two x) -> p h two x", two=2, x=64)
                            ov_ = dst[:, pb, :].rearrange("p (h two x) -> p h two x", two=2, x=64)
                            tav = ta[:].rearrange("p (h x) -> p h x", x=64); tbv = tb[:].rearrange("p (h x) -> p h x", x=64)
                            P.tt(tav, xv[:, :, 0, :], cos4, ALU.mult, [skey, K('rp_t')], [ta.name], eng=eng)
                            P.tt(tbv, xv[:, :, 1, :], sin4, ALU.mult, [skey, K('rp_t')], [tb.name], eng=eng)
                            P.tt(ov_[:, :, 0, :], tav, tbv, ALU.subtract, [ta.name, tb.name], [dkey], eng=eng)
                            P.tt(tav, xv[:, :, 0, :], sin4, ALU.mult, [skey, K('rp_t')], [ta.name], eng=eng)
                            P.tt(tbv, xv[:, :, 1, :], cos4, ALU.mult, [skey, K('rp_t')], [tb.name], eng=eng)
                            P.tt(ov_[:, :, 1, :], tav, tbv, ALU.add, [ta.name, tb.name], [dkey], eng=eng)
                    gsrc_, gkey_, ksl_, kkey_, qsrc_, qkey_ = srcs(pb)
                    P.mm(pc[:, :], M[:, 257:385], gsrc_, True, True, ['M', gkey_], ['pc'])
                    P.act(Ek[:], pc[:, :], AF.Exp, ['pc'], ['Ek'])
                    P.tt(khat[:, pb, :], ksl_(0, 512), Ek[:], ALU.mult, [kkey_, 'Ek'], [K('khat')])

                def emit_front(si, h):
                    pb = si % 2; sl = h % 2; hs = slice(h * 128, (h + 1) * 128)
                    gsrc_, gkey_, ksl_, kkey_, qsrc_, qkey_ = srcs(pb)
                    pP = pPs[sl]; pT = pTs[sl]; kP = 'pP%d' % sl; kT = 'pT%d' % sl
                    P.mm(pP[:, 0:257], gsrc_[:, hs], M[:, 0:257], True, True, [gkey_, 'M'], [kP])
                    P.act(E[:, sl, :], pP[:, 0:257], AF.Exp, [kP], [('E', sl)])
                    P.act(E2[:, sl, :], pP[:, 0:128], AF.Exp, [kP], [('E2', sl)], scale=-1.0)
                    P.tr(pT[:, 0, 0:C], qsrc_[:, hs], self.identb[:, :], [qkey_, 'identb'], [kT])
                    P.tr(pT[:, 1, 0:C], ksl_(h * 128, (h + 1) * 128), self.identb[:, :], [kkey_, 'identb'], [kT])
                    P.stt(qtl[:, sl, :], pT[:, 0, 0:C], br['qs'], E[:, sl, 0:128], ALU.mult, ALU.mult, [kT, ('E', sl)], [('qtl', sl)])
                    P.stt(qtp[:, sl, :], pT[:, 0, 0:C], br['qs'], E[:, sl, 128:256], ALU.mult, ALU.mult, [kT, ('E', sl)], [('qtp', sl)])
                    P.tt(ktl[:, sl, :], pT[:, 1, 0:C], E2[:, sl, :], ALU.mult, [kT, ('E2', sl)], [('ktl', sl)])

                def emit_back(si, h):
                    pb = si % 2; sl = h % 2; hs = slice(h * 128, (h + 1) * 128)
                    K = lambda nm: (nm, pb)
                    if need_o(order[si]):
                        P.mm(pS[:, 0:C], ktl[:, sl, :], qtl[:, sl, :], True, True, [('ktl', sl), ('qtl', sl)], ['pS'])
                        P.tt(scb[:, sl, :], pS[:, 0:C], M[:, 385:513], ALU.mult, ['pS', 'M'], [('scb', sl)])
                    for ec in (range(dvc) if need_o(order[si]) else ()):
                        col = h * dv + ec * 128; ci = col // 128; so = ci % 4
                        P.mm(pO[:, so, :], v_t[:, pb, col:col + 128], scb[:, sl, :], True, False, [K('v_t'), ('scb', sl)], ['pO'])
                        P.mm(pO[:, so, :], Sb[:, col:col + 128], qtp[:, sl, :], False, True, [('Sb', h), ('qtp', sl)], ['pO'])
                        if d == 0:
                            P.cp(ofs[:, pb, ci, :], pO[:, so, :], ['pO'], [('ofs', pb)], eng='act')
                        else:
                            P.tt(osum[:, pb, ci, :], pO[:, so, :], of_t[:, pb, ci, :], ALU.add, ['pO', K('of_t')], [K('osum')])
                    P.mm(pU[:, 0:dv], khat[:, pb, hs], v_t[:, pb, h * dv:(h + 1) * dv], True, True, [K('khat'), K('v_t')], ['pU'])
                    P.stt(S[:, h * dv:(h + 1) * dv], S[:, h * dv:(h + 1) * dv], E[:, sl, 256:257], pU[:, 0:dv],
                          ALU.mult, ALU.add, [('S', h), ('E', sl), 'pU'], [('S', h)])
                    P.cp(Sb[:, h * dv:(h + 1) * dv], S[:, h * dv:(h + 1) * dv], [('S', h)], [('Sb', h)], eng='act')

                def emit_out(si):
                    n = order[si]; t0 = n * C; pb = si % 2
                    K = lambda nm: (nm, pb)
                    if not need_o(n):
                        return
                    if d == 0:
                        P.dma('act', ofv[:, :, t0:t0 + C], ofs[:, pb], reads=[('ofs', pb)], writes=[('ofT', n)])
                        return
                    osm = osum[:, pb]
                    P.act(sg[:], gate_t[:, pb], AF.Silu, [K('gate_t')], ['sg'])
                    if br['norm'] == 'ln':
                        for h in range(H):
                            for ec in range(dvc):
                                P.mm(pc[:, h * C:(h + 1) * C], self.onesf[:], osm[:, h * dvc + ec, :], ec == 0, ec == dvc - 1, ['onesf', K('osum')], ['pc'])
                        P.ts(mean[:, 0:H, :], pc[:, 0:H * C].rearrange("p (h c) -> p h c", c=C), 1.0 / dv, ALU.mult, ['pc'], ['mean'])
                        for ci in range(8):
                            P.tt(osm[:, ci, :], osm[:, ci, :], mean[:, ci // dvc, :], ALU.subtract, [K('osum'), 'mean'], [K('osum')])
                    P.act(sqt[:], osm, AF.Square, [K('osum')], ['sqt'])
                    for h in range(H):
                        for ec in range(dvc):
                            P.mm(pc[:, h * C:(h + 1) * C], self.onesf[:], sqt[:, h * dvc + ec, :], ec == 0, ec == dvc - 1, ['onesf', 'sqt'], ['pc'])
                    P.act(rs[:, 0:H, :], pc[:, 0:H * C].rearrange("p (h c) -> p h c", c=C), AF.Sqrt, ['pc'], ['rs'], bias=EPS, scale=1.0 / dv)
                    P.op('dve', lambda e: e.reciprocal(out=rs[:, 0:H, :], in_=rs[:, 0:H, :]), ['rs'], ['rs'])
                    for ci in range(8):
                        P.stt(otmp[:, ci, :], osm[:, ci, :], gn[:, (ci % dvc):(ci % dvc) + 1], rs[:, ci // dvc, :], ALU.mult, ALU.mult,
                              [K('osum'), 'gn', 'rs'], ['otmp'])
                    P.tt(ob[:, pb], otmp[:], sg[:], ALU.mult, ['otmp', 'sg'], [('ob', pb)])
                    P.dma('act', ov[:, :, t0:t0 + C], ob[:, pb], reads=[('ob', pb)], writes=[('oT', n)])

                nch = len(order)
                emit_prep(0)
                for si in range(nch):
                    emit_front(si, 0)
                    for h in range(H):
                        if h + 1 < H:
                            emit_front(si, h + 1)
                        emit_back(si, h)
                        if h == H // 2 - 1 and si + 1 < nch:
                            emit_prep(si + 1)
                    emit_out(si)
                P.barrier()

    def s5_powers(self, src_re, src_im, src_dt, Pn, Fn, pw_re, pw_im, coef_re, coef_im, tag):
        P = self.P
        from contextlib import ExitStack
        with ExitStack() as es:
            def mk(nm, shape, dt=F32):
                return es.enter_context(self.sb(tag + nm, shape, dt))
            lre = mk("lre", [Pn, Fn]); lim = mk("lim", [Pn, Fn]); dtv = mk("dtv", [Pn, Fn])
            a = mk("a", [Pn, Fn]); th = mk("th", [Pn, Fn]); kki = mk("kki", [Pn, 17, Fn], I32); kk = mk("kk", [Pn, 17, Fn])
            A = mk("A", [Pn, 17, Fn]); TH = mk("TH", [Pn, 17 * Fn]); SN = mk("SN", [Pn, 17 * Fn]); CS = mk("CS", [Pn, 17 * Fn])
            t1 = mk("t1", [Pn, Fn]); t2 = mk("t2", [Pn, Fn]); den = mk("den", [Pn, Fn])
            P.dma('sp', lre[:], src_re, writes=['lre']); P.dma('sp', lim[:], src_im, writes=['lim']); P.dma('sp', dtv[:], src_dt, writes=['dtv'])
            P.act(dtv[:], dtv[:], AF.Exp, ['dtv'], ['dtv'])
            P.ts(lre[:], lre[:], -1e-4, ALU.min, ['lre'], ['lre'])
            P.tt(a[:], lre[:], dtv[:], ALU.mult, ['lre', 'dtv'], ['a'])
            P.tt(th[:], lim[:], dtv[:], ALU.mult, ['lim', 'dtv'], ['th'])
            P.op('pool', lambda e: e.iota(kki[:], pattern=[[1, 17], [0, Fn]], base=0, channel_multiplier=0), [], ['kki'])
            P.cp(kk[:], kki[:], ['kki'], ['kk'])
            P.tt(A[:], kk[:], a[:].unsqueeze(1).to_broadcast([Pn, 17, Fn]), ALU.mult, ['kk', 'a'], ['A'])
            P.tt(TH[:].rearrange("p (k f) -> p k f", f=Fn), kk[:], th[:].unsqueeze(1).to_broadcast([Pn, 17, Fn]), ALU.mult, ['kk', 'th'], [tag + 'scx'])
            P.act(A[:], A[:], AF.Exp, ['A'], ['A'])
            self.sincos(TH[:], [Pn, 17 * Fn], SN[:], CS[:], tag + 'sc')
            P.tt(pw_re, A[:], CS[:].rearrange("p (k f) -> p k f", f=Fn), ALU.mult, ['A'], ['pwre'])
            P.tt(pw_im, A[:], SN[:].rearrange("p (k f) -> p k f", f=Fn), ALU.mult, ['A'], ['pwim'])
            if coef_re is not None:
                P.ts(t1[:], pw_re[:, 1, :], -1.0, ALU.add, ['pwre'], ['t1'])
                P.tt(den[:], lre[:], lre[:], ALU.mult, ['lre'], ['den'])
                P.tt(t2[:], lim[:], lim[:], ALU.mult, ['lim'], ['t2'])
                P.tt(den[:], den[:], t2[:], ALU.add, ['den', 't2'], ['den'])
                P.op('dve', lambda e: e.reciprocal(out=den[:], in_=den[:]), ['den'], ['den'])
                P.tt(coef_re, t1[:], lre[:], ALU.mult, ['t1', 'lre'], ['cre'])
                P.tt(t2[:], pw_im[:, 1, :], lim[:], ALU.mult, ['pwim', 'lim'], ['t2'])
                P.tt(coef_re, coef_re, t2[:], ALU.add, ['cre', 't2'], ['cre'])
                P.tt(coef_re, coef_re, den[:], ALU.mult, ['cre', 'den'], ['cre'])
                P.tt(coef_im, pw_im[:, 1, :], lre[:], ALU.mult, ['pwim', 'lre'], ['cim'])
                P.tt(t2[:], t1[:], lim[:], ALU.mult, ['t1', 'lim'], ['t2'])
                P.tt(coef_im, coef_im, t2[:], ALU.subtract, ['cim', 't2'], ['cim'])
                P.tt(coef_im, coef_im, den[:], ALU.mult, ['cim', 'den'], ['cim'])
            P.barrier()

    def cmul(self, out_re, out_im, a_re, a_im, b_re, b_im, t, neg_im=False, eng='dve'):
        P = self.P
        P.tt(out_re, a_re, b_re, ALU.mult, ['cm_in'], ['cm_re'], eng=eng)
        P.tt(t, a_im, b_im, ALU.mult, ['cm_in'], ['cm_t'], eng=eng)
        P.tt(out_re, out_re, t, ALU.subtract, ['cm_re', 'cm_t'], ['cm_re'], eng=eng)
        P.tt(out_im, a_re, b_im, ALU.mult, ['cm_in'], ['cm_im'], eng=eng)
        P.tt(t, a_im, b_re, ALU.mult, ['cm_in', 'cm_re'], ['cm_t'], eng=eng)
        if neg_im:
            P.stt(out_im, out_im, -1.0, t, ALU.mult, ALU.subtract, ['cm_im', 'cm_t'], ['cm_im'])
        else:
            P.tt(out_im, out_im, t, ALU.add, ['cm_im', 'cm_t'], ['cm_im'], eng=eng)

    def phase_s5(self, layer, last=False):
        c = self.cfg; P = self.P; T = self.T; L = c.L; CTX = c.CTX
        nlo = c.need_lo(last); nhi = c.need_hi(last)
        otiles = tiles(nlo, nhi, 512) if last else tiles(0, L, 512)
        from contextlib import ExitStack
        SB = 16; NB = L // SB; NBc = CTX // SB; NBP = NB + 2
        sb = self.sb; ps = self.ps
        urows = T['fmb'][2048:3072, :]
        for d in (0, 1):
            with ExitStack() as esd:
                KtL = esd.enter_context(sb("KtL", [128, 8, 16, 128], BF16))
                PWPr = esd.enter_context(sb("PWPr", [128, 17, 32], F32)); PWPi = esd.enter_context(sb("PWPi", [128, 17, 32], F32))
                self.s5_powers(T['s5_lamP'][layer, d, 0], T['s5_lamP'][layer, d, 1], T['s5_dtP'][layer, d], 128, 32,
                               PWPr[:], PWPi[:], None, None, 'pp')
                with ExitStack() as es:
                    def mk(nm, shape, dt=F32):
                        return es.enter_context(sb(nm, shape, dt))
                    pwr = mk("pwr", [64, 17, 64]); pwi = mk("pwi", [64, 17, 64]); cre = mk("cre", [64, 64]); cim = mk("cim", [64, 64])
                    self.s5_powers(T['s5_lam64'][layer, d, 0], T['s5_lam64'][layer, d, 1], T['s5_dt64'][layer, d], 64, 64,
                                   pwr[:], pwi[:], cre[:], cim[:], 'p64')
                    B64 = mk("B64", [64, 2, 64, 16]); C64 = mk("C64", [64, 2, 64, 16])
                    bbr = mk("bbr", [64, 64, 16]); bbi = mk("bbi", [64, 64, 16]); tt_ = mk("tt_", [64, 64, 16])
                    Wr = mk("Wr", [64, 64, 16]); Wi = mk("Wi", [64, 64, 16])
                    Wpr = mk("Wpr", [64, 64 * 128], BF16); Wpi = mk("Wpi", [64, 64 * 128], BF16)
                    Cpr = mk("Cpr", [64, 64 * 128], BF16); Cpi = mk("Cpi", [64, 64 * 128], BF16)
                    pk0 = es.enter_context(ps("pk0", [128, 512], F32)); pk1 = es.enter_context(ps("pk1", [128, 512], F32))
                    pks = [pk0, pk1]
                    P.dma('sp', B64[:], T['s5_b64'][layer, d].rearrange("x p g h -> p x g h"), writes=['B64'])
                    P.dma('sp', C64[:], T['s5_c64'][layer, d].rearrange("x p g h -> p x g h"), writes=['C64'])
                    for t_ in (Wpr, Wpi, Cpr, Cpi):
                        P.memset(t_[:], 0.0, ['pad' + t_.name])
                    P.barrier()
                    bc = lambda ap: ap.unsqueeze(2).to_broadcast([64, 64, 16])
                    self.cmul(bbr[:], bbi[:], bc(cre[:]), bc(cim[:]), B64[:, 0], B64[:, 1], tt_[:])
                    P.barrier()

                    def diag(t_):
                        b_ = t_[:]
                        return bass.AP(b_.tensor, b_.offset, [list(b_.ap[0]), [1024, 8], [144, 8], [1, 16]])
                    P.cp(diag(Cpr), C64[:, 0].rearrange("p (a b) h -> p a b h", b=8), [], ['cpr'])
                    P.cp(diag(Cpi), C64[:, 1].rearrange("p (a b) h -> p a b h", b=8), [], ['cpi'])
                    P.barrier()
                    for tau in range(16):
                        self.cmul(Wr[:], Wi[:], bc(pwr[:, tau, :]), bc(pwi[:, tau, :]), bbr[:], bbi[:], tt_[:], neg_im=True)
                        P.cp(diag(Wpr), Wr[:].rearrange("p (a b) h -> p a b h", b=8), ['cm_re'], ['Wpr'])
                        P.cp(diag(Wpi), Wi[:].rearrange("p (a b) h -> p a b h", b=8), ['cm_im'], ['Wpi'])
                        for gb in range(8):
                            pk = pks[gb % 2]; pkey = 'pk%d' % (gb % 2)
                            for g8 in range(8):
                                g = gb * 8 + g8
                                P.mm(pk[:, 0:128], Wpr[:, g * 128:(g + 1) * 128], Cpr[:, g * 128:(g + 1) * 128], g8 == 0, False, ['Wpr'], [pkey])
                                P.mm(pk[:, 0:128], Wpi[:, g * 128:(g + 1) * 128], Cpi[:, g * 128:(g + 1) * 128], False, g8 == 7, ['Wpi'], [pkey])
                            P.cp(KtL[:, gb, tau, :], pk[:, 0:128], [pkey], ['KtL'], eng=('act' if gb % 2 else 'dve'))
                    P.barrier()
                with ExitStack() as es:
                    pfr = es.enter_context(sb("pfrA", [128, 17, 64], F32)); pfi = es.enter_context(sb("pfiA", [128, 17, 64], F32))
                    cfr = es.enter_context(sb("cfrA", [128, 64], F32)); cfi = es.enter_context(sb("cfiA", [128, 64], F32))
                    for gb in range(8):
                        self.s5_powers(T['s5_lamF'][layer, d, 0][:, gb, :], T['s5_lamF'][layer, d, 1][:, gb, :], T['s5_dtF'][layer, d][:, gb, :],
                                       128, 64, pfr[:], pfi[:], cfr[:], cfi[:], 'pf')
                        P.dma('sp', T['s5pf'][gb, 0], pfr[:], reads=[], writes=['d1'])
                        P.dma('sp', T['s5pf'][gb, 1], pfi[:], reads=[], writes=['d2'])
                        P.dma('sp', T['s5cf'][gb, 0], cfr[:], reads=[], writes=['d3'])
                        P.dma('sp', T['s5cf'][gb, 1], cfi[:], reads=[], writes=['d4'])
                        P.barrier()
                with ExitStack() as es:
                    def mk(nm, shape, dt=F32):
                        return es.enter_context(sb(nm, shape, dt))
                    Ar = mk("Ar", [128, 32, NBP]); Ai = mk("Ai", [128, 32, NBP])
                    Pad = mk("Pad", [128, 4, 16, 2, 128], BF16)
                    Abr = mk("Abr", [128, 4, NBP], BF16); Abi = mk("Abi", [128, 4, NBP], BF16)
                    uT0 = mk("uT0", [128, L], BF16); uTs = [uT0, uT0]
                    maskQ = mk("maskQ", [128, 4, 128]); maskC = mk("maskC", [128, 2, 16])
                    pfr = mk("pfr", [128, 17, 64]); pfi = mk("pfi", [128, 17, 64]); cfr = mk("cfr", [128, 64]); cfi = mk("cfi", [128, 64])
                    BF_ = mk("BF_", [128, 2, 64]); bfr = mk("bfr", [128, 64]); bfi = mk("bfi", [128, 64]); tf = mk("tf", [128, 64])
                    Vr = mk("Vr", [128, 64]); Vi = mk("Vi", [128, 64])
                    cP = mk("cP", [128, 2, 32, 16]); CLr = mk("CLr", [128, 4, 16]); CLi = mk("CLi", [128, 4, 16]); tc_ = mk("tc_", [128, 4, 16])
                    s1 = mk("s1", [128, 32]); s2 = mk("s2", [128, 32]); s3 = mk("s3", [128, 32]); s4 = mk("s4", [128, 32])
                    ysb0 = mk("ysb0", [128, 512]); ysb1 = mk("ysb1", [128, 512]); yf_t = mk("yf_t", [128, 512]); y2 = mk("y2", [128, 512])
                    zb = mk("zb", [128, 512], BF16); dT = mk("dT", [128, 8])
                    pw0 = es.enter_context(ps("pw0", [128, 512], F32)); pw1 = es.enter_context(ps("pw1", [128, 512], F32))
                    py0 = es.enter_context(ps("py0", [128, 512], F32)); py1 = es.enter_context(ps("py1", [128, 512], F32))
                    P.dma('sp', maskQ[:], T['s5_maskQ'], writes=['maskQ'])
                    P.dma('sp', maskC[:], T['s5_maskC'], writes=['maskC'])
                    P.dma('sp', cP[:], T['s5_cP'][layer, d].rearrange("x p q h -> p x q h"), writes=['cP'])
                    P.dma('sp', dT[:], T['s5_dT'][layer], writes=['dT'])
                    P.memset(Pad[:], 0.0, ['Pad'])
                    P.memset(Ar[:], 0.0, ['A']); P.memset(Ai[:], 0.0, ['A'])
                    P.barrier()
                    nwp = 0
                    for gb in range(8):
                        uT = uTs[gb % 2]
                        P.dma('sp', uT[:], urows[gb * 128:(gb + 1) * 128, :], writes=[('uT', 0)])
                        P.dma('sp', pfr[:], T['s5pf'][gb, 0], writes=['cm_in'])
                        P.dma('sp', pfi[:], T['s5pf'][gb, 1], writes=['cm_in'])
                        P.dma('sp', cfr[:], T['s5cf'][gb, 0], writes=['cm_in'])
                        P.dma('sp', cfi[:], T['s5cf'][gb, 1], writes=['cm_in'])
                        P.dma('sp', BF_[:], T['s5_bF'][layer, d][:, :, gb, :].rearrange("x p f -> p x f"), writes=['cm_in'])
                        P.barrier()
                        self.cmul(bfr[:], bfi[:], cfr[:], cfi[:], BF_[:, 0], BF_[:, 1], tf[:])
                        P.barrier()
                        for j in range(16):
                            pw_ = (15 - j) if d == 0 else j
                            self.cmul(Vr[:], Vi[:], pfr[:, pw_, :], pfi[:, pw_, :], bfr[:], bfi[:], tf[:])
                            for x, V in ((0, Vr), (1, Vi)):
                                P.tt(Pad[:, :, j, x, :].rearrange("p q (a b) -> p q a b", b=64),
                                     V[:].unsqueeze(1).unsqueeze(1).to_broadcast([128, 4, 2, 64]),
                                     maskQ[:].rearrange("p q (a b) -> p q a b", b=64), ALU.mult,
                                     ['cm_re', 'cm_im', 'maskQ'], ['Pad'])
                        uv = uT[:].rearrange("p (n j) -> p n j", j=16)
                        for q in range(4):
                            pair = gb * 4 + q
                            for x, Ax in ((0, Ar), (1, Ai)):
                                pw = (pw0, pw1)[nwp % 2]; pwk = 'pw%d' % (nwp % 2); nwp += 1
                                for j in range(16):
                                    P.mm(pw[:, 0:NB], Pad[:, q, j, x, :], uv[:, :, j], j == 0, j == 15, ['Pad', ('uT', 0)], [pwk])
                                if d == 0:
                                    P.cp(Ax[:, pair, 1:NB + 1], pw[:, 0:NB], [pwk], ['A'], eng=('act' if x else 'dve'))
                                else:
                                    P.cp(Ax[:, pair, 0:NBc], pw[:, 0:NBc], [pwk], ['A'], eng=('act' if x else 'dve'))
                                    P.cp(Ax[:, pair, NBc + 1:NB + 1], pw[:, NBc:NB], [pwk], ['A'], eng=('act' if x else 'dve'))
                        P.barrier()
                    s5stop = getattr(c, 's5stop', 9)
                    if s5stop <= 2:
                        continue
                    ar = PWPr[:, 16, :]; ai = PWPi[:, 16, :]
                    if d == 0:
                        steps = [(n + 1, n) for n in range(NB) if n * SB < nhi]
                    else:
                        steps = [(n, n + 1) for n in range(NBc - 1, -1, -1)] + ['copy'] + [(n + 1, n + 2) for n in range(NB - 1, NBc - 1, -1)]
                    for st in steps:
                        if st == 'copy':
                            P.cp(Ar[:, :, NB + 1], Ar[:, :, 0], ['A'], ['A'])
                            P.cp(Ai[:, :, NB + 1], Ai[:, :, 0], ['A'], ['A'])
                            continue
                        pos, prev = st
                        P.tt(s1[:], ar, Ar[:, :, prev], ALU.mult, ['A'], ['s1'])
                        P.tt(s2[:], ai, Ai[:, :, prev], ALU.mult, ['A'], ['s2'])
                        P.tt(s3[:], ar, Ai[:, :, prev], ALU.mult, ['A'], ['s3'])
                        P.tt(s4[:], ai, Ar[:, :, prev], ALU.mult, ['A'], ['s4'])
                        P.tt(s1[:], s1[:], s2[:], ALU.subtract, ['s1', 's2'], ['s1'])
                        P.tt(s3[:], s3[:], s4[:], ALU.add, ['s3', 's4'], ['s3'])
                        P.tt(Ar[:, :, pos], Ar[:, :, pos], s1[:], ALU.add, ['A', 's1'], ['A'])
                        P.tt(Ai[:, :, pos], Ai[:, :, pos], s3[:], ALU.add, ['A', 's3'], ['A'])
                    P.barrier()
                    if s5stop <= 3:
                        continue
                    P.memset(Pad[:], 0.0, ['Pad'])
                    P.barrier()
                    npy = 0
                    for gb in range(8):
                        uT = uTs[gb % 2]
                        P.dma('sp', uT[:], urows[gb * 128:(gb + 1) * 128, :], writes=[('uT', 0)])
                        for i in range(16):
                            pw_ = (i + 1) if d == 0 else (16 - i)
                            bq = lambda ap: ap[:, gb * 4:gb * 4 + 4].unsqueeze(2).to_broadcast([128, 4, 16])
                            self.cmul(CLr[:], CLi[:], cP[:, 0, gb * 4:gb * 4 + 4, :], cP[:, 1, gb * 4:gb * 4 + 4, :],
                                      bq(PWPr[:, pw_, :]), bq(PWPi[:, pw_, :]), tc_[:], neg_im=True)
                            for x, CL in ((0, CLr), (1, CLi)):
                                pb_ = Pad[:, 0, i, x, :]
                                dst_ = bass.AP(pb_.tensor, pb_.offset, [list(pb_.ap[0]), [16 * 2 * 128 + 32, 4], [16, 2], [1, 16]])
                                P.tt(dst_, CL[:].unsqueeze(2).to_broadcast([128, 4, 2, 16]),
                                     maskC[:].unsqueeze(1).to_broadcast([128, 4, 2, 16]), ALU.mult,
                                     ['cm_re', 'cm_im', 'maskC'], ['Pad'])
                        uv = uT[:].rearrange("p (n j) -> p n j", j=16)
                        P.cp(Abr[:], Ar[:, gb * 4:gb * 4 + 4, :], ['A'], ['Ab'])
                        P.cp(Abi[:], Ai[:, gb * 4:gb * 4 + 4, :], ['A'], ['Ab'], eng='act')
                        for (t0, w) in otiles:
                            py = (py0, py1)[npy % 2]; pyk = 'py%d' % (npy % 2); ysb = (ysb0, ysb1)[npy % 2]; ysk = 'ysb%d' % (npy % 2); npy += 1
                            n0 = t0 // SB; nbt = w // SB
                            pv = py[:, 0:w].rearrange("p (n j) -> p n j", j=16)
                            runs = []
                            if d == 0:
                                runs.append((n0, nbt, n0))
                            else:
                                a0 = n0; a1 = min(n0 + nbt, NBc)
                                if a1 > a0:
                                    runs.append((a0, a1 - a0, a0 + 1))
                                b0 = max(n0, NBc); b1 = n0 + nbt
                                if b1 > b0:
                                    runs.append((b0, b1 - b0, b0 + 2))
                            mms = []
                            for tau in range(16):
                                if d == 0:
                                    mms.append((pv[:, :, tau:16], KtL[:, gb, tau, :], uv[:, n0:n0 + nbt, 0:16 - tau]))
                                else:
                                    mms.append((pv[:, :, 0:16 - tau], KtL[:, gb, tau, :], uv[:, n0:n0 + nbt, tau:16]))
                            for q in range(4):
                                pair = gb * 4 + q
                                for i in range(16):
                                    for x, Ax in ((0, Abr), (1, Abi)):
                                        for (r0, rn, p0) in runs:
                                            mms.append((pv[:, r0 - n0:r0 - n0 + rn, i], Pad[:, q, i, x, :], Ax[:, q, p0:p0 + rn]))
                            for mi, (o_, l_, r_) in enumerate(mms):
                                P.mm(o_, l_, r_, mi == 0, mi == len(mms) - 1, ['KtL', 'Pad', 'Ab', ('uT', 0)], [pyk])
                            yrow = T['yfT'][gb * 128:(gb + 1) * 128, t0:t0 + w]
                            if d == 0:
                                P.cp(ysb[:, :w], py[:, :w], [pyk], [ysk], eng='act')
                                P.dma('act', yrow, ysb[:, :w], reads=[ysk], writes=[('yfT', npy)])
                            else:
                                P.dma('sp', yf_t[:, :w], yrow, writes=['yf_t'])
                                P.tt(ysb[:, :w], py[:, :w], yf_t[:, :w], ALU.add, [pyk, 'yf_t'], [ysk])
                                P.stt(ysb[:, :w], uT[:, t0:t0 + w], dT[:, gb:gb + 1], ysb[:, :w], ALU.mult, ALU.add, [('uT', 0), ysk, 'dT'], [ysk])
                                P.tt(y2[:, :w], ysb[:, :w], ysb[:, :w], ALU.mult, [ysk], ['y2'])
                                P.ts(y2[:, :w], y2[:, :w], 0.044715, ALU.mult, ['y2'], ['y2'], s2=1.0, op1=ALU.add)
                                P.tt(y2[:, :w], y2[:, :w], ysb[:, :w], ALU.mult, ['y2', ysk], ['y2'])
                                P.act(y2[:, :w], y2[:, :w], AF.Sigmoid, ['y2'], ['y2'], scale=1.5957691216057308)
                                P.tt(zb[:, :w], y2[:, :w], ysb[:, :w], ALU.mult, ['y2', ysk], ['zb'])
                                P.dma('act', T['zT'][gb * 128:(gb + 1) * 128, t0:t0 + w], zb[:, :w], reads=['zb'], writes=[('zT', npy)])
                        P.barrier()
                    P.barrier()
        if getattr(c, 's5stop', 9) <= 4:
            return
        with ExitStack() as es:
            def mk(nm, shape, dt=F32):
                return es.enter_context(sb(nm, shape, dt))
            Wg = mk("Wg", [128, 8, 1024], BF16); bg = mk("bg", [128, 8])
            z0 = mk("z0", [128, 8, 512], BF16); z1 = mk("z1", [128, 8, 512], BF16); g0 = mk("g0", [128, 8, 512], BF16); g1 = mk("g1", [128, 8, 512], BF16)
            sgt = mk("sgt", [128, 512]); sg2 = mk("sg2", [128, 512]); ob0 = mk("obx0", [128, 8, 512], BF16); ob1 = mk("obx1", [128, 8, 512], BF16)
            pg0 = es.enter_context(ps("pg0", [128, 512], F32)); pg1 = es.enter_context(ps("pg1", [128, 512], F32))
            P.dma('pool', Wg[:], T['w_glu'][layer].rearrange("(k p) c -> p k c", p=128), writes=['Wg'])
            P.dma('sp', bg[:], T['b_gluT'][layer], writes=['bg'])
            zv = T['zT'].rearrange("(k p) t -> p k t", p=128)
            gv_ = T['fmb'][3072:4096, :].rearrange("(k p) t -> p k t", p=128)
            ovs = T['oT'][2048:3072, :].rearrange("(k p) t -> p k t", p=128)
            npg = 0
            for ti, (t0, w) in enumerate(otiles):
                zt = (z0, z1)[ti % 2]; gt_ = (g0, g1)[ti % 2]; obx = (ob0, ob1)[ti % 2]; kz = ('z', ti % 2); kg = ('g', ti % 2); ko = ('obx', ti % 2)
                P.dma('sp', zt[:, :, :w], zv[:, :, t0:t0 + w], writes=[kz])
                P.dma('sp', gt_[:, :, :w], gv_[:, :, t0:t0 + w], writes=[kg])
                for oc in range(8):
                    pg = (pg0, pg1)[npg % 2]; pgk = 'pg%d' % (npg % 2); npg += 1
                    for k in range(8):
                        P.mm(pg[:, :w], Wg[:, k, oc * 128:(oc + 1) * 128], zt[:, k, :w], k == 0, k == 7, ['Wg', kz], [pgk])
                    P.act(sgt[:, :w], pg[:, :w], AF.Sigmoid, [pgk, 'bg'], ['sgt'], bias=bg[:, oc:oc + 1])
                    P.tt(sgt[:, :w], sgt[:, :w], zt[:, oc, :w], ALU.mult, ['sgt', kz], ['sgt'])
                    P.act(sg2[:, :w], gt_[:, oc, :w], AF.Silu, [kg], ['sg2'])
                    P.tt(obx[:, oc, :w], sgt[:, :w], sg2[:, :w], ALU.mult, ['sgt', 'sg2'], [ko])
                P.dma('act', ovs[:, :, t0:t0 + w], obx[:, :, :w], reads=[ko], writes=[('oTs', ti)])
            P.barrier()

    def sincos(self, x, shape, sin_out, cos_out, tag):
        P = self.P
        with self.sb(tag + "_ni", shape, I32) as ni, self.sb(tag + "_nf", shape, F32) as nf, \
                self.sb(tag + "_r", shape, F32) as r, self.sb(tag + "_m", shape, F32) as m:
            for which, out in ((0, sin_out), (1, cos_out)):
                kx = tag + 'x'
                if which == 1:
                    P.ts(r[:], x, PI / 2, ALU.add, [kx], [tag + 'r0'])
                    src = r[:]
                else:
                    P.cp(r[:], x, [kx], [tag + 'r0'])
                    src = r[:]
                P.ts(ni[:], src, 1.0 / TWO_PI, ALU.mult, [tag + 'r0'], [tag + 'ni'])
                P.cp(nf[:], ni[:], [tag + 'ni'], [tag + 'nf'])
                P.stt(r[:], nf[:], -TWO_PI, src, ALU.mult, ALU.add, [tag + 'nf', tag + 'r0'], [tag + 'r0'])
                P.ts(m[:], r[:], PI, ALU.is_gt, [tag + 'r0'], [tag + 'm'], s2=TWO_PI, op1=ALU.mult)
                P.tt(r[:], r[:], m[:], ALU.subtract, [tag + 'r0', tag + 'm'], [tag + 'r0'])
                P.ts(m[:], r[:], -PI, ALU.is_lt, [tag + 'r0'], [tag + 'm'], s2=TWO_PI, op1=ALU.mult)
                P.tt(r[:], r[:], m[:], ALU.add, [tag + 'r0', tag + 'm'], [tag + 'r0'])
                P.act(out, r[:], AF.Sin, [tag + 'r0'], [tag + 'out%d' % which])
            P.barrier()

    def phase_rope(self):
        c = self.cfg; P = self.P; T = self.T; L = c.L
        with self.sb("fi", [128, 32], I32) as fi, self.sb("fr", [128, 32], F32) as fr, self.sb("pp", [128, 2], F32) as pp, \
                self.sb("ang", [128, 64], F32) as ang, self.sb("sn", [128, 64], F32) as sn, self.sb("cs", [128, 64], F32) as cs, \
                self.sb("rp", [128, 512], F32) as rp:
            P.op('pool', lambda e: e.iota(fi[:], pattern=[[1, 32]], base=0, channel_multiplier=0), [], ['fi'])
            P.cp(fr[:], fi[:], ['fi'], ['fr'])
            P.act(fr[:], fr[:], AF.Exp, ['fr'], ['fr'], scale=-float(np.log(10000.0)) / 32.0)
            P.barrier()
            for (t0, w) in tiles(0, L, 128):
                P.dma('sp', pp[:w, :], T['pos'][t0:t0 + w, :], writes=['pp'])
                P.ts(ang[:w, 0:32], fr[:w, :], pp[:w, 0:1], ALU.mult, ['pp'], ['rpx'])
                P.ts(ang[:w, 32:64], fr[:w, :], pp[:w, 1:2], ALU.mult, ['pp'], ['rpx'])
                self.sincos(ang[:, :], [128, 64], sn[:, :], cs[:, :], 'rp')
                for h in range(4):
                    P.cp(rp[:, h * 64:(h + 1) * 64], cs[:, :], [], ['rp'])
                    P.cp(rp[:, 256 + h * 64:256 + (h + 1) * 64], sn[:, :], [], ['rp'], eng='pool')
                P.dma('sp', T['rope'][t0:t0 + w, :], rp[:w, :], reads=['rp'], writes=['ropeD'])
                P.barrier()

    def build(self):
        c = self.cfg; nc = self.nc; P = self.P; T = None
        self.declare(); T = self.T
        stack = []
        with nc.Block() as block:
            P.start()
            self.consts(stack)
            skip_pre = bool(getattr(c, 'inject', ()))
            if not skip_pre:
                self.phase_rope()
            hs = [T['hT0'], T['hT1'], T['hT2']]
            for layer in range(c.DEPTH):
                last = (layer == c.DEPTH - 1)
                if not getattr(c, 'noada', False):
                    self.phase_ada(layer)
                if c.stop == 'ada':
                    break
                if not skip_pre:
                    self.phase_norm(hs[layer], 'hn', T['hnT'])
                    if c.stop == 'norm':
                        break
                    self.phase_inproj(layer)
                if c.stop == 'inproj':
                    break
                for name in ('hg', 'gl', 'rt'):
                    if c.stop is None or name in c.stop:
                        if name == 'hg' or getattr(c, 'oldgla', False):
                            self.phase_gla(layer, name, last)
                        else:
                            self.phase_gla2(layer, name, last)
                if c.stop is None or 's5' in c.stop:
                    self.phase_s5(layer, last)
                if c.stop is not None and 'out' not in c.stop:
                    break
                self.phase_outproj(layer, hs[layer], hs[layer + 1], last)
            if c.stop is None or 'out' in c.stop:
                self.phase_norm(hs[c.DEPTH], 'final', T['outT'])
            P.barrier()
            for cm in reversed(stack):
                cm.__exit__(None, None, None)
            P.finish()
        return nc


TM_COLS = np.concatenate([np.arange(0, 4096), np.arange(5120, 7168), np.arange(10272, 12320)])
FM_COLS = np.concatenate([np.arange(4096, 5120), np.arange(7200, 8224), np.arange(8224, 9248), np.arange(9248, 10272),
                          np.arange(12320, 13344), np.arange(7168, 7200)])


def fmT(v, nchunk):
    return np.ascontiguousarray(np.asarray(v, np.float32).reshape(nchunk, 128).T)


def gla_consts():
    C = 64; s = 32; f = np.float32
    out = np.zeros((2, 64, 449), f)
    j = np.arange(64)[:, None]; i = np.arange(64)[None, :]
    for d in (0, 1):
        T = (j <= i) if d == 0 else (j >= i)
        blk = i // s
        m = blk * s + s // 2
        if d == 0:
            QO = (j >= blk * s) & (j <= i)
            Tm = (j <= m)
            I = 1
            KO = (i < I * s) & (j > i) & (j <= I * s - 1)
        else:
            QO = (j <= blk * s + s - 1) & (j >= i)
            Tm = (j >= m)
            I = 0
            KO = (i >= (I + 1) * s) & (j >= (I + 1) * s) & (j < i)
        QD = T.astype(f) - Tm.astype(f)
        M = out[d]
        M[:, 0:64] = QO; M[:, 64:128] = QD; M[:, 128:192] = -QD; M[:, 192:256] = KO
        M[:, 256:320] = T; M[:, 320] = 1.0
        M[:, 321:385] = (j > i) if d == 0 else (j < i)
        M[:, 385:449] = T
    return out


def gla_consts2():
    C = 128; r = 64; f = np.float32
    out = np.zeros((2, C, 513), f)
    j = np.arange(C)[:, None]; i = np.arange(C)[None, :]
    for d in (0, 1):
        T = (j <= i) if d == 0 else (j >= i)
        R = (j <= r) if d == 0 else (j >= r)
        M = out[d]
        M[:, 0:128] = T.astype(f) - (R & (i >= 0)).astype(f)
        M[:, 128:256] = T; M[:, 256] = 1.0
        M[:, 257:385] = (j > i) if d == 0 else (j < i)
        M[:, 385:513] = T
    return out


def prep_shared(cfg, inp, flip=False):
    DP = cfg.DEPTH; f = np.float32
    sh = {}
    sh['w_ada'] = np.ascontiguousarray(inp['w_ada'][:DP], f)
    sh['b_adaT'] = np.stack([fmT(inp['b_ada'][l], 96) for l in range(DP)])
    sh['norm_gT'] = np.stack([fmT(inp['norm_g'][l], 32) for l in range(DP)])
    sh['final_gT'] = fmT(inp['final_norm_g'], 32)
    w_in = np.asarray(inp['w_in'][:DP], f)
    sh['w_tm'] = np.ascontiguousarray(w_in[:, :, TM_COLS])
    sh['w_fm'] = np.ascontiguousarray(w_in[:, :, FM_COLS])
    sh['w_out'] = np.ascontiguousarray(inp['w_out'][:DP], f)
    sh['w_glu'] = np.ascontiguousarray(inp['s5_w_glu'][:DP], f)
    sh['b_gluT'] = np.stack([fmT(inp['s5_b_glu'][l], 8) for l in range(DP)])
    lb = np.asarray(inp['hgrn_lb_logits'], f)
    sh['hg_lbrep'] = np.ascontiguousarray(np.broadcast_to(lb[None], (128, 2, 2, 1024)))
    sh['hg_ngT'] = np.asarray(inp['hgrn_norm_g'][:DP], f).reshape(DP, 128, 1).copy()
    wg = np.concatenate([np.asarray(inp['gla_w_gk'][:DP], f), np.asarray(inp['gla_b_gk'][:DP], f)[:, :, None, :]], axis=2)
    sh['gla_wg'] = np.ascontiguousarray(wg)
    sh['gla_ngT'] = np.stack([fmT(inp['gla_norm_g'][l], 2) for l in range(DP)])
    sh['ret_ngT'] = np.stack([fmT(inp['ret_norm_g'][l], 2) for l in range(DP)])
    rl = np.asarray(inp['ret_decay_logit'][:DP], f)
    sh['ret_lrep'] = np.ascontiguousarray(np.broadcast_to(rl[:, :, None, :], (DP, 2, 128, 4)))
    L = cfg.L
    pos = np.zeros((L, 2), f)
    t = np.arange(cfg.LAT)
    pos[cfg.CTX:, 0] = t // 64; pos[cfg.CTX:, 1] = t % 64
    if flip:
        pos[cfg.CTX:] = pos[cfg.CTX:][::-1].copy()
    sh['pos'] = pos
    sh['gl_consts'] = gla_consts()
    sh['gl_consts2'] = gla_consts2()
    lam = np.stack([np.asarray(inp['s5_lam_re'][:DP], f), np.asarray(inp['s5_lam_im'][:DP], f)], axis=2)
    dt = np.asarray(inp['s5_log_dt'][:DP], f)
    B = np.stack([np.asarray(inp['s5_b_re'][:DP], f), np.asarray(inp['s5_b_im'][:DP], f)], axis=2)
    Cm = np.stack([np.asarray(inp['s5_c_re'][:DP], f), np.asarray(inp['s5_c_im'][:DP], f)], axis=2)
    sh['s5_lamP'] = np.ascontiguousarray(lam.reshape(DP, 2, 2, 32, 2, 64).transpose(0, 1, 2, 4, 5, 3).reshape(DP, 2, 2, 128, 32))
    dtb = np.broadcast_to(dt[:, :, :, None], (DP, 2, 64, 64))
    sh['s5_dtP'] = np.ascontiguousarray(dtb.reshape(DP, 2, 32, 2, 64).transpose(0, 1, 3, 4, 2).reshape(DP, 2, 128, 32))
    sh['s5_lam64'] = np.ascontiguousarray(lam.transpose(0, 1, 2, 4, 3))
    sh['s5_dt64'] = np.ascontiguousarray(dtb.transpose(0, 1, 3, 2))
    sh['s5_b64'] = np.ascontiguousarray(B.transpose(0, 1, 2, 4, 3, 5))
    sh['s5_c64'] = np.ascontiguousarray(Cm.transpose(0, 1, 2, 5, 3, 4))
    lamF = np.broadcast_to(lam.reshape(DP, 2, 2, 8, 8, 1, 64), (DP, 2, 2, 8, 8, 16, 64))
    sh['s5_lamF'] = np.ascontiguousarray(lamF.transpose(0, 1, 2, 4, 5, 3, 6).reshape(DP, 2, 2, 128, 8, 64))
    dtF = np.broadcast_to(dt.reshape(DP, 2, 8, 8, 1, 1), (DP, 2, 8, 8, 16, 64))
    sh['s5_dtF'] = np.ascontiguousarray(dtF.transpose(0, 1, 3, 4, 2, 5).reshape(DP, 2, 128, 8, 64))
    BF = B.reshape(DP, 2, 2, 8, 8, 64, 16)
    sh['s5_bF'] = np.ascontiguousarray(BF.transpose(0, 1, 2, 4, 6, 3, 5).reshape(DP, 2, 2, 128, 8, 64))
    CP = Cm.reshape(DP, 2, 2, 32, 2, 16, 64)
    sh['s5_cP'] = np.ascontiguousarray(CP.transpose(0, 1, 2, 4, 6, 3, 5).reshape(DP, 2, 2, 128, 32, 16))
    dd = np.asarray(inp['s5_d'][:DP], f).reshape(DP, 8, 128)
    sh['s5_dT'] = np.ascontiguousarray(dd.transpose(0, 2, 1))
    row = np.arange(128)[:, None, None]; qq = np.arange(4)[None, :, None]; col = np.arange(128)[None, None, :]
    sh['s5_maskQ'] = ((row // 32 == qq) & ((row % 32) // 16 == col // 64)).astype(f)
    sh['s5_maskC'] = np.ascontiguousarray(np.broadcast_to((np.arange(128)[:, None, None] // 64 == np.arange(2)[None, :, None]), (128, 2, 16))).astype(f)
    return sh


def swap_dirs(inp):
    o = dict(inp)
    w = np.array(inp['w_in'], np.float32, copy=True)
    w[:, :, 1024:2048] = inp['w_in'][:, :, 2048:3072]; w[:, :, 2048:3072] = inp['w_in'][:, :, 1024:2048]
    w[:, :, 7168:7184] = inp['w_in'][:, :, 7184:7200]; w[:, :, 7184:7200] = inp['w_in'][:, :, 7168:7184]
    o['w_in'] = w
    for k in ('hgrn_lb_logits', 'gla_w_gk', 'gla_b_gk', 's5_lam_re', 's5_lam_im', 's5_log_dt', 's5_b_re', 's5_b_im',
              's5_c_re', 's5_c_im', 'ret_decay_logit'):
        o[k] = np.ascontiguousarray(np.asarray(inp[k])[:, ::-1])
    return o


def prep_core(cfg, inp, b, flip=False):
    f = np.float32
    m = {}
    cx = np.asarray(inp['ctx'][b], f); xx = np.asarray(inp['x'][b], f)
    if flip:
        cx = cx[::-1]; xx = xx[::-1]
    h0 = np.concatenate([cx, xx], axis=0)
    m['hT0'] = np.ascontiguousarray(h0.T)
    c2 = np.stack([np.asarray(inp['c'][b], f), np.asarray(inp['c_ctx'], f)], axis=0)
    m['c2T'] = np.ascontiguousarray(c2.reshape(2, 32, 128).transpose(2, 1, 0))
    return m


_CACHE = {}


def run_pairs(cfg, inputs, nb):
    key = (cfg.CTX, cfg.LAT, cfg.DEPTH)
    if key not in _CACHE:
        _CACHE[key] = Builder(cfg).build()
    nc = _CACHE[key]
    sh0 = prep_shared(cfg, inputs, False)
    sh1 = prep_shared(cfg, swap_dirs(inputs), True)
    for k in sh0:
        if k in sh1 and sh0[k].shape == sh1[k].shape and sh0[k].nbytes > (1 << 20) and np.array_equal(sh0[k], sh1[k]):
            sh1[k] = sh0[k]
    in_maps = []
    for core in range(2 * nb):
        flip = core >= nb
        m = dict(sh1 if flip else sh0)
        m.update(prep_core(cfg, inputs, core % nb, flip))
        in_maps.append(m)
    res = run_bass_kernel_spmd(nc, in_maps, core_ids=list(range(2 * nb)))
    outs = []
    for b in range(nb):
        first = np.asarray(res.results[b]['outT']).T
        second = np.asarray(res.results[b + nb]['outT']).T[::-1]
        outs.append(np.concatenate([first, second], axis=0))
    return np.stack(outs, axis=0).astype(np.float32)


def kernel(**inputs):
    cfg = Cfg()
    cfg.half = True
    return run_pairs(cfg, inputs, 4)
```

```python
import numpy as np
import concourse.bass as bass
import concourse.mybir as mybir
from concourse.bass_utils import run_bass_kernel_spmd

F32 = mybir.dt.float32; BF16 = mybir.dt.bfloat16; I32 = mybir.dt.int32
AF = mybir.ActivationFunctionType; ALU = mybir.AluOpType
D = 4096; KC = 32; EPS = 1e-6
TWO_PI = 6.283185307179586; PI = 3.141592653589793


class Prog:
    NDS = 8

    def __init__(self, nc):
        self.nc = nc
        self.eng = {'pe': nc.tensor, 'act': nc.scalar, 'dve': nc.vector, 'pool': nc.gpsimd, 'sp': nc.sync}
        self.sem = {}; self.cnt = {}; self.dsem = {}; self.dcnt = {}
        self.waited = {k: {} for k in self.eng}
        self.res = {}
        self._stack = []
        self.nins = 0

    def start(self):
        nc = self.nc
        for k in self.eng:
            cm = nc.semaphore("s_" + k); self._stack.append(cm); self.sem[k] = cm.__enter__(); self.cnt[k] = 0
        for k in ['sp', 'pool', 'act']:
            self.dsem[k] = []
            for i in range(self.NDS):
                cm = nc.semaphore("d_%s%d" % (k, i)); self._stack.append(cm); self.dsem[k].append(cm.__enter__())
            self.dcnt[k] = 0
        self.last_dma_tok = {k: [None] * self.NDS for k in self.dsem}

    def finish(self):
        for cm in reversed(self._stack):
            cm.__exit__(None, None, None)

    def _wait(self, e, tok):
        if tok is None:
            return
        sem, val, owner = tok
        w = self.waited[e]
        key = id(sem)
        if w.get(key, 0) >= val:
            return
        if owner == e and e == 'pe':
            return
        self.eng[e].wait_ge(sem, val)
        w[key] = val

    def _deps(self, e, reads, writes):
        toks = []
        for k in reads:
            st = self.res.get(k)
            if st and st['w']:
                toks.append(st['w'])
        for k in writes:
            st = self.res.get(k)
            if st:
                if st['w']:
                    toks.append(st['w'])
                toks.extend(st['r'])
        for t in toks:
            self._wait(e, t)

    def _update(self, tok, reads, writes):
        for k in reads:
            st = self.res.setdefault(k, {'w': None, 'r': []})
            st['r'].append(tok)
            if len(st['r']) > 48:
                best = {}
                for t in st['r']:
                    kk = id(t[0])
                    if kk not in best or best[kk][1] < t[1]:
                        best[kk] = t
                st['r'] = list(best.values())
        for k in writes:
            self.res[k] = {'w': tok, 'r': []}

    def op(self, e, fn, reads=(), writes=()):
        self._deps(e, reads, writes)
        ins = fn(self.eng[e])
        self.cnt[e] += 1
        self.nins += 1
        ins.then_inc(self.sem[e], 1)
        tok = (self.sem[e], self.cnt[e], e)
        self._update(tok, reads, writes)
        return tok

    def dma(self, q, out, in_, reads=(), writes=(), **kw):
        self._deps(q, reads, writes)
        i = self.dcnt[q]; k = i % self.NDS
        prev = self.last_dma_tok[q][k]
        if prev is not None:
            self._wait(q, prev)
        ins = self.eng[q].dma_start(out=out, in_=in_, **kw)
        val = 16 * (i // self.NDS + 1)
        ins.then_inc(self.dsem[q][k], 16)
        tok = (self.dsem[q][k], val, 'dma_' + q)
        self.last_dma_tok[q][k] = tok
        self.dcnt[q] += 1
        self.nins += 1
        self._update(tok, reads, writes)
        return tok

    def barrier(self):
        toks = []
        for e in self.eng:
            if self.cnt[e] > 0:
                toks.append((self.sem[e], self.cnt[e], e))
        for q in self.dsem:
            for t in self.last_dma_tok[q]:
                if t is not None:
                    toks.append(t)
        for e in self.eng:
            for t in toks:
                self._wait(e, t)
        self.res = {}

    def mm(self, out, lhsT, rhs, start, stop, reads, writes):
        return self.op('pe', lambda e: e.matmul(out, lhsT=lhsT, rhs=rhs, start=start, stop=stop), reads, writes)

    def tr(self, out, in_, ident, reads, writes):
        return self.op('pe', lambda e: e.transpose(out=out, in_=in_, identity=ident), reads, writes)

    def act(self, out, in_, func, reads, writes, bias=None, scale=None, eng='act'):
        kw = {}
        if bias is not None:
            kw['bias'] = bias
        if scale is not None:
            kw['scale'] = scale
        return self.op(eng, lambda e: e.activation(out=out, in_=in_, func=func, **kw), reads, writes)

    def tt(self, out, in0, in1, op, reads, writes, eng='dve'):
        return self.op(eng, lambda e: e.tensor_tensor(out=out, in0=in0, in1=in1, op=op), reads, writes)

    def ts(self, out, in0, s1, op0, reads, writes, s2=None, op1=None, eng='dve'):
        if op1 is None:
            return self.op(eng, lambda e: e.tensor_scalar(out=out, in0=in0, scalar1=s1, scalar2=None, op0=op0), reads, writes)
        return self.op(eng, lambda e: e.tensor_scalar(out=out, in0=in0, scalar1=s1, scalar2=s2, op0=op0, op1=op1), reads, writes)

    def stt(self, out, in0, scalar, in1, op0, op1, reads, writes):
        return self.op('dve', lambda e: e.scalar_tensor_tensor(out=out, in0=in0, scalar=scalar, in1=in1, op0=op0, op1=op1), reads, writes)

    def cp(self, out, in_, reads, writes, eng='dve'):
        if eng == 'act':
            return self.op('act', lambda e: e.copy(out=out, in_=in_), reads, writes)
        return self.op(eng, lambda e: e.tensor_copy(out=out, in_=in_), reads, writes)

    def memset(self, ap, val, writes, eng='pool'):
        return self.op(eng, lambda e: e.memset(ap, val), (), writes)


class Cfg:
    def __init__(self, CTX=256, LAT=4096, DEPTH=2, debug=False, stop=None):
        self.CTX = CTX; self.LAT = LAT; self.DEPTH = DEPTH; self.L = CTX + LAT
        self.debug = debug; self.stop = stop
        self.half = False

    def need_hi(self, last):
        return (self.CTX + self.LAT // 2) if (last and self.half) else self.L

    def need_lo(self, last):
        return self.CTX if last else 0


def tiles(lo, hi, w):
    out = []
    t = lo
    while t < hi:
        ww = min(w, hi - t)
        out.append((t, ww))
        t += ww
    return out


class Builder:
    def __init__(self, cfg):
        self.cfg = cfg
        self.nc = bass.Bass("TRN2", target_bir_lowering=False)
        self.P = Prog(self.nc)
        self.T = {}
        self.dbg_names = []

    def din(self, name, shape, dt=F32):
        self.T[name] = self.nc.dram_tensor(name, list(shape), dt, kind="ExternalInput").ap()

    def dscr(self, name, shape, dt, out=False):
        if name in getattr(self.cfg, 'inject', ()):
            self.T[name] = self.nc.dram_tensor(name, list(shape), dt, kind="ExternalInput").ap()
            return
        if out or (self.cfg.debug and name in self.cfg.debug):
            self.T[name] = self.nc.dram_tensor(name, list(shape), dt, kind="ExternalOutput").ap()
            self.dbg_names.append(name)
        else:
            self.T[name] = self.nc.dram_tensor(name, list(shape), dt).ap()

    def sb(self, name, shape, dt):
        self._uid = getattr(self, '_uid', 0) + 1
        return self.nc.sbuf_tensor("%s_%d" % (name, self._uid), list(shape), dt)

    def ps(self, name, shape, dt=F32):
        self._uid = getattr(self, '_uid', 0) + 1
        return self.nc.psum_tensor("%s_%d" % (name, self._uid), list(shape), dt)

    def declare(self):
        c = self.cfg; L = c.L; DP = c.DEPTH
        self.din("hT0", [D, L]); self.din("c2T", [128, 32, 2])
        self.din("w_ada", [DP, D, 3 * D]); self.din("b_adaT", [DP, 128, 96])
        self.din("norm_gT", [DP, 128, 32]); self.din("final_gT", [128, 32])
        self.din("w_tm", [DP, D, 8192]); self.din("w_fm", [DP, D, 5152])
        self.din("w_out", [DP, D, D]); self.din("w_glu", [DP, 1024, 1024]); self.din("b_gluT", [DP, 128, 8])
        self.din("hg_lbrep", [128, 2, 2, 1024]); self.din("hg_ngT", [DP, 128, 1])
        self.din("gla_wg", [DP, 2, 17, 512]); self.din("gla_ngT", [DP, 128, 2])
        self.din("ret_ngT", [DP, 128, 2]); self.din("ret_lrep", [DP, 2, 128, 4])
        self.din("pos", [L, 2]); self.din("gl_consts", [2, 64, 449]); self.din("gl_consts2", [2, 128, 513])
        self.din("s5_lamP", [DP, 2, 2, 128, 32])
        self.din("s5_dtP", [DP, 2, 128, 32])
        self.din("s5_lam64", [DP, 2, 2, 64, 64])
        self.din("s5_dt64", [DP, 2, 64, 64])
        self.din("s5_b64", [DP, 2, 2, 64, 64, 16])
        self.din("s5_c64", [DP, 2, 2, 64, 64, 16])
        self.din("s5_lamF", [DP, 2, 2, 128, 8, 64])
        self.din("s5_dtF", [DP, 2, 128, 8, 64])
        self.din("s5_bF", [DP, 2, 2, 128, 8, 64])
        self.din("s5_cP", [DP, 2, 2, 128, 32, 16])
        self.din("s5_dT", [DP, 128, 8]); self.din("s5_maskQ", [128, 4, 128]); self.din("s5_maskC", [128, 2, 16])
        self.dscr("hT1", [D, L], F32); self.dscr("hT2", [D, L], F32)
        self.dscr("hnT", [D, L], BF16)
        self.dscr("tmb", [L, 8192], BF16); self.dscr("tmf", [L, 2048], F32)
        self.dscr("fmb", [5120, L], BF16); self.dscr("lrT", [32, L], F32)
        self.dscr("ofT", [3072, L], F32); self.dscr("oT", [D, L], BF16)
        self.dscr("rope", [L, 512], F32)
        self.dscr("s5pf", [8, 2, 128, 17, 64], F32); self.dscr("s5cf", [8, 2, 128, 64], F32)
        self.dscr("yfT", [1024, L], F32); self.dscr("zT", [1024, L], BF16)
        self.dscr("outT", [D, (c.LAT // 2) if c.half else c.LAT], F32, out=True)

    def consts(self, stack):
        P = self.P
        def mk(name, shape, dt):
            cm = self.sb(name, shape, dt); stack.append(cm); return cm.__enter__()
        self.onesf = mk("onesf", [128, 128], F32)
        self.onesb = mk("onesb", [128, 128], BF16)
        self.identb = mk("identb", [128, 128], BF16)
        self.identf = mk("identf", [128, 128], F32)
        self.sT = mk("sT", [128, 32, 2], BF16)
        self.GS = mk("GS", [128, 6, 32], F32)
        P.memset(self.onesf[:], 1.0, ['onesf'])
        P.memset(self.onesb[:], 1.0, ['onesb'])
        P.op('pool', lambda e: e.affine_select(out=self.identf[:], in_=self.onesf[:], pattern=[[1, 128]], compare_op=ALU.is_equal,
                                               fill=0.0, base=0, channel_multiplier=-1), ['onesf'], ['identf'])
        P.cp(self.identb[:], self.identf[:], ['identf'], ['identb'])
        with self.sb("c2f", [128, 32, 2], F32) as c2f:
            P.dma('sp', c2f[:], self.T['c2T'], writes=['c2f'])
            P.act(self.sT[:], c2f[:], AF.Silu, ['c2f'], ['sT'])
            P.barrier()

    def phase_ada(self, layer):
        P = self.P; T = self.T; GS = self.GS
        wv = T['w_ada'][layer].rearrange("(k p) c -> p k c", p=128)
        with self.sb("wa0", [128, 32, 256], BF16) as wa0, self.sb("wa1", [128, 32, 256], BF16) as wa1, \
                self.ps("pa", [128, 512], F32) as pa, self.sb("modT", [128, 96, 2], F32) as modT, \
                self.sb("bT", [128, 96], F32) as bT, self.sb("ng", [128, 32], F32) as ng:
            was = [wa0, wa1]
            P.dma('sp', bT[:], T['b_adaT'][layer], writes=['bT'])
            P.dma('sp', ng[:], T['norm_gT'][layer], writes=['ng'])
            for cb in range(48):
                wa = was[cb % 2]; key = ('wa', cb % 2)
                P.dma('pool', wa[:], wv[:, :, cb * 256:(cb + 1) * 256], writes=[key])
                for s in range(2):
                    j = cb * 2 + s
                    for k in range(32):
                        P.mm(pa[:, 2 * j:2 * j + 2], wa[:, k, s * 128:(s + 1) * 128], self.sT[:, k, :], k == 0, k == 31,
                             [key, 'sT'], ['pa'])
            P.tt(modT[:], pa[:, 0:192].rearrange("p (j r) -> p j r", r=2), bT[:].unsqueeze(2).to_broadcast([128, 96, 2]), ALU.add,
                 ['pa', 'bT'], ['modT'])
            for r in range(2):
                P.stt(GS[:, 3 * r + 0, :], modT[:, 32:64, r], 1.0, ng[:], ALU.add, ALU.mult, ['modT', 'ng'], ['GS'])
                P.cp(GS[:, 3 * r + 1, :], modT[:, 0:32, r], ['modT'], ['GS'])
                P.cp(GS[:, 3 * r + 2, :], modT[:, 64:96, r], ['modT'], ['GS'])
            P.barrier()

    def phase_norm(self, hsrc, mode, dst):
        c = self.cfg; P = self.P; T = self.T; GS = self.GS; L = c.L; CTX = c.CTX
        TW = 256
        hv = hsrc.rearrange("(k p) t -> p k t", p=128)
        dv = dst.rearrange("(k p) t -> p k t", p=128)
        toks = tiles(0, L, TW) if mode == 'hn' else tiles(CTX, c.need_hi(True), TW)
        with self.sb("hTa", [128, 32, TW], F32) as hTa, self.sb("hTb", [128, 32, TW], F32) as hTb, \
                self.sb("sq", [128, 32, TW], BF16) as sq, self.sb("hna", [128, 32, TW], BF16) as hna, \
                self.sb("hnb", [128, 32, TW], BF16) as hnb, self.sb("rstd", [128, TW], F32) as rstd, \
                self.sb("fg", [128, 32], F32) as fg, \
                self.ps("pssa", [128, 512], F32) as pssa, self.ps("pssb", [128, 512], F32) as pssb:
            hTs = [hTa, hTb]; hns = [hna, hnb]; psss = [pssa, pssb]
            if mode == 'final':
                P.dma('sp', fg[:], T['final_gT'], writes=['fg'])
            for it, (t0, w) in enumerate(toks):
                b = it % 2
                hT = hTs[b]; hn = hns[b]; pss = psss[b]
                P.dma('sp', hT[:, :, :w], hv[:, :, t0:t0 + w], writes=[('hT', b)])
                P.act(sq[:, :, :w], hT[:, :, :w], AF.Square, [('hT', b)], ['sq'])
                for k in range(32):
                    P.mm(pss[:, :w], self.onesb[:], sq[:, k, :w], k == 0, k == 31, ['sq', 'onesb'], [('pss', b)])
                P.act(rstd[:, :w], pss[:, :w], AF.Sqrt, [('pss', b)], ['rstd'], bias=EPS, scale=1.0 / D)
                P.op('dve', lambda e: e.reciprocal(out=rstd[:, :w], in_=rstd[:, :w]), ['rstd'], ['rstd'])
                P.tt(hT[:, :, :w], hT[:, :, :w], rstd[:, :w].unsqueeze(1).to_broadcast([128, 32, w]), ALU.mult,
                     [('hT', b), 'rstd'], [('hT', b)])
                if mode == 'hn':
                    gi = 3 if t0 < CTX else 0
                    for k in range(32):
                        if k % 2 == 0:
                            P.act(hn[:, k, :w], hT[:, k, :w], AF.Identity, [('hT', b), 'GS'], [('hn', b, k)],
                                  bias=GS[:, gi + 1, k:k + 1], scale=GS[:, gi, k:k + 1])
                        else:
                            P.ts(hn[:, k, :w], hT[:, k, :w], GS[:, gi, k:k + 1], ALU.mult, [('hT', b), 'GS'], [('hn', b, k)],
                                 s2=GS[:, gi + 1, k:k + 1], op1=ALU.add)
                    P.dma('act', dv[:, :, t0:t0 + w], hn[:, :, :w], reads=[('hn', b, k) for k in range(32)], writes=[('dst', it)])
                else:
                    P.tt(hT[:, :, :w], hT[:, :, :w], fg[:].unsqueeze(2).to_broadcast([128, 32, w]), ALU.mult,
                         [('hT', b), 'fg'], [('hT', b)])
                    P.dma('act', dv[:, :, t0 - CTX:t0 - CTX + w], hT[:, :, :w], reads=[('hT', b)], writes=[('dst', it)])
            P.barrier()

    def phase_inproj(self, layer, last=False):
        c = self.cfg; P = self.P; T = self.T; L = c.L
        blocks = []
        for bi in range(16):
            blocks.append(('tm', 'w_tm', bi * 512, 512))
        for bi in range(10):
            blocks.append(('fm', 'w_fm', bi * 512, 512))
        blocks.append(('lr', 'w_fm', 5120, 32))
        hv = T['hnT'].rearrange("(k p) t -> p k t", p=128)
        ttiles_all = tiles(0, L, 512)
        ttiles_need = tiles(c.need_lo(last), c.need_hi(last), 512) if last else ttiles_all
        with self.sb("Wb0", [128, 32, 512], BF16) as Wb0, self.sb("Wb1", [128, 32, 512], BF16) as Wb1, \
                self.sb("hb0", [128, 32, 512], BF16) as hb0, self.sb("hb1", [128, 32, 512], BF16) as hb1, \
                self.sb("stgf", [128, 4, 512], F32) as stgf, self.sb("stgb", [128, 4, 512], BF16) as stgb, \
                self.ps("pd", [128, 4, 512], F32) as pd:
            Wbs = [Wb0, Wb1]; hbs = [hb0, hb1]
            gt = 0; gp = 0
            for bi, (kind, wn, c0, ncol) in enumerate(blocks):
                Wb = Wbs[bi % 2]; wkey = ('Wb', bi % 2)
                wv = T[wn][layer].rearrange("(k p) c -> p k c", p=128)
                P.dma('pool', Wb[:, :, :ncol], wv[:, :, c0:c0 + ncol], writes=[wkey])
                qonly = (kind == 'tm' and c0 in (0, 512, 4096, 6144)) or (kind == 'fm' and c0 not in (2048, 2560))
                for (t0, w) in (ttiles_need if qonly else ttiles_all):
                    hb = hbs[gt % 2]; hkey = ('hb', gt % 2); gt += 1
                    P.dma('sp', hb[:, :, :w], hv[:, :, t0:t0 + w], writes=[hkey])
                    if kind in ('fm', 'lr'):
                        nsub = max(1, ncol // 128); m = min(128, ncol)
                        for cs in range(nsub):
                            bk = gp % 4; gp += 1
                            for k in range(32):
                                P.mm(pd[:m, bk, :w], Wb[:, k, cs * 128:cs * 128 + m], hb[:, k, :w], k == 0, k == 31,
                                     [wkey, hkey], [('pd', bk)])
                            if kind == 'fm':
                                stg = stgb; skey = ('stgb', bk); dst = T['fmb'][c0 + cs * 128:c0 + cs * 128 + m, t0:t0 + w]
                            else:
                                stg = stgf; skey = ('stgf', bk); dst = T['lrT'][0:32, t0:t0 + w]
                            P.cp(stg[:m, bk, :w], pd[:m, bk, :w], [('pd', bk)], [skey], eng=('act' if bk % 2 else 'dve'))
                            P.dma('act', dst, stg[:m, bk, :w], reads=[skey], writes=[('o', gp)])
                    else:
                        isf = (1024 <= c0 < 3072)
                        for ts in range(w // 128):
                            bk = gp % 4; gp += 1
                            for k in range(32):
                                P.mm(pd[:, bk, :], hb[:, k, ts * 128:(ts + 1) * 128], Wb[:, k, :], k == 0, k == 31,
                                     [wkey, hkey], [('pd', bk)])
                            r0 = t0 + ts * 128
                            if isf:
                                stg = stgf; skey = ('stgf', bk); dst = T['tmf'][r0:r0 + 128, c0 - 1024:c0 - 1024 + 512]
                            else:
                                stg = stgb; skey = ('stgb', bk); dst = T['tmb'][r0:r0 + 128, c0:c0 + 512]
                            P.cp(stg[:, bk, :], pd[:, bk, :], [('pd', bk)], [skey], eng=('act' if bk % 2 else 'dve'))
                            P.dma('act', dst, stg[:, bk, :], reads=[skey], writes=[('o', gp)])
            P.barrier()

    def phase_outproj(self, layer, hsrc, hdst, last):
        c = self.cfg; P = self.P; T = self.T; L = c.L; CTX = c.CTX; GS = self.GS
        wv = T['w_out'][layer].rearrange("(k p) c -> p k c", p=128)
        ov = T['oT'].rearrange("(k p) t -> p k t", p=128)
        ttiles = ([] if last else tiles(0, CTX, 512)) + tiles(CTX, c.need_hi(last), 512)
        with self.sb("Wo0", [128, 32, 512], BF16) as Wb0, self.sb("Wo1", [128, 32, 512], BF16) as Wb1, \
                self.sb("ob0", [128, 32, 512], BF16) as hb0, self.sb("ob1", [128, 32, 512], BF16) as hb1, \
                self.sb("hold0", [128, 4, 512], F32) as hold0, self.sb("hold1", [128, 4, 512], F32) as hold1, \
                self.ps("po", [128, 4, 512], F32) as pd:
            Wbs = [Wb0, Wb1]; hbs = [hb0, hb1]; holds = [hold0, hold1]
            gt = 0; gp = 0
            for fb in range(8):
                Wb = Wbs[fb % 2]; wkey = ('Wb', fb % 2)
                P.dma('pool', Wb[:], wv[:, :, fb * 512:(fb + 1) * 512], writes=[wkey])
                hs = hsrc[fb * 512:(fb + 1) * 512, :].rearrange("(f p) t -> p f t", p=128)
                hd = hdst[fb * 512:(fb + 1) * 512, :].rearrange("(f p) t -> p f t", p=128)
                for (t0, w) in ttiles:
                    hb = hbs[gt % 2]; hkey = ('hb', gt % 2); hold = holds[gt % 2]; okey = ('hold', gt % 2); gt += 1
                    P.dma('sp', hb[:, :, :w], ov[:, :, t0:t0 + w], writes=[hkey])
                    P.dma('sp', hold[:, :, :w], hs[:, :, t0:t0 + w], writes=[okey])
                    gi = 3 if t0 < CTX else 0
                    for fs in range(4):
                        bk = gp % 4; gp += 1
                        fch = fb * 4 + fs
                        for k in range(32):
                            P.mm(pd[:, bk, :w], Wb[:, k, fs * 128:(fs + 1) * 128], hb[:, k, :w], k == 0, k == 31,
                                 [wkey, hkey], [('pd', bk)])
                        P.stt(hold[:, fs, :w], pd[:, bk, :w], GS[:, gi + 2, fch:fch + 1], hold[:, fs, :w], ALU.mult, ALU.add,
                              [('pd', bk), okey, 'GS'], [okey])
                    P.dma('act', hd[:, :, t0:t0 + w], hold[:, :, :w], reads=[okey], writes=[('o', gt)])
            P.barrier()

    BR = {
        'hg': dict(H=8, dv=128, q=0, k=None, v=3072, gate=0, orow=0, ofrow=0, gs=1.0, qs=1.0, norm='rms'),
        'gl': dict(H=4, dv=256, q=4096, k=4608, v=5120, gate=1024, orow=1024, ofrow=1024, gs=-1.0 / 16.0, qs=128.0 ** -0.5, norm='rms'),
        'rt': dict(H=4, dv=256, q=6144, k=6656, v=7168, gate=4096, orow=3072, ofrow=2048, gs=1.0, qs=128.0 ** -0.5, norm='ln'),
    }

    def phase_gla(self, layer, name, last=False):
        c = self.cfg; P = self.P; T = self.T; L = c.L; CTX = c.CTX
        nlo = c.need_lo(last); nhi = c.need_hi(last)
        br = self.BR[name]; H = br['H']; dv = br['dv']; dvc = dv // 128; HW = H * 128; C = 64
        NCH = L // C; ctxch = CTX // C; r = C // 2
        orders = [[n for n in range(NCH) if n * C < nhi], list(range(ctxch - 1, -1, -1)) + list(range(NCH - 1, ctxch - 1, -1))]
        need_o = lambda n: (nlo <= n * C < nhi)
        ofv = T['ofT'][br['ofrow']:br['ofrow'] + 1024, :].rearrange("(c p) t -> p c t", p=128)
        ov = T['oT'][br['orow']:br['orow'] + 1024, :].rearrange("(c p) t -> p c t", p=128)
        gv = T['fmb'][br['gate']:br['gate'] + 1024, :].rearrange("(c p) t -> p c t", p=128)
        sb = self.sb; ps = self.ps
        from contextlib import ExitStack
        with ExitStack() as es:
            Mcat = es.enter_context(sb("Mcat", [C, 449], F32))
            kdz = es.enter_context(sb("kdz", [128, 2, 128], BF16))
            koz = es.enter_context(sb("koz", [128, 2, 1, 64], BF16))
            qtd = es.enter_context(sb("qtd", [128, 2, C], BF16))
            TriS = es.enter_context(sb("TriS", [C, C], F32))
            mask = es.enter_context(sb("mask", [C, C], F32))
            ctmp = es.enter_context(sb("ctmp", [C, C], F32))
            q_t = es.enter_context(sb("q_t", [C, 2, HW], BF16))
            v_t = es.enter_context(sb("v_t", [C, 2, 1024], BF16))
            z_t = es.enter_context(sb("z_t", [C, 2, 1024], F32))
            k_t = es.enter_context(sb("k_t", [C, 2, HW], BF16))
            lr_t = es.enter_context(sb("lr_t", [17, 2, C], F32))
            rp_t = es.enter_context(sb("rp_t", [C, 2, 512], F32))
            of_t = es.enter_context(sb("of_t", [128, 2, 8, C], F32))
            gate_t = es.enter_context(sb("gate_t", [128, 2, 8, C], BF16))
            sig = es.enter_context(sb("sig", [C, 1024], F32))
            ff = es.enter_context(sb("ff", [C, 1024], F32))
            graw = es.enter_context(sb("graw", [C, 2, 1024], F32))
            kt = es.enter_context(sb("kt", [C, 2, 1024], BF16))
            qr = es.enter_context(sb("qr", [C, 2, HW], BF16))
            khat = es.enter_context(sb("khat", [C, 2, 1024], BF16))
            Ek = es.enter_context(sb("Ek", [C, 512], F32))
            rt1 = es.enter_context(sb("rt1", [C, 256], F32))
            rt2 = es.enter_context(sb("rt2", [C, 256], F32))
            rt3 = es.enter_context(sb("rt3", [C, 256], F32))
            rt4 = es.enter_context(sb("rt4", [C, 256], F32))
            E13 = es.enter_context(sb("E13", [128, 2, 321], F32))
            E2 = es.enter_context(sb("E2", [128, 2, C], F32))
            qtl = es.enter_context(sb("qtl", [128, 2, C], BF16))
            qtp = es.enter_context(sb("qtp", [128, 2, C], BF16))
            ktl = es.enter_context(sb("ktl", [128, 2, C], BF16))
            scb = es.enter_context(sb("scb", [C, 2, C], BF16))
            ofs = es.enter_context(sb("ofs", [128, 2, 8, C], F32))
            osum = es.enter_context(sb("osum", [128, 2, 8, C], F32))
            sg = es.enter_context(sb("sg", [128, 8, C], F32))
            sqt = es.enter_context(sb("sqt", [128, 8, C], F32))
            rs = es.enter_context(sb("rs", [128, 8, C], F32))
            mean = es.enter_context(sb("mean", [128, 8, C], F32))
            ob = es.enter_context(sb("ob", [128, 2, 8, C], BF16))
            otmp = es.enter_context(sb("otmp", [128, 8, C], F32))
            S = es.enter_context(sb("S", [128, 1024], F32))
            Sb = es.enter_context(sb("Sb", [128, 1024], BF16))
            lbr = es.enter_context(sb("lbr", [128, 1024], F32))
            oml = es.enter_context(sb("oml", [128, 1024], F32))
            lg = es.enter_context(sb("lg", [128, 2, 1024], F32))
            wg = es.enter_context(sb("wg", [17, 512], F32))
            gn = es.enter_context(sb("gn", [128, 2], F32))
            rl = es.enter_context(sb("rl", [128, 4], F32))
            gconst = es.enter_context(sb("gconst", [C, 512], F32))
            pc = es.enter_context(ps("pc", [128, 512], F32))
            pP0 = es.enter_context(ps("pP0", [128, 512], F32)); pP1 = es.enter_context(ps("pP1", [128, 512], F32)); pPs = [pP0, pP1]
            pT0 = es.enter_context(ps("pT0", [128, 8, 128], BF16)); pT1 = es.enter_context(ps("pT1", [128, 8, 128], BF16)); pTs = [pT0, pT1]
            pS = es.enter_context(ps("pS", [128, 512], F32))
            pO = es.enter_context(ps("pO", [128, 8, C], F32))
            pU = es.enter_context(ps("pU", [128, 2, 256], F32))
            if name == 'hg':
                P.dma('sp', gn[:, 0:1], T['hg_ngT'][layer], writes=['gn'])
            elif name == 'gl':
                P.dma('sp', gn[:], T['gla_ngT'][layer], writes=['gn'])
            else:
                P.dma('sp', gn[:], T['ret_ngT'][layer], writes=['gn'])
            P.memset(lr_t[:], 1.0, ['lr0', 'lr1'])
            step = 0
            for d in (0, 1):
                gs = br['gs']
                P.dma('sp', Mcat[:, :], T['gl_consts'][d], writes=['Mcat'])
                P.ts(Mcat[:, 0:385], Mcat[:, 0:385], gs, ALU.mult, ['Mcat'], ['Mcat'])
                P.memset(kdz[:], 0.0, ['kdz0', 'kdz1'])
                P.memset(koz[:], 0.0, ['koz0', 'koz1'])
                if name == 'hg':
                    P.dma('sp', lg[:], T['hg_lbrep'][:, :, d, :], writes=['lg'])
                    P.tt(lbr[:], lg[:, 1, :], lg[:, 0, :], ALU.subtract, ['lg'], ['lbr'])
                    P.act(lbr[:], lbr[:], AF.Sigmoid, ['lbr'], ['lbr'])
                    P.ts(lbr[:], lbr[:], float(layer), ALU.mult, ['lbr'], ['lbr'])
                    P.ts(oml[:], lbr[:], -1.0, ALU.mult, ['lbr'], ['oml'], s2=1.0, op1=ALU.add)
                elif name == 'gl':
                    P.dma('sp', wg[:], T['gla_wg'][layer, d], writes=['wg'])
                else:
                    P.dma('sp', rl[:], T['ret_lrep'][layer, d], writes=['rl'])
                    P.act(rl[:], rl[:], AF.Exp, ['rl'], ['rl'], scale=-1.0)
                    P.act(rl[:], rl[:], AF.Ln, ['rl'], ['rl'], bias=1.0)
                    P.ts(rl[:], rl[:], -1.0, ALU.mult, ['rl'], ['rl'])
                    for h in range(4):
                        P.ts(gconst[:, h * 128:(h + 1) * 128], self.onesf[:C, :], rl[:C, h:h + 1], ALU.mult, ['onesf', 'rl'], ['gconst'])
                P.memset(S[:], 0.0, [('S', h) for h in range(H)])
                P.memset(Sb[:], 0.0, [('Sb', h) for h in range(H)])
                order = orders[d]

                def emit_prep(si):
                    n = order[si]; t0 = n * C; pb = si % 2
                    K = lambda nm: (nm, pb)
                    P.dma('sp', q_t[:, pb, :], T['tmb'][t0:t0 + C, br['q']:br['q'] + HW], writes=[K('q_t')])
                    P.dma('sp', v_t[:, pb, :], T['tmb'][t0:t0 + C, br['v']:br['v'] + 1024], writes=[K('v_t')])
                    if name == 'hg':
                        P.dma('sp', z_t[:, pb, :], T['tmf'][t0:t0 + C, d * 1024:(d + 1) * 1024], writes=[K('z_t')])
                    else:
                        P.dma('sp', k_t[:, pb, :], T['tmb'][t0:t0 + C, br['k']:br['k'] + HW], writes=[K('k_t')])
                    if name == 'gl':
                        P.dma('sp', lr_t[0:16, pb, :], T['lrT'][d * 16:(d + 1) * 16, t0:t0 + C], writes=[('lr%d' % pb)])
                    if name == 'rt':
                        P.dma('sp', rp_t[:, pb, :], T['rope'][t0:t0 + C, :], writes=[K('rp_t')])
                    if d == 1 and need_o(n):
                        P.dma('sp', of_t[:, pb], ofv[:, :, t0:t0 + C], writes=[K('of_t')])
                        P.dma('sp', gate_t[:, pb], gv[:, :, t0:t0 + C], writes=[K('gate_t')])
                    if name == 'hg':
                        P.act(sig[:], z_t[:, pb, :], AF.Sigmoid, [K('z_t')], ['sig'])
                        P.tt(ff[:], sig[:], oml[:C, :], ALU.mult, ['sig', 'oml'], ['ff'])
                        P.tt(ff[:], ff[:], lbr[:C, :], ALU.add, ['ff', 'lbr'], ['ff'])
                        P.ts(ff[:], ff[:], 1e-6, ALU.max, ['ff'], ['ff'])
                        P.act(graw[:, pb, :], ff[:], AF.Ln, ['ff'], [K('graw')])
                        P.ts(ff[:], sig[:], -1.0, ALU.mult, ['sig', 'ff'], ['ff'], s2=1.0, op1=ALU.add)
                        P.tt(kt[:, pb, :], ff[:], oml[:C, :], ALU.mult, ['ff', 'oml'], [K('kt')], eng='pool')
                    elif name == 'gl':
                        P.mm(pc[:C, :], lr_t[0:17, pb, :], wg[0:17, :], True, True, ['lr%d' % pb, 'wg'], ['pc'])
                        P.act(sig[:, 0:512], pc[:C, :], AF.Exp, ['pc'], ['sig'], scale=-1.0)
                        P.act(graw[:, pb, 0:512], sig[:, 0:512], AF.Ln, ['sig'], [K('graw')], bias=1.0)
                    else:
                        cos4 = rp_t[:, pb, 0:256].rearrange("p (h x) -> p h x", x=64)
                        sin4 = rp_t[:, pb, 256:512].rearrange("p (h x) -> p h x", x=64)
                        for (src, skey, dst, dkey, eng, ta, tb) in ((q_t, K('q_t'), qr, K('qr'), 'dve', rt1, rt2), (k_t, K('k_t'), kt, K('kt'), 'pool', rt3, rt4)):
                            xv = src[:, pb, :].rearrange("p (h two x) -> p h two x", two=2, x=64)
                            ov_ = dst[:, pb, 0:512].rearrange("p (h two x) -> p h two x", two=2, x=64)
                            tav = ta[:].rearrange("p (h x) -> p h x", x=64); tbv = tb[:].rearrange("p (h x) -> p h x", x=64)
                            P.tt(tav, xv[:, :, 0, :], cos4, ALU.mult, [skey, K('rp_t')], [ta.name], eng=eng)
                            P.tt(tbv, xv[:, :, 1, :], sin4, ALU.mult, [skey, K('rp_t')], [tb.name], eng=eng)
                            P.tt(ov_[:, :, 0, :], tav, tbv, ALU.subtract, [ta.name, tb.name], [dkey], eng=eng)
                            P.tt(tav, xv[:, :, 0, :], sin4, ALU.mult, [skey, K('rp_t')], [ta.name], eng=eng)
                            P.tt(tbv, xv[:, :, 1, :], cos4, ALU.mult, [skey, K('rp_t')], [tb.name], eng=eng)
                            P.tt(ov_[:, :, 1, :], tav, tbv, ALU.add, [ta.name, tb.name], [dkey], eng=eng)
                    gsrc_, gkey_, ksl_, kkey_ = srcs(pb)
                    for hf in range(HW // 512):
                        cs = slice(hf * 512, (hf + 1) * 512)
                        P.mm(pc[:C, :], Mcat[:, 321:385], gsrc_[:, cs], True, True, ['Mcat', gkey_], ['pc'])
                        P.act(Ek[:], pc[:C, :], AF.Exp, ['pc'], ['Ek'])
                        P.tt(khat[:, pb, cs], ksl_(hf * 512, (hf + 1) * 512), Ek[:], ALU.mult, [kkey_, 'Ek'], [('khat', pb, hf)])

                def srcs(pb):
                    K = lambda nm: (nm, pb)
                    if name == 'rt':
                        gsrc_ = gconst; gkey_ = 'gconst'
                    else:
                        gsrc_ = graw[:, pb, :]; gkey_ = K('graw')
                    if name == 'gl':
                        ksl_ = lambda a, b: k_t[:, pb, a:b]; kkey_ = K('k_t')
                    else:
                        ksl_ = lambda a, b: kt[:, pb, a:b]; kkey_ = K('kt')
                    return gsrc_, gkey_, ksl_, kkey_

                def emit_front(si, h):
                    pb = si % 2; sl = h % 2; hs = slice(h * 128, (h + 1) * 128)
                    K = lambda nm: (nm, pb)
                    gsrc_, gkey_, ksl_, kkey_ = srcs(pb)
                    if name == 'rt':
                        qsrc = qr[:, pb, 0:512]; qkey = K('qr')
                    else:
                        qsrc = q_t[:, pb, :]; qkey = K('q_t')
                    pP = pPs[sl]; pT = pTs[sl]; kP = 'pP%d' % sl; kT = 'pT%d' % sl
                    P.mm(pP[:, 0:321], gsrc_[:, hs], Mcat[:, 0:321], True, True, [gkey_, 'Mcat'], [kP])
                    P.act(E13[:, sl, :], pP[:, 0:321], AF.Exp, [kP], [('E13', sl)])
                    if not need_o(order[si]):
                        return
                    P.ts(E13[:, sl, 64:192], E13[:, sl, 64:192], 2.0e17, ALU.min, [('E13', sl)], [('E13', sl)])
                    P.tr(pT[:, 0, 0:C], qsrc[:, hs], self.identb[:C, :C], [qkey, 'identb'], [kT])
                    P.tr(pT[:, 1, 0:C], ksl_(h * 128, (h + 1) * 128), self.identb[:C, :C], [kkey_, 'identb'], [kT])
                    P.stt(qtl[:, sl, :], pT[:, 0, 0:C], br['qs'], E13[:, sl, 0:64], ALU.mult, ALU.mult, [kT, ('E13', sl)], [('qtl', sl)])
                    P.stt(qtd[:, sl, :], pT[:, 0, 0:C], br['qs'], E13[:, sl, 64:128], ALU.mult, ALU.mult, [kT, ('E13', sl)], [('qtd', sl)])
                    P.stt(qtp[:, sl, :], pT[:, 0, 0:C], br['qs'], E13[:, sl, 256:320], ALU.mult, ALU.mult, [kT, ('E13', sl)], [('qtp', sl)])
                    kbase = kdz[:, sl, :]
                    kd_out = bass.AP(kbase.tensor, kbase.offset, [list(kbase.ap[0]), [96, 2], [1, 32]])
                    P.tt(kd_out, pT[:, 1, 0:C].rearrange("p (a b) -> p a b", b=32), E13[:, sl, 128:192].rearrange("p (a b) -> p a b", b=32),
                         ALU.mult, [kT, ('E13', sl)], ['kdz%d' % sl])
                    lo, hi = (0, 32) if d == 0 else (32, 64)
                    P.tt(koz[:, sl, 0, lo:hi], pT[:, 1, lo:hi], E13[:, sl, 192 + lo:192 + hi], ALU.mult,
                         [kT, ('E13', sl)], ['koz%d' % sl])

                def emit_back(si, h):
                    pb = si % 2; sl = h % 2; hs = slice(h * 128, (h + 1) * 128)
                    K = lambda nm: (nm, pb)
                    offI = 1 if d == 0 else 0
                    for I in (range(2) if need_o(order[si]) else ()):
                        has_off = (I == offI)
                        P.mm(pS[:C, 32 * I:32 * I + 32], kdz[:, sl, 64 * I:64 * I + 64], qtd[:, sl, 32 * I:32 * I + 32], True, not has_off,
                             ['kdz%d' % sl, ('qtd', sl)], ['pS'])
                        if has_off:
                            P.mm(pS[:C, 32 * I:32 * I + 32], koz[:, sl, 0, :], qtl[:, sl, 32 * I:32 * I + 32], False, True,
                                 ['koz%d' % sl, ('qtl', sl)], ['pS'])
                    if need_o(order[si]):
                        P.tt(scb[:, sl, :], pS[:C, 0:C], Mcat[:, 385:449], ALU.mult, ['pS', 'Mcat'], [('scb', sl)])
                    for ec in (range(dvc) if need_o(order[si]) else ()):
                        col = h * dv + ec * 128; ci = col // 128; so = ci % 4
                        P.mm(pO[:, so, :], v_t[:, pb, col:col + 128], scb[:, sl, :], True, False, [K('v_t'), ('scb', sl)], ['pO'])
                        P.mm(pO[:, so, :], Sb[:, col:col + 128], qtp[:, sl, :], False, True, [('Sb', h), ('qtp', sl)], ['pO'])
                        if d == 0:
                            P.cp(ofs[:, pb, ci, :], pO[:, so, :], ['pO'], [('ofs', pb)], eng='act')
                        else:
                            P.tt(osum[:, pb, ci, :], pO[:, so, :], of_t[:, pb, ci, :], ALU.add, ['pO', K('of_t')], [K('osum')])
                    P.mm(pU[:, 0, 0:dv], khat[:, pb, hs], v_t[:, pb, h * dv:(h + 1) * dv], True, True,
                         [('khat', pb, (h * 128) // 512), K('v_t')], ['pU'])
                    P.stt(S[:, h * dv:(h + 1) * dv], S[:, h * dv:(h + 1) * dv], E13[:, sl, 320:321], pU[:, 0, 0:dv],
                          ALU.mult, ALU.add, [('S', h), ('E13', sl), 'pU'], [('S', h)])
                    P.cp(Sb[:, h * dv:(h + 1) * dv], S[:, h * dv:(h + 1) * dv], [('S', h)], [('Sb', h)], eng='act')

                def emit_out(si):
                    n = order[si]; t0 = n * C; pb = si % 2
                    K = lambda nm: (nm, pb)
                    if not need_o(n):
                        return
                    if d == 0:
                        P.dma('act', ofv[:, :, t0:t0 + C], ofs[:, pb], reads=[('ofs', pb)], writes=[('ofT', n)])
                        return
                    osm = osum[:, pb]
                    P.act(sg[:], gate_t[:, pb], AF.Silu, [K('gate_t')], ['sg'])
                    if br['norm'] == 'ln':
                        for h in range(H):
                            for ec in range(dvc):
                                P.mm(pc[:, h * C:(h + 1) * C], self.onesf[:], osm[:, h * dvc + ec, :], ec == 0, ec == dvc - 1, ['onesf', K('osum')], ['pc'])
                        P.ts(mean[:, 0:H, :], pc[:, 0:H * C].rearrange("p (h c) -> p h c", c=C), 1.0 / dv, ALU.mult, ['pc'], ['mean'])
                        for ci in range(8):
                            P.tt(osm[:, ci, :], osm[:, ci, :], mean[:, ci // dvc, :], ALU.subtract, [K('osum'), 'mean'], [K('osum')])
                    P.act(sqt[:], osm, AF.Square, [K('osum')], ['sqt'])
                    for h in range(H):
                        for ec in range(dvc):
                            P.mm(pc[:, h * C:(h + 1) * C], self.onesf[:], sqt[:, h * dvc + ec, :], ec == 0, ec == dvc - 1, ['onesf', 'sqt'], ['pc'])
                    P.act(rs[:, 0:H, :], pc[:, 0:H * C].rearrange("p (h c) -> p h c", c=C), AF.Sqrt, ['pc'], ['rs'], bias=EPS, scale=1.0 / dv)
                    P.op('dve', lambda e: e.reciprocal(out=rs[:, 0:H, :], in_=rs[:, 0:H, :]), ['rs'], ['rs'])
                    for ci in range(8):
                        P.stt(otmp[:, ci, :], osm[:, ci, :], gn[:, (ci % dvc):(ci % dvc) + 1], rs[:, ci // dvc, :], ALU.mult, ALU.mult,
                              [K('osum'), 'gn', 'rs'], ['otmp'])
                    P.tt(ob[:, pb], otmp[:], sg[:], ALU.mult, ['otmp', 'sg'], [('ob', pb)])
                    P.dma('act', ov[:, :, t0:t0 + C], ob[:, pb], reads=[('ob', pb)], writes=[('oT', n)])

                nch = len(order)
                PIPE = getattr(c, 'pipe', True)
                if PIPE:
                    emit_prep(0)
                for si in range(nch):
                    if not PIPE:
                        emit_prep(si)
                        for h in range(H):
                            emit_front(si, h)
                            emit_back(si, h)
                        emit_out(si)
                        continue
                    emit_front(si, 0)
                    for h in range(H):
                        if h + 1 < H:
                            emit_front(si, h + 1)
                        emit_back(si, h)
                        if h == H // 2 - 1 and si + 1 < nch:
                            emit_prep(si + 1)
                    emit_out(si)
                P.barrier()

    def phase_gla2(self, layer, name, last=False):
        c = self.cfg; P = self.P; T = self.T; L = c.L; CTX = c.CTX
        nlo = c.need_lo(last); nhi = c.need_hi(last)
        from contextlib import ExitStack
        br = self.BR[name]; H = br['H']; dv = br['dv']; dvc = dv // 128; HW = H * 128; C = 128
        assert H == 4 and dv == 256
        NCH = L // C; ctxch = CTX // C
        orders = [[n for n in range(NCH) if n * C < nhi], list(range(ctxch - 1, -1, -1)) + list(range(NCH - 1, ctxch - 1, -1))]
        need_o = lambda n: (nlo <= n * C < nhi)
        ofv = T['ofT'][br['ofrow']:br['ofrow'] + 1024, :].rearrange("(c p) t -> p c t", p=128)
        ov = T['oT'][br['orow']:br['orow'] + 1024, :].rearrange("(c p) t -> p c t", p=128)
        gv = T['fmb'][br['gate']:br['gate'] + 1024, :].rearrange("(c p) t -> p c t", p=128)
        sb = self.sb; ps = self.ps
        with ExitStack() as es:
            def mk(nm, shape, dt=F32):
                return es.enter_context(sb(nm, shape, dt))
            M = mk("M2", [128, 513]); q_t = mk("q_t2", [128, 2, 512], BF16); v_t = mk("v_t2", [128, 2, 1024], BF16)
            k_t = mk("k_t2", [128, 2, 512], BF16); lr_t = mk("lr_t2", [17, 2, C]); rp_t = mk("rp_t2", [128, 2, 512])
            of_t = mk("of_t2", [128, 2, 8, C]); gate_t = mk("gate_t2", [128, 2, 8, C], BF16)
            sig = mk("sig2", [128, 512]); graw = mk("graw2", [128, 2, 512]); kt = mk("kt2", [128, 2, 512], BF16)
            qr = mk("qr2", [128, 2, 512], BF16); khat = mk("khat2", [128, 2, 512], BF16); Ek = mk("Ek2", [128, 512])
            rt1 = mk("rt1b", [128, 256]); rt2 = mk("rt2b", [128, 256]); rt3 = mk("rt3b", [128, 256]); rt4 = mk("rt4b", [128, 256])
            E = mk("E_2", [128, 2, 257]); E2 = mk("E2_2", [128, 2, 128])
            qtl = mk("qtl2", [128, 2, C], BF16); qtp = mk("qtp2", [128, 2, C], BF16); ktl = mk("ktl2", [128, 2, C], BF16)
            scb = mk("scb2", [128, 2, C], BF16); ofs = mk("ofs2", [128, 2, 8, C]); osum = mk("osum2", [128, 2, 8, C])
            sg = mk("sg2_", [128, 8, C]); sqt = mk("sqt2", [128, 8, C]); rs = mk("rs2", [128, 8, C]); mean = mk("mean2", [128, 8, C])
            otmp = mk("otmp2", [128, 8, C]); ob = mk("ob2", [128, 2, 8, C], BF16)
            S = mk("S2", [128, 1024]); Sb = mk("Sb2", [128, 1024], BF16)
            wg = mk("wg2", [17, 512]); gn = mk("gn2", [128, 2]); rl = mk("rl2", [128, 4]); gconst = mk("gconst2", [128, 512])
            pc = es.enter_context(ps("pc2", [128, 512], F32))
            pPs = [es.enter_context(ps("pPa", [128, 512], F32)), es.enter_context(ps("pPb", [128, 512], F32))]
            pTs = [es.enter_context(ps("pTa", [128, 8, 128], BF16)), es.enter_context(ps("pTb", [128, 8, 128], BF16))]
            pS = es.enter_context(ps("pS2", [128, 512], F32)); pO = es.enter_context(ps("pO2", [128, 4, C], F32))
            pU = es.enter_context(ps("pU2", [128, 512], F32))
            P.dma('sp', gn[:], T['gla_ngT' if name == 'gl' else 'ret_ngT'][layer], writes=['gn'])
            P.memset(lr_t[:], 1.0, ['lr0', 'lr1'])
            for d in (0, 1):
                gs = br['gs']
                P.dma('sp', M[:, :], T['gl_consts2'][d], writes=['M'])
                P.ts(M[:, 0:385], M[:, 0:385], gs, ALU.mult, ['M'], ['M'])
                if name == 'gl':
                    P.dma('sp', wg[:], T['gla_wg'][layer, d], writes=['wg'])
                else:
                    P.dma('sp', rl[:], T['ret_lrep'][layer, d], writes=['rl'])
                    P.act(rl[:], rl[:], AF.Exp, ['rl'], ['rl'], scale=-1.0)
                    P.act(rl[:], rl[:], AF.Ln, ['rl'], ['rl'], bias=1.0)
                    P.ts(rl[:], rl[:], -1.0, ALU.mult, ['rl'], ['rl'])
                    for h in range(4):
                        P.ts(gconst[:, h * 128:(h + 1) * 128], self.onesf[:, :], rl[:, h:h + 1], ALU.mult, ['onesf', 'rl'], ['gconst'])
                P.memset(S[:], 0.0, [('S', h) for h in range(H)])
                P.memset(Sb[:], 0.0, [('Sb', h) for h in range(H)])
                order = orders[d]

                def srcs(pb):
                    K = lambda nm: (nm, pb)
                    if name == 'rt':
                        return gconst[:, :], 'gconst', (lambda a, b: kt[:, pb, a:b]), K('kt'), qr[:, pb, :], K('qr')
                    return graw[:, pb, :], K('graw'), (lambda a, b: k_t[:, pb, a:b]), K('k_t'), q_t[:, pb, :], K('q_t')

                def emit_prep(si):
                    n = order[si]; t0 = n * C; pb = si % 2
                    K = lambda nm: (nm, pb)
                    P.dma('sp', q_t[:, pb, :], T['tmb'][t0:t0 + C, br['q']:br['q'] + HW], writes=[K('q_t')])
                    P.dma('sp', v_t[:, pb, :], T['tmb'][t0:t0 + C, br['v']:br['v'] + 1024], writes=[K('v_t')])
                    P.dma('sp', k_t[:, pb, :], T['tmb'][t0:t0 + C, br['k']:br['k'] + HW], writes=[K('k_t')])
                    if name == 'gl':
                        P.dma('sp', lr_t[0:16, pb, :], T['lrT'][d * 16:(d + 1) * 16, t0:t0 + C], writes=[('lr%d' % pb)])
                    else:
                        P.dma('sp', rp_t[:, pb, :], T['rope'][t0:t0 + C, :], writes=[K('rp_t')])
                    if d == 1 and need_o(n):
                        P.dma('sp', of_t[:, pb], ofv[:, :, t0:t0 + C], writes=[K('of_t')])
                        P.dma('sp', gate_t[:, pb], gv[:, :, t0:t0 + C], writes=[K('gate_t')])
                    if name == 'gl':
                        P.mm(pc[:, :], lr_t[0:17, pb, :], wg[0:17, :], True, True, ['lr%d' % pb, 'wg'], ['pc'])
                        P.act(sig[:], pc[:, :], AF.Exp, ['pc'], ['sig'], scale=-1.0)
                        P.act(graw[:, pb, :], sig[:], AF.Ln, ['sig'], [K('graw')], bias=1.0)
                    else:
                        cos4 = rp_t[:, pb, 0:256].rearrange("p (h x) -> p h x", x=64)
                        sin4 = rp_t[:, pb, 256:512].rearrange("p (h x) -> p h x", x=64)
                        for (src, skey, dst, dkey, eng, ta, tb) in ((q_t, K('q_t'), qr, K('qr'), 'dve', rt1, rt2), (k_t, K('k_t'), kt, K('kt'), 'pool', rt3, rt4)):
                            xv = src[:, pb, :].rearrange("p (h two x) -> p h two x", two=2, x=64)
                            ov_ = dst[:, pb, :].rearrange("p (h two x) -> p h two x", two=2, x=64)
                            tav = ta[:].rearrange("p (h x) -> p h x", x=64); tbv = tb[:].rearrange("p (h x) -> p h x", x=64)
                            P.tt(tav, xv[:, :, 0, :], cos4, ALU.mult, [skey, K('rp_t')], [ta.name], eng=eng)
                            P.tt(tbv, xv[:, :, 1, :], sin4, ALU.mult, [skey, K('rp_t')], [tb.name], eng=eng)
                            P.tt(ov_[:, :, 0, :], tav, tbv, ALU.subtract, [ta.name, tb.name], [dkey], eng=eng)
                            P.tt(tav, xv[:, :, 0, :], sin4, ALU.mult, [skey, K('rp_t')], [ta.name], eng=eng)
                            P.tt(tbv, xv[:, :, 1, :], cos4, ALU.mult, [skey, K('rp_t')], [tb.name], eng=eng)
                            P.tt(ov_[:, :, 1, :], tav, tbv, ALU.add, [ta.name, tb.name], [dkey], eng=eng)
                    gsrc_, gkey_, ksl_, kkey_, qsrc_, qkey_ = srcs(pb)
                    P.mm(pc[:, :], M[:, 257:385], gsrc_, True, True, ['M', gkey_], ['pc'])
                    P.act(Ek[:], pc[:, :], AF.Exp, ['pc'], ['Ek'])
                    P.tt(khat[:, pb, :], ksl_(0, 512), Ek[:], ALU.mult, [kkey_, 'Ek'], [K('khat')])

                def emit_front(si, h):
                    pb = si % 2; sl = h % 2; hs = slice(h * 128, (h + 1) * 128)
                    gsrc_, gkey_, ksl_, kkey_, qsrc_, qkey_ = srcs(pb)
                    pP = pPs[sl]; pT = pTs[sl]; kP = 'pP%d' % sl; kT = 'pT%d' % sl
                    P.mm(pP[:, 0:257], gsrc_[:, hs], M[:, 0:257], True, True, [gkey_, 'M'], [kP])
                    P.act(E[:, sl, :], pP[:, 0:257], AF.Exp, [kP], [('E', sl)])
                    if not need_o(order[si]):
                        return
                    P.act(E2[:, sl, :], pP[:, 0:128], AF.Exp, [kP], [('E2', sl)], scale=-1.0)
                    P.tr(pT[:, 0, 0:C], qsrc_[:, hs], self.identb[:, :], [qkey_, 'identb'], [kT])
                    P.tr(pT[:, 1, 0:C], ksl_(h * 128, (h + 1) * 128), self.identb[:, :], [kkey_, 'identb'], [kT])
                    P.stt(qtl[:, sl, :], pT[:, 0, 0:C], br['qs'], E[:, sl, 0:128], ALU.mult, ALU.mult, [kT, ('E', sl)], [('qtl', sl)])
                    P.stt(qtp[:, sl, :], pT[:, 0, 0:C], br['qs'], E[:, sl, 128:256], ALU.mult, ALU.mult, [kT, ('E', sl)], [('qtp', sl)])
                    P.tt(ktl[:, sl, :], pT[:, 1, 0:C], E2[:, sl, :], ALU.mult, [kT, ('E2', sl)], [('ktl', sl)])

                def emit_back(si, h):
                    pb = si % 2; sl = h % 2; hs = slice(h * 128, (h + 1) * 128)
                    K = lambda nm: (nm, pb)
                    if need_o(order[si]):
                        P.mm(pS[:, 0:C], ktl[:, sl, :], qtl[:, sl, :], True, True, [('ktl', sl), ('qtl', sl)], ['pS'])
                        P.tt(scb[:, sl, :], pS[:, 0:C], M[:, 385:513], ALU.mult, ['pS', 'M'], [('scb', sl)])
                    for ec in (range(dvc) if need_o(order[si]) else ()):
                        col = h * dv + ec * 128; ci = col // 128; so = ci % 4
                        P.mm(pO[:, so, :], v_t[:, pb, col:col + 128], scb[:, sl, :], True, False, [K('v_t'), ('scb', sl)], ['pO'])
                        P.mm(pO[:, so, :], Sb[:, col:col + 128], qtp[:, sl, :], False, True, [('Sb', h), ('qtp', sl)], ['pO'])
                        if d == 0:
                            P.cp(ofs[:, pb, ci, :], pO[:, so, :], ['pO'], [('ofs', pb)], eng='act')
                        else:
                            P.tt(osum[:, pb, ci, :], pO[:, so, :], of_t[:, pb, ci, :], ALU.add, ['pO', K('of_t')], [K('osum')])
                    P.mm(pU[:, 0:dv], khat[:, pb, hs], v_t[:, pb, h * dv:(h + 1) * dv], True, True, [K('khat'), K('v_t')], ['pU'])
                    P.stt(S[:, h * dv:(h + 1) * dv], S[:, h * dv:(h + 1) * dv], E[:, sl, 256:257], pU[:, 0:dv],
                          ALU.mult, ALU.add, [('S', h), ('E', sl), 'pU'], [('S', h)])
                    P.cp(Sb[:, h * dv:(h + 1) * dv], S[:, h * dv:(h + 1) * dv], [('S', h)], [('Sb', h)], eng='act')

                def emit_out(si):
                    n = order[si]; t0 = n * C; pb = si % 2
                    K = lambda nm: (nm, pb)
                    if not need_o(n):
                        return
                    if d == 0:
                        P.dma('act', ofv[:, :, t0:t0 + C], ofs[:, pb], reads=[('ofs', pb)], writes=[('ofT', n)])
                        return
                    osm = osum[:, pb]
                    P.act(sg[:], gate_t[:, pb], AF.Silu, [K('gate_t')], ['sg'])
                    if br['norm'] == 'ln':
                        for h in range(H):
                            for ec in range(dvc):
                                P.mm(pc[:, h * C:(h + 1) * C], self.onesf[:], osm[:, h * dvc + ec, :], ec == 0, ec == dvc - 1, ['onesf', K('osum')], ['pc'])
                        P.ts(mean[:, 0:H, :], pc[:, 0:H * C].rearrange("p (h c) -> p h c", c=C), 1.0 / dv, ALU.mult, ['pc'], ['mean'])
                        for ci in range(8):
                            P.tt(osm[:, ci, :], osm[:, ci, :], mean[:, ci // dvc, :], ALU.subtract, [K('osum'), 'mean'], [K('osum')])
                    P.act(sqt[:], osm, AF.Square, [K('osum')], ['sqt'])
                    for h in range(H):
                        for ec in range(dvc):
                            P.mm(pc[:, h * C:(h + 1) * C], self.onesf[:], sqt[:, h * dvc + ec, :], ec == 0, ec == dvc - 1, ['onesf', 'sqt'], ['pc'])
                    P.act(rs[:, 0:H, :], pc[:, 0:H * C].rearrange("p (h c) -> p h c", c=C), AF.Sqrt, ['pc'], ['rs'], bias=EPS, scale=1.0 / dv)
                    P.op('dve', lambda e: e.reciprocal(out=rs[:, 0:H, :], in_=rs[:, 0:H, :]), ['rs'], ['rs'])
                    for ci in range(8):
                        P.stt(otmp[:, ci, :], osm[:, ci, :], gn[:, (ci % dvc):(ci % dvc) + 1], rs[:, ci // dvc, :], ALU.mult, ALU.mult,
                              [K('osum'), 'gn', 'rs'], ['otmp'])
                    P.tt(ob[:, pb], otmp[:], sg[:], ALU.mult, ['otmp', 'sg'], [('ob', pb)])
                    P.dma('act', ov[:, :, t0:t0 + C], ob[:, pb], reads=[('ob', pb)], writes=[('oT', n)])

                nch = len(order)
                emit_prep(0)
                for si in range(nch):
                    emit_front(si, 0)
                    for h in range(H):
                        if h + 1 < H:
                            emit_front(si, h + 1)
                        emit_back(si, h)
                        if h == H // 2 - 1 and si + 1 < nch:
                            emit_prep(si + 1)
                    emit_out(si)
                P.barrier()

    def s5_powers(self, src_re, src_im, src_dt, Pn, Fn, pw_re, pw_im, coef_re, coef_im, tag):
        P = self.P
        from contextlib import ExitStack
        with ExitStack() as es:
            def mk(nm, shape, dt=F32):
                return es.enter_context(self.sb(tag + nm, shape, dt))
            lre = mk("lre", [Pn, Fn]); lim = mk("lim", [Pn, Fn]); dtv = mk("dtv", [Pn, Fn])
            a = mk("a", [Pn, Fn]); th = mk("th", [Pn, Fn]); kki = mk("kki", [Pn, 17, Fn], I32); kk = mk("kk", [Pn, 17, Fn])
            A = mk("A", [Pn, 17, Fn]); TH = mk("TH", [Pn, 17 * Fn]); SN = mk("SN", [Pn, 17 * Fn]); CS = mk("CS", [Pn, 17 * Fn])
            t1 = mk("t1", [Pn, Fn]); t2 = mk("t2", [Pn, Fn]); den = mk("den", [Pn, Fn])
            P.dma('sp', lre[:], src_re, writes=['lre']); P.dma('sp', lim[:], src_im, writes=['lim']); P.dma('sp', dtv[:], src_dt, writes=['dtv'])
            P.act(dtv[:], dtv[:], AF.Exp, ['dtv'], ['dtv'])
            P.ts(lre[:], lre[:], -1e-4, ALU.min, ['lre'], ['lre'])
            P.tt(a[:], lre[:], dtv[:], ALU.mult, ['lre', 'dtv'], ['a'])
            P.tt(th[:], lim[:], dtv[:], ALU.mult, ['lim', 'dtv'], ['th'])
            P.op('pool', lambda e: e.iota(kki[:], pattern=[[1, 17], [0, Fn]], base=0, channel_multiplier=0), [], ['kki'])
            P.cp(kk[:], kki[:], ['kki'], ['kk'])
            P.tt(A[:], kk[:], a[:].unsqueeze(1).to_broadcast([Pn, 17, Fn]), ALU.mult, ['kk', 'a'], ['A'])
            P.tt(TH[:].rearrange("p (k f) -> p k f", f=Fn), kk[:], th[:].unsqueeze(1).to_broadcast([Pn, 17, Fn]), ALU.mult, ['kk', 'th'], [tag + 'scx'])
            P.act(A[:], A[:], AF.Exp, ['A'], ['A'])
            self.sincos(TH[:], [Pn, 17 * Fn], SN[:], CS[:], tag + 'sc')
            P.tt(pw_re, A[:], CS[:].rearrange("p (k f) -> p k f", f=Fn), ALU.mult, ['A'], ['pwre'])
            P.tt(pw_im, A[:], SN[:].rearrange("p (k f) -> p k f", f=Fn), ALU.mult, ['A'], ['pwim'])
            if coef_re is not None:
                P.ts(t1[:], pw_re[:, 1, :], -1.0, ALU.add, ['pwre'], ['t1'])
                P.tt(den[:], lre[:], lre[:], ALU.mult, ['lre'], ['den'])
                P.tt(t2[:], lim[:], lim[:], ALU.mult, ['lim'], ['t2'])
                P.tt(den[:], den[:], t2[:], ALU.add, ['den', 't2'], ['den'])
                P.op('dve', lambda e: e.reciprocal(out=den[:], in_=den[:]), ['den'], ['den'])
                P.tt(coef_re, t1[:], lre[:], ALU.mult, ['t1', 'lre'], ['cre'])
                P.tt(t2[:], pw_im[:, 1, :], lim[:], ALU.mult, ['pwim', 'lim'], ['t2'])
                P.tt(coef_re, coef_re, t2[:], ALU.add, ['cre', 't2'], ['cre'])
                P.tt(coef_re, coef_re, den[:], ALU.mult, ['cre', 'den'], ['cre'])
                P.tt(coef_im, pw_im[:, 1, :], lre[:], ALU.mult, ['pwim', 'lre'], ['cim'])
                P.tt(t2[:], t1[:], lim[:], ALU.mult, ['t1', 'lim'], ['t2'])
                P.tt(coef_im, coef_im, t2[:], ALU.subtract, ['cim', 't2'], ['cim'])
                P.tt(coef_im, coef_im, den[:], ALU.mult, ['cim', 'den'], ['cim'])
            P.barrier()

    def cmul(self, out_re, out_im, a_re, a_im, b_re, b_im, t, neg_im=False, eng='dve'):
        P = self.P
        P.tt(out_re, a_re, b_re, ALU.mult, ['cm_in'], ['cm_re'], eng=eng)
        P.tt(t, a_im, b_im, ALU.mult, ['cm_in'], ['cm_t'], eng=eng)
        P.tt(out_re, out_re, t, ALU.subtract, ['cm_re', 'cm_t'], ['cm_re'], eng=eng)
        P.tt(out_im, a_re, b_im, ALU.mult, ['cm_in'], ['cm_im'], eng=eng)
        P.tt(t, a_im, b_re, ALU.mult, ['cm_in', 'cm_re'], ['cm_t'], eng=eng)
        if neg_im:
            P.stt(out_im, out_im, -1.0, t, ALU.mult, ALU.subtract, ['cm_im', 'cm_t'], ['cm_im'])
        else:
            P.tt(out_im, out_im, t, ALU.add, ['cm_im', 'cm_t'], ['cm_im'], eng=eng)

    def phase_s5(self, layer, last=False):
        c = self.cfg; P = self.P; T = self.T; L = c.L; CTX = c.CTX
        nlo = c.need_lo(last); nhi = c.need_hi(last)
        otiles = tiles(nlo, nhi, 512) if last else tiles(0, L, 512)
        from contextlib import ExitStack
        SB = 16; NB = L // SB; NBc = CTX // SB; NBP = NB + 2
        sb = self.sb; ps = self.ps
        urows = T['fmb'][2048:3072, :]
        for d in (0, 1):
            with ExitStack() as esd:
                KtL = esd.enter_context(sb("KtL", [128, 8, 16, 128], BF16))
                PWPr = esd.enter_context(sb("PWPr", [128, 17, 32], F32)); PWPi = esd.enter_context(sb("PWPi", [128, 17, 32], F32))
                self.s5_powers(T['s5_lamP'][layer, d, 0], T['s5_lamP'][layer, d, 1], T['s5_dtP'][layer, d], 128, 32,
                               PWPr[:], PWPi[:], None, None, 'pp')
                with ExitStack() as es:
                    def mk(nm, shape, dt=F32):
                        return es.enter_context(sb(nm, shape, dt))
                    pwr = mk("pwr", [64, 17, 64]); pwi = mk("pwi", [64, 17, 64]); cre = mk("cre", [64, 64]); cim = mk("cim", [64, 64])
                    self.s5_powers(T['s5_lam64'][layer, d, 0], T['s5_lam64'][layer, d, 1], T['s5_dt64'][layer, d], 64, 64,
                                   pwr[:], pwi[:], cre[:], cim[:], 'p64')
                    B64 = mk("B64", [64, 2, 64, 16]); C64 = mk("C64", [64, 2, 64, 16])
                    bbr = mk("bbr", [64, 64, 16]); bbi = mk("bbi", [64, 64, 16]); tt_ = mk("tt_", [64, 64, 16])
                    Wr = mk("Wr", [64, 64, 16]); Wi = mk("Wi", [64, 64, 16])
                    Wpr = mk("Wpr", [64, 64 * 128], BF16); Wpi = mk("Wpi", [64, 64 * 128], BF16)
                    Cpr = mk("Cpr", [64, 64 * 128], BF16); Cpi = mk("Cpi", [64, 64 * 128], BF16)
                    pk0 = es.enter_context(ps("pk0", [128, 512], F32)); pk1 = es.enter_context(ps("pk1", [128, 512], F32))
                    pks = [pk0, pk1]
                    P.dma('sp', B64[:], T['s5_b64'][layer, d].rearrange("x p g h -> p x g h"), writes=['B64'])
                    P.dma('sp', C64[:], T['s5_c64'][layer, d].rearrange("x p g h -> p x g h"), writes=['C64'])
                    for t_ in (Wpr, Wpi, Cpr, Cpi):
                        P.memset(t_[:], 0.0, ['pad' + t_.name])
                    P.barrier()
                    bc = lambda ap: ap.unsqueeze(2).to_broadcast([64, 64, 16])
                    self.cmul(bbr[:], bbi[:], bc(cre[:]), bc(cim[:]), B64[:, 0], B64[:, 1], tt_[:])
                    P.barrier()

                    def diag(t_):
                        b_ = t_[:]
                        return bass.AP(b_.tensor, b_.offset, [list(b_.ap[0]), [1024, 8], [144, 8], [1, 16]])
                    P.cp(diag(Cpr), C64[:, 0].rearrange("p (a b) h -> p a b h", b=8), [], ['cpr'])
                    P.cp(diag(Cpi), C64[:, 1].rearrange("p (a b) h -> p a b h", b=8), [], ['cpi'])
                    P.barrier()
                    for tau in range(16):
                        self.cmul(Wr[:], Wi[:], bc(pwr[:, tau, :]), bc(pwi[:, tau, :]), bbr[:], bbi[:], tt_[:], neg_im=True)
                        P.cp(diag(Wpr), Wr[:].rearrange("p (a b) h -> p a b h", b=8), ['cm_re'], ['Wpr'])
                        P.cp(diag(Wpi), Wi[:].rearrange("p (a b) h -> p a b h", b=8), ['cm_im'], ['Wpi'])
                        for gb in range(8):
                            pk = pks[gb % 2]; pkey = 'pk%d' % (gb % 2)
                            for g8 in range(8):
                                g = gb * 8 + g8
                                P.mm(pk[:, 0:128], Wpr[:, g * 128:(g + 1) * 128], Cpr[:, g * 128:(g + 1) * 128], g8 == 0, False, ['Wpr'], [pkey])
                                P.mm(pk[:, 0:128], Wpi[:, g * 128:(g + 1) * 128], Cpi[:, g * 128:(g + 1) * 128], False, g8 == 7, ['Wpi'], [pkey])
                            P.cp(KtL[:, gb, tau, :], pk[:, 0:128], [pkey], ['KtL'], eng=('act' if gb % 2 else 'dve'))
                    P.barrier()
                with ExitStack() as es:
                    pfr = es.enter_context(sb("pfrA", [128, 17, 64], F32)); pfi = es.enter_context(sb("pfiA", [128, 17, 64], F32))
                    cfr = es.enter_context(sb("cfrA", [128, 64], F32)); cfi = es.enter_context(sb("cfiA", [128, 64], F32))
                    for gb in range(8):
                        self.s5_powers(T['s5_lamF'][layer, d, 0][:, gb, :], T['s5_lamF'][layer, d, 1][:, gb, :], T['s5_dtF'][layer, d][:, gb, :],
                                       128, 64, pfr[:], pfi[:], cfr[:], cfi[:], 'pf')
                        P.dma('sp', T['s5pf'][gb, 0], pfr[:], reads=[], writes=['d1'])
                        P.dma('sp', T['s5pf'][gb, 1], pfi[:], reads=[], writes=['d2'])
                        P.dma('sp', T['s5cf'][gb, 0], cfr[:], reads=[], writes=['d3'])
                        P.dma('sp', T['s5cf'][gb, 1], cfi[:], reads=[], writes=['d4'])
                        P.barrier()
                with ExitStack() as es:
                    def mk(nm, shape, dt=F32):
                        return es.enter_context(sb(nm, shape, dt))
                    AA = mk("AA", [128, 2, 32, NBP]); Ar = AA[:, 0]; Ai = AA[:, 1]
                    ar2 = mk("ar2", [128, 2, 32]); nai = mk("nai", [128, 32]); st_ = mk("st_", [128, 2, 32]); su_ = mk("su_", [128, 2, 32])
                    Pad = mk("Pad", [128, 4, 16, 2, 128], BF16)
                    Abr = mk("Abr", [128, 4, NBP], BF16); Abi = mk("Abi", [128, 4, NBP], BF16)
                    uT0 = mk("uT0", [128, L], BF16); uTs = [uT0, uT0]
                    maskQ = mk("maskQ", [128, 4, 128]); maskC = mk("maskC", [128, 2, 16])
                    pfr = mk("pfr", [128, 17, 64]); pfi = mk("pfi", [128, 17, 64]); cfr = mk("cfr", [128, 64]); cfi = mk("cfi", [128, 64])
                    BF_ = mk("BF_", [128, 2, 64]); bfr = mk("bfr", [128, 64]); bfi = mk("bfi", [128, 64]); tf = mk("tf", [128, 64])
                    Vr = mk("Vr", [128, 64]); Vi = mk("Vi", [128, 64])
                    cP = mk("cP", [128, 2, 32, 16]); CLr = mk("CLr", [128, 4, 16]); CLi = mk("CLi", [128, 4, 16]); tc_ = mk("tc_", [128, 4, 16])
                    s1 = mk("s1", [128, 32]); s2 = mk("s2", [128, 32]); s3 = mk("s3", [128, 32]); s4 = mk("s4", [128, 32])
                    ysb0 = mk("ysb0", [128, 512]); ysb1 = mk("ysb1", [128, 512]); yf_t = mk("yf_t", [128, 512]); y2 = mk("y2", [128, 512])
                    zb = mk("zb", [128, 512], BF16); dT = mk("dT", [128, 8])
                    pw0 = es.enter_context(ps("pw0", [128, 512], F32)); pw1 = es.enter_context(ps("pw1", [128, 512], F32))
                    py0 = es.enter_context(ps("py0", [128, 512], F32)); py1 = es.enter_context(ps("py1", [128, 512], F32))
                    P.dma('sp', maskQ[:], T['s5_maskQ'], writes=['maskQ'])
                    P.dma('sp', maskC[:], T['s5_maskC'], writes=['maskC'])
                    P.dma('sp', cP[:], T['s5_cP'][layer, d].rearrange("x p q h -> p x q h"), writes=['cP'])
                    P.dma('sp', dT[:], T['s5_dT'][layer], writes=['dT'])
                    P.memset(Pad[:], 0.0, ['Pad'])
                    P.memset(AA[:], 0.0, ['A'])
                    P.barrier()
                    nwp = 0
                    for gb in range(8):
                        uT = uTs[gb % 2]
                        P.dma('sp', uT[:], urows[gb * 128:(gb + 1) * 128, :], writes=[('uT', 0)])
                        P.dma('sp', pfr[:], T['s5pf'][gb, 0], writes=['cm_in'])
                        P.dma('sp', pfi[:], T['s5pf'][gb, 1], writes=['cm_in'])
                        P.dma('sp', cfr[:], T['s5cf'][gb, 0], writes=['cm_in'])
                        P.dma('sp', cfi[:], T['s5cf'][gb, 1], writes=['cm_in'])
                        P.dma('sp', BF_[:], T['s5_bF'][layer, d][:, :, gb, :].rearrange("x p f -> p x f"), writes=['cm_in'])
                        P.barrier()
                        self.cmul(bfr[:], bfi[:], cfr[:], cfi[:], BF_[:, 0], BF_[:, 1], tf[:])
                        P.barrier()
                        for j in range(16):
                            pw_ = (15 - j) if d == 0 else j
                            self.cmul(Vr[:], Vi[:], pfr[:, pw_, :], pfi[:, pw_, :], bfr[:], bfi[:], tf[:])
                            for x, V in ((0, Vr), (1, Vi)):
                                P.tt(Pad[:, :, j, x, :].rearrange("p q (a b) -> p q a b", b=64),
                                     V[:].unsqueeze(1).unsqueeze(1).to_broadcast([128, 4, 2, 64]),
                                     maskQ[:].rearrange("p q (a b) -> p q a b", b=64), ALU.mult,
                                     ['cm_re', 'cm_im', 'maskQ'], ['Pad'])
                        uv = uT[:].rearrange("p (n j) -> p n j", j=16)
                        for q in range(4):
                            pair = gb * 4 + q
                            for x, Ax in ((0, Ar), (1, Ai)):
                                pw = (pw0, pw1)[nwp % 2]; pwk = 'pw%d' % (nwp % 2); nwp += 1
                                for j in range(16):
                                    P.mm(pw[:, 0:NB], Pad[:, q, j, x, :], uv[:, :, j], j == 0, j == 15, ['Pad', ('uT', 0)], [pwk])
                                if d == 0:
                                    P.cp(Ax[:, pair, 1:NB + 1], pw[:, 0:NB], [pwk], ['A'], eng=('act' if x else 'dve'))
                                else:
                                    P.cp(Ax[:, pair, 0:NBc], pw[:, 0:NBc], [pwk], ['A'], eng=('act' if x else 'dve'))
                                    P.cp(Ax[:, pair, NBc + 1:NB + 1], pw[:, NBc:NB], [pwk], ['A'], eng=('act' if x else 'dve'))
                        P.barrier()
                    s5stop = getattr(c, 's5stop', 9)
                    if s5stop <= 2:
                        continue
                    ar = PWPr[:, 16, :]; ai = PWPi[:, 16, :]
                    P.cp(ar2[:, 0, :], ar, [], ['ar2']); P.cp(ar2[:, 1, :], ar, [], ['ar2'])
                    P.ts(nai[:], ai, -1.0, ALU.mult, [], ['nai'])
                    if d == 0:
                        steps = [(n + 1, n) for n in range(NB) if n * SB < nhi]
                    else:
                        steps = [(n, n + 1) for n in range(NBc - 1, -1, -1)] + ['copy'] + [(n + 1, n + 2) for n in range(NB - 1, NBc - 1, -1)]
                    for st in steps:
                        if st == 'copy':
                            P.cp(AA[:, :, :, NB + 1], AA[:, :, :, 0], ['A'], ['A'])
                            continue
                        pos, prev = st
                        P.tt(st_[:], AA[:, :, :, prev], ar2[:], ALU.mult, ['A', 'ar2'], ['st_'])
                        P.tt(su_[:, 0, :], AA[:, 1, :, prev], nai[:], ALU.mult, ['A', 'nai'], ['su_'])
                        P.tt(su_[:, 1, :], AA[:, 0, :, prev], ai, ALU.mult, ['A'], ['su_'])
                        P.tt(st_[:], st_[:], su_[:], ALU.add, ['st_', 'su_'], ['st_'])
                        P.tt(AA[:, :, :, pos], AA[:, :, :, pos], st_[:], ALU.add, ['A', 'st_'], ['A'])
                    P.barrier()
                    if s5stop <= 3:
                        continue
                    P.memset(Pad[:], 0.0, ['Pad'])
                    P.barrier()
                    npy = 0
                    for gb in range(8):
                        uT = uTs[gb % 2]
                        P.dma('sp', uT[:], urows[gb * 128:(gb + 1) * 128, :], writes=[('uT', 0)])
                        for i in range(16):
                            pw_ = (i + 1) if d == 0 else (16 - i)
                            bq = lambda ap: ap[:, gb * 4:gb * 4 + 4].unsqueeze(2).to_broadcast([128, 4, 16])
                            self.cmul(CLr[:], CLi[:], cP[:, 0, gb * 4:gb * 4 + 4, :], cP[:, 1, gb * 4:gb * 4 + 4, :],
                                      bq(PWPr[:, pw_, :]), bq(PWPi[:, pw_, :]), tc_[:], neg_im=True)
                            for x, CL in ((0, CLr), (1, CLi)):
                                pb_ = Pad[:, 0, i, x, :]
                                dst_ = bass.AP(pb_.tensor, pb_.offset, [list(pb_.ap[0]), [16 * 2 * 128 + 32, 4], [16, 2], [1, 16]])
                                P.tt(dst_, CL[:].unsqueeze(2).to_broadcast([128, 4, 2, 16]),
                                     maskC[:].unsqueeze(1).to_broadcast([128, 4, 2, 16]), ALU.mult,
                                     ['cm_re', 'cm_im', 'maskC'], ['Pad'])
                        uv = uT[:].rearrange("p (n j) -> p n j", j=16)
                        P.cp(Abr[:], Ar[:, gb * 4:gb * 4 + 4, :], ['A'], ['Ab'])
                        P.cp(Abi[:], Ai[:, gb * 4:gb * 4 + 4, :], ['A'], ['Ab'], eng='act')
                        for (t0, w) in otiles:
                            py = (py0, py1)[npy % 2]; pyk = 'py%d' % (npy % 2); ysb = (ysb0, ysb1)[npy % 2]; ysk = 'ysb%d' % (npy % 2); npy += 1
                            n0 = t0 // SB; nbt = w // SB
                            pv = py[:, 0:w].rearrange("p (n j) -> p n j", j=16)
                            runs = []
                            if d == 0:
                                runs.append((n0, nbt, n0))
                            else:
                                a0 = n0; a1 = min(n0 + nbt, NBc)
                                if a1 > a0:
                                    runs.append((a0, a1 - a0, a0 + 1))
                                b0 = max(n0, NBc); b1 = n0 + nbt
                                if b1 > b0:
                                    runs.append((b0, b1 - b0, b0 + 2))
                            mms = []
                            for tau in range(16):
                                if d == 0:
                                    mms.append((pv[:, :, tau:16], KtL[:, gb, tau, :], uv[:, n0:n0 + nbt, 0:16 - tau]))
                                else:
                                    mms.append((pv[:, :, 0:16 - tau], KtL[:, gb, tau, :], uv[:, n0:n0 + nbt, tau:16]))
                            for q in range(4):
                                pair = gb * 4 + q
                                for i in range(16):
                                    for x, Ax in ((0, Abr), (1, Abi)):
                                        for (r0, rn, p0) in runs:
                                            mms.append((pv[:, r0 - n0:r0 - n0 + rn, i], Pad[:, q, i, x, :], Ax[:, q, p0:p0 + rn]))
                            for mi, (o_, l_, r_) in enumerate(mms):
                                P.mm(o_, l_, r_, mi == 0, mi == len(mms) - 1, ['KtL', 'Pad', 'Ab', ('uT', 0)], [pyk])
                            yrow = T['yfT'][gb * 128:(gb + 1) * 128, t0:t0 + w]
                            if d == 0:
                                P.cp(ysb[:, :w], py[:, :w], [pyk], [ysk], eng='act')
                                P.dma('act', yrow, ysb[:, :w], reads=[ysk], writes=[('yfT', npy)])
                            else:
                                P.dma('sp', yf_t[:, :w], yrow, writes=['yf_t'])
                                P.tt(ysb[:, :w], py[:, :w], yf_t[:, :w], ALU.add, [pyk, 'yf_t'], [ysk])
                                P.stt(ysb[:, :w], uT[:, t0:t0 + w], dT[:, gb:gb + 1], ysb[:, :w], ALU.mult, ALU.add, [('uT', 0), ysk, 'dT'], [ysk])
                                P.tt(y2[:, :w], ysb[:, :w], ysb[:, :w], ALU.mult, [ysk], ['y2'])
                                P.ts(y2[:, :w], y2[:, :w], 0.044715, ALU.mult, ['y2'], ['y2'], s2=1.0, op1=ALU.add)
                                P.tt(y2[:, :w], y2[:, :w], ysb[:, :w], ALU.mult, ['y2', ysk], ['y2'])
                                P.act(y2[:, :w], y2[:, :w], AF.Sigmoid, ['y2'], ['y2'], scale=1.5957691216057308)
                                P.tt(zb[:, :w], y2[:, :w], ysb[:, :w], ALU.mult, ['y2', ysk], ['zb'])
                                P.dma('act', T['zT'][gb * 128:(gb + 1) * 128, t0:t0 + w], zb[:, :w], reads=['zb'], writes=[('zT', npy)])
                        P.barrier()
                    P.barrier()
        if getattr(c, 's5stop', 9) <= 4:
            return
        with ExitStack() as es:
            def mk(nm, shape, dt=F32):
                return es.enter_context(sb(nm, shape, dt))
            Wg = mk("Wg", [128, 8, 1024], BF16); bg = mk("bg", [128, 8])
            z0 = mk("z0", [128, 8, 512], BF16); z1 = mk("z1", [128, 8, 512], BF16); g0 = mk("g0", [128, 8, 512], BF16); g1 = mk("g1", [128, 8, 512], BF16)
            sgt = mk("sgt", [128, 512]); sg2 = mk("sg2", [128, 512]); ob0 = mk("obx0", [128, 8, 512], BF16); ob1 = mk("obx1", [128, 8, 512], BF16)
            pg0 = es.enter_context(ps("pg0", [128, 512], F32)); pg1 = es.enter_context(ps("pg1", [128, 512], F32))
            P.dma('pool', Wg[:], T['w_glu'][layer].rearrange("(k p) c -> p k c", p=128), writes=['Wg'])
            P.dma('sp', bg[:], T['b_gluT'][layer], writes=['bg'])
            zv = T['zT'].rearrange("(k p) t -> p k t", p=128)
            gv_ = T['fmb'][3072:4096, :].rearrange("(k p) t -> p k t", p=128)
            ovs = T['oT'][2048:3072, :].rearrange("(k p) t -> p k t", p=128)
            npg = 0
            for ti, (t0, w) in enumerate(otiles):
                zt = (z0, z1)[ti % 2]; gt_ = (g0, g1)[ti % 2]; obx = (ob0, ob1)[ti % 2]; kz = ('z', ti % 2); kg = ('g', ti % 2); ko = ('obx', ti % 2)
                P.dma('sp', zt[:, :, :w], zv[:, :, t0:t0 + w], writes=[kz])
                P.dma('sp', gt_[:, :, :w], gv_[:, :, t0:t0 + w], writes=[kg])
                for oc in range(8):
                    pg = (pg0, pg1)[npg % 2]; pgk = 'pg%d' % (npg % 2); npg += 1
                    for k in range(8):
                        P.mm(pg[:, :w], Wg[:, k, oc * 128:(oc + 1) * 128], zt[:, k, :w], k == 0, k == 7, ['Wg', kz], [pgk])
                    P.act(sgt[:, :w], pg[:, :w], AF.Sigmoid, [pgk, 'bg'], ['sgt'], bias=bg[:, oc:oc + 1])
                    P.tt(sgt[:, :w], sgt[:, :w], zt[:, oc, :w], ALU.mult, ['sgt', kz], ['sgt'])
                    P.act(sg2[:, :w], gt_[:, oc, :w], AF.Silu, [kg], ['sg2'])
                    P.tt(obx[:, oc, :w], sgt[:, :w], sg2[:, :w], ALU.mult, ['sgt', 'sg2'], [ko])
                P.dma('act', ovs[:, :, t0:t0 + w], obx[:, :, :w], reads=[ko], writes=[('oTs', ti)])
            P.barrier()

    def sincos(self, x, shape, sin_out, cos_out, tag):
        P = self.P
        with self.sb(tag + "_ni", shape, I32) as ni, self.sb(tag + "_nf", shape, F32) as nf, \
                self.sb(tag + "_r", shape, F32) as r, self.sb(tag + "_m", shape, F32) as m:
            for which, out in ((0, sin_out), (1, cos_out)):
                kx = tag + 'x'
                if which == 1:
                    P.ts(r[:], x, PI / 2, ALU.add, [kx], [tag + 'r0'])
                    src = r[:]
                else:
                    P.cp(r[:], x, [kx], [tag + 'r0'])
                    src = r[:]
                P.ts(ni[:], src, 1.0 / TWO_PI, ALU.mult, [tag + 'r0'], [tag + 'ni'])
                P.cp(nf[:], ni[:], [tag + 'ni'], [tag + 'nf'])
                P.stt(r[:], nf[:], -TWO_PI, src, ALU.mult, ALU.add, [tag + 'nf', tag + 'r0'], [tag + 'r0'])
                P.ts(m[:], r[:], PI, ALU.is_gt, [tag + 'r0'], [tag + 'm'], s2=TWO_PI, op1=ALU.mult)
                P.tt(r[:], r[:], m[:], ALU.subtract, [tag + 'r0', tag + 'm'], [tag + 'r0'])
                P.ts(m[:], r[:], -PI, ALU.is_lt, [tag + 'r0'], [tag + 'm'], s2=TWO_PI, op1=ALU.mult)
                P.tt(r[:], r[:], m[:], ALU.add, [tag + 'r0', tag + 'm'], [tag + 'r0'])
                P.act(out, r[:], AF.Sin, [tag + 'r0'], [tag + 'out%d' % which])
            P.barrier()

    def phase_rope(self):
        c = self.cfg; P = self.P; T = self.T; L = c.L
        with self.sb("fi", [128, 32], I32) as fi, self.sb("fr", [128, 32], F32) as fr, self.sb("pp", [128, 2], F32) as pp, \
                self.sb("ang", [128, 64], F32) as ang, self.sb("sn", [128, 64], F32) as sn, self.sb("cs", [128, 64], F32) as cs, \
                self.sb("rp", [128, 512], F32) as rp:
            P.op('pool', lambda e: e.iota(fi[:], pattern=[[1, 32]], base=0, channel_multiplier=0), [], ['fi'])
            P.cp(fr[:], fi[:], ['fi'], ['fr'])
            P.act(fr[:], fr[:], AF.Exp, ['fr'], ['fr'], scale=-float(np.log(10000.0)) / 32.0)
            P.barrier()
            for (t0, w) in tiles(0, L, 128):
                P.dma('sp', pp[:w, :], T['pos'][t0:t0 + w, :], writes=['pp'])
                P.ts(ang[:w, 0:32], fr[:w, :], pp[:w, 0:1], ALU.mult, ['pp'], ['rpx'])
                P.ts(ang[:w, 32:64], fr[:w, :], pp[:w, 1:2], ALU.mult, ['pp'], ['rpx'])
                self.sincos(ang[:, :], [128, 64], sn[:, :], cs[:, :], 'rp')
                for h in range(4):
                    P.cp(rp[:, h * 64:(h + 1) * 64], cs[:, :], [], ['rp'])
                    P.cp(rp[:, 256 + h * 64:256 + (h + 1) * 64], sn[:, :], [], ['rp'], eng='pool')
                P.dma('sp', T['rope'][t0:t0 + w, :], rp[:w, :], reads=['rp'], writes=['ropeD'])
                P.barrier()

    def build(self):
        c = self.cfg; nc = self.nc; P = self.P; T = None
        self.declare(); T = self.T
        stack = []
        with nc.Block() as block:
            P.start()
            self.consts(stack)
            skip_pre = bool(getattr(c, 'inject', ()))
            if not skip_pre:
                self.phase_rope()
            hs = [T['hT0'], T['hT1'], T['hT2']]
            for layer in range(c.DEPTH):
                last = (layer == c.DEPTH - 1)
                if not getattr(c, 'noada', False):
                    self.phase_ada(layer)
                if c.stop == 'ada':
                    break
                if not skip_pre:
                    self.phase_norm(hs[layer], 'hn', T['hnT'])
                    if c.stop == 'norm':
                        break
                    self.phase_inproj(layer, last)
                if c.stop == 'inproj':
                    break
                for name in ('hg', 'gl', 'rt'):
                    if c.stop is None or name in c.stop:
                        if name == 'hg' or getattr(c, 'oldgla', False):
                            self.phase_gla(layer, name, last)
                        else:
                            self.phase_gla2(layer, name, last)
                if c.stop is None or 's5' in c.stop:
                    self.phase_s5(layer, last)
                if c.stop is not None and 'out' not in c.stop:
                    break
                self.phase_outproj(layer, hs[layer], hs[layer + 1], last)
            if c.stop is None or 'out' in c.stop:
                self.phase_norm(hs[c.DEPTH], 'final', T['outT'])
            P.barrier()
            for cm in reversed(stack):
                cm.__exit__(None, None, None)
            P.finish()
        return nc


TM_COLS = np.concatenate([np.arange(0, 4096), np.arange(5120, 7168), np.arange(10272, 12320)])
FM_COLS = np.concatenate([np.arange(4096, 5120), np.arange(7200, 8224), np.arange(8224, 9248), np.arange(9248, 10272),
                          np.arange(12320, 13344), np.arange(7168, 7200)])


def fmT(v, nchunk):
    return np.ascontiguousarray(np.asarray(v, np.float32).reshape(nchunk, 128).T)


def gla_consts():
    C = 64; s = 32; f = np.float32
    out = np.zeros((2, 64, 449), f)
    j = np.arange(64)[:, None]; i = np.arange(64)[None, :]
    for d in (0, 1):
        T = (j <= i) if d == 0 else (j >= i)
        blk = i // s
        m = blk * s + s // 2
        if d == 0:
            QO = (j >= blk * s) & (j <= i)
            Tm = (j <= m)
            I = 1
            KO = (i < I * s) & (j > i) & (j <= I * s - 1)
        else:
            QO = (j <= blk * s + s - 1) & (j >= i)
            Tm = (j >= m)
            I = 0
            KO = (i >= (I + 1) * s) & (j >= (I + 1) * s) & (j < i)
        QD = T.astype(f) - Tm.astype(f)
        M = out[d]
        M[:, 0:64] = QO; M[:, 64:128] = QD; M[:, 128:192] = -QD; M[:, 192:256] = KO
        M[:, 256:320] = T; M[:, 320] = 1.0
        M[:, 321:385] = (j > i) if d == 0 else (j < i)
        M[:, 385:449] = T
    return out


def gla_consts2():
    C = 128; r = 64; f = np.float32
    out = np.zeros((2, C, 513), f)
    j = np.arange(C)[:, None]; i = np.arange(C)[None, :]
    for d in (0, 1):
        T = (j <= i) if d == 0 else (j >= i)
        R = (j <= r) if d == 0 else (j >= r)
        M = out[d]
        M[:, 0:128] = T.astype(f) - (R & (i >= 0)).astype(f)
        M[:, 128:256] = T; M[:, 256] = 1.0
        M[:, 257:385] = (j > i) if d == 0 else (j < i)
        M[:, 385:513] = T
    return out


def prep_shared(cfg, inp, flip=False):
    DP = cfg.DEPTH; f = np.float32
    sh = {}
    sh['w_ada'] = np.ascontiguousarray(inp['w_ada'][:DP], f)
    sh['b_adaT'] = np.stack([fmT(inp['b_ada'][l], 96) for l in range(DP)])
    sh['norm_gT'] = np.stack([fmT(inp['norm_g'][l], 32) for l in range(DP)])
    sh['final_gT'] = fmT(inp['final_norm_g'], 32)
    w_in = np.asarray(inp['w_in'][:DP], f)
    sh['w_tm'] = np.ascontiguousarray(w_in[:, :, TM_COLS])
    sh['w_fm'] = np.ascontiguousarray(w_in[:, :, FM_COLS])
    sh['w_out'] = np.ascontiguousarray(inp['w_out'][:DP], f)
    sh['w_glu'] = np.ascontiguousarray(inp['s5_w_glu'][:DP], f)
    sh['b_gluT'] = np.stack([fmT(inp['s5_b_glu'][l], 8) for l in range(DP)])
    lb = np.asarray(inp['hgrn_lb_logits'], f)
    sh['hg_lbrep'] = np.ascontiguousarray(np.broadcast_to(lb[None], (128, 2, 2, 1024)))
    sh['hg_ngT'] = np.asarray(inp['hgrn_norm_g'][:DP], f).reshape(DP, 128, 1).copy()
    wg = np.concatenate([np.asarray(inp['gla_w_gk'][:DP], f), np.asarray(inp['gla_b_gk'][:DP], f)[:, :, None, :]], axis=2)
    sh['gla_wg'] = np.ascontiguousarray(wg)
    sh['gla_ngT'] = np.stack([fmT(inp['gla_norm_g'][l], 2) for l in range(DP)])
    sh['ret_ngT'] = np.stack([fmT(inp['ret_norm_g'][l], 2) for l in range(DP)])
    rl = np.asarray(inp['ret_decay_logit'][:DP], f)
    sh['ret_lrep'] = np.ascontiguousarray(np.broadcast_to(rl[:, :, None, :], (DP, 2, 128, 4)))
    L = cfg.L
    pos = np.zeros((L, 2), f)
    t = np.arange(cfg.LAT)
    pos[cfg.CTX:, 0] = t // 64; pos[cfg.CTX:, 1] = t % 64
    if flip:
        pos[cfg.CTX:] = pos[cfg.CTX:][::-1].copy()
    sh['pos'] = pos
    sh['gl_consts'] = gla_consts()
    sh['gl_consts2'] = gla_consts2()
    lam = np.stack([np.asarray(inp['s5_lam_re'][:DP], f), np.asarray(inp['s5_lam_im'][:DP], f)], axis=2)
    dt = np.asarray(inp['s5_log_dt'][:DP], f)
    B = np.stack([np.asarray(inp['s5_b_re'][:DP], f), np.asarray(inp['s5_b_im'][:DP], f)], axis=2)
    Cm = np.stack([np.asarray(inp['s5_c_re'][:DP], f), np.asarray(inp['s5_c_im'][:DP], f)], axis=2)
    sh['s5_lamP'] = np.ascontiguousarray(lam.reshape(DP, 2, 2, 32, 2, 64).transpose(0, 1, 2, 4, 5, 3).reshape(DP, 2, 2, 128, 32))
    dtb = np.broadcast_to(dt[:, :, :, None], (DP, 2, 64, 64))
    sh['s5_dtP'] = np.ascontiguousarray(dtb.reshape(DP, 2, 32, 2, 64).transpose(0, 1, 3, 4, 2).reshape(DP, 2, 128, 32))
    sh['s5_lam64'] = np.ascontiguousarray(lam.transpose(0, 1, 2, 4, 3))
    sh['s5_dt64'] = np.ascontiguousarray(dtb.transpose(0, 1, 3, 2))
    sh['s5_b64'] = np.ascontiguousarray(B.transpose(0, 1, 2, 4, 3, 5))
    sh['s5_c64'] = np.ascontiguousarray(Cm.transpose(0, 1, 2, 5, 3, 4))
    lamF = np.broadcast_to(lam.reshape(DP, 2, 2, 8, 8, 1, 64), (DP, 2, 2, 8, 8, 16, 64))
    sh['s5_lamF'] = np.ascontiguousarray(lamF.transpose(0, 1, 2, 4, 5, 3, 6).reshape(DP, 2, 2, 128, 8, 64))
    dtF = np.broadcast_to(dt.reshape(DP, 2, 8, 8, 1, 1), (DP, 2, 8, 8, 16, 64))
    sh['s5_dtF'] = np.ascontiguousarray(dtF.transpose(0, 1, 3, 4, 2, 5).reshape(DP, 2, 128, 8, 64))
    BF = B.reshape(DP, 2, 2, 8, 8, 64, 16)
    sh['s5_bF'] = np.ascontiguousarray(BF.transpose(0, 1, 2, 4, 6, 3, 5).reshape(DP, 2, 2, 128, 8, 64))
    CP = Cm.reshape(DP, 2, 2, 32, 2, 16, 64)
    sh['s5_cP'] = np.ascontiguousarray(CP.transpose(0, 1, 2, 4, 6, 3, 5).reshape(DP, 2, 2, 128, 32, 16))
    dd = np.asarray(inp['s5_d'][:DP], f).reshape(DP, 8, 128)
    sh['s5_dT'] = np.ascontiguousarray(dd.transpose(0, 2, 1))
    row = np.arange(128)[:, None, None]; qq = np.arange(4)[None, :, None]; col = np.arange(128)[None, None, :]
    sh['s5_maskQ'] = ((row // 32 == qq) & ((row % 32) // 16 == col // 64)).astype(f)
    sh['s5_maskC'] = np.ascontiguousarray(np.broadcast_to((np.arange(128)[:, None, None] // 64 == np.arange(2)[None, :, None]), (128, 2, 16))).astype(f)
    return sh


def swap_dirs(inp):
    o = dict(inp)
    w = np.array(inp['w_in'], np.float32, copy=True)
    w[:, :, 1024:2048] = inp['w_in'][:, :, 2048:3072]; w[:, :, 2048:3072] = inp['w_in'][:, :, 1024:2048]
    w[:, :, 7168:7184] = inp['w_in'][:, :, 7184:7200]; w[:, :, 7184:7200] = inp['w_in'][:, :, 7168:7184]
    o['w_in'] = w
    for k in ('hgrn_lb_logits', 'gla_w_gk', 'gla_b_gk', 's5_lam_re', 's5_lam_im', 's5_log_dt', 's5_b_re', 's5_b_im',
              's5_c_re', 's5_c_im', 'ret_decay_logit'):
        o[k] = np.ascontiguousarray(np.asarray(inp[k])[:, ::-1])
    return o


def prep_core(cfg, inp, b, flip=False):
    f = np.float32
    m = {}
    cx = np.asarray(inp['ctx'][b], f); xx = np.asarray(inp['x'][b], f)
    if flip:
        cx = cx[::-1]; xx = xx[::-1]
    h0 = np.concatenate([cx, xx], axis=0)
    m['hT0'] = np.ascontiguousarray(h0.T)
    c2 = np.stack([np.asarray(inp['c'][b], f), np.asarray(inp['c_ctx'], f)], axis=0)
    m['c2T'] = np.ascontiguousarray(c2.reshape(2, 32, 128).transpose(2, 1, 0))
    return m


_CACHE = {}


def run_pairs(cfg, inputs, nb):
    key = (cfg.CTX, cfg.LAT, cfg.DEPTH)
    if key not in _CACHE:
        _CACHE[key] = Builder(cfg).build()
    nc = _CACHE[key]
    sh0 = prep_shared(cfg, inputs, False)
    sh1 = prep_shared(cfg, swap_dirs(inputs), True)
    for k in sh0:
        if k in sh1 and sh0[k].shape == sh1[k].shape and sh0[k].nbytes > (1 << 20) and np.array_equal(sh0[k], sh1[k]):
            sh1[k] = sh0[k]
    in_maps = []
    for core in range(2 * nb):
        flip = core >= nb
        m = dict(sh1 if flip else sh0)
        m.update(prep_core(cfg, inputs, core % nb, flip))
        in_maps.append(m)
    res = run_bass_kernel_spmd(nc, in_maps, core_ids=list(range(2 * nb)))
    outs = []
    for b in range(nb):
        first = np.asarray(res.results[b]['outT']).T
        second = np.asarray(res.results[b + nb]['outT']).T[::-1]
        outs.append(np.concatenate([first, second], axis=0))
    return np.stack(outs, axis=0).astype(np.float32)


def kernel(**inputs):
    cfg = Cfg()
    cfg.half = True
    return run_pairs(cfg, inputs, 4)
```

```python
import numpy as np
import concourse.bass as bass
import concourse.mybir as mybir
from concourse.bass_utils import run_bass_kernel_spmd

F32 = mybir.dt.float32; BF16 = mybir.dt.bfloat16; I32 = mybir.dt.int32
AF = mybir.ActivationFunctionType; ALU = mybir.AluOpType
D = 4096; KC = 32; EPS = 1e-6
TWO_PI = 6.283185307179586; PI = 3.141592653589793


class Prog:
    NDS = 8

    def __init__(self, nc):
        self.nc = nc
        self.eng = {'pe': nc.tensor, 'act': nc.scalar, 'dve': nc.vector, 'pool': nc.gpsimd, 'sp': nc.sync}
        self.sem = {}; self.cnt = {}; self.dsem = {}; self.dcnt = {}
        self.waited = {k: {} for k in self.eng}
        self.res = {}
        self._stack = []
        self.nins = 0

    def start(self):
        nc = self.nc
        for k in self.eng:
            cm = nc.semaphore("s_" + k); self._stack.append(cm); self.sem[k] = cm.__enter__(); self.cnt[k] = 0
        for k in ['sp', 'pool', 'act']:
            self.dsem[k] = []
            for i in range(self.NDS):
                cm = nc.semaphore("d_%s%d" % (k, i)); self._stack.append(cm); self.dsem[k].append(cm.__enter__())
            self.dcnt[k] = 0
        self.last_dma_tok = {k: [None] * self.NDS for k in self.dsem}

    def finish(self):
        for cm in reversed(self._stack):
            cm.__exit__(None, None, None)

    def _wait(self, e, tok):
        if tok is None:
            return
        sem, val, owner = tok
        w = self.waited[e]
        key = id(sem)
        if w.get(key, 0) >= val:
            return
        if owner == e and e == 'pe':
            return
        self.eng[e].wait_ge(sem, val)
        w[key] = val

    def _deps(self, e, reads, writes):
        toks = []
        for k in reads:
            st = self.res.get(k)
            if st and st['w']:
                toks.append(st['w'])
        for k in writes:
            st = self.res.get(k)
            if st:
                if st['w']:
                    toks.append(st['w'])
                toks.extend(st['r'])
        for t in toks:
            self._wait(e, t)

    def _update(self, tok, reads, writes):
        for k in reads:
            st = self.res.setdefault(k, {'w': None, 'r': []})
            st['r'].append(tok)
            if len(st['r']) > 48:
                best = {}
                for t in st['r']:
                    kk = id(t[0])
                    if kk not in best or best[kk][1] < t[1]:
                        best[kk] = t
                st['r'] = list(best.values())
        for k in writes:
            self.res[k] = {'w': tok, 'r': []}

    def op(self, e, fn, reads=(), writes=()):
        self._deps(e, reads, writes)
        ins = fn(self.eng[e])
        self.cnt[e] += 1
        self.nins += 1
        ins.then_inc(self.sem[e], 1)
        tok = (self.sem[e], self.cnt[e], e)
        self._update(tok, reads, writes)
        return tok

    def dma(self, q, out, in_, reads=(), writes=(), **kw):
        self._deps(q, reads, writes)
        i = self.dcnt[q]; k = i % self.NDS
        prev = self.last_dma_tok[q][k]
        if prev is not None:
            self._wait(q, prev)
        ins = self.eng[q].dma_start(out=out, in_=in_, **kw)
        val = 16 * (i // self.NDS + 1)
        ins.then_inc(self.dsem[q][k], 16)
        tok = (self.dsem[q][k], val, 'dma_' + q)
        self.last_dma_tok[q][k] = tok
        self.dcnt[q] += 1
        self.nins += 1
        self._update(tok, reads, writes)
        return tok

    def barrier(self):
        toks = []
        for e in self.eng:
            if self.cnt[e] > 0:
                toks.append((self.sem[e], self.cnt[e], e))
        for q in self.dsem:
            for t in self.last_dma_tok[q]:
                if t is not None:
                    toks.append(t)
        for e in self.eng:
            for t in toks:
                self._wait(e, t)
        self.res = {}

    def mm(self, out, lhsT, rhs, start, stop, reads, writes):
        return self.op('pe', lambda e: e.matmul(out, lhsT=lhsT, rhs=rhs, start=start, stop=stop), reads, writes)

    def tr(self, out, in_, ident, reads, writes):
        return self.op('pe', lambda e: e.transpose(out=out, in_=in_, identity=ident), reads, writes)

    def act(self, out, in_, func, reads, writes, bias=None, scale=None, eng='act'):
        kw = {}
        if bias is not None:
            kw['bias'] = bias
        if scale is not None:
            kw['scale'] = scale
        return self.op(eng, lambda e: e.activation(out=out, in_=in_, func=func, **kw), reads, writes)

    def tt(self, out, in0, in1, op, reads, writes, eng='dve'):
        return self.op(eng, lambda e: e.tensor_tensor(out=out, in0=in0, in1=in1, op=op), reads, writes)

    def ts(self, out, in0, s1, op0, reads, writes, s2=None, op1=None, eng='dve'):
        if op1 is None:
            return self.op(eng, lambda e: e.tensor_scalar(out=out, in0=in0, scalar1=s1, scalar2=None, op0=op0), reads, writes)
        return self.op(eng, lambda e: e.tensor_scalar(out=out, in0=in0, scalar1=s1, scalar2=s2, op0=op0, op1=op1), reads, writes)

    def stt(self, out, in0, scalar, in1, op0, op1, reads, writes):
        return self.op('dve', lambda e: e.scalar_tensor_tensor(out=out, in0=in0, scalar=scalar, in1=in1, op0=op0, op1=op1), reads, writes)

    def cp(self, out, in_, reads, writes, eng='dve'):
        if eng == 'act':
            return self.op('act', lambda e: e.copy(out=out, in_=in_), reads, writes)
        return self.op(eng, lambda e: e.tensor_copy(out=out, in_=in_), reads, writes)

    def memset(self, ap, val, writes, eng='pool'):
        return self.op(eng, lambda e: e.memset(ap, val), (), writes)


class Cfg:
    def __init__(self, CTX=256, LAT=4096, DEPTH=2, debug=False, stop=None):
        self.CTX = CTX; self.LAT = LAT; self.DEPTH = DEPTH; self.L = CTX + LAT
        self.debug = debug; self.stop = stop
        self.half = False

    def need_hi(self, last):
        return (self.CTX + self.LAT // 2) if (last and self.half) else self.L

    def need_lo(self, last):
        return self.CTX if last else 0


def tiles(lo, hi, w):
    out = []
    t = lo
    while t < hi:
        ww = min(w, hi - t)
        out.append((t, ww))
        t += ww
    return out


class Builder:
    def __init__(self, cfg):
        self.cfg = cfg
        self.nc = bass.Bass("TRN2", target_bir_lowering=False)
        self.P = Prog(self.nc)
        self.T = {}
        self.dbg_names = []

    def din(self, name, shape, dt=F32):
        self.T[name] = self.nc.dram_tensor(name, list(shape), dt, kind="ExternalInput").ap()

    def dscr(self, name, shape, dt, out=False):
        if name in getattr(self.cfg, 'inject', ()):
            self.T[name] = self.nc.dram_tensor(name, list(shape), dt, kind="ExternalInput").ap()
            return
        if out or (self.cfg.debug and name in self.cfg.debug):
            self.T[name] = self.nc.dram_tensor(name, list(shape), dt, kind="ExternalOutput").ap()
            self.dbg_names.append(name)
        else:
            self.T[name] = self.nc.dram_tensor(name, list(shape), dt).ap()

    def sb(self, name, shape, dt):
        self._uid = getattr(self, '_uid', 0) + 1
        return self.nc.sbuf_tensor("%s_%d" % (name, self._uid), list(shape), dt)

    def ps(self, name, shape, dt=F32):
        self._uid = getattr(self, '_uid', 0) + 1
        return self.nc.psum_tensor("%s_%d" % (name, self._uid), list(shape), dt)

    def declare(self):
        c = self.cfg; L = c.L; DP = c.DEPTH
        self.din("hT0", [D, L]); self.din("c2T", [128, 32, 2])
        self.din("w_ada", [DP, D, 3 * D]); self.din("b_adaT", [DP, 128, 96])
        self.din("norm_gT", [DP, 128, 32]); self.din("final_gT", [128, 32])
        self.din("w_tm", [DP, D, 8192]); self.din("w_fm", [DP, D, 5152])
        self.din("w_out", [DP, D, D]); self.din("w_glu", [DP, 1024, 1024]); self.din("b_gluT", [DP, 128, 8])
        self.din("hg_lbrep", [128, 2, 2, 1024]); self.din("hg_ngT", [DP, 128, 1])
        self.din("gla_wg", [DP, 2, 17, 512]); self.din("gla_ngT", [DP, 128, 2])
        self.din("ret_ngT", [DP, 128, 2]); self.din("ret_lrep", [DP, 2, 128, 4])
        self.din("pos", [L, 2]); self.din("gl_consts", [2, 64, 449]); self.din("gl_consts2", [2, 128, 513])
        self.din("s5_lamP", [DP, 2, 2, 128, 32])
        self.din("s5_dtP", [DP, 2, 128, 32])
        self.din("s5_lam64", [DP, 2, 2, 64, 64])
        self.din("s5_dt64", [DP, 2, 64, 64])
        self.din("s5_b64", [DP, 2, 2, 64, 64, 16])
        self.din("s5_c64", [DP, 2, 2, 64, 64, 16])
        self.din("s5_lamF", [DP, 2, 2, 128, 8, 64])
        self.din("s5_dtF", [DP, 2, 128, 8, 64])
        self.din("s5_bF", [DP, 2, 2, 128, 8, 64])
        self.din("s5_cP", [DP, 2, 2, 128, 32, 16])
        self.din("s5_dT", [DP, 128, 8]); self.din("s5_maskQ", [128, 4, 128]); self.din("s5_maskC", [128, 2, 16])
        self.dscr("hT1", [D, L], F32); self.dscr("hT2", [D, L], F32)
        self.dscr("hnT", [D, L], BF16)
        self.dscr("tmb", [L, 8192], BF16); self.dscr("tmf", [L, 2048], F32)
        self.dscr("fmb", [5120, L], BF16); self.dscr("lrT", [32, L], F32)
        self.dscr("ofT", [3072, L], F32); self.dscr("oT", [D, L], BF16)
        self.dscr("rope", [L, 512], F32)
        self.dscr("s5pf", [8, 2, 128, 17, 64], F32); self.dscr("s5cf", [8, 2, 128, 64], F32)
        self.dscr("yfT", [1024, L], F32); self.dscr("zT", [1024, L], BF16)
        self.dscr("outT", [D, (c.LAT // 2) if c.half else c.LAT], F32, out=True)

    def consts(self, stack):
        P = self.P
        def mk(name, shape, dt):
            cm = self.sb(name, shape, dt); stack.append(cm); return cm.__enter__()
        self.onesf = mk("onesf", [128, 128], F32)
        self.onesb = mk("onesb", [128, 128], BF16)
        self.identb = mk("identb", [128, 128], BF16)
        self.identf = mk("identf", [128, 128], F32)
        self.sT = mk("sT", [128, 32, 2], BF16)
        self.GS = mk("GS", [128, 6, 32], F32)
        P.memset(self.onesf[:], 1.0, ['onesf'])
        P.memset(self.onesb[:], 1.0, ['onesb'])
        P.op('pool', lambda e: e.affine_select(out=self.identf[:], in_=self.onesf[:], pattern=[[1, 128]], compare_op=ALU.is_equal,
                                               fill=0.0, base=0, channel_multiplier=-1), ['onesf'], ['identf'])
        P.cp(self.identb[:], self.identf[:], ['identf'], ['identb'])
        with self.sb("c2f", [128, 32, 2], F32) as c2f:
            P.dma('sp', c2f[:], self.T['c2T'], writes=['c2f'])
            P.act(self.sT[:], c2f[:], AF.Silu, ['c2f'], ['sT'])
            P.barrier()

    def phase_ada(self, layer):
        P = self.P; T = self.T; GS = self.GS
        wv = T['w_ada'][layer].rearrange("(k p) c -> p k c", p=128)
        with self.sb("wa0", [128, 32, 256], BF16) as wa0, self.sb("wa1", [128, 32, 256], BF16) as wa1, \
                self.ps("pa", [128, 512], F32) as pa, self.sb("modT", [128, 96, 2], F32) as modT, \
                self.sb("bT", [128, 96], F32) as bT, self.sb("ng", [128, 32], F32) as ng:
            was = [wa0, wa1]
            P.dma('sp', bT[:], T['b_adaT'][layer], writes=['bT'])
            P.dma('sp', ng[:], T['norm_gT'][layer], writes=['ng'])
            for cb in range(48):
                wa = was[cb % 2]; key = ('wa', cb % 2)
                P.dma('pool', wa[:], wv[:, :, cb * 256:(cb + 1) * 256], writes=[key])
                for s in range(2):
                    j = cb * 2 + s
                    for k in range(32):
                        P.mm(pa[:, 2 * j:2 * j + 2], wa[:, k, s * 128:(s + 1) * 128], self.sT[:, k, :], k == 0, k == 31,
                             [key, 'sT'], ['pa'])
            P.tt(modT[:], pa[:, 0:192].rearrange("p (j r) -> p j r", r=2), bT[:].unsqueeze(2).to_broadcast([128, 96, 2]), ALU.add,
                 ['pa', 'bT'], ['modT'])
            for r in range(2):
                P.stt(GS[:, 3 * r + 0, :], modT[:, 32:64, r], 1.0, ng[:], ALU.add, ALU.mult, ['modT', 'ng'], ['GS'])
                P.cp(GS[:, 3 * r + 1, :], modT[:, 0:32, r], ['modT'], ['GS'])
                P.cp(GS[:, 3 * r + 2, :], modT[:, 64:96, r], ['modT'], ['GS'])
            P.barrier()

    def phase_norm(self, hsrc, mode, dst):
        c = self.cfg; P = self.P; T = self.T; GS = self.GS; L = c.L; CTX = c.CTX
        TW = 256
        hv = hsrc.rearrange("(k p) t -> p k t", p=128)
        dv = dst.rearrange("(k p) t -> p k t", p=128)
        toks = tiles(0, L, TW) if mode == 'hn' else tiles(CTX, c.need_hi(True), TW)
        with self.sb("hTa", [128, 32, TW], F32) as hTa, self.sb("hTb", [128, 32, TW], F32) as hTb, \
                self.sb("sq", [128, 32, TW], BF16) as sq, self.sb("hna", [128, 32, TW], BF16) as hna, \
                self.sb("hnb", [128, 32, TW], BF16) as hnb, self.sb("rstd", [128, TW], F32) as rstd, \
                self.sb("fg", [128, 32], F32) as fg, \
                self.ps("pssa", [128, 512], F32) as pssa, self.ps("pssb", [128, 512], F32) as pssb:
            hTs = [hTa, hTb]; hns = [hna, hnb]; psss = [pssa, pssb]
            if mode == 'final':
                P.dma('sp', fg[:], T['final_gT'], writes=['fg'])
            for it, (t0, w) in enumerate(toks):
                b = it % 2
                hT = hTs[b]; hn = hns[b]; pss = psss[b]
                P.dma('sp', hT[:, :, :w], hv[:, :, t0:t0 + w], writes=[('hT', b)])
                P.act(sq[:, :, :w], hT[:, :, :w], AF.Square, [('hT', b)], ['sq'])
                for k in range(32):
                    P.mm(pss[:, :w], self.onesb[:], sq[:, k, :w], k == 0, k == 31, ['sq', 'onesb'], [('pss', b)])
                P.act(rstd[:, :w], pss[:, :w], AF.Sqrt, [('pss', b)], ['rstd'], bias=EPS, scale=1.0 / D)
                P.op('dve', lambda e: e.reciprocal(out=rstd[:, :w], in_=rstd[:, :w]), ['rstd'], ['rstd'])
                P.tt(hT[:, :, :w], hT[:, :, :w], rstd[:, :w].unsqueeze(1).to_broadcast([128, 32, w]), ALU.mult,
                     [('hT', b), 'rstd'], [('hT', b)])
                if mode == 'hn':
                    gi = 3 if t0 < CTX else 0
                    for k in range(32):
                        if k % 2 == 0:
                            P.act(hn[:, k, :w], hT[:, k, :w], AF.Identity, [('hT', b), 'GS'], [('hn', b, k)],
                                  bias=GS[:, gi + 1, k:k + 1], scale=GS[:, gi, k:k + 1])
                        else:
                            P.ts(hn[:, k, :w], hT[:, k, :w], GS[:, gi, k:k + 1], ALU.mult, [('hT', b), 'GS'], [('hn', b, k)],
                                 s2=GS[:, gi + 1, k:k + 1], op1=ALU.add)
                    P.dma('act', dv[:, :, t0:t0 + w], hn[:, :, :w], reads=[('hn', b, k) for k in range(32)], writes=[('dst', it)])
                else:
                    P.tt(hT[:, :, :w], hT[:, :, :w], fg[:].unsqueeze(2).to_broadcast([128, 32, w]), ALU.mult,
                         [('hT', b), 'fg'], [('hT', b)])
                    P.dma('act', dv[:, :, t0 - CTX:t0 - CTX + w], hT[:, :, :w], reads=[('hT', b)], writes=[('dst', it)])
            P.barrier()

    def phase_inproj(self, layer, last=False):
        c = self.cfg; P = self.P; T = self.T; L = c.L
        blocks = []
        for bi in range(16):
            blocks.append(('tm', 'w_tm', bi * 512, 512))
        for bi in range(10):
            blocks.append(('fm', 'w_fm', bi * 512, 512))
        blocks.append(('lr', 'w_fm', 5120, 32))
        hv = T['hnT'].rearrange("(k p) t -> p k t", p=128)
        ttiles_all = tiles(0, L, 512)
        ttiles_need = tiles(c.need_lo(last), c.need_hi(last), 512) if last else ttiles_all
        with self.sb("Wb0", [128, 32, 512], BF16) as Wb0, self.sb("Wb1", [128, 32, 512], BF16) as Wb1, \
                self.sb("hb0", [128, 32, 512], BF16) as hb0, self.sb("hb1", [128, 32, 512], BF16) as hb1, \
                self.sb("stgf", [128, 4, 512], F32) as stgf, self.sb("stgb", [128, 4, 512], BF16) as stgb, \
                self.ps("pd", [128, 4, 512], F32) as pd:
            Wbs = [Wb0, Wb1]; hbs = [hb0, hb1]
            gt = 0; gp = 0
            for bi, (kind, wn, c0, ncol) in enumerate(blocks):
                Wb = Wbs[bi % 2]; wkey = ('Wb', bi % 2)
                wv = T[wn][layer].rearrange("(k p) c -> p k c", p=128)
                P.dma('pool', Wb[:, :, :ncol], wv[:, :, c0:c0 + ncol], writes=[wkey])
                qonly = (kind == 'tm' and c0 in (0, 512, 4096, 6144)) or (kind == 'fm' and c0 not in (2048, 2560))
                for (t0, w) in (ttiles_need if qonly else ttiles_all):
                    hb = hbs[gt % 2]; hkey = ('hb', gt % 2); gt += 1
                    P.dma('sp', hb[:, :, :w], hv[:, :, t0:t0 + w], writes=[hkey])
                    if kind in ('fm', 'lr'):
                        nsub = max(1, ncol // 128); m = min(128, ncol)
                        for cs in range(nsub):
                            bk = gp % 4; gp += 1
                            for k in range(32):
                                P.mm(pd[:m, bk, :w], Wb[:, k, cs * 128:cs * 128 + m], hb[:, k, :w], k == 0, k == 31,
                                     [wkey, hkey], [('pd', bk)])
                            if kind == 'fm':
                                stg = stgb; skey = ('stgb', bk); dst = T['fmb'][c0 + cs * 128:c0 + cs * 128 + m, t0:t0 + w]
                            else:
                                stg = stgf; skey = ('stgf', bk); dst = T['lrT'][0:32, t0:t0 + w]
                            P.cp(stg[:m, bk, :w], pd[:m, bk, :w], [('pd', bk)], [skey], eng=('act' if bk % 2 else 'dve'))
                            P.dma('act', dst, stg[:m, bk, :w], reads=[skey], writes=[('o', gp)])
                    else:
                        isf = (1024 <= c0 < 3072)
                        for ts in range(w // 128):
                            bk = gp % 4; gp += 1
                            for k in range(32):
                                P.mm(pd[:, bk, :], hb[:, k, ts * 128:(ts + 1) * 128], Wb[:, k, :], k == 0, k == 31,
                                     [wkey, hkey], [('pd', bk)])
                            r0 = t0 + ts * 128
                            if isf:
                                stg = stgf; skey = ('stgf', bk); dst = T['tmf'][r0:r0 + 128, c0 - 1024:c0 - 1024 + 512]
                            else:
                                stg = stgb; skey = ('stgb', bk); dst = T['tmb'][r0:r0 + 128, c0:c0 + 512]
                            P.cp(stg[:, bk, :], pd[:, bk, :], [('pd', bk)], [skey], eng=('act' if bk % 2 else 'dve'))
                            P.dma('act', dst, stg[:, bk, :], reads=[skey], writes=[('o', gp)])
            P.barrier()

    def phase_outproj(self, layer, hsrc, hdst, last):
        c = self.cfg; P = self.P; T = self.T; L = c.L; CTX = c.CTX; GS = self.GS
        wv = T['w_out'][layer].rearrange("(k p) c -> p k c", p=128)
        ov = T['oT'].rearrange("(k p) t -> p k t", p=128)
        ttiles = ([] if last else tiles(0, CTX, 512)) + tiles(CTX, c.need_hi(last), 512)
        with self.sb("Wo0", [128, 32, 512], BF16) as Wb0, self.sb("Wo1", [128, 32, 512], BF16) as Wb1, \
                self.sb("ob0", [128, 32, 512], BF16) as hb0, self.sb("ob1", [128, 32, 512], BF16) as hb1, \
                self.sb("hold0", [128, 4, 512], F32) as hold0, self.sb("hold1", [128, 4, 512], F32) as hold1, \
                self.ps("po", [128, 4, 512], F32) as pd:
            Wbs = [Wb0, Wb1]; hbs = [hb0, hb1]; holds = [hold0, hold1]
            gt = 0; gp = 0
            for fb in range(8):
                Wb = Wbs[fb % 2]; wkey = ('Wb', fb % 2)
                P.dma('pool', Wb[:], wv[:, :, fb * 512:(fb + 1) * 512], writes=[wkey])
                hs = hsrc[fb * 512:(fb + 1) * 512, :].rearrange("(f p) t -> p f t", p=128)
                hd = hdst[fb * 512:(fb + 1) * 512, :].rearrange("(f p) t -> p f t", p=128)
                for (t0, w) in ttiles:
                    hb = hbs[gt % 2]; hkey = ('hb', gt % 2); hold = holds[gt % 2]; okey = ('hold', gt % 2); gt += 1
                    P.dma('sp', hb[:, :, :w], ov[:, :, t0:t0 + w], writes=[hkey])
                    P.dma('sp', hold[:, :, :w], hs[:, :, t0:t0 + w], writes=[okey])
                    gi = 3 if t0 < CTX else 0
                    for fs in range(4):
                        bk = gp % 4; gp += 1
                        fch = fb * 4 + fs
                        for k in range(32):
                            P.mm(pd[:, bk, :w], Wb[:, k, fs * 128:(fs + 1) * 128], hb[:, k, :w], k == 0, k == 31,
                                 [wkey, hkey], [('pd', bk)])
                        P.stt(hold[:, fs, :w], pd[:, bk, :w], GS[:, gi + 2, fch:fch + 1], hold[:, fs, :w], ALU.mult, ALU.add,
                              [('pd', bk), okey, 'GS'], [okey])
                    P.dma('act', hd[:, :, t0:t0 + w], hold[:, :, :w], reads=[okey], writes=[('o', gt)])
            P.barrier()

    BR = {
        'hg': dict(H=8, dv=128, q=0, k=None, v=3072, gate=0, orow=0, ofrow=0, gs=1.0, qs=1.0, norm='rms'),
        'gl': dict(H=4, dv=256, q=4096, k=4608, v=5120, gate=1024, orow=1024, ofrow=1024, gs=-1.0 / 16.0, qs=128.0 ** -0.5, norm='rms'),
        'rt': dict(H=4, dv=256, q=6144, k=6656, v=7168, gate=4096, orow=3072, ofrow=2048, gs=1.0, qs=128.0 ** -0.5, norm='ln'),
    }

    def phase_gla(self, layer, name, last=False):
        c = self.cfg; P = self.P; T = self.T; L = c.L; CTX = c.CTX
        nlo = c.need_lo(last); nhi = c.need_hi(last)
        br = self.BR[name]; H = br['H']; dv = br['dv']; dvc = dv // 128; HW = H * 128; C = 64
        NCH = L // C; ctxch = CTX // C; r = C // 2
        orders = [[n for n in range(NCH) if n * C < nhi], list(range(ctxch - 1, -1, -1)) + list(range(NCH - 1, ctxch - 1, -1))]
        need_o = lambda n: (nlo <= n * C < nhi)
        ofv = T['ofT'][br['ofrow']:br['ofrow'] + 1024, :].rearrange("(c p) t -> p c t", p=128)
        ov = T['oT'][br['orow']:br['orow'] + 1024, :].rearrange("(c p) t -> p c t", p=128)
        gv = T['fmb'][br['gate']:br['gate'] + 1024, :].rearrange("(c p) t -> p c t", p=128)
        sb = self.sb; ps = self.ps
        from contextlib import ExitStack
        with ExitStack() as es:
            Mcat = es.enter_context(sb("Mcat", [C, 449], F32))
            kdz = es.enter_context(sb("kdz", [128, 2, 128], BF16))
            koz = es.enter_context(sb("koz", [128, 2, 1, 64], BF16))
            qtd = es.enter_context(sb("qtd", [128, 2, C], BF16))
            TriS = es.enter_context(sb("TriS", [C, C], F32))
            mask = es.enter_context(sb("mask", [C, C], F32))
            ctmp = es.enter_context(sb("ctmp", [C, C], F32))
            q_t = es.enter_context(sb("q_t", [C, 2, HW], BF16))
            v_t = es.enter_context(sb("v_t", [C, 2, 1024], BF16))
            z_t = es.enter_context(sb("z_t", [C, 2, 1024], F32))
            k_t = es.enter_context(sb("k_t", [C, 2, HW], BF16))
            lr_t = es.enter_context(sb("lr_t", [17, 2, C], F32))
            rp_t = es.enter_context(sb("rp_t", [C, 2, 512], F32))
            of_t = es.enter_context(sb("of_t", [128, 2, 8, C], F32))
            gate_t = es.enter_context(sb("gate_t", [128, 2, 8, C], BF16))
            sig = es.enter_context(sb("sig", [C, 1024], F32))
            ff = es.enter_context(sb("ff", [C, 1024], F32))
            graw = es.enter_context(sb("graw", [C, 2, 1024], F32))
            kt = es.enter_context(sb("kt", [C, 2, 1024], BF16))
            qr = es.enter_context(sb("qr", [C, 2, HW], BF16))
            khat = es.enter_context(sb("khat", [C, 2, 1024], BF16))
            Ek = es.enter_context(sb("Ek", [C, 512], F32))
            rt1 = es.enter_context(sb("rt1", [C, 256], F32))
            rt2 = es.enter_context(sb("rt2", [C, 256], F32))
            rt3 = es.enter_context(sb("rt3", [C, 256], F32))
            rt4 = es.enter_context(sb("rt4", [C, 256], F32))
            E13 = es.enter_context(sb("E13", [128, 2, 321], F32))
            E2 = es.enter_context(sb("E2", [128, 2, C], F32))
            qtl = es.enter_context(sb("qtl", [128, 2, C], BF16))
            qtp = es.enter_context(sb("qtp", [128, 2, C], BF16))
            ktl = es.enter_context(sb("ktl", [128, 2, C], BF16))
            scb = es.enter_context(sb("scb", [C, 2, C], BF16))
            ofs = es.enter_context(sb("ofs", [128, 2, 8, C], F32))
            osum = es.enter_context(sb("osum", [128, 2, 8, C], F32))
            sg = es.enter_context(sb("sg", [128, 8, C], F32))
            sqt = es.enter_context(sb("sqt", [128, 8, C], F32))
            rs = es.enter_context(sb("rs", [128, 8, C], F32))
            mean = es.enter_context(sb("mean", [128, 8, C], F32))
            ob = es.enter_context(sb("ob", [128, 2, 8, C], BF16))
            otmp = es.enter_context(sb("otmp", [128, 8, C], F32))
            S = es.enter_context(sb("S", [128, 1024], F32))
            Sb = es.enter_context(sb("Sb", [128, 1024], BF16))
            lbr = es.enter_context(sb("lbr", [128, 1024], F32))
            oml = es.enter_context(sb("oml", [128, 1024], F32))
            lg = es.enter_context(sb("lg", [128, 2, 1024], F32))
            wg = es.enter_context(sb("wg", [17, 512], F32))
            gn = es.enter_context(sb("gn", [128, 2], F32))
            rl = es.enter_context(sb("rl", [128, 4], F32))
            gconst = es.enter_context(sb("gconst", [C, 512], F32))
            pc = es.enter_context(ps("pc", [128, 512], F32))
            pP0 = es.enter_context(ps("pP0", [128, 512], F32)); pP1 = es.enter_context(ps("pP1", [128, 512], F32)); pPs = [pP0, pP1]
            pT0 = es.enter_context(ps("pT0", [128, 8, 128], BF16)); pT1 = es.enter_context(ps("pT1", [128, 8, 128], BF16)); pTs = [pT0, pT1]
            pS = es.enter_context(ps("pS", [128, 512], F32))
            pO = es.enter_context(ps("pO", [128, 8, C], F32))
            pU = es.enter_context(ps("pU", [128, 2, 256], F32))
            if name == 'hg':
                P.dma('sp', gn[:, 0:1], T['hg_ngT'][layer], writes=['gn'])
            elif name == 'gl':
                P.dma('sp', gn[:], T['gla_ngT'][layer], writes=['gn'])
            else:
                P.dma('sp', gn[:], T['ret_ngT'][layer], writes=['gn'])
            P.memset(lr_t[:], 1.0, ['lr0', 'lr1'])
            step = 0
            for d in (0, 1):
                gs = br['gs']
                P.dma('sp', Mcat[:, :], T['gl_consts'][d], writes=['Mcat'])
                P.ts(Mcat[:, 0:385], Mcat[:, 0:385], gs, ALU.mult, ['Mcat'], ['Mcat'])
                P.memset(kdz[:], 0.0, ['kdz0', 'kdz1'])
                P.memset(koz[:], 0.0, ['koz0', 'koz1'])
                if name == 'hg':
                    P.dma('sp', lg[:], T['hg_lbrep'][:, :, d, :], writes=['lg'])
                    P.tt(lbr[:], lg[:, 1, :], lg[:, 0, :], ALU.subtract, ['lg'], ['lbr'])
                    P.act(lbr[:], lbr[:], AF.Sigmoid, ['lbr'], ['lbr'])
                    P.ts(lbr[:], lbr[:], float(layer), ALU.mult, ['lbr'], ['lbr'])
                    P.ts(oml[:], lbr[:], -1.0, ALU.mult, ['lbr'], ['oml'], s2=1.0, op1=ALU.add)
                elif name == 'gl':
                    P.dma('sp', wg[:], T['gla_wg'][layer, d], writes=['wg'])
                else:
                    P.dma('sp', rl[:], T['ret_lrep'][layer, d], writes=['rl'])
                    P.act(rl[:], rl[:], AF.Exp, ['rl'], ['rl'], scale=-1.0)
                    P.act(rl[:], rl[:], AF.Ln, ['rl'], ['rl'], bias=1.0)
                    P.ts(rl[:], rl[:], -1.0, ALU.mult, ['rl'], ['rl'])
                    for h in range(4):
                        P.ts(gconst[:, h * 128:(h + 1) * 128], self.onesf[:C, :], rl[:C, h:h + 1], ALU.mult, ['onesf', 'rl'], ['gconst'])
                P.memset(S[:], 0.0, [('S', h) for h in range(H)])
                P.memset(Sb[:], 0.0, [('Sb', h) for h in range(H)])
                order = orders[d]

                def emit_prep(si):
                    n = order[si]; t0 = n * C; pb = si % 2
                    K = lambda nm: (nm, pb)
                    P.dma('sp', q_t[:, pb, :], T['tmb'][t0:t0 + C, br['q']:br['q'] + HW], writes=[K('q_t')])
                    P.dma('sp', v_t[:, pb, :], T['tmb'][t0:t0 + C, br['v']:br['v'] + 1024], writes=[K('v_t')])
                    if name == 'hg':
                        P.dma('sp', z_t[:, pb, :], T['tmf'][t0:t0 + C, d * 1024:(d + 1) * 1024], writes=[K('z_t')])
                    else:
                        P.dma('sp', k_t[:, pb, :], T['tmb'][t0:t0 + C, br['k']:br['k'] + HW], writes=[K('k_t')])
                    if name == 'gl':
                        P.dma('sp', lr_t[0:16, pb, :], T['lrT'][d * 16:(d + 1) * 16, t0:t0 + C], writes=[('lr%d' % pb)])
                    if name == 'rt':
                        P.dma('sp', rp_t[:, pb, :], T['rope'][t0:t0 + C, :], writes=[K('rp_t')])
                    if d == 1 and need_o(n):
                        P.dma('sp', of_t[:, pb], ofv[:, :, t0:t0 + C], writes=[K('of_t')])
                        P.dma('sp', gate_t[:, pb], gv[:, :, t0:t0 + C], writes=[K('gate_t')])
                    if name == 'hg':
                        P.act(sig[:], z_t[:, pb, :], AF.Sigmoid, [K('z_t')], ['sig'])
                        P.tt(ff[:], sig[:], oml[:C, :], ALU.mult, ['sig', 'oml'], ['ff'])
                        P.tt(ff[:], ff[:], lbr[:C, :], ALU.add, ['ff', 'lbr'], ['ff'])
                        P.ts(ff[:], ff[:], 1e-6, ALU.max, ['ff'], ['ff'])
                        P.act(graw[:, pb, :], ff[:], AF.Ln, ['ff'], [K('graw')])
                        P.ts(ff[:], sig[:], -1.0, ALU.mult, ['sig', 'ff'], ['ff'], s2=1.0, op1=ALU.add)
                        P.tt(kt[:, pb, :], ff[:], oml[:C, :], ALU.mult, ['ff', 'oml'], [K('kt')], eng='pool')
                    elif name == 'gl':
                        P.mm(pc[:C, :], lr_t[0:17, pb, :], wg[0:17, :], True, True, ['lr%d' % pb, 'wg'], ['pc'])
                        P.act(sig[:, 0:512], pc[:C, :], AF.Exp, ['pc'], ['sig'], scale=-1.0)
                        P.act(graw[:, pb, 0:512], sig[:, 0:512], AF.Ln, ['sig'], [K('graw')], bias=1.0)
                    else:
                        cos4 = rp_t[:, pb, 0:256].rearrange("p (h x) -> p h x", x=64)
                        sin4 = rp_t[:, pb, 256:512].rearrange("p (h x) -> p h x", x=64)
                        for (src, skey, dst, dkey, eng, ta, tb) in ((q_t, K('q_t'), qr, K('qr'), 'dve', rt1, rt2), (k_t, K('k_t'), kt, K('kt'), 'pool', rt3, rt4)):
                            xv = src[:, pb, :].rearrange("p (h two x) -> p h two x", two=2, x=64)
                            ov_ = dst[:, pb, 0:512].rearrange("p (h two x) -> p h two x", two=2, x=64)
                            tav = ta[:].rearrange("p (h x) -> p h x", x=64); tbv = tb[:].rearrange("p (h x) -> p h x", x=64)
                            P.tt(tav, xv[:, :, 0, :], cos4, ALU.mult, [skey, K('rp_t')], [ta.name], eng=eng)
                            P.tt(tbv, xv[:, :, 1, :], sin4, ALU.mult, [skey, K('rp_t')], [tb.name], eng=eng)
                            P.tt(ov_[:, :, 0, :], tav, tbv, ALU.subtract, [ta.name, tb.name], [dkey], eng=eng)
                            P.tt(tav, xv[:, :, 0, :], sin4, ALU.mult, [skey, K('rp_t')], [ta.name], eng=eng)
                            P.tt(tbv, xv[:, :, 1, :], cos4, ALU.mult, [skey, K('rp_t')], [tb.name], eng=eng)
                            P.tt(ov_[:, :, 1, :], tav, tbv, ALU.add, [ta.name, tb.name], [dkey], eng=eng)
                    gsrc_, gkey_, ksl_, kkey_ = srcs(pb)
                    for hf in range(HW // 512):
                        cs = slice(hf * 512, (hf + 1) * 512)
                        P.mm(pc[:C, :], Mcat[:, 321:385], gsrc_[:, cs], True, True, ['Mcat', gkey_], ['pc'])
                        P.act(Ek[:], pc[:C, :], AF.Exp, ['pc'], ['Ek'])
                        P.tt(khat[:, pb, cs], ksl_(hf * 512, (hf + 1) * 512), Ek[:], ALU.mult, [kkey_, 'Ek'], [('khat', pb, hf)])

                def srcs(pb):
                    K = lambda nm: (nm, pb)
                    if name == 'rt':
                        gsrc_ = gconst; gkey_ = 'gconst'
                    else:
                        gsrc_ = graw[:, pb, :]; gkey_ = K('graw')
                    if name == 'gl':
                        ksl_ = lambda a, b: k_t[:, pb, a:b]; kkey_ = K('k_t')
                    else:
                        ksl_ = lambda a, b: kt[:, pb, a:b]; kkey_ = K('kt')
                    return gsrc_, gkey_, ksl_, kkey_

                def emit_front(si, h):
                    pb = si % 2; sl = h % 2; hs = slice(h * 128, (h + 1) * 128)
                    K = lambda nm: (nm, pb)
                    gsrc_, gkey_, ksl_, kkey_ = srcs(pb)
                    if name == 'rt':
                        qsrc = qr[:, pb, 0:512]; qkey = K('qr')
                    else:
                        qsrc = q_t[:, pb, :]; qkey = K('q_t')
                    pP = pPs[sl]; pT = pTs[sl]; kP = 'pP%d' % sl; kT = 'pT%d' % sl
                    P.mm(pP[:, 0:321], gsrc_[:, hs], Mcat[:, 0:321], True, True, [gkey_, 'Mcat'], [kP])
                    P.act(E13[:, sl, :], pP[:, 0:321], AF.Exp, [kP], [('E13', sl)])
                    if not need_o(order[si]):
                        return
                    P.ts(E13[:, sl, 64:192], E13[:, sl, 64:192], 2.0e17, ALU.min, [('E13', sl)], [('E13', sl)])
                    P.tr(pT[:, 0, 0:C], qsrc[:, hs], self.identb[:C, :C], [qkey, 'identb'], [kT])
                    P.tr(pT[:, 1, 0:C], ksl_(h * 128, (h + 1) * 128), self.identb[:C, :C], [kkey_, 'identb'], [kT])
                    P.stt(qtl[:, sl, :], pT[:, 0, 0:C], br['qs'], E13[:, sl, 0:64], ALU.mult, ALU.mult, [kT, ('E13', sl)], [('qtl', sl)])
                    P.stt(qtd[:, sl, :], pT[:, 0, 0:C], br['qs'], E13[:, sl, 64:128], ALU.mult, ALU.mult, [kT, ('E13', sl)], [('qtd', sl)])
                    P.stt(qtp[:, sl, :], pT[:, 0, 0:C], br['qs'], E13[:, sl, 256:320], ALU.mult, ALU.mult, [kT, ('E13', sl)], [('qtp', sl)])
                    kbase = kdz[:, sl, :]
                    kd_out = bass.AP(kbase.tensor, kbase.offset, [list(kbase.ap[0]), [96, 2], [1, 32]])
                    P.tt(kd_out, pT[:, 1, 0:C].rearrange("p (a b) -> p a b", b=32), E13[:, sl, 128:192].rearrange("p (a b) -> p a b", b=32),
                         ALU.mult, [kT, ('E13', sl)], ['kdz%d' % sl])
                    lo, hi = (0, 32) if d == 0 else (32, 64)
                    P.tt(koz[:, sl, 0, lo:hi], pT[:, 1, lo:hi], E13[:, sl, 192 + lo:192 + hi], ALU.mult,
                         [kT, ('E13', sl)], ['koz%d' % sl])

                def emit_back(si, h):
                    pb = si % 2; sl = h % 2; hs = slice(h * 128, (h + 1) * 128)
                    K = lambda nm: (nm, pb)
                    offI = 1 if d == 0 else 0
                    for I in (range(2) if need_o(order[si]) else ()):
                        has_off = (I == offI)
                        P.mm(pS[:C, 32 * I:32 * I + 32], kdz[:, sl, 64 * I:64 * I + 64], qtd[:, sl, 32 * I:32 * I + 32], True, not has_off,
                             ['kdz%d' % sl, ('qtd', sl)], ['pS'])
                        if has_off:
                            P.mm(pS[:C, 32 * I:32 * I + 32], koz[:, sl, 0, :], qtl[:, sl, 32 * I:32 * I + 32], False, True,
                                 ['koz%d' % sl, ('qtl', sl)], ['pS'])
                    if need_o(order[si]):
                        P.tt(scb[:, sl, :], pS[:C, 0:C], Mcat[:, 385:449], ALU.mult, ['pS', 'Mcat'], [('scb', sl)])
                    for ec in (range(dvc) if need_o(order[si]) else ()):
                        col = h * dv + ec * 128; ci = col // 128; so = ci % 4
                        P.mm(pO[:, so, :], v_t[:, pb, col:col + 128], scb[:, sl, :], True, False, [K('v_t'), ('scb', sl)], ['pO'])
                        P.mm(pO[:, so, :], Sb[:, col:col + 128], qtp[:, sl, :], False, True, [('Sb', h), ('qtp', sl)], ['pO'])
                        if d == 0:
                            P.cp(ofs[:, pb, ci, :], pO[:, so, :], ['pO'], [('ofs', pb)], eng='act')
                        else:
                            P.tt(osum[:, pb, ci, :], pO[:, so, :], of_t[:, pb, ci, :], ALU.add, ['pO', K('of_t')], [K('osum')])
                    P.mm(pU[:, 0, 0:dv], khat[:, pb, hs], v_t[:, pb, h * dv:(h + 1) * dv], True, True,
                         [('khat', pb, (h * 128) // 512), K('v_t')], ['pU'])
                    P.stt(S[:, h * dv:(h + 1) * dv], S[:, h * dv:(h + 1) * dv], E13[:, sl, 320:321], pU[:, 0, 0:dv],
                          ALU.mult, ALU.add, [('S', h), ('E13', sl), 'pU'], [('S', h)])
                    P.cp(Sb[:, h * dv:(h + 1) * dv], S[:, h * dv:(h + 1) * dv], [('S', h)], [('Sb', h)], eng='act')

                def emit_out(si):
                    n = order[si]; t0 = n * C; pb = si % 2
                    K = lambda nm: (nm, pb)
                    if not need_o(n):
                        return
                    if d == 0:
                        P.dma('act', ofv[:, :, t0:t0 + C], ofs[:, pb], reads=[('ofs', pb)], writes=[('ofT', n)])
                        return
                    osm = osum[:, pb]
                    P.act(sg[:], gate_t[:, pb], AF.Silu, [K('gate_t')], ['sg'])
                    if br['norm'] == 'ln':
                        for h in range(H):
                            for ec in range(dvc):
                                P.mm(pc[:, h * C:(h + 1) * C], self.onesf[:], osm[:, h * dvc + ec, :], ec == 0, ec == dvc - 1, ['onesf', K('osum')], ['pc'])
                        P.ts(mean[:, 0:H, :], pc[:, 0:H * C].rearrange("p (h c) -> p h c", c=C), 1.0 / dv, ALU.mult, ['pc'], ['mean'])
                        for ci in range(8):
                            P.tt(osm[:, ci, :], osm[:, ci, :], mean[:, ci // dvc, :], ALU.subtract, [K('osum'), 'mean'], [K('osum')])
                    P.act(sqt[:], osm, AF.Square, [K('osum')], ['sqt'])
                    for h in range(H):
                        for ec in range(dvc):
                            P.mm(pc[:, h * C:(h + 1) * C], self.onesf[:], sqt[:, h * dvc + ec, :], ec == 0, ec == dvc - 1, ['onesf', 'sqt'], ['pc'])
                    P.act(rs[:, 0:H, :], pc[:, 0:H * C].rearrange("p (h c) -> p h c", c=C), AF.Sqrt, ['pc'], ['rs'], bias=EPS, scale=1.0 / dv)
                    P.op('dve', lambda e: e.reciprocal(out=rs[:, 0:H, :], in_=rs[:, 0:H, :]), ['rs'], ['rs'])
                    for ci in range(8):
                        P.stt(otmp[:, ci, :], osm[:, ci, :], gn[:, (ci % dvc):(ci % dvc) + 1], rs[:, ci // dvc, :], ALU.mult, ALU.mult,
                              [K('osum'), 'gn', 'rs'], ['otmp'])
                    P.tt(ob[:, pb], otmp[:], sg[:], ALU.mult, ['otmp', 'sg'], [('ob', pb)])
                    P.dma('act', ov[:, :, t0:t0 + C], ob[:, pb], reads=[('ob', pb)], writes=[('oT', n)])

                nch = len(order)
                PIPE = getattr(c, 'pipe', True)
                if PIPE:
                    emit_prep(0)
                for si in range(nch):
                    if not PIPE:
                        emit_prep(si)
                        for h in range(H):
                            emit_front(si, h)
                            emit_back(si, h)
                        emit_out(si)
                        continue
                    emit_front(si, 0)
                    for h in range(H):
                        if h + 1 < H:
                            emit_front(si, h + 1)
                        emit_back(si, h)
                        if h == H // 2 - 1 and si + 1 < nch:
                            emit_prep(si + 1)
                    emit_out(si)
                P.barrier()

    def phase_gla2(self, layer, name, last=False):
        c = self.cfg; P = self.P; T = self.T; L = c.L; CTX = c.CTX
        nlo = c.need_lo(last); nhi = c.need_hi(last)
        from contextlib import ExitStack
        br = self.BR[name]; H = br['H']; dv = br['dv']; dvc = dv // 128; HW = H * 128; C = 128
        assert H == 4 and dv == 256
        NCH = L // C; ctxch = CTX // C
        orders = [[n for n in range(NCH) if n * C < nhi], list(range(ctxch - 1, -1, -1)) + list(range(NCH - 1, ctxch - 1, -1))]
        need_o = lambda n: (nlo <= n * C < nhi)
        ofv = T['ofT'][br['ofrow']:br['ofrow'] + 1024, :].rearrange("(c p) t -> p c t", p=128)
        ov = T['oT'][br['orow']:br['orow'] + 1024, :].rearrange("(c p) t -> p c t", p=128)
        gv = T['fmb'][br['gate']:br['gate'] + 1024, :].rearrange("(c p) t -> p c t", p=128)
        sb = self.sb; ps = self.ps
        with ExitStack() as es:
            def mk(nm, shape, dt=F32):
                return es.enter_context(sb(nm, shape, dt))
            M = mk("M2", [128, 513]); q_t = mk("q_t2", [128, 2, 512], BF16); v_t = mk("v_t2", [128, 2, 1024], BF16)
            k_t = mk("k_t2", [128, 2, 512], BF16); lr_t = mk("lr_t2", [17, 2, C]); rp_t = mk("rp_t2", [128, 2, 512])
            of_t = mk("of_t2", [128, 2, 8, C]); gate_t = mk("gate_t2", [128, 2, 8, C], BF16)
            sig = mk("sig2", [128, 512]); graw = mk("graw2", [128, 2, 512]); kt = mk("kt2", [128, 2, 512], BF16)
            qr = mk("qr2", [128, 2, 512], BF16); khat = mk("khat2", [128, 2, 512], BF16); Ek = mk("Ek2", [128, 512])
            rt1 = mk("rt1b", [128, 256]); rt2 = mk("rt2b", [128, 256]); rt3 = mk("rt3b", [128, 256]); rt4 = mk("rt4b", [128, 256])
            E = mk("E_2", [128, 2, 257]); E2 = mk("E2_2", [128, 2, 128])
            qtl = mk("qtl2", [128, 2, C], BF16); qtp = mk("qtp2", [128, 2, C], BF16); ktl = mk("ktl2", [128, 2, C], BF16)
            scb = mk("scb2", [128, 2, C], BF16); ofs = mk("ofs2", [128, 2, 8, C]); osum = mk("osum2", [128, 2, 8, C])
            sg = mk("sg2_", [128, 8, C]); sqt = mk("sqt2", [128, 8, C]); rs = mk("rs2", [128, 8, C]); mean = mk("mean2", [128, 8, C])
            otmp = mk("otmp2", [128, 8, C]); ob = mk("ob2", [128, 2, 8, C], BF16)
            S = mk("S2", [128, 1024]); Sb = mk("Sb2", [128, 1024], BF16)
            wg = mk("wg2", [17, 512]); gn = mk("gn2", [128, 2]); rl = mk("rl2", [128, 4]); gconst = mk("gconst2", [128, 512])
            pc = es.enter_context(ps("pc2", [128, 512], F32))
            pPs = [es.enter_context(ps("pPa", [128, 512], F32)), es.enter_context(ps("pPb", [128, 512], F32))]
            pTs = [es.enter_context(ps("pTa", [128, 8, 128], BF16)), es.enter_context(ps("pTb", [128, 8, 128], BF16))]
            pS = es.enter_context(ps("pS2", [128, 512], F32)); pO = es.enter_context(ps("pO2", [128, 4, C], F32))
            pU = es.enter_context(ps("pU2", [128, 512], F32))
            P.dma('sp', gn[:], T['gla_ngT' if name == 'gl' else 'ret_ngT'][layer], writes=['gn'])
            P.memset(lr_t[:], 1.0, ['lr0', 'lr1'])
            for d in (0, 1):
                gs = br['gs']
                P.dma('sp', M[:, :], T['gl_consts2'][d], writes=['M'])
                P.ts(M[:, 0:385], M[:, 0:385], gs, ALU.mult, ['M'], ['M'])
                if name == 'gl':
                    P.dma('sp', wg[:], T['gla_wg'][layer, d], writes=['wg'])
                else:
                    P.dma('sp', rl[:], T['ret_lrep'][layer, d], writes=['rl'])
                    P.act(rl[:], rl[:], AF.Exp, ['rl'], ['rl'], scale=-1.0)
                    P.act(rl[:], rl[:], AF.Ln, ['rl'], ['rl'], bias=1.0)
                    P.ts(rl[:], rl[:], -1.0, ALU.mult, ['rl'], ['rl'])
                    for h in range(4):
                        P.ts(gconst[:, h * 128:(h + 1) * 128], self.onesf[:, :], rl[:, h:h + 1], ALU.mult, ['onesf', 'rl'], ['gconst'])
                P.memset(S[:], 0.0, [('S', h) for h in range(H)])
                P.memset(Sb[:], 0.0, [('Sb', h) for h in range(H)])
                order = orders[d]

                def srcs(pb):
                    K = lambda nm: (nm, pb)
                    if name == 'rt':
                        return gconst[:, :], 'gconst', (lambda a, b: kt[:, pb, a:b]), K('kt'), qr[:, pb, :], K('qr')
                    return graw[:, pb, :], K('graw'), (lambda a, b: k_t[:, pb, a:b]), K('k_t'), q_t[:, pb, :], K('q_t')

                def emit_prep(si):
                    n = order[si]; t0 = n * C; pb = si % 2
                    K = lambda nm: (nm, pb)
                    P.dma('sp', q_t[:, pb, :], T['tmb'][t0:t0 + C, br['q']:br['q'] + HW], writes=[K('q_t')])
                    P.dma('sp', v_t[:, pb, :], T['tmb'][t0:t0 + C, br['v']:br['v'] + 1024], writes=[K('v_t')])
                    P.dma('sp', k_t[:, pb, :], T['tmb'][t0:t0 + C, br['k']:br['k'] + HW], writes=[K('k_t')])
                    if name == 'gl':
                        P.dma('sp', lr_t[0:16, pb, :], T['lrT'][d * 16:(d + 1) * 16, t0:t0 + C], writes=[('lr%d' % pb)])
                    else:
                        P.dma('sp', rp_t[:, pb, :], T['rope'][t0:t0 + C, :], writes=[K('rp_t')])
                    if d == 1 and need_o(n):
                        P.dma('sp', of_t[:, pb], ofv[:, :, t0:t0 + C], writes=[K('of_t')])
                        P.dma('sp', gate_t[:, pb], gv[:, :, t0:t0 + C], writes=[K('gate_t')])
                    if name == 'gl':
                        P.mm(pc[:, :], lr_t[0:17, pb, :], wg[0:17, :], True, True, ['lr%d' % pb, 'wg'], ['pc'])
                        P.act(sig[:], pc[:, :], AF.Exp, ['pc'], ['sig'], scale=-1.0)
                        P.act(graw[:, pb, :], sig[:], AF.Ln, ['sig'], [K('graw')], bias=1.0)
                    else:
                        cos4 = rp_t[:, pb, 0:256].rearrange("p (h x) -> p h x", x=64)
                        sin4 = rp_t[:, pb, 256:512].rearrange("p (h x) -> p h x", x=64)
                        for (src, skey, dst, dkey, eng, ta, tb) in ((q_t, K('q_t'), qr, K('qr'), 'dve', rt1, rt2), (k_t, K('k_t'), kt, K('kt'), 'pool', rt3, rt4)):
                            xv = src[:, pb, :].rearrange("p (h two x) -> p h two x", two=2, x=64)
                            ov_ = dst[:, pb, :].rearrange("p (h two x) -> p h two x", two=2, x=64)
                            tav = ta[:].rearrange("p (h x) -> p h x", x=64); tbv = tb[:].rearrange("p (h x) -> p h x", x=64)
                            P.tt(tav, xv[:, :, 0, :], cos4, ALU.mult, [skey, K('rp_t')], [ta.name], eng=eng)
                            P.tt(tbv, xv[:, :, 1, :], sin4, ALU.mult, [skey, K('rp_t')], [tb.name], eng=eng)
                            P.tt(ov_[:, :, 0, :], tav, tbv, ALU.subtract, [ta.name, tb.name], [dkey], eng=eng)
                            P.tt(tav, xv[:, :, 0, :], sin4, ALU.mult, [skey, K('rp_t')], [ta.name], eng=eng)
                            P.tt(tbv, xv[:, :, 1, :], cos4, ALU.mult, [skey, K('rp_t')], [tb.name], eng=eng)
                            P.tt(ov_[:, :, 1, :], tav, tbv, ALU.add, [ta.name, tb.name], [dkey], eng=eng)
                    gsrc_, gkey_, ksl_, kkey_, qsrc_, qkey_ = srcs(pb)
                    P.mm(pc[:, :], M[:, 257:385], gsrc_, True, True, ['M', gkey_], ['pc'])
                    P.act(Ek[:], pc[:, :], AF.Exp, ['pc'], ['Ek'])
                    P.tt(khat[:, pb, :], ksl_(0, 512), Ek[:], ALU.mult, [kkey_, 'Ek'], [K('khat')])

                def emit_front(si, h):
                    pb = si % 2; sl = h % 2; hs = slice(h * 128, (h + 1) * 128)
                    gsrc_, gkey_, ksl_, kkey_, qsrc_, qkey_ = srcs(pb)
                    pP = pPs[sl]; pT = pTs[sl]; kP = 'pP%d' % sl; kT = 'pT%d' % sl
                    P.mm(pP[:, 0:257], gsrc_[:, hs], M[:, 0:257], True, True, [gkey_, 'M'], [kP])
                    P.act(E[:, sl, :], pP[:, 0:257], AF.Exp, [kP], [('E', sl)])
                    if not need_o(order[si]):
                        return
                    P.act(E2[:, sl, :], pP[:, 0:128], AF.Exp, [kP], [('E2', sl)], scale=-1.0)
                    P.tr(pT[:, 0, 0:C], qsrc_[:, hs], self.identb[:, :], [qkey_, 'identb'], [kT])
                    P.tr(pT[:, 1, 0:C], ksl_(h * 128, (h + 1) * 128), self.identb[:, :], [kkey_, 'identb'], [kT])
                    P.stt(qtl[:, sl, :], pT[:, 0, 0:C], br['qs'], E[:, sl, 0:128], ALU.mult, ALU.mult, [kT, ('E', sl)], [('qtl', sl)])
                    P.stt(qtp[:, sl, :], pT[:, 0, 0:C], br['qs'], E[:, sl, 128:256], ALU.mult, ALU.mult, [kT, ('E', sl)], [('qtp', sl)])
                    P.tt(ktl[:, sl, :], pT[:, 1, 0:C], E2[:, sl, :], ALU.mult, [kT, ('E2', sl)], [('ktl', sl)])

                def emit_back(si, h):
                    pb = si % 2; sl = h % 2; hs = slice(h * 128, (h + 1) * 128)
                    K = lambda nm: (nm, pb)
                    if need_o(order[si]):
                        P.mm(pS[:, 0:C], ktl[:, sl, :], qtl[:, sl, :], True, True, [('ktl', sl), ('qtl', sl)], ['pS'])
                        P.tt(scb[:, sl, :], pS[:, 0:C], M[:, 385:513], ALU.mult, ['pS', 'M'], [('scb', sl)])
                    for ec in (range(dvc) if need_o(order[si]) else ()):
                        col = h * dv + ec * 128; ci = col // 128; so = ci % 4
                        P.mm(pO[:, so, :], v_t[:, pb, col:col + 128], scb[:, sl, :], True, False, [K('v_t'), ('scb', sl)], ['pO'])
                        P.mm(pO[:, so, :], Sb[:, col:col + 128], qtp[:, sl, :], False, True, [('Sb', h), ('qtp', sl)], ['pO'])
                        if d == 0:
                            P.cp(ofs[:, pb, ci, :], pO[:, so, :], ['pO'], [('ofs', pb)], eng='act')
                        else:
                            P.tt(osum[:, pb, ci, :], pO[:, so, :], of_t[:, pb, ci, :], ALU.add, ['pO', K('of_t')], [K('osum')])
                    P.mm(pU[:, 0:dv], khat[:, pb, hs], v_t[:, pb, h * dv:(h + 1) * dv], True, True, [K('khat'), K('v_t')], ['pU'])
                    P.stt(S[:, h * dv:(h + 1) * dv], S[:, h * dv:(h + 1) * dv], E[:, sl, 256:257], pU[:, 0:dv],
                          ALU.mult, ALU.add, [('S', h), ('E', sl), 'pU'], [('S', h)])
                    P.cp(Sb[:, h * dv:(h + 1) * dv], S[:, h * dv:(h + 1) * dv], [('S', h)], [('Sb', h)], eng='act')

                def emit_out(si):
                    n = order[si]; t0 = n * C; pb = si % 2
                    K = lambda nm: (nm, pb)
                    if not need_o(n):
                        return
                    if d == 0:
                        P.dma('act', ofv[:, :, t0:t0 + C], ofs[:, pb], reads=[('ofs', pb)], writes=[('ofT', n)])
                        return
                    osm = osum[:, pb]
                    P.act(sg[:], gate_t[:, pb], AF.Silu, [K('gate_t')], ['sg'])
                    if br['norm'] == 'ln':
                        for h in range(H):
                            for ec in range(dvc):
                                P.mm(pc[:, h * C:(h + 1) * C], self.onesf[:], osm[:, h * dvc + ec, :], ec == 0, ec == dvc - 1, ['onesf', K('osum')], ['pc'])
                        P.ts(mean[:, 0:H, :], pc[:, 0:H * C].rearrange("p (h c) -> p h c", c=C), 1.0 / dv, ALU.mult, ['pc'], ['mean'])
                        for ci in range(8):
                            P.tt(osm[:, ci, :], osm[:, ci, :], mean[:, ci // dvc, :], ALU.subtract, [K('osum'), 'mean'], [K('osum')])
                    P.act(sqt[:], osm, AF.Square, [K('osum')], ['sqt'])
                    for h in range(H):
                        for ec in range(dvc):
                            P.mm(pc[:, h * C:(h + 1) * C], self.onesf[:], sqt[:, h * dvc + ec, :], ec == 0, ec == dvc - 1, ['onesf', 'sqt'], ['pc'])
                    P.act(rs[:, 0:H, :], pc[:, 0:H * C].rearrange("p (h c) -> p h c", c=C), AF.Sqrt, ['pc'], ['rs'], bias=EPS, scale=1.0 / dv)
                    P.op('dve', lambda e: e.reciprocal(out=rs[:, 0:H, :], in_=rs[:, 0:H, :]), ['rs'], ['rs'])
                    for ci in range(8):
                        P.stt(otmp[:, ci, :], osm[:, ci, :], gn[:, (ci % dvc):(ci % dvc) + 1], rs[:, ci // dvc, :], ALU.mult, ALU.mult,
                              [K('osum'), 'gn', 'rs'], ['otmp'])
                    P.tt(ob[:, pb], otmp[:], sg[:], ALU.mult, ['otmp', 'sg'], [('ob', pb)])
                    P.dma('act', ov[:, :, t0:t0 + C], ob[:, pb], reads=[('ob', pb)], writes=[('oT', n)])

                nch = len(order)
                emit_prep(0)
                for si in range(nch):
                    emit_front(si, 0)
                    for h in range(H):
                        if h + 1 < H:
                            emit_front(si, h + 1)
                        emit_back(si, h)
                        if h == H // 2 - 1 and si + 1 < nch:
                            emit_prep(si + 1)
                    emit_out(si)
                P.barrier()

    def s5_powers(self, src_re, src_im, src_dt, Pn, Fn, pw_re, pw_im, coef_re, coef_im, tag):
        P = self.P
        from contextlib import ExitStack
        with ExitStack() as es:
            def mk(nm, shape, dt=F32):
                return es.enter_context(self.sb(tag + nm, shape, dt))
            lre = mk("lre", [Pn, Fn]); lim = mk("lim", [Pn, Fn]); dtv = mk("dtv", [Pn, Fn])
            a = mk("a", [Pn, Fn]); th = mk("th", [Pn, Fn]); kki = mk("kki", [Pn, 17, Fn], I32); kk = mk("kk", [Pn, 17, Fn])
            A = mk("A", [Pn, 17, Fn]); TH = mk("TH", [Pn, 17 * Fn]); SN = mk("SN", [Pn, 17 * Fn]); CS = mk("CS", [Pn, 17 * Fn])
            t1 = mk("t1", [Pn, Fn]); t2 = mk("t2", [Pn, Fn]); den = mk("den", [Pn, Fn])
            P.dma('sp', lre[:], src_re, writes=['lre']); P.dma('sp', lim[:], src_im, writes=['lim']); P.dma('sp', dtv[:], src_dt, writes=['dtv'])
            P.act(dtv[:], dtv[:], AF.Exp, ['dtv'], ['dtv'])
            P.ts(lre[:], lre[:], -1e-4, ALU.min, ['lre'], ['lre'])
            P.tt(a[:], lre[:], dtv[:], ALU.mult, ['lre', 'dtv'], ['a'])
            P.tt(th[:], lim[:], dtv[:], ALU.mult, ['lim', 'dtv'], ['th'])
            P.op('pool', lambda e: e.iota(kki[:], pattern=[[1, 17], [0, Fn]], base=0, channel_multiplier=0), [], ['kki'])
            P.cp(kk[:], kki[:], ['kki'], ['kk'])
            P.tt(A[:], kk[:], a[:].unsqueeze(1).to_broadcast([Pn, 17, Fn]), ALU.mult, ['kk', 'a'], ['A'])
            P.tt(TH[:].rearrange("p (k f) -> p k f", f=Fn), kk[:], th[:].unsqueeze(1).to_broadcast([Pn, 17, Fn]), ALU.mult, ['kk', 'th'], [tag + 'scx'])
            P.act(A[:], A[:], AF.Exp, ['A'], ['A'])
            self.sincos(TH[:], [Pn, 17 * Fn], SN[:], CS[:], tag + 'sc')
            P.tt(pw_re, A[:], CS[:].rearrange("p (k f) -> p k f", f=Fn), ALU.mult, ['A'], ['pwre'])
            P.tt(pw_im, A[:], SN[:].rearrange("p (k f) -> p k f", f=Fn), ALU.mult, ['A'], ['pwim'])
            if coef_re is not None:
                P.ts(t1[:], pw_re[:, 1, :], -1.0, ALU.add, ['pwre'], ['t1'])
                P.tt(den[:], lre[:], lre[:], ALU.mult, ['lre'], ['den'])
                P.tt(t2[:], lim[:], lim[:], ALU.mult, ['lim'], ['t2'])
                P.tt(den[:], den[:], t2[:], ALU.add, ['den', 't2'], ['den'])
                P.op('dve', lambda e: e.reciprocal(out=den[:], in_=den[:]), ['den'], ['den'])
                P.tt(coef_re, t1[:], lre[:], ALU.mult, ['t1', 'lre'], ['cre'])
                P.tt(t2[:], pw_im[:, 1, :], lim[:], ALU.mult, ['pwim', 'lim'], ['t2'])
                P.tt(coef_re, coef_re, t2[:], ALU.add, ['cre', 't2'], ['cre'])
                P.tt(coef_re, coef_re, den[:], ALU.mult, ['cre', 'den'], ['cre'])
                P.tt(coef_im, pw_im[:, 1, :], lre[:], ALU.mult, ['pwim', 'lre'], ['cim'])
                P.tt(t2[:], t1[:], lim[:], ALU.mult, ['t1', 'lim'], ['t2'])
                P.tt(coef_im, coef_im, t2[:], ALU.subtract, ['cim', 't2'], ['cim'])
                P.tt(coef_im, coef_im, den[:], ALU.mult, ['cim', 'den'], ['cim'])
            P.barrier()

    def cmul(self, out_re, out_im, a_re, a_im, b_re, b_im, t, neg_im=False, eng='dve'):
        P = self.P
        P.tt(out_re, a_re, b_re, ALU.mult, ['cm_in'], ['cm_re'], eng=eng)
        P.tt(t, a_im, b_im, ALU.mult, ['cm_in'], ['cm_t'], eng=eng)
        P.tt(out_re, out_re, t, ALU.subtract, ['cm_re', 'cm_t'], ['cm_re'], eng=eng)
        P.tt(out_im, a_re, b_im, ALU.mult, ['cm_in'], ['cm_im'], eng=eng)
        P.tt(t, a_im, b_re, ALU.mult, ['cm_in', 'cm_re'], ['cm_t'], eng=eng)
        if neg_im:
            P.stt(out_im, out_im, -1.0, t, ALU.mult, ALU.subtract, ['cm_im', 'cm_t'], ['cm_im'])
        else:
            P.tt(out_im, out_im, t, ALU.add, ['cm_im', 'cm_t'], ['cm_im'], eng=eng)

    def phase_s5(self, layer, last=False):
        c = self.cfg; P = self.P; T = self.T; L = c.L; CTX = c.CTX
        nlo = c.need_lo(last); nhi = c.need_hi(last)
        otiles = tiles(nlo, nhi, 512) if last else tiles(0, L, 512)
        from contextlib import ExitStack
        SB = 16; NB = L // SB; NBc = CTX // SB; NBP = NB + 2
        sb = self.sb; ps = self.ps
        urows = T['fmb'][2048:3072, :]
        for d in (0, 1):
            with ExitStack() as esd:
                KtL = esd.enter_context(sb("KtL", [128, 8, 16, 128], BF16))
                PWPr = esd.enter_context(sb("PWPr", [128, 17, 32], F32)); PWPi = esd.enter_context(sb("PWPi", [128, 17, 32], F32))
                self.s5_powers(T['s5_lamP'][layer, d, 0], T['s5_lamP'][layer, d, 1], T['s5_dtP'][layer, d], 128, 32,
                               PWPr[:], PWPi[:], None, None, 'pp')
                with ExitStack() as es:
                    def mk(nm, shape, dt=F32):
                        return es.enter_context(sb(nm, shape, dt))
                    pwr = mk("pwr", [64, 17, 64]); pwi = mk("pwi", [64, 17, 64]); cre = mk("cre", [64, 64]); cim = mk("cim", [64, 64])
                    self.s5_powers(T['s5_lam64'][layer, d, 0], T['s5_lam64'][layer, d, 1], T['s5_dt64'][layer, d], 64, 64,
                                   pwr[:], pwi[:], cre[:], cim[:], 'p64')
                    B64 = mk("B64", [64, 2, 64, 16]); C64 = mk("C64", [64, 2, 64, 16])
                    bbr = mk("bbr", [64, 64, 16]); bbi = mk("bbi", [64, 64, 16]); tt_ = mk("tt_", [64, 64, 16])
                    Wr = mk("Wr", [64, 64, 16]); Wi = mk("Wi", [64, 64, 16])
                    Wpr = mk("Wpr", [64, 64 * 128], BF16); Wpi = mk("Wpi", [64, 64 * 128], BF16)
                    Cpr = mk("Cpr", [64, 64 * 128], BF16); Cpi = mk("Cpi", [64, 64 * 128], BF16)
                    pk0 = es.enter_context(ps("pk0", [128, 512], F32)); pk1 = es.enter_context(ps("pk1", [128, 512], F32))
                    pks = [pk0, pk1]
                    P.dma('sp', B64[:], T['s5_b64'][layer, d].rearrange("x p g h -> p x g h"), writes=['B64'])
                    P.dma('sp', C64[:], T['s5_c64'][layer, d].rearrange("x p g h -> p x g h"), writes=['C64'])
                    for t_ in (Wpr, Wpi, Cpr, Cpi):
                        P.memset(t_[:], 0.0, ['pad' + t_.name])
                    P.barrier()
                    bc = lambda ap: ap.unsqueeze(2).to_broadcast([64, 64, 16])
                    self.cmul(bbr[:], bbi[:], bc(cre[:]), bc(cim[:]), B64[:, 0], B64[:, 1], tt_[:])
                    P.barrier()

                    def diag(t_):
                        b_ = t_[:]
                        return bass.AP(b_.tensor, b_.offset, [list(b_.ap[0]), [1024, 8], [144, 8], [1, 16]])
                    P.cp(diag(Cpr), C64[:, 0].rearrange("p (a b) h -> p a b h", b=8), [], ['cpr'])
                    P.cp(diag(Cpi), C64[:, 1].rearrange("p (a b) h -> p a b h", b=8), [], ['cpi'])
                    P.barrier()
                    for tau in range(16):
                        self.cmul(Wr[:], Wi[:], bc(pwr[:, tau, :]), bc(pwi[:, tau, :]), bbr[:], bbi[:], tt_[:], neg_im=True)
                        P.cp(diag(Wpr), Wr[:].rearrange("p (a b) h -> p a b h", b=8), ['cm_re'], ['Wpr'])
                        P.cp(diag(Wpi), Wi[:].rearrange("p (a b) h -> p a b h", b=8), ['cm_im'], ['Wpi'])
                        for gb in range(8):
                            pk = pks[gb % 2]; pkey = 'pk%d' % (gb % 2)
                            for g8 in range(8):
                                g = gb * 8 + g8
                                P.mm(pk[:, 0:128], Wpr[:, g * 128:(g + 1) * 128], Cpr[:, g * 128:(g + 1) * 128], g8 == 0, False, ['Wpr'], [pkey])
                                P.mm(pk[:, 0:128], Wpi[:, g * 128:(g + 1) * 128], Cpi[:, g * 128:(g + 1) * 128], False, g8 == 7, ['Wpi'], [pkey])
                            P.cp(KtL[:, gb, tau, :], pk[:, 0:128], [pkey], ['KtL'], eng=('act' if gb % 2 else 'dve'))
                    P.barrier()
                with ExitStack() as es:
                    pfr = es.enter_context(sb("pfrA", [128, 17, 64], F32)); pfi = es.enter_context(sb("pfiA", [128, 17, 64], F32))
                    cfr = es.enter_context(sb("cfrA", [128, 64], F32)); cfi = es.enter_context(sb("cfiA", [128, 64], F32))
                    for gb in range(8):
                        self.s5_powers(T['s5_lamF'][layer, d, 0][:, gb, :], T['s5_lamF'][layer, d, 1][:, gb, :], T['s5_dtF'][layer, d][:, gb, :],
                                       128, 64, pfr[:], pfi[:], cfr[:], cfi[:], 'pf')
                        P.dma('sp', T['s5pf'][gb, 0], pfr[:], reads=[], writes=['d1'])
                        P.dma('sp', T['s5pf'][gb, 1], pfi[:], reads=[], writes=['d2'])
                        P.dma('sp', T['s5cf'][gb, 0], cfr[:], reads=[], writes=['d3'])
                        P.dma('sp', T['s5cf'][gb, 1], cfi[:], reads=[], writes=['d4'])
                        P.barrier()
                with ExitStack() as es:
                    def mk(nm, shape, dt=F32):
                        return es.enter_context(sb(nm, shape, dt))
                    AA = mk("AA", [128, 2, 32, NBP]); Ar = AA[:, 0]; Ai = AA[:, 1]
                    ar2 = mk("ar2", [128, 2, 32]); nai = mk("nai", [128, 32]); st_ = mk("st_", [128, 2, 32]); su_ = mk("su_", [128, 2, 32])
                    Pad = mk("Pad", [128, 4, 16, 2, 128], BF16)
                    Abr = mk("Abr", [128, 4, NBP], BF16); Abi = mk("Abi", [128, 4, NBP], BF16)
                    uT0 = mk("uT0", [128, L], BF16); uTs = [uT0, uT0]
                    maskQ = mk("maskQ", [128, 4, 128]); maskC = mk("maskC", [128, 2, 16])
                    pfr = mk("pfr", [128, 17, 64]); pfi = mk("pfi", [128, 17, 64]); cfr = mk("cfr", [128, 64]); cfi = mk("cfi", [128, 64])
                    BF_ = mk("BF_", [128, 2, 64]); bfr = mk("bfr", [128, 64]); bfi = mk("bfi", [128, 64]); tf = mk("tf", [128, 64])
                    Vr = mk("Vr", [128, 64]); Vi = mk("Vi", [128, 64])
                    cP = mk("cP", [128, 2, 32, 16]); CLr = mk("CLr", [128, 4, 16]); CLi = mk("CLi", [128, 4, 16]); tc_ = mk("tc_", [128, 4, 16])
                    s1 = mk("s1", [128, 32]); s2 = mk("s2", [128, 32]); s3 = mk("s3", [128, 32]); s4 = mk("s4", [128, 32])
                    ysb0 = mk("ysb0", [128, 512]); ysb1 = mk("ysb1", [128, 512]); yf_t = mk("yf_t", [128, 512]); y2 = mk("y2", [128, 512])
                    zb = mk("zb", [128, 512], BF16); dT = mk("dT", [128, 8])
                    pw0 = es.enter_context(ps("pw0", [128, 512], F32)); pw1 = es.enter_context(ps("pw1", [128, 512], F32))
                    py0 = es.enter_context(ps("py0", [128, 512], F32)); py1 = es.enter_context(ps("py1", [128, 512], F32))
                    P.dma('sp', maskQ[:], T['s5_maskQ'], writes=['maskQ'])
                    P.dma('sp', maskC[:], T['s5_maskC'], writes=['maskC'])
                    P.dma('sp', cP[:], T['s5_cP'][layer, d].rearrange("x p q h -> p x q h"), writes=['cP'])
                    P.dma('sp', dT[:], T['s5_dT'][layer], writes=['dT'])
                    P.memset(Pad[:], 0.0, ['Pad'])
                    P.memset(AA[:], 0.0, ['A'])
                    P.barrier()
                    nwp = 0
                    for gb in range(8):
                        uT = uTs[gb % 2]
                        P.dma('sp', uT[:], urows[gb * 128:(gb + 1) * 128, :], writes=[('uT', 0)])
                        P.dma('sp', pfr[:], T['s5pf'][gb, 0], writes=['cm_in'])
                        P.dma('sp', pfi[:], T['s5pf'][gb, 1], writes=['cm_in'])
                        P.dma('sp', cfr[:], T['s5cf'][gb, 0], writes=['cm_in'])
                        P.dma('sp', cfi[:], T['s5cf'][gb, 1], writes=['cm_in'])
                        P.dma('sp', BF_[:], T['s5_bF'][layer, d][:, :, gb, :].rearrange("x p f -> p x f"), writes=['cm_in'])
                        P.barrier()
                        self.cmul(bfr[:], bfi[:], cfr[:], cfi[:], BF_[:, 0], BF_[:, 1], tf[:])
                        P.barrier()
                        for j in range(16):
                            pw_ = (15 - j) if d == 0 else j
                            self.cmul(Vr[:], Vi[:], pfr[:, pw_, :], pfi[:, pw_, :], bfr[:], bfi[:], tf[:])
                            for x, V in ((0, Vr), (1, Vi)):
                                P.tt(Pad[:, :, j, x, :].rearrange("p q (a b) -> p q a b", b=64),
                                     V[:].unsqueeze(1).unsqueeze(1).to_broadcast([128, 4, 2, 64]),
                                     maskQ[:].rearrange("p q (a b) -> p q a b", b=64), ALU.mult,
                                     ['cm_re', 'cm_im', 'maskQ'], ['Pad'])
                        uv = uT[:].rearrange("p (n j) -> p n j", j=16)
                        for q in range(4):
                            pair = gb * 4 + q
                            for x, Ax in ((0, Ar), (1, Ai)):
                                pw = (pw0, pw1)[nwp % 2]; pwk = 'pw%d' % (nwp % 2); nwp += 1
                                for j in range(16):
                                    P.mm(pw[:, 0:NB], Pad[:, q, j, x, :], uv[:, :, j], j == 0, j == 15, ['Pad', ('uT', 0)], [pwk])
                                if d == 0:
                                    P.cp(Ax[:, pair, 1:NB + 1], pw[:, 0:NB], [pwk], ['A'], eng=('act' if x else 'dve'))
                                else:
                                    P.cp(Ax[:, pair, 0:NBc], pw[:, 0:NBc], [pwk], ['A'], eng=('act' if x else 'dve'))
                                    P.cp(Ax[:, pair, NBc + 1:NB + 1], pw[:, NBc:NB], [pwk], ['A'], eng=('act' if x else 'dve'))
                        P.barrier()
                    s5stop = getattr(c, 's5stop', 9)
                    if s5stop <= 2:
                        continue
                    ar = PWPr[:, 16, :]; ai = PWPi[:, 16, :]
                    P.cp(ar2[:, 0, :], ar, [], ['ar2']); P.cp(ar2[:, 1, :], ar, [], ['ar2'])
                    P.ts(nai[:], ai, -1.0, ALU.mult, [], ['nai'])
                    if d == 0:
                        steps = [(n + 1, n) for n in range(NB) if n * SB < nhi]
                    else:
                        steps = [(n, n + 1) for n in range(NBc - 1, -1, -1)] + ['copy'] + [(n + 1, n + 2) for n in range(NB - 1, NBc - 1, -1)]
                    for st in steps:
                        if st == 'copy':
                            P.cp(AA[:, :, :, NB + 1], AA[:, :, :, 0], ['A'], ['A'])
                            continue
                        pos, prev = st
                        P.tt(st_[:], AA[:, :, :, prev], ar2[:], ALU.mult, ['A', 'ar2'], ['st_'])
                        P.tt(su_[:, 0, :], AA[:, 1, :, prev], nai[:], ALU.mult, ['A', 'nai'], ['su_'])
                        P.tt(su_[:, 1, :], AA[:, 0, :, prev], ai, ALU.mult, ['A'], ['su_'])
                        P.tt(st_[:], st_[:], su_[:], ALU.add, ['st_', 'su_'], ['st_'])
                        P.tt(AA[:, :, :, pos], AA[:, :, :, pos], st_[:], ALU.add, ['A', 'st_'], ['A'])
                    P.barrier()
                    if s5stop <= 3:
                        continue
                    P.memset(Pad[:], 0.0, ['Pad'])
                    P.barrier()
                    npy = 0
                    for gb in range(8):
                        uT = uTs[gb % 2]
                        P.dma('sp', uT[:], urows[gb * 128:(gb + 1) * 128, :], writes=[('uT', 0)])
                        for i in range(16):
                            pw_ = (i + 1) if d == 0 else (16 - i)
                            bq = lambda ap: ap[:, gb * 4:gb * 4 + 4].unsqueeze(2).to_broadcast([128, 4, 16])
                            self.cmul(CLr[:], CLi[:], cP[:, 0, gb * 4:gb * 4 + 4, :], cP[:, 1, gb * 4:gb * 4 + 4, :],
                                      bq(PWPr[:, pw_, :]), bq(PWPi[:, pw_, :]), tc_[:], neg_im=True)
                            for x, CL in ((0, CLr), (1, CLi)):
                                pb_ = Pad[:, 0, i, x, :]
                                dst_ = bass.AP(pb_.tensor, pb_.offset, [list(pb_.ap[0]), [16 * 2 * 128 + 32, 4], [16, 2], [1, 16]])
                                P.tt(dst_, CL[:].unsqueeze(2).to_broadcast([128, 4, 2, 16]),
                                     maskC[:].unsqueeze(1).to_broadcast([128, 4, 2, 16]), ALU.mult,
                                     ['cm_re', 'cm_im', 'maskC'], ['Pad'])
                        uv = uT[:].rearrange("p (n j) -> p n j", j=16)
                        P.cp(Abr[:], Ar[:, gb * 4:gb * 4 + 4, :], ['A'], ['Ab'])
                        P.cp(Abi[:], Ai[:, gb * 4:gb * 4 + 4, :], ['A'], ['Ab'], eng='act')
                        for (t0, w) in otiles:
                            py = (py0, py1)[npy % 2]; pyk = 'py%d' % (npy % 2); ysb = (ysb0, ysb1)[npy % 2]; ysk = 'ysb%d' % (npy % 2); npy += 1
                            n0 = t0 // SB; nbt = w // SB
                            pv = py[:, 0:w].rearrange("p (n j) -> p n j", j=16)
                            runs = []
                            if d == 0:
                                runs.append((n0, nbt, n0))
                            else:
                                a0 = n0; a1 = min(n0 + nbt, NBc)
                                if a1 > a0:
                                    runs.append((a0, a1 - a0, a0 + 1))
                                b0 = max(n0, NBc); b1 = n0 + nbt
                                if b1 > b0:
                                    runs.append((b0, b1 - b0, b0 + 2))
                            mms = []
                            for tau in range(16):
                                if d == 0:
                                    mms.append((pv[:, :, tau:16], KtL[:, gb, tau, :], uv[:, n0:n0 + nbt, 0:16 - tau]))
                                else:
                                    mms.append((pv[:, :, 0:16 - tau], KtL[:, gb, tau, :], uv[:, n0:n0 + nbt, tau:16]))
                            for q in range(4):
                                pair = gb * 4 + q
                                for i in range(16):
                                    for x, Ax in ((0, Abr), (1, Abi)):
                                        for (r0, rn, p0) in runs:
                                            mms.append((pv[:, r0 - n0:r0 - n0 + rn, i], Pad[:, q, i, x, :], Ax[:, q, p0:p0 + rn]))
                            for mi, (o_, l_, r_) in enumerate(mms):
                                P.mm(o_, l_, r_, mi == 0, mi == len(mms) - 1, ['KtL', 'Pad', 'Ab', ('uT', 0)], [pyk])
                            yrow = T['yfT'][gb * 128:(gb + 1) * 128, t0:t0 + w]
                            if d == 0:
                                P.cp(ysb[:, :w], py[:, :w], [pyk], [ysk], eng='act')
                                P.dma('act', yrow, ysb[:, :w], reads=[ysk], writes=[('yfT', npy)])
                            else:
                                P.dma('sp', yf_t[:, :w], yrow, writes=['yf_t'])
                                P.tt(ysb[:, :w], py[:, :w], yf_t[:, :w], ALU.add, [pyk, 'yf_t'], [ysk])
                                P.stt(ysb[:, :w], uT[:, t0:t0 + w], dT[:, gb:gb + 1], ysb[:, :w], ALU.mult, ALU.add, [('uT', 0), ysk, 'dT'], [ysk])
                                P.tt(y2[:, :w], ysb[:, :w], ysb[:, :w], ALU.mult, [ysk], ['y2'])
                                P.ts(y2[:, :w], y2[:, :w], 0.044715, ALU.mult, ['y2'], ['y2'], s2=1.0, op1=ALU.add)
                                P.tt(y2[:, :w], y2[:, :w], ysb[:, :w], ALU.mult, ['y2', ysk], ['y2'])
                                P.act(y2[:, :w], y2[:, :w], AF.Sigmoid, ['y2'], ['y2'], scale=1.5957691216057308)
                                P.tt(zb[:, :w], y2[:, :w], ysb[:, :w], ALU.mult, ['y2', ysk], ['zb'])
                                P.dma('act', T['zT'][gb * 128:(gb + 1) * 128, t0:t0 + w], zb[:, :w], reads=['zb'], writes=[('zT', npy)])
                        P.barrier()
                    P.barrier()
        if getattr(c, 's5stop', 9) <= 4:
            return
        with ExitStack() as es:
            def mk(nm, shape, dt=F32):
                return es.enter_context(sb(nm, shape, dt))
            Wg = mk("Wg", [128, 8, 1024], BF16); bg = mk("bg", [128, 8])
            z0 = mk("z0", [128, 8, 512], BF16); z1 = mk("z1", [128, 8, 512], BF16); g0 = mk("g0", [128, 8, 512], BF16); g1 = mk("g1", [128, 8, 512], BF16)
            sgt = mk("sgt", [128, 512]); sg2 = mk("sg2", [128, 512]); ob0 = mk("obx0", [128, 8, 512], BF16); ob1 = mk("obx1", [128, 8, 512], BF16)
            pg0 = es.enter_context(ps("pg0", [128, 512], F32)); pg1 = es.enter_context(ps("pg1", [128, 512], F32))
            P.dma('pool', Wg[:], T['w_glu'][layer].rearrange("(k p) c -> p k c", p=128), writes=['Wg'])
            P.dma('sp', bg[:], T['b_gluT'][layer], writes=['bg'])
            zv = T['zT'].rearrange("(k p) t -> p k t", p=128)
            gv_ = T['fmb'][3072:4096, :].rearrange("(k p) t -> p k t", p=128)
            ovs = T['oT'][2048:3072, :].rearrange("(k p) t -> p k t", p=128)
            npg = 0
            for ti, (t0, w) in enumerate(otiles):
                zt = (z0, z1)[ti % 2]; gt_ = (g0, g1)[ti % 2]; obx = (ob0, ob1)[ti % 2]; kz = ('z', ti % 2); kg = ('g', ti % 2); ko = ('obx', ti % 2)
                P.dma('sp', zt[:, :, :w], zv[:, :, t0:t0 + w], writes=[kz])
                P.dma('sp', gt_[:, :, :w], gv_[:, :, t0:t0 + w], writes=[kg])
                for oc in range(8):
                    pg = (pg0, pg1)[npg % 2]; pgk = 'pg%d' % (npg % 2); npg += 1
                    for k in range(8):
                        P.mm(pg[:, :w], Wg[:, k, oc * 128:(oc + 1) * 128], zt[:, k, :w], k == 0, k == 7, ['Wg', kz], [pgk])
                    P.act(sgt[:, :w], pg[:, :w], AF.Sigmoid, [pgk, 'bg'], ['sgt'], bias=bg[:, oc:oc + 1])
                    P.tt(sgt[:, :w], sgt[:, :w], zt[:, oc, :w], ALU.mult, ['sgt', kz], ['sgt'])
                    P.act(sg2[:, :w], gt_[:, oc, :w], AF.Silu, [kg], ['sg2'])
                    P.tt(obx[:, oc, :w], sgt[:, :w], sg2[:, :w], ALU.mult, ['sgt', 'sg2'], [ko])
                P.dma('act', ovs[:, :, t0:t0 + w], obx[:, :, :w], reads=[ko], writes=[('oTs', ti)])
            P.barrier()

    def sincos(self, x, shape, sin_out, cos_out, tag):
        P = self.P
        with self.sb(tag + "_ni", shape, I32) as ni, self.sb(tag + "_nf", shape, F32) as nf, \
                self.sb(tag + "_r", shape, F32) as r, self.sb(tag + "_m", shape, F32) as m:
            for which, out in ((0, sin_out), (1, cos_out)):
                kx = tag + 'x'
                if which == 1:
                    P.ts(r[:], x, PI / 2, ALU.add, [kx], [tag + 'r0'])
                    src = r[:]
                else:
                    P.cp(r[:], x, [kx], [tag + 'r0'])
                    src = r[:]
                P.ts(ni[:], src, 1.0 / TWO_PI, ALU.mult, [tag + 'r0'], [tag + 'ni'])
                P.cp(nf[:], ni[:], [tag + 'ni'], [tag + 'nf'])
                P.stt(r[:], nf[:], -TWO_PI, src, ALU.mult, ALU.add, [tag + 'nf', tag + 'r0'], [tag + 'r0'])
                P.ts(m[:], r[:], PI, ALU.is_gt, [tag + 'r0'], [tag + 'm'], s2=TWO_PI, op1=ALU.mult)
                P.tt(r[:], r[:], m[:], ALU.subtract, [tag + 'r0', tag + 'm'], [tag + 'r0'])
                P.ts(m[:], r[:], -PI, ALU.is_lt, [tag + 'r0'], [tag + 'm'], s2=TWO_PI, op1=ALU.mult)
                P.tt(r[:], r[:], m[:], ALU.add, [tag + 'r0', tag + 'm'], [tag + 'r0'])
                P.act(out, r[:], AF.Sin, [tag + 'r0'], [tag + 'out%d' % which])
            P.barrier()

    def phase_rope(self):
        c = self.cfg; P = self.P; T = self.T; L = c.L
        NT = L // 128
        with self.sb("fi", [128, 32], I32) as fi, self.sb("fr", [128, 32], F32) as fr, self.sb("pp", [128, NT, 2], F32) as pp, \
                self.sb("ang", [128, NT * 64], F32) as ang, self.sb("sn", [128, NT * 64], F32) as sn, self.sb("cs", [128, NT * 64], F32) as cs, \
                self.sb("rp", [128, NT, 512], F32) as rp:
            P.op('pool', lambda e: e.iota(fi[:], pattern=[[1, 32]], base=0, channel_multiplier=0), [], ['fi'])
            P.cp(fr[:], fi[:], ['fi'], ['fr'])
            P.act(fr[:], fr[:], AF.Exp, ['fr'], ['fr'], scale=-float(np.log(10000.0)) / 32.0)
            P.dma('sp', pp[:], T['pos'].rearrange("(t p) c -> p t c", p=128), writes=['pp'])
            angv = ang[:].rearrange("p (t x) -> p t x", x=64)
            frb = fr[:].unsqueeze(1).to_broadcast([128, NT, 32])
            P.tt(angv[:, :, 0:32], frb, pp[:, :, 0:1].to_broadcast([128, NT, 32]), ALU.mult, ['pp', 'fr'], ['rpx'])
            P.tt(angv[:, :, 32:64], frb, pp[:, :, 1:2].to_broadcast([128, NT, 32]), ALU.mult, ['pp', 'fr'], ['rpx'])
            self.sincos(ang[:], [128, NT * 64], sn[:], cs[:], 'rp')
            csv = cs[:].rearrange("p (t x) -> p t x", x=64); snv = sn[:].rearrange("p (t x) -> p t x", x=64)
            for h in range(4):
                P.cp(rp[:, :, h * 64:(h + 1) * 64], csv, [], ['rp'])
                P.cp(rp[:, :, 256 + h * 64:256 + (h + 1) * 64], snv, [], ['rp'], eng='pool')
            P.dma('sp', T['rope'].rearrange("(t p) c -> p t c", p=128), rp[:], reads=['rp'], writes=['ropeD'])
            P.barrier()

    def build(self):
        c = self.cfg; nc = self.nc; P = self.P; T = None
        self.declare(); T = self.T
        stack = []
        with nc.Block() as block:
            P.start()
            self.consts(stack)
            skip_pre = bool(getattr(c, 'inject', ()))
            if not skip_pre:
                self.phase_rope()
            hs = [T['hT0'], T['hT1'], T['hT2']]
            for layer in range(c.DEPTH):
                last = (layer == c.DEPTH - 1)
                if not getattr(c, 'noada', False):
                    self.phase_ada(layer)
                if c.stop == 'ada':
                    break
                if not skip_pre:
                    self.phase_norm(hs[layer], 'hn', T['hnT'])
                    if c.stop == 'norm':
                        break
                    self.phase_inproj(layer, last)
                if c.stop == 'inproj':
                    break
                for name in ('hg', 'gl', 'rt'):
                    if c.stop is None or name in c.stop:
                        if name == 'hg' or getattr(c, 'oldgla', False):
                            self.phase_gla(layer, name, last)
                        else:
                            self.phase_gla2(layer, name, last)
                if c.stop is None or 's5' in c.stop:
                    self.phase_s5(layer, last)
                if c.stop is not None and 'out' not in c.stop:
                    break
                self.phase_outproj(layer, hs[layer], hs[layer + 1], last)
            if c.stop is None or 'out' in c.stop:
                self.phase_norm(hs[c.DEPTH], 'final', T['outT'])
            P.barrier()
            for cm in reversed(stack):
                cm.__exit__(None, None, None)
            P.finish()
        return nc


TM_COLS = np.concatenate([np.arange(0, 4096), np.arange(5120, 7168), np.arange(10272, 12320)])
FM_COLS = np.concatenate([np.arange(4096, 5120), np.arange(7200, 8224), np.arange(8224, 9248), np.arange(9248, 10272),
                          np.arange(12320, 13344), np.arange(7168, 7200)])


def fmT(v, nchunk):
    return np.ascontiguousarray(np.asarray(v, np.float32).reshape(nchunk, 128).T)


def gla_consts():
    C = 64; s = 32; f = np.float32
    out = np.zeros((2, 64, 449), f)
    j = np.arange(64)[:, None]; i = np.arange(64)[None, :]
    for d in (0, 1):
        T = (j <= i) if d == 0 else (j >= i)
        blk = i // s
        m = blk * s + s // 2
        if d == 0:
            QO = (j >= blk * s) & (j <= i)
            Tm = (j <= m)
            I = 1
            KO = (i < I * s) & (j > i) & (j <= I * s - 1)
        else:
            QO = (j <= blk * s + s - 1) & (j >= i)
            Tm = (j >= m)
            I = 0
            KO = (i >= (I + 1) * s) & (j >= (I + 1) * s) & (j < i)
        QD = T.astype(f) - Tm.astype(f)
        M = out[d]
        M[:, 0:64] = QO; M[:, 64:128] = QD; M[:, 128:192] = -QD; M[:, 192:256] = KO
        M[:, 256:320] = T; M[:, 320] = 1.0
        M[:, 321:385] = (j > i) if d == 0 else (j < i)
        M[:, 385:449] = T
    return out


def gla_consts2():
    C = 128; r = 64; f = np.float32
    out = np.zeros((2, C, 513), f)
    j = np.arange(C)[:, None]; i = np.arange(C)[None, :]
    for d in (0, 1):
        T = (j <= i) if d == 0 else (j >= i)
        R = (j <= r) if d == 0 else (j >= r)
        M = out[d]
        M[:, 0:128] = T.astype(f) - (R & (i >= 0)).astype(f)
        M[:, 128:256] = T; M[:, 256] = 1.0
        M[:, 257:385] = (j > i) if d == 0 else (j < i)
        M[:, 385:513] = T
    return out


def prep_shared(cfg, inp, flip=False):
    DP = cfg.DEPTH; f = np.float32
    sh = {}
    sh['w_ada'] = np.ascontiguousarray(inp['w_ada'][:DP], f)
    sh['b_adaT'] = np.stack([fmT(inp['b_ada'][l], 96) for l in range(DP)])
    sh['norm_gT'] = np.stack([fmT(inp['norm_g'][l], 32) for l in range(DP)])
    sh['final_gT'] = fmT(inp['final_norm_g'], 32)
    w_in = np.asarray(inp['w_in'][:DP], f)
    sh['w_tm'] = np.ascontiguousarray(w_in[:, :, TM_COLS])
    sh['w_fm'] = np.ascontiguousarray(w_in[:, :, FM_COLS])
    sh['w_out'] = np.ascontiguousarray(inp['w_out'][:DP], f)
    sh['w_glu'] = np.ascontiguousarray(inp['s5_w_glu'][:DP], f)
    sh['b_gluT'] = np.stack([fmT(inp['s5_b_glu'][l], 8) for l in range(DP)])
    lb = np.asarray(inp['hgrn_lb_logits'], f)
    sh['hg_lbrep'] = np.ascontiguousarray(np.broadcast_to(lb[None], (128, 2, 2, 1024)))
    sh['hg_ngT'] = np.asarray(inp['hgrn_norm_g'][:DP], f).reshape(DP, 128, 1).copy()
    wg = np.concatenate([np.asarray(inp['gla_w_gk'][:DP], f), np.asarray(inp['gla_b_gk'][:DP], f)[:, :, None, :]], axis=2)
    sh['gla_wg'] = np.ascontiguousarray(wg)
    sh['gla_ngT'] = np.stack([fmT(inp['gla_norm_g'][l], 2) for l in range(DP)])
    sh['ret_ngT'] = np.stack([fmT(inp['ret_norm_g'][l], 2) for l in range(DP)])
    rl = np.asarray(inp['ret_decay_logit'][:DP], f)
    sh['ret_lrep'] = np.ascontiguousarray(np.broadcast_to(rl[:, :, None, :], (DP, 2, 128, 4)))
    L = cfg.L
    pos = np.zeros((L, 2), f)
    t = np.arange(cfg.LAT)
    pos[cfg.CTX:, 0] = t // 64; pos[cfg.CTX:, 1] = t % 64
    if flip:
        pos[cfg.CTX:] = pos[cfg.CTX:][::-1].copy()
    sh['pos'] = pos
    sh['gl_consts'] = gla_consts()
    sh['gl_consts2'] = gla_consts2()
    lam = np.stack([np.asarray(inp['s5_lam_re'][:DP], f), np.asarray(inp['s5_lam_im'][:DP], f)], axis=2)
    dt = np.asarray(inp['s5_log_dt'][:DP], f)
    B = np.stack([np.asarray(inp['s5_b_re'][:DP], f), np.asarray(inp['s5_b_im'][:DP], f)], axis=2)
    Cm = np.stack([np.asarray(inp['s5_c_re'][:DP], f), np.asarray(inp['s5_c_im'][:DP], f)], axis=2)
    sh['s5_lamP'] = np.ascontiguousarray(lam.reshape(DP, 2, 2, 32, 2, 64).transpose(0, 1, 2, 4, 5, 3).reshape(DP, 2, 2, 128, 32))
    dtb = np.broadcast_to(dt[:, :, :, None], (DP, 2, 64, 64))
    sh['s5_dtP'] = np.ascontiguousarray(dtb.reshape(DP, 2, 32, 2, 64).transpose(0, 1, 3, 4, 2).reshape(DP, 2, 128, 32))
    sh['s5_lam64'] = np.ascontiguousarray(lam.transpose(0, 1, 2, 4, 3))
    sh['s5_dt64'] = np.ascontiguousarray(dtb.transpose(0, 1, 3, 2))
    sh['s5_b64'] = np.ascontiguousarray(B.transpose(0, 1, 2, 4, 3, 5))
    sh['s5_c64'] = np.ascontiguousarray(Cm.transpose(0, 1, 2, 5, 3, 4))
    lamF = np.broadcast_to(lam.reshape(DP, 2, 2, 8, 8, 1, 64), (DP, 2, 2, 8, 8, 16, 64))
    sh['s5_lamF'] = np.ascontiguousarray(lamF.transpose(0, 1, 2, 4, 5, 3, 6).reshape(DP, 2, 2, 128, 8, 64))
    dtF = np.broadcast_to(dt.reshape(DP, 2, 8, 8, 1, 1), (DP, 2, 8, 8, 16, 64))
    sh['s5_dtF'] = np.ascontiguousarray(dtF.transpose(0, 1, 3, 4, 2, 5).reshape(DP, 2, 128, 8, 64))
    BF = B.reshape(DP, 2, 2, 8, 8, 64, 16)
    sh['s5_bF'] = np.ascontiguousarray(BF.transpose(0, 1, 2, 4, 6, 3, 5).reshape(DP, 2, 2, 128, 8, 64))
    CP = Cm.reshape(DP, 2, 2, 32, 2, 16, 64)
    sh['s5_cP'] = np.ascontiguousarray(CP.transpose(0, 1, 2, 4, 6, 3, 5).reshape(DP, 2, 2, 128, 32, 16))
    dd = np.asarray(inp['s5_d'][:DP], f).reshape(DP, 8, 128)
    sh['s5_dT'] = np.ascontiguousarray(dd.transpose(0, 2, 1))
    row = np.arange(128)[:, None, None]; qq = np.arange(4)[None, :, None]; col = np.arange(128)[None, None, :]
    sh['s5_maskQ'] = ((row // 32 == qq) & ((row % 32) // 16 == col // 64)).astype(f)
    sh['s5_maskC'] = np.ascontiguousarray(np.broadcast_to((np.arange(128)[:, None, None] // 64 == np.arange(2)[None, :, None]), (128, 2, 16))).astype(f)
    return sh


def swap_dirs(inp):
    o = dict(inp)
    w = np.array(inp['w_in'], np.float32, copy=True)
    w[:, :, 1024:2048] = inp['w_in'][:, :, 2048:3072]; w[:, :, 2048:3072] = inp['w_in'][:, :, 1024:2048]
    w[:, :, 7168:7184] = inp['w_in'][:, :, 7184:7200]; w[:, :, 7184:7200] = inp['w_in'][:, :, 7168:7184]
    o['w_in'] = w
    for k in ('hgrn_lb_logits', 'gla_w_gk', 'gla_b_gk', 's5_lam_re', 's5_lam_im', 's5_log_dt', 's5_b_re', 's5_b_im',
              's5_c_re', 's5_c_im', 'ret_decay_logit'):
        o[k] = np.ascontiguousarray(np.asarray(inp[k])[:, ::-1])
    return o


def prep_core(cfg, inp, b, flip=False):
    f = np.float32
    m = {}
    cx = np.asarray(inp['ctx'][b], f); xx = np.asarray(inp['x'][b], f)
    if flip:
        cx = cx[::-1]; xx = xx[::-1]
    h0 = np.concatenate([cx, xx], axis=0)
    m['hT0'] = np.ascontiguousarray(h0.T)
    c2 = np.stack([np.asarray(inp['c'][b], f), np.asarray(inp['c_ctx'], f)], axis=0)
    m['c2T'] = np.ascontiguousarray(c2.reshape(2, 32, 128).transpose(2, 1, 0))
    return m


_CACHE = {}


def run_pairs(cfg, inputs, nb):
    key = (cfg.CTX, cfg.LAT, cfg.DEPTH)
    if key not in _CACHE:
        _CACHE[key] = Builder(cfg).build()
    nc = _CACHE[key]
    sh0 = prep_shared(cfg, inputs, False)
    sh1 = prep_shared(cfg, swap_dirs(inputs), True)
    for k in sh0:
        if k in sh1 and sh0[k].shape == sh1[k].shape and sh0[k].nbytes > (1 << 20) and np.array_equal(sh0[k], sh1[k]):
            sh1[k] = sh0[k]
    in_maps = []
    for core in range(2 * nb):
        flip = core >= nb
        m = dict(sh1 if flip else sh0)
        m.update(prep_core(cfg, inputs, core % nb, flip))
        in_maps.append(m)
    res = run_bass_kernel_spmd(nc, in_maps, core_ids=list(range(2 * nb)))
    outs = []
    for b in range(nb):
        first = np.asarray(res.results[b]['outT']).T
        second = np.asarray(res.results[b + nb]['outT']).T[::-1]
        outs.append(np.concatenate([first, second], axis=0))
    return np.stack(outs, axis=0).astype(np.float32)


def kernel(**inputs):
    cfg = Cfg()
    cfg.half = True
    return run_pairs(cfg, inputs, 4)
```
